# Optimizing a Trainium2 kernel written in Bass

```python
import math
import jax
import jax.numpy as jnp
from jax import lax
import numpy as np

D_MODEL = 2048
BATCH = 4
SEQ = 2048
DEPTH = 2

CTX_LEN = 256
GRID_W = 64
EPS = 1e-6

SSD_HEADS = 16
SSD_HEAD_DIM = 64
SSD_INNER = SSD_HEADS * SSD_HEAD_DIM
SSD_GROUPS = 4
SSD_HPG = SSD_HEADS // SSD_GROUPS
SSD_STATE = 128
SSD_CONV = 5
SSD_CHUNK = 128
SSD_BC = SSD_GROUPS * SSD_STATE
SSD_CONV_CH = SSD_INNER + 2 * SSD_BC
SSD_IN = SSD_INNER + SSD_CONV_CH + SSD_HEADS

MLA_HEADS = 16
MLA_Q_RANK = 512
MLA_KV_RANK = 256
MLA_NOPE = 64
MLA_ROPE = 32
MLA_V = 64
MLA_INNER = MLA_HEADS * MLA_V
MLA_IN = MLA_Q_RANK + MLA_KV_RANK + MLA_ROPE
ATTN_BLOCK = 128
ROPE_BASE = 10000.0

S5_INNER = 1024
S5_GROUP = 16
S5_GROUPS = S5_INNER // S5_GROUP
S5_STATE = 64

N_BRANCH = 3
GATE_IN = N_BRANCH * D_MODEL
PROJ_IN = SSD_IN + MLA_IN + S5_INNER + GATE_IN

N_EXPERTS = 16
EXPERT_FF = 2048
CAPACITY_FACTOR = 2

kernel_name = 'hybrid_ssd_mla_s5_ec_dit'


def rms_norm(x, gain):
    xf = x.astype(jnp.float32)
    y = xf * lax.rsqrt(jnp.mean(xf * xf, axis=-1, keepdims=True) + EPS)
    return (y * gain.astype(jnp.float32)).astype(x.dtype)


def _flip(t):
    return jnp.flip(t, axis=1)


def _ident(t):
    return t


def dw_conv(x, w, b):
    k = w.shape[0]
    pad = (k - 1) // 2
    y = lax.conv_general_dilated(x, w[:, None, :].astype(x.dtype), window_strides=(1,),
                                 padding=[(pad, pad)], dimension_numbers=('NWC', 'WIO', 'NWC'),
                                 feature_group_count=x.shape[-1])
    return y + b.astype(x.dtype)


def axial_rope_tables(n_rows):
    n_freq = MLA_ROPE // 4
    inv = ROPE_BASE ** (-jnp.arange(n_freq, dtype=jnp.float32) / n_freq)
    rows = jnp.repeat(jnp.arange(n_rows, dtype=jnp.float32), GRID_W)
    cols = jnp.tile(jnp.arange(GRID_W, dtype=jnp.float32), n_rows)
    ang = jnp.concatenate([rows[:, None] * inv, cols[:, None] * inv], axis=-1)
    return jnp.cos(ang), jnp.sin(ang)


def apply_rope(x, cos, sin):
    x1 = x[..., 0::2]
    x2 = x[..., 1::2]
    cos = cos.astype(x.dtype)
    sin = sin.astype(x.dtype)
    return jnp.stack([x1 * cos - x2 * sin, x1 * sin + x2 * cos], axis=-1).reshape(x.shape)


def ssd_chunked(x, dt, a_neg, bm, cm, h0):
    b, l, g, hg, p = x.shape
    n = bm.shape[-1]
    nc = l // SSD_CHUNK
    q = SSD_CHUNK
    xf = (x * dt[..., None]).reshape(b, nc, q, g, hg, p)
    a = (dt * a_neg).reshape(b, nc, q, g, hg)
    bc = bm.reshape(b, nc, q, g, n)
    cc = cm.reshape(b, nc, q, g, n)
    acs = jnp.cumsum(a, axis=2)
    diff = acs[:, :, :, None] - acs[:, :, None, :]
    mask = jnp.tril(jnp.ones((q, q), bool))[:, :, None, None]
    decay = jnp.exp(jnp.where(mask, diff, -jnp.inf))
    scores = jnp.einsum('bclgn,bcsgn->bclsg', cc, bc)
    y_diag = jnp.einsum('bclsgh,bcsghp->bclghp', scores[..., None] * decay, xf)
    xw = xf * jnp.exp(acs[:, :, -1:] - acs)[..., None]
    states = jnp.einsum('bclgn,bclghp->bcghpn', bc, xw)
    chunk_decay = jnp.exp(acs[:, :, -1])

    def step(h, inp):
        s, d = inp
        return h * d[..., None, None] + s, h

    h_final, h_prev = lax.scan(step, h0, (jnp.moveaxis(states, 1, 0), jnp.moveaxis(chunk_decay, 1, 0)))
    h_prev = jnp.moveaxis(h_prev, 0, 1)
    y_off = jnp.einsum('bclgn,bcghpn->bclghp', cc, h_prev) * jnp.exp(acs)[..., None]
    return (y_diag + y_off).reshape(b, l, g, hg, p), h_final


def ssd_mixer(px, pc, conv_w, conv_b, a_log, dt_bias, d_skip, norm_gain, need_ctx):
    f32 = jnp.float32

    def prep(p):
        b, l, _ = p.shape
        z = p[..., :SSD_INNER]
        xbc = jax.nn.silu(dw_conv(p[..., SSD_INNER:SSD_INNER + SSD_CONV_CH], conv_w, conv_b)).astype(f32)
        xs = xbc[..., :SSD_INNER].reshape(b, l, SSD_GROUPS, SSD_HPG, SSD_HEAD_DIM)
        bm = xbc[..., SSD_INNER:SSD_INNER + SSD_BC].reshape(b, l, SSD_GROUPS, SSD_STATE)
        cm = xbc[..., SSD_INNER + SSD_BC:].reshape(b, l, SSD_GROUPS, SSD_STATE)
        dt = p[..., SSD_INNER + SSD_CONV_CH:].astype(f32).reshape(b, l, SSD_GROUPS, SSD_HPG)
        return z, xs, dt, bm, cm

    zx, xx, dtx, bx, cx = prep(px)
    zc, xc, dtc, bc, cc = prep(pc)
    b = px.shape[0]
    yx = jnp.zeros_like(xx)
    yc = jnp.zeros_like(xc)
    for d in range(2):
        flip = _flip if d == 1 else _ident
        a_neg = -jnp.exp(a_log[d].astype(f32)).reshape(SSD_GROUPS, SSD_HPG)
        bias = dt_bias[d].astype(f32).reshape(SSD_GROUPS, SSD_HPG)
        dsk = d_skip[d].astype(f32).reshape(SSD_GROUPS, SSD_HPG)[..., None]
        dt_c = jax.nn.softplus(dtc + bias)
        dt_x = jax.nn.softplus(dtx + bias)
        h0 = jnp.zeros((b, SSD_GROUPS, SSD_HPG, SSD_HEAD_DIM, SSD_STATE), f32)
        y_c, h_c = ssd_chunked(flip(xc), flip(dt_c), a_neg, flip(bc), flip(cc), h0)
        y_x, _ = ssd_chunked(flip(xx), flip(dt_x), a_neg, flip(bx), flip(cx), h_c)
        yx = yx + flip(y_x) + dsk * xx
        yc = yc + flip(y_c) + dsk * xc

    def finish(y, z):
        y = y.reshape(z.shape) * jax.nn.silu(z.astype(f32))
        return rms_norm(y, norm_gain).astype(z.dtype)

    return finish(yx, zx), (finish(yc, zc) if need_ctx else None)


def block_attention(q_nope, q_pe, k_nope, k_pe, v):
    b, lq, h, _ = q_nope.shape
    nb = lq // ATTN_BLOCK
    scale = (MLA_NOPE + MLA_ROPE) ** -0.5

    def blocks(t):
        return jnp.moveaxis(t.reshape((b, nb, ATTN_BLOCK) + t.shape[2:]), 1, 0)

    def one(qs):
        qn, qp = qs
        s = jnp.einsum('bqhd,bkhd->bhqk', qn, k_nope) + jnp.einsum('bqhr,bkr->bhqk', qp, k_pe)
        p = jax.nn.softmax(s.astype(jnp.float32) * scale, axis=-1).astype(v.dtype)
        return jnp.einsum('bhqk,bkhd->bqhd', p, v)

    o = lax.map(one, (blocks(q_nope), blocks(q_pe)))
    return jnp.moveaxis(o, 0, 1).reshape(b, lq, h * v.shape[-1])


def mla_mixer(px, pc, q_a_gain, kv_a_gain, w_q_b, w_kv_b, q_gain, k_gain, cos, sin, need_ctx):
    def prep(p):
        b, l, _ = p.shape
        q_a = p[..., :MLA_Q_RANK]
        kv_a = p[..., MLA_Q_RANK:MLA_Q_RANK + MLA_KV_RANK]
        k_pe = p[..., MLA_Q_RANK + MLA_KV_RANK:]
        q = (rms_norm(q_a, q_a_gain) @ w_q_b).reshape(b, l, MLA_HEADS, MLA_NOPE + MLA_ROPE)
        kv = (rms_norm(kv_a, kv_a_gain) @ w_kv_b).reshape(b, l, MLA_HEADS, MLA_NOPE + MLA_V)
        q_nope = rms_norm(q[..., :MLA_NOPE], q_gain[:MLA_NOPE])
        q_pe = rms_norm(q[..., MLA_NOPE:], q_gain[MLA_NOPE:])
        k_nope = rms_norm(kv[..., :MLA_NOPE], k_gain[:MLA_NOPE])
        k_pe = rms_norm(k_pe, k_gain[MLA_NOPE:])
        return q_nope, q_pe, k_nope, k_pe, kv[..., MLA_NOPE:]

    qn_x, qp_x, kn_x, kp_x, v_x = prep(px)
    qp_x = apply_rope(qp_x, cos[:, None], sin[:, None])
    kp_x = apply_rope(kp_x, cos, sin)
    qn_c, qp_c, kn_c, kp_c, v_c = prep(pc)
    out_x = block_attention(qn_x, qp_x, jnp.concatenate([kn_x, kn_c], axis=1),
                            jnp.concatenate([kp_x, kp_c], axis=1), jnp.concatenate([v_x, v_c], axis=1))
    out_c = block_attention(qn_c, qp_c, kn_c, kp_c, v_c) if need_ctx else None
    return out_x, out_c


def s5_discretise(lam_re, lam_im, log_dt, b_re, b_im):
    f32 = jnp.float32
    lam_re = lam_re.astype(f32)
    lam_im = lam_im.astype(f32)
    dt = jnp.exp(log_dt.astype(f32))[:, None]
    mag = jnp.exp(lam_re * dt)
    ab_re = mag * jnp.cos(lam_im * dt)
    ab_im = mag * jnp.sin(lam_im * dt)
    den = lam_re * lam_re + lam_im * lam_im
    nr = ab_re - 1.0
    f_re = ((nr * lam_re + ab_im * lam_im) / den)[..., None]
    f_im = ((ab_im * lam_re - nr * lam_im) / den)[..., None]
    b_re = b_re.astype(f32)
    b_im = b_im.astype(f32)
    return ab_re, ab_im, f_re * b_re - f_im * b_im, f_re * b_im + f_im * b_re


def _complex_affine_combine(e1, e2):
    a1r, a1i, b1r, b1i = e1
    a2r, a2i, b2r, b2i = e2
    return (a2r * a1r - a2i * a1i, a2r * a1i + a2i * a1r,
            a2r * b1r - a2i * b1i + b2r, a2r * b1i + a2i * b1r + b2i)


def s5_scan(u, ab_re, ab_im, bb_re, bb_im, h0_re, h0_im):
    bu_re = jnp.einsum('blgs,gps->blgp', u, bb_re)
    bu_im = jnp.einsum('blgs,gps->blgp', u, bb_im)
    bu_re = bu_re.at[:, 0].add(ab_re * h0_re - ab_im * h0_im)
    bu_im = bu_im.at[:, 0].add(ab_re * h0_im + ab_im * h0_re)
    l = u.shape[1]
    a_re = jnp.broadcast_to(ab_re, (1, l) + ab_re.shape)
    a_im = jnp.broadcast_to(ab_im, (1, l) + ab_im.shape)
    _, _, xr, xi = lax.associative_scan(_complex_affine_combine, (a_re, a_im, bu_re, bu_im), axis=1)
    return xr, xi


def s5_readout(xr, xi, c_re, c_im):
    return (jnp.einsum('blgp,gsp->blgs', xr, c_re.astype(jnp.float32))
            - jnp.einsum('blgp,gsp->blgs', xi, c_im.astype(jnp.float32)))


def s5_mixer(ux, uc, lam_re, lam_im, log_dt, b_re, b_im, c_re, c_im, d_skip, w_glu, need_ctx):
    f32 = jnp.float32
    b, l, _ = ux.shape
    lc = uc.shape[1]
    ugx = ux.astype(f32).reshape(b, l, S5_GROUPS, S5_GROUP)
    ugc = uc.astype(f32).reshape(b, lc, S5_GROUPS, S5_GROUP)
    yx = jnp.zeros_like(ugx)
    yc = jnp.zeros_like(ugc)
    for d in range(2):
        flip = _flip if d == 1 else _ident
        ab_re, ab_im, bb_re, bb_im = s5_discretise(lam_re[d], lam_im[d], log_dt[d], b_re[d], b_im[d])
        h0 = jnp.zeros((b, S5_GROUPS, S5_STATE), f32)
        sc_re, sc_im = s5_scan(flip(ugc), ab_re, ab_im, bb_re, bb_im, h0, h0)
        sx_re, sx_im = s5_scan(flip(ugx), ab_re, ab_im, bb_re, bb_im, sc_re[:, -1], sc_im[:, -1])
        yx = yx + flip(s5_readout(sx_re, sx_im, c_re[d], c_im[d]))
        if need_ctx:
            yc = yc + flip(s5_readout(sc_re, sc_im, c_re[d], c_im[d]))

    def finish(y, u):
        y = y.reshape(u.shape) + d_skip.astype(f32) * u.astype(f32)
        y = jax.nn.gelu(y)
        y = y * jax.nn.sigmoid(y @ w_glu.astype(f32))
        return y.astype(u.dtype)

    return finish(yx, ux), (finish(yc, uc) if need_ctx else None)


def merge_branches(y_ssd, y_mla, y_s5, gate_logits, w_b_ssd, w_b_mla, w_b_s5, w_o):
    g = jax.nn.sigmoid(gate_logits.astype(jnp.float32)).astype(y_ssd.dtype)
    g_ssd, g_mla, g_s5 = jnp.split(g, N_BRANCH, axis=-1)
    m = g_ssd * (y_ssd @ w_b_ssd) + g_mla * (y_mla @ w_b_mla) + g_s5 * (y_s5 @ w_b_s5)
    return m @ w_o


def ec_moe(h, w_router, w_gate, w_up, w_down):
    b, n, _ = h.shape
    cap = CAPACITY_FACTOR * n // N_EXPERTS
    aff = jax.nn.softmax(jnp.einsum('bnd,de->bne', h, w_router).astype(jnp.float32), axis=-1)
    g, idx = lax.top_k(jnp.swapaxes(aff, 1, 2), cap)
    bidx = jnp.arange(b)[:, None, None]
    xs = h[bidx, idx]
    hid = jax.nn.silu(jnp.einsum('becd,edf->becf', xs, w_gate)) * jnp.einsum('becd,edf->becf', xs, w_up)
    ys = jnp.einsum('becf,efd->becd', hid, w_down) * g[..., None].astype(h.dtype)
    return jnp.zeros_like(h).at[bidx, idx].add(ys)


def setup_inputs(seed: int = 0) -> dict:
    key = jax.random.key(seed)
    keys = jax.random.split(key, 64)
    counter = [0]

    def nk():
        k = keys[counter[0]]
        counter[0] += 1
        return k

    def nrm(shape, scale):
        return scale * jax.random.normal(nk(), shape, jnp.float32)

    def gain(shape):
        return 1.0 + nrm(shape, 0.02)

    def unif(shape, lo, hi):
        return jax.random.uniform(nk(), shape, jnp.float32, lo, hi)

    dt0 = jnp.exp(unif((DEPTH, 2, SSD_HEADS), math.log(1e-3), math.log(1e-1)))
    n_idx = jnp.arange(S5_STATE, dtype=jnp.float32)
    inp = {}
    inp['x'] = nrm((BATCH, SEQ, D_MODEL), 1.0)
    inp['c'] = nrm((BATCH, D_MODEL), 1.0)
    inp['ctx'] = nrm((BATCH, CTX_LEN, D_MODEL), 1.0)
    inp['c_ctx'] = nrm((D_MODEL,), 1.0)
    inp['norm1_gain'] = gain((DEPTH, D_MODEL))
    inp['norm2_gain'] = gain((DEPTH, D_MODEL))
    inp['w_mod'] = nrm((DEPTH, D_MODEL, 6 * D_MODEL), 0.2 * D_MODEL ** -0.5)
    inp['b_mod'] = nrm((DEPTH, 6 * D_MODEL), 0.02)
    inp['w_in'] = nrm((DEPTH, D_MODEL, PROJ_IN), D_MODEL ** -0.5)
    inp['ssd_conv_w'] = nrm((DEPTH, SSD_CONV, SSD_CONV_CH), SSD_CONV ** -0.5)
    inp['ssd_conv_b'] = nrm((DEPTH, SSD_CONV_CH), 0.02)
    inp['ssd_a_log'] = jnp.log(unif((DEPTH, 2, SSD_HEADS), 1.0, 16.0))
    inp['ssd_dt_bias'] = dt0 + jnp.log(-jnp.expm1(-dt0))
    inp['ssd_d'] = gain((DEPTH, 2, SSD_HEADS))
    inp['ssd_norm_gain'] = gain((DEPTH, SSD_INNER))
    inp['mla_q_a_gain'] = gain((DEPTH, MLA_Q_RANK))
    inp['mla_kv_a_gain'] = gain((DEPTH, MLA_KV_RANK))
    inp['mla_w_q_b'] = nrm((DEPTH, MLA_Q_RANK, MLA_HEADS * (MLA_NOPE + MLA_ROPE)), MLA_Q_RANK ** -0.5)
    inp['mla_w_kv_b'] = nrm((DEPTH, MLA_KV_RANK, MLA_HEADS * (MLA_NOPE + MLA_V)), MLA_KV_RANK ** -0.5)
    inp['mla_q_gain'] = gain((DEPTH, MLA_NOPE + MLA_ROPE))
    inp['mla_k_gain'] = gain((DEPTH, MLA_NOPE + MLA_ROPE))
    inp['s5_lam_re'] = -0.5 + nrm((DEPTH, 2, S5_GROUPS, S5_STATE), 0.01)
    inp['s5_lam_im'] = math.pi * n_idx + nrm((DEPTH, 2, S5_GROUPS, S5_STATE), 0.01)
    inp['s5_log_dt'] = unif((DEPTH, 2, S5_GROUPS), math.log(1e-3), math.log(1e-1))
    inp['s5_b_re'] = nrm((DEPTH, 2, S5_GROUPS, S5_STATE, S5_GROUP), (2 * S5_GROUP) ** -0.5)
    inp['s5_b_im'] = nrm((DEPTH, 2, S5_GROUPS, S5_STATE, S5_GROUP), (2 * S5_GROUP) ** -0.5)
    inp['s5_c_re'] = nrm((DEPTH, 2, S5_GROUPS, S5_GROUP, S5_STATE), (2 * S5_STATE) ** -0.5)
    inp['s5_c_im'] = nrm((DEPTH, 2, S5_GROUPS, S5_GROUP, S5_STATE), (2 * S5_STATE) ** -0.5)
    inp['s5_d'] = nrm((DEPTH, S5_INNER), 1.0)
    inp['s5_w_glu'] = nrm((DEPTH, S5_INNER, S5_INNER), S5_INNER ** -0.5)
    inp['w_branch_ssd'] = nrm((DEPTH, SSD_INNER, D_MODEL), SSD_INNER ** -0.5)
    inp['w_branch_mla'] = nrm((DEPTH, MLA_INNER, D_MODEL), MLA_INNER ** -0.5)
    inp['w_branch_s5'] = nrm((DEPTH, S5_INNER, D_MODEL), S5_INNER ** -0.5)
    inp['w_out'] = nrm((DEPTH, D_MODEL, D_MODEL), D_MODEL ** -0.5)
    inp['moe_router'] = nrm((DEPTH, D_MODEL, N_EXPERTS), D_MODEL ** -0.5)
    inp['moe_w_gate'] = nrm((DEPTH, N_EXPERTS, D_MODEL, EXPERT_FF), D_MODEL ** -0.5)
    inp['moe_w_up'] = nrm((DEPTH, N_EXPERTS, D_MODEL, EXPERT_FF), D_MODEL ** -0.5)
    inp['moe_w_down'] = nrm((DEPTH, N_EXPERTS, EXPERT_FF, D_MODEL), EXPERT_FF ** -0.5)
    return inp


def reference(x, c, ctx, c_ctx, norm1_gain, norm2_gain, w_mod, b_mod, w_in,
              ssd_conv_w, ssd_conv_b, ssd_a_log, ssd_dt_bias, ssd_d, ssd_norm_gain,
              mla_q_a_gain, mla_kv_a_gain, mla_w_q_b, mla_w_kv_b, mla_q_gain, mla_k_gain,
              s5_lam_re, s5_lam_im, s5_log_dt, s5_b_re, s5_b_im, s5_c_re, s5_c_im, s5_d, s5_w_glu,
              w_branch_ssd, w_branch_mla, w_branch_s5, w_out,
              moe_router, moe_w_gate, moe_w_up, moe_w_down):
    n_tok = x.shape[1]
    n_rows = n_tok // GRID_W
    cos, sin = axial_rope_tables(n_rows)
    c_act = jax.nn.silu(c)
    cc_act = jax.nn.silu(c_ctx)
    xc = ctx
    o1 = SSD_IN
    o2 = o1 + MLA_IN
    o3 = o2 + S5_INNER
    for i in range(DEPTH):
        need_ctx = i < DEPTH - 1
        mod_x = (c_act @ w_mod[i] + b_mod[i])[:, None, :]
        mod_c = (cc_act @ w_mod[i] + b_mod[i])[None, None, :]
        shx1, scx1, gx1, shx2, scx2, gx2 = jnp.split(mod_x, 6, axis=-1)
        shc1, scc1, gc1, shc2, scc2, gc2 = jnp.split(mod_c, 6, axis=-1)
        hx = rms_norm(x, norm1_gain[i]) * (1.0 + scx1) + shx1
        hc = rms_norm(xc, norm1_gain[i]) * (1.0 + scc1) + shc1
        px = hx @ w_in[i]
        pc = hc @ w_in[i]
        ssd_x, ssd_c = ssd_mixer(px[..., :o1], pc[..., :o1], ssd_conv_w[i], ssd_conv_b[i], ssd_a_log[i],
                                 ssd_dt_bias[i], ssd_d[i], ssd_norm_gain[i], need_ctx)
        mla_x, mla_c = mla_mixer(px[..., o1:o2], pc[..., o1:o2], mla_q_a_gain[i], mla_kv_a_gain[i],
                                 mla_w_q_b[i], mla_w_kv_b[i], mla_q_gain[i], mla_k_gain[i], cos, sin, need_ctx)
        s5_x, s5_c = s5_mixer(px[..., o2:o3], pc[..., o2:o3], s5_lam_re[i], s5_lam_im[i], s5_log_dt[i],
                              s5_b_re[i], s5_b_im[i], s5_c_re[i], s5_c_im[i], s5_d[i], s5_w_glu[i], need_ctx)
        x = x + gx1 * merge_branches(ssd_x, mla_x, s5_x, px[..., o3:], w_branch_ssd[i],
                                     w_branch_mla[i], w_branch_s5[i], w_out[i])
        if need_ctx:
            xc = xc + gc1 * merge_branches(ssd_c, mla_c, s5_c, pc[..., o3:], w_branch_ssd[i],
                                           w_branch_mla[i], w_branch_s5[i], w_out[i])
        hx2 = rms_norm(x, norm2_gain[i]) * (1.0 + scx2) + shx2
        x = x + gx2 * ec_moe(hx2, moe_router[i], moe_w_gate[i], moe_w_up[i], moe_w_down[i])
        if need_ctx:
            hc2 = rms_norm(xc, norm2_gain[i]) * (1.0 + scc2) + shc2
            xc = xc + gc2 * ec_moe(hc2, moe_router[i], moe_w_gate[i], moe_w_up[i], moe_w_down[i])
    return x
```

```python
from contextlib import ExitStack
import numpy as np
import concourse.bass as bass
import concourse.mybir as mybir
from concourse.bass_utils import run_bass_kernel_spmd

F32 = mybir.dt.float32
F32R = mybir.dt.float32r


def R(ap):
    return ap.bitcast(F32R)
I32 = mybir.dt.int32
U32 = mybir.dt.uint32
AF = mybir.ActivationFunctionType
ALU = mybir.AluOpType
AX = mybir.AxisListType

D = 2048
NB = 4
SEQ = 2048
CTX = 256
LT = SEQ + CTX
EPS = 1e-6
PROJ_IN = 11056
NCORES = 8
NCORES_F = 4


_QMAP = {}


class Sched:
    NDS = 8

    def __init__(self, nc, ctx):
        self.nc = nc
        self.ctx = ctx
        self.E = {'pe': nc.tensor, 'act': nc.scalar, 'dve': nc.vector, 'pool': nc.gpsimd, 'sp': nc.sync}
        self.sem = {k: ctx.enter_context(nc.semaphore('s_' + k)) for k in ['pe', 'act', 'dve', 'pool']}
        self.cnt = {k: 0 for k in self.sem}
        self.seen = {e: {} for e in self.E}
        self.dsem = {q: [ctx.enter_context(nc.semaphore(f'd_{q}{i}')) for i in range(self.NDS)]
                     for q in ['sp', 'pool', 'act']}
        self.dcnt = {q: 0 for q in self.dsem}
        self.last_w = {}
        self.readers = {}
        self.ps = [ctx.enter_context(nc.psum_tensor(f'ps{i}', [128, 512], F32)) for i in range(8)]
        self.psi = 0
        self.nrot = 8
        self.stage_id = 0
        self.qmap = dict(_QMAP)
        self.fused = False
        self.out_tokens = []

    def sb(self, name, shape, dt=F32):
        return self.ctx.enter_context(self.nc.sbuf_tensor(f"s{self.stage_id}_{name}", list(shape), dt))

    def round_r(self, eng, ap, keys):
        if eng == 'act':
            self.op('act', lambda e: e.activation(out=R(ap), in_=ap, func=AF.Copy), r=keys, w=keys)
        else:
            self.op(eng, lambda e: e.tensor_copy(out=R(ap), in_=ap), r=keys, w=keys)

    def barrier(self):
        for e in self.E:
            for f in self.sem:
                if f != e and self.cnt[f] > 0:
                    self._wait(e, (self.sem[f], self.cnt[f], f))
            for q in self.dsem:
                n = self.dcnt[q]
                for i in range(self.NDS):
                    k = (n - i + self.NDS - 1) // self.NDS if n > i else 0
                    if k > 0:
                        self._wait(e, (self.dsem[q][i], 16 * k, 'dma_' + q))
        self.last_w = {}
        self.readers = {}
        self.out_tokens = []

    def next_ps(self):
        i = self.psi
        self.psi = (self.psi + 1) % self.nrot
        return self.ps[i], ('ps', i)

    def _wait(self, e, tok):
        sem, val, src = tok
        if src == e and e == 'pe':
            return
        sid = id(sem)
        if self.seen[e].get(sid, 0) >= val:
            return
        self.E[e].wait_ge(sem, val)
        self.seen[e][sid] = val

    def _deps(self, e, r, w):
        toks = []
        for k in r:
            if k in self.last_w:
                toks.append(self.last_w[k])
        for k in w:
            if k in self.last_w:
                toks.append(self.last_w[k])
            toks.extend(self.readers.get(k, {}).values())
        for t in toks:
            self._wait(e, t)

    def _record(self, tok, r, w):
        for k in r:
            d = self.readers.setdefault(k, {})
            sid = id(tok[0])
            if sid not in d or d[sid][1] < tok[1]:
                d[sid] = tok
        for k in w:
            self.last_w[k] = tok
            self.readers[k] = {}

    def op(self, e, fn, r=(), w=()):
        self._deps(e, r, w)
        ins = fn(self.E[e])
        self.cnt[e] += 1
        ins.then_inc(self.sem[e], 1)
        tok = (self.sem[e], self.cnt[e], e)
        self._record(tok, r, w)
        return tok

    def _guard_dma(self, q):
        n = self.dcnt[q]
        sem = self.dsem[q][n % self.NDS]
        prev = 16 * (n // self.NDS)
        if prev > 0 and self.seen[q].get(id(sem), 0) < prev:
            self.E[q].wait_ge(sem, prev)
            self.seen[q][id(sem)] = prev
        return sem, prev

    def _finish_dma(self, q, ins, r, w):
        n = self.dcnt[q]
        sem = self.dsem[q][n % self.NDS]
        prev = 16 * (n // self.NDS)
        ins.then_inc(sem, 16)
        self.dcnt[q] += 1
        tok = (sem, prev + 16, 'dma_' + q)
        self._record(tok, r, w)
        return tok

    def dma(self, q, out, in_, r=(), w=(), is_out=False, **kw):
        q = self.qmap.get(q, q)
        self._deps(q, r, w)
        self._guard_dma(q)
        ins = self.E[q].dma_start(out=out, in_=in_, **kw)
        tok = self._finish_dma(q, ins, r, w)
        if is_out:
            self.out_tokens.append(tok)
        return tok

    def finish(self):
        if self.fused:
            self.barrier()
            return
        for t in self.out_tokens:
            self._wait('sp', t)


_F = {'nc': None, 'S': None, 'io': {}}


def new_nc():
    if _F['nc'] is not None:
        return _F['nc']
    return bass.Bass("TRN2", target_bir_lowering=False)


def _io(nc, name, shape, dt, kind):
    if _F['nc'] is not None:
        ap = _F['io'][name]
        assert tuple(ap.shape) == tuple(shape), (name, ap.shape, shape)
        return ap
    return nc.dram_tensor(name, list(shape), dt, kind=kind).ap()


def din(nc, name, shape, dt=F32):
    return _io(nc, name, shape, dt, "ExternalInput")


def dout(nc, name, shape, dt=F32):
    return _io(nc, name, shape, dt, "ExternalOutput")


def get_sched(nc, ctx):
    if _F['S'] is None:
        return Sched(nc, ctx)
    S = _F['S']
    S.ctx = ctx
    S.stage_id += 1
    S.nrot = 8
    S.psi = 0
    return S


def make_ident(S, name='ident'):
    nc = S.nc
    idt = S.sb(name, [128, 128])
    S.op('pool', lambda e: e.memset(idt[:], 1.0), w=[name])
    S.op('pool', lambda e: e.affine_select(out=idt[:], in_=idt[:], pattern=[[-1, 128]], compare_op=ALU.is_equal,
                                           fill=0.0, base=0, channel_multiplier=1), r=[name], w=[name])
    return idt


def build_A():
    nc = new_nc()
    cT = din(nc, "cT", [128, 16 * 5])
    w = din(nc, "w", [D, 1536])
    b = din(nc, "b", [128, 12])
    o = dout(nc, "o", [128, 60])
    with ExitStack() as ctx:
        S = get_sched(nc, ctx)
        ct = S.sb('ct', [128, 16, 5])
        ca = S.sb('ca', [128, 16, 5])
        bt = S.sb('bt', [128, 12])
        wt = S.sb('wt', [128, 16, 1536])
        ot = S.sb('ot', [128, 12, 5])
        S.dma('sp', ct[:].rearrange("p k r -> p (k r)"), cT[:, :], w=['ct'])
        S.dma('sp', bt[:], b[:, :], w=['bt'])
        wv = w.rearrange("(k p) n -> p k n", p=128)
        for g in range(8):
            q = 'sp' if g % 2 == 0 else 'pool'
            S.dma(q, wt[:, 2 * g:2 * g + 2, :], wv[:, 2 * g:2 * g + 2, :], w=[('wt', g)])
        S.op('act', lambda e: e.activation(out=ca[:], in_=ct[:], func=AF.Silu), r=['ct'], w=['ca'])
        ps, pk = S.next_ps()
        for m in range(12):
            for k in range(16):
                S.op('pe', lambda e: e.matmul(ps[:, m * 5:m * 5 + 5], lhsT=wt[:, k, m * 128:(m + 1) * 128],
                                              rhs=ca[:, k, :], start=(k == 0), stop=(k == 15)),
                     r=['ca', ('wt', k // 2)], w=[pk])
        for m in range(12):
            S.op('act', lambda e: e.activation(out=ot[:, m, :], in_=ps[:, m * 5:m * 5 + 5], func=AF.Identity,
                                               bias=bt[:, m:m + 1], scale=1.0), r=[pk, 'bt'], w=['ot'])
        S.dma('sp', o[:, :], ot[:].rearrange("p m r -> p (m r)"), r=['ot'], is_out=True)
        S.finish()
    return nc


def run_A(inputs, layer):
    c5 = np.concatenate([inputs['c'], inputs['c_ctx'][None, :]], axis=0)
    cT = np.ascontiguousarray(c5.reshape(5, 16, 128).transpose(2, 1, 0)).reshape(128, 80)
    wm = inputs['w_mod'][layer]
    bm = inputs['b_mod'][layer]
    in_maps = []
    for j in range(NCORES):
        in_maps.append({"cT": cT, "w": np.ascontiguousarray(wm[:, j * 1536:(j + 1) * 1536]),
                        "b": np.ascontiguousarray(bm[j * 1536:(j + 1) * 1536].reshape(12, 128).T)})
    res = run_bass_kernel_spmd(build_A(), in_maps, core_ids=list(range(NCORES)))
    modT = np.concatenate([r["o"].reshape(128, 12, 5) for r in res.results], axis=1)
    return modT


NT_B = 9
B_BLOCKS = ([(0, 512, 'tm'), (512, 1024, 'tm')] + [(1024 + 512 * i, 1536 + 512 * i, 'fm') for i in range(4)]
            + [(3072, 3584, 'tm'), (3584, 3888, 'tm'), (3888, 4400, 'fm'), (4400, 4912, 'fm')]
            + [(4912 + 512 * i, 5424 + 512 * i, 'fm') for i in range(12)])


def tm_col(c):
    return c if c < 1024 else c - 2048


def fm_row(c):
    return c - 1024 if c < 3072 else c - 1840


def norm_mod_tiles(S, xin, NT, ms, g1, hT, ident, pref, rr=False):
    xb = [S.sb(f'{pref}xb{i}', [128, D]) for i in range(2)]
    junk = S.sb(pref + 'junk', [128, D])
    ss = S.sb(pref + 'ss', [128, NT])
    rs = S.sb(pref + 'rs', [128, NT])
    S.op('dve', lambda e: e.memset(ss[:], 0.0), w=[pref + 'ss'])
    for t in range(NT):
        xt = xb[t % 2]
        xk = (pref + 'xb', t % 2)
        S.dma('sp', xt[:], xin[t * 128:(t + 1) * 128, :], w=[xk])
        S.op('act', lambda e: e.activation(out=junk[:], in_=xt[:], func=AF.Square, accum_out=ss[:, t:t + 1]),
             r=[xk], w=[pref + 'junk', pref + 'ss'])
        S.op('dve', lambda e: e.tensor_scalar(out=rs[:, t:t + 1], in0=ss[:, t:t + 1], scalar1=1.0 / D, scalar2=EPS,
                                              op0=ALU.mult, op1=ALU.add), r=[pref + 'ss'], w=[pref + 'rs'])
        S.op('act', lambda e: e.activation(out=rs[:, t:t + 1], in_=rs[:, t:t + 1], func=AF.Sqrt),
             r=[pref + 'rs'], w=[pref + 'rs'])
        S.op('dve', lambda e: e.reciprocal(out=rs[:, t:t + 1], in_=rs[:, t:t + 1]), r=[pref + 'rs'], w=[pref + 'rs'])
        S.op('dve', lambda e: e.tensor_scalar(out=xt[:], in0=xt[:], scalar1=rs[:, t:t + 1], scalar2=None,
                                              op0=ALU.mult), r=[xk, pref + 'rs'], w=[xk])
        for kk in range(4):
            ps, pk = S.next_ps()
            for j in range(4):
                k = kk * 4 + j
                S.op('pe', lambda e: e.transpose(out=ps[:, j * 128:(j + 1) * 128], in_=xt[:, k * 128:(k + 1) * 128],
                                                 identity=ident[:]), r=[xk, 'ident'], w=[pk])
            for j in range(4):
                k = kk * 4 + j
                S.op('act', lambda e: e.activation(out=(R(hT[:, k, t * 128:(t + 1) * 128]) if rr else hT[:, k, t * 128:(t + 1) * 128]), in_=ps[:, j * 128:(j + 1) * 128],
                                                   func=AF.Identity, scale=g1[:, t, k:k + 1], bias=ms[:, t, 1, k:k + 1]),
                     r=[pk, 'g1', 'ms'], w=[('hT', t)])


def load_mod(S, msel, gn, NT):
    ms = S.sb('ms', [128, NT, 2, 16])
    gnt = S.sb('gnt', [128, 16])
    g1 = S.sb('g1', [128, NT, 16])
    S.dma('sp', ms[:].rearrange("p t s k -> p (t s k)"), msel[:, :], w=['ms'])
    S.dma('sp', gnt[:], gn[:, :], w=['gnt'])
    for t in range(NT):
        S.op('dve', lambda e: e.scalar_tensor_tensor(out=g1[:, t, :], in0=ms[:, t, 0, :], scalar=1.0, in1=gnt[:],
                                                     op0=ALU.add, op1=ALU.mult), r=['ms', 'gnt'], w=['g1'])
    return ms, g1


def build_B():
    nc = new_nc()
    NT = NT_B
    NTOK = NT * 128
    xin = din(nc, "x", [NTOK, D])
    msel = din(nc, "msel", [128, NT * 16 * 2])
    gn = din(nc, "gn", [128, 16])
    w = din(nc, "w", [D, PROJ_IN])
    otm = dout(nc, "otm", [NTOK, 1840])
    ofm = dout(nc, "ofm", [9216, NTOK])
    with ExitStack() as ctx:
        S = get_sched(nc, ctx)
        ident = make_ident(S)
        hT = S.sb('hT', [128, 16, NTOK])
        ms, g1 = load_mod(S, msel, gn, NT)
        norm_mod_tiles(S, xin, NT, ms, g1, hT, ident, 'n1', rr=True)
        wbuf = [S.sb(f'wb{i}', [128, 16, 512]) for i in range(2)]
        obuf = [S.sb(f'ob{i}', [128, 512]) for i in range(4)]
        wv = w.rearrange("(k p) n -> p k n", p=128)
        oi = 0
        hkeys = [('hT', t) for t in range(NT)]
        for bi, (c0, c1, kind) in enumerate(B_BLOCKS):
            nw = c1 - c0
            wb = wbuf[bi % 2]
            S.dma('pool', R(wb[:, 0:8, :nw]), wv[:, 0:8, c0:c1], w=[('wb', bi % 2, 0)])
            S.dma('pool', R(wb[:, 8:16, :nw]), wv[:, 8:16, c0:c1], w=[('wb', bi % 2, 1)])
            if kind == 'tm':
                jobs = [('tm', t, None) for t in range(NT)]
            else:
                jobs = [('fm', m, rng) for m in range(nw // 128) for rng in [(0, 512), (512, 1024), (1024, NTOK)]]
            for kind_, a, rng in jobs:
                ps, pk = S.next_ps()
                if kind_ == 'tm':
                    t = a
                    n = nw
                    for k in range(16):
                        S.op('pe', lambda e: e.matmul(ps[:, :nw], lhsT=R(hT[:, k, t * 128:(t + 1) * 128]), rhs=R(wb[:, k, :nw]),
                                                      start=(k == 0), stop=(k == 15)),
                             r=[('hT', t), ('wb', bi % 2, k // 8)], w=[pk])
                    dst = otm[t * 128:(t + 1) * 128, tm_col(c0):tm_col(c0) + nw]
                else:
                    m = a
                    n0, n1 = rng
                    n = n1 - n0
                    for k in range(16):
                        S.op('pe', lambda e: e.matmul(ps[:, :n], lhsT=R(wb[:, k, m * 128:(m + 1) * 128]), rhs=R(hT[:, k, n0:n1]),
                                                      start=(k == 0), stop=(k == 15)),
                             r=hkeys + [('wb', bi % 2, k // 8)], w=[pk])
                    fr = fm_row(c0) + m * 128
                    dst = ofm[fr:fr + 128, n0:n1]
                ob = obuf[oi % 4]
                ok = ('ob', oi % 4)
                if oi % 2 == 0:
                    S.op('act', lambda e: e.activation(out=ob[:, :n], in_=ps[:, :n], func=AF.Copy), r=[pk], w=[ok])
                else:
                    S.op('dve', lambda e: e.tensor_copy(out=ob[:, :n], in_=ps[:, :n]), r=[pk], w=[ok])
                S.dma('sp', dst, ob[:, :n], r=[ok], is_out=True)
                oi += 1
        S.finish()
    return nc


def core_rows(b, h):
    return (0, 1152) if h == 0 else (1152, 2304)


def mod_rows(modT, b, which):
    sl = modT[:, which * 16:(which + 1) * 16, :]
    return sl[:, :, b], sl[:, :, 4]


def tile_is_ctx(h, t):
    return h == 0 and t < 2


def make_msel(modT, b, h, sc_i, sh_i, NT=9):
    scx, scc = mod_rows(modT, b, sc_i)
    shx, shc = mod_rows(modT, b, sh_i)
    ms = np.zeros((128, NT, 2, 16), np.float32)
    for t in range(NT):
        if tile_is_ctx(h, t):
            ms[:, t, 0, :] = scc
            ms[:, t, 1, :] = shc
        else:
            ms[:, t, 0, :] = scx
            ms[:, t, 1, :] = shx
    return ms.reshape(128, -1)


def run_B(xseq, modT, inputs, layer):
    gn = np.ascontiguousarray(inputs['norm1_gain'][layer].reshape(16, 128).T)
    w = inputs['w_in'][layer]
    in_maps = []
    for c in range(NCORES):
        b, h = c // 2, c % 2
        r0, r1 = core_rows(b, h)
        in_maps.append({"x": np.ascontiguousarray(xseq[b, r0:r1]), "msel": make_msel(modT, b, h, 1, 0),
                        "gn": gn, "w": w})
    res = run_bass_kernel_spmd(build_B(), in_maps, core_ids=list(range(NCORES)))
    px_tm = np.zeros((NB, LT, 1840), np.float32)
    px_fm = np.zeros((NB, 9216, LT), np.float32)
    for c in range(NCORES):
        b, h = c // 2, c % 2
        r0, r1 = core_rows(b, h)
        px_tm[b, r0:r1] = res.results[c]["otm"]
        px_fm[b, :, r0:r1] = res.results[c]["ofm"]
    return px_tm, px_fm


def bc(ap, axis, shape):
    return ap.unsqueeze(axis).to_broadcast(list(shape))


def make_masks(S, transposed=False):
    sfx = 'T' if transposed else ''
    cm, pm = (1, -1) if transposed else (-1, 1)
    U8 = S.sb('U8' + sfx, [128, 8, 128])
    ones = S.sb('ones' + sfx, [128, 128])
    nm8 = S.sb('nm8' + sfx, [128, 8, 128])
    S.op('pool', lambda e: e.memset(ones[:], 1.0), w=['ones' + sfx])
    S.op('pool', lambda e: e.memset(U8[:], 1.0), w=['U8' + sfx])
    S.op('pool', lambda e: e.affine_select(out=U8[:], in_=U8[:], pattern=[[0, 8], [pm, 128]], compare_op=ALU.is_ge,
                                           fill=0.0, base=0, channel_multiplier=cm), r=['U8' + sfx], w=['U8' + sfx])
    S.op('pool', lambda e: e.memset(nm8[:], 0.0), w=['nm8' + sfx])
    S.op('pool', lambda e: e.affine_select(out=nm8[:], in_=nm8[:], pattern=[[0, 8], [pm, 128]], compare_op=ALU.is_ge,
                                           fill=-30000.0, base=0, channel_multiplier=cm), r=['nm8' + sfx], w=['nm8' + sfx])
    return U8, ones, nm8


NTL = LT // 128


def build_C1():
    nc = new_nc()
    xbc = din(nc, "xbc", [2048, LT])
    dtin = din(nc, "dt", [LT, 16])
    cw = din(nc, "cw", [128, 16 * 5])
    cb = din(nc, "cb", [128, 16])
    abd = din(nc, "abd", [128, 96])
    yo = [dout(nc, "y0", [LT, 1024]), dout(nc, "y1", [LT, 1024])]
    with ExitStack() as ctx:
        S = get_sched(nc, ctx)
        ident = make_ident(S)
        U8, ones, nm8 = make_masks(S)
        U8T, _, nm8T = make_masks(S, transposed=True)
        cwt = S.sb('cwt', [128, 16, 5])
        cbt = S.sb('cbt', [128, 16])
        abt = S.sb('abt', [128, 3, 2, 16])
        S.dma('sp', cwt[:].rearrange("p c j -> p (c j)"), cw[:, :], w=['cwt'])
        S.dma('sp', cbt[:], cb[:, :], w=['cbt'])
        S.dma('sp', abt[:].rearrange("p a d h -> p (a d h)"), abd[:, :], w=['abt'])
        dt_all = S.sb('dt_all', [128, NTL, 16])
        dtv = S.sb('dtv', [128, 2, NTL, 16])
        a_all = S.sb('a_all', [128, 2, NTL, 16])
        aneg = S.sb('aneg', [128, 2, 16])
        S.dma('sp', dt_all[:], dtin.rearrange("(t p) h -> p t h", p=128), w=['dt_all'])
        S.op('act', lambda e: e.activation(out=aneg[:], in_=abt[:, 0, :, :], func=AF.Exp), r=['abt'], w=['aneg'])
        S.op('dve', lambda e: e.tensor_scalar(out=aneg[:], in0=aneg[:], scalar1=-1.0, scalar2=None, op0=ALU.mult),
             r=['aneg'], w=['aneg'])
        for d in range(2):
            S.op('dve', lambda e: e.tensor_tensor(out=dtv[:, d], in0=dt_all[:], in1=bc(abt[:, 1, d, :], 1, [128, NTL, 16]),
                                                  op=ALU.add), r=['dt_all', 'abt'], w=['dtv'])
            S.op('act', lambda e: e.activation(out=dtv[:, d], in_=dtv[:, d], func=AF.Exp), r=['dtv'], w=['dtv'])
            S.op('act', lambda e: e.activation(out=dtv[:, d], in_=dtv[:, d], func=AF.Ln, bias=1.0, scale=1.0), r=['dtv'], w=['dtv'])
            S.op('dve', lambda e: e.tensor_tensor(out=a_all[:, d], in0=dtv[:, d], in1=bc(aneg[:, d, :], 1, [128, NTL, 16]), op=ALU.mult),
                 r=['dtv', 'aneg'], w=['a_all'])

        pb = [S.sb(f'pb{i}', [128, LT + 8]) for i in range(2)]
        for i in range(2):
            S.op('pool', lambda e: e.memset(pb[i][:], 0.0), w=[('pb', i)])
        acc = S.sb('acc', [128, LT])
        tmpx = S.sb('tmpx', [128, LT])
        BT = S.sb('BT', [128, 2, LT])
        CT = S.sb('CT', [128, 2, LT])
        x_tm = S.sb('x_tm', [128, NTL, 512])
        B_tm = S.sb('B_tm', [128, NTL, 256])
        y_acc = S.sb('y_acc', [128, NTL, 512])
        hst = S.sb('hst', [128, 512])
        sm = S.sb('sm', [128, 64])
        aU = S.sb('aU', [128, 8, 128])
        tmp = S.sb('tmp', [128, 8, 128])
        MT = S.sb('MT', [128, 8, 128])
        xdt = S.sb('xdt', [128, 8, 64])
        xw = S.sb('xw', [128, 8, 64])
        ydsb = S.sb('ydsb', [128, 8, 64])
        segs = [(0, 0, 256), (260, 256, 2048)]
        ci_glob = 0
        for hf in range(2):
            chunks = ([('x', i, 4 * hf + i) for i in range(4)] + [('B', i, 8 + 2 * hf + i) for i in range(2)]
                      + [('C', i, 12 + 2 * hf + i) for i in range(2)])
            for kind, i, ch in chunks:
                p = pb[ci_glob % 2]
                pk = ('pb', ci_glob % 2)
                ci_glob += 1
                S.dma('sp', p[:, 2:258], xbc[ch * 128:(ch + 1) * 128, 0:256], w=[pk])
                S.dma('sp', p[:, 262:2310], xbc[ch * 128:(ch + 1) * 128, 256:LT], w=[pk])
                for (oi, oo, n) in segs:
                    S.op('dve', lambda e: e.tensor_scalar(out=acc[:, oo:oo + n], in0=p[:, oi:oi + n],
                                                          scalar1=cwt[:, ch, 0:1], scalar2=None, op0=ALU.mult),
                         r=[pk, 'cwt'], w=['acc'])
                    for j in range(1, 5):
                        S.op('dve', lambda e: e.scalar_tensor_tensor(out=acc[:, oo:oo + n], in0=p[:, oi + j:oi + j + n],
                                                                     scalar=cwt[:, ch, j:j + 1], in1=acc[:, oo:oo + n],
                                                                     op0=ALU.mult, op1=ALU.add),
                             r=[pk, 'cwt', 'acc'], w=['acc'])
                if kind == 'x':
                    dst, dk = tmpx[:], 'tmpx'
                elif kind == 'B':
                    dst, dk = BT[:, i, :], ('BT', i)
                else:
                    dst, dk = CT[:, i, :], ('CT', i)
                S.op('act', lambda e: e.activation(out=dst, in_=acc[:], func=AF.Silu, bias=cbt[:, ch:ch + 1], scale=1.0),
                     r=['acc', 'cbt'], w=[dk])
                if kind in ('x', 'B'):
                    for t0 in range(0, NTL, 4):
                        nt = min(4, NTL - t0)
                        ps, pk2 = S.next_ps()
                        for tt in range(nt):
                            t = t0 + tt
                            S.op('pe', lambda e: e.transpose(out=ps[:, tt * 128:(tt + 1) * 128],
                                                             in_=dst[:, t * 128:(t + 1) * 128], identity=ident[:]),
                                 r=[dk, 'ident'], w=[pk2])
                        if kind == 'x':
                            o_ap = x_tm[:, t0:t0 + nt, i * 128:(i + 1) * 128]
                            ok = 'x_tm'
                        else:
                            o_ap = B_tm[:, t0:t0 + nt, i * 128:(i + 1) * 128]
                            ok = 'B_tm'
                        S.op('act', lambda e: e.activation(out=o_ap, in_=ps[:, 0:nt * 128].rearrange("p (t c) -> p t c", c=128),
                                                           func=AF.Copy), r=[pk2], w=[ok])
            h0 = hf * 8
            for d in range(2):
              Um, nmm = (U8, nm8) if d == 0 else (U8T, nm8T)
              U = Um[:, 0, :]
              order = list(range(NTL)) if d == 0 else [1, 0] + list(range(NTL - 1, 1, -1))
              if True:
                  S.op('dve', lambda e: e.tensor_tensor(
                      out=y_acc[:].rearrange("p t (h d) -> p t h d", d=64), in0=x_tm[:].rearrange("p t (h d) -> p t h d", d=64),
                      in1=abt[:, 2, d, h0:h0 + 8].unsqueeze(1).unsqueeze(3).to_broadcast([128, NTL, 8, 64]), op=ALU.mult),
                      r=['x_tm', 'abt'], w=['y_acc'])
                  S.op('dve', lambda e: e.memset(hst[:], 0.0), w=['hst'])
                  for t in order:
                      tsl = slice(t * 128, (t + 1) * 128)
                      a_t = a_all[:, d, t, h0:h0 + 8]
                      dtv_t = dtv[:, d, t, h0:h0 + 8]
                      psA, kA = S.next_ps()
                      S.op('pe', lambda e: e.matmul(psA[:, 0:8], lhsT=U, rhs=a_t, start=True, stop=True),
                           r=['U8', 'U8T', 'a_all'], w=[kA])
                      S.op('pe', lambda e: e.matmul(psA[:, 8:16], lhsT=ones[:], rhs=a_t, start=True, stop=True),
                           r=['ones', 'a_all'], w=[kA])
                      S.op('dve', lambda e: e.tensor_copy(out=sm[:, 0:16], in_=psA[:, 0:16]), r=[kA], w=['sm'])
                      S.op('dve', lambda e: e.tensor_tensor(out=sm[:, 24:32], in0=sm[:, 8:16], in1=sm[:, 0:8], op=ALU.subtract),
                           r=['sm'], w=['sm'])
                      S.op('act', lambda e: e.activation(out=sm[:, 16:24], in_=sm[:, 0:8], func=AF.Exp), r=['sm'], w=['sm'])
                      S.op('act', lambda e: e.activation(out=sm[:, 24:32], in_=sm[:, 24:32], func=AF.Exp), r=['sm'], w=['sm'])
                      S.op('act', lambda e: e.activation(out=sm[:, 32:40], in_=sm[:, 8:16], func=AF.Exp), r=['sm'], w=['sm'])
                      S.op('dve', lambda e: e.tensor_tensor(out=sm[:, 24:32], in0=sm[:, 24:32], in1=dtv_t, op=ALU.mult),
                           r=['sm', 'dtv'], w=['sm'])
                      S.op('dve', lambda e: e.tensor_tensor(out=aU[:], in0=Um[:], in1=bc(a_t, 2, [128, 8, 128]), op=ALU.mult),
                           r=['U8', 'U8T', 'a_all'], w=['aU'])
                      psB = []
                      for q in range(2):
                          pq, kq = S.next_ps()
                          S.op('pe', lambda e: e.matmul(pq[:, :], lhsT=ones[:], rhs=aU[:, 4 * q:4 * q + 4, :].rearrange("p h l -> p (h l)"),
                                                        start=True, stop=True), r=['ones', 'aU'], w=[kq])
                          psB.append((pq, kq))
                      for q in range(2):
                          pq, kq = psB[q]
                          S.op('dve', lambda e: e.tensor_tensor(out=tmp[:, 4 * q:4 * q + 4, :],
                                                                in0=pq[:, :].rearrange("p (h l) -> p h l", l=128),
                                                                in1=bc(sm[:, 4 * q:4 * q + 4], 2, [128, 4, 128]), op=ALU.subtract),
                               r=[kq, 'sm'], w=['tmp'])
                      S.op('dve', lambda e: e.tensor_tensor(out=tmp[:], in0=tmp[:], in1=nmm[:], op=ALU.add),
                           r=['tmp', 'nm8', 'nm8T'], w=['tmp'])
                      S.op('act', lambda e: e.activation(out=tmp[:], in_=tmp[:], func=AF.Exp), r=['tmp'], w=['tmp'])
                      psS, kS = S.next_ps()
                      for gi in range(2):
                          S.op('pe', lambda e: e.matmul(psS[:, gi * 128:(gi + 1) * 128], lhsT=BT[:, gi, tsl], rhs=CT[:, gi, tsl],
                                                        start=True, stop=True), r=[('BT', gi), ('CT', gi)], w=[kS])
                      for gi in range(2):
                          S.op('dve', lambda e: e.tensor_tensor(out=MT[:, 4 * gi:4 * gi + 4, :], in0=tmp[:, 4 * gi:4 * gi + 4, :],
                                                                in1=bc(psS[:, gi * 128:(gi + 1) * 128], 1, [128, 4, 128]),
                                                                op=ALU.mult), r=['tmp', kS], w=['MT'])
                      xt3 = x_tm[:, t, :].rearrange("p (h d) -> p h d", d=64)
                      S.op('dve', lambda e: e.tensor_tensor(out=xdt[:], in0=xt3, in1=bc(dtv_t, 2, [128, 8, 64]), op=ALU.mult),
                           r=['x_tm', 'dtv'], w=['xdt'])
                      S.op('dve', lambda e: e.tensor_tensor(out=xw[:], in0=xt3, in1=bc(sm[:, 24:32], 2, [128, 8, 64]), op=ALU.mult),
                           r=['x_tm', 'sm'], w=['xw'])
                      psY, kY = S.next_ps()
                      for hh in range(8):
                          S.op('pe', lambda e: e.matmul(psY[:, hh * 64:(hh + 1) * 64], lhsT=MT[:, hh, :], rhs=xdt[:, hh, :],
                                                        start=True, stop=True), r=['MT', 'xdt'], w=[kY])
                      psO, kO = S.next_ps()
                      for gi in range(2):
                          S.op('pe', lambda e: e.matmul(psO[:, gi * 256:(gi + 1) * 256], lhsT=CT[:, gi, tsl],
                                                        rhs=hst[:, gi * 256:(gi + 1) * 256], start=True, stop=True),
                               r=[('CT', gi), 'hst'], w=[kO])
                      S.op('act', lambda e: e.activation(out=ydsb[:].rearrange("p h d -> p (h d)"), in_=psY[:, :], func=AF.Copy),
                           r=[kY], w=['ydsb'])
                      S.op('dve', lambda e: e.tensor_tensor(out=xdt[:], in0=psO[:, :].rearrange("p (h d) -> p h d", d=64),
                                                            in1=bc(sm[:, 16:24], 2, [128, 8, 64]), op=ALU.mult),
                           r=[kO, 'sm'], w=['xdt'])
                      S.op('pool', lambda e: e.tensor_tensor(out=ydsb[:], in0=ydsb[:], in1=xdt[:], op=ALU.add),
                           r=['ydsb', 'xdt'], w=['ydsb'])
                      S.op('pool', lambda e: e.tensor_tensor(out=y_acc[:, t, :], in0=y_acc[:, t, :],
                                                             in1=ydsb[:].rearrange("p h d -> p (h d)"), op=ALU.add),
                           r=['ydsb', 'y_acc'], w=['y_acc'])
                      psH, kH = S.next_ps()
                      for gi in range(2):
                          S.op('pe', lambda e: e.matmul(psH[:, gi * 256:(gi + 1) * 256], lhsT=B_tm[:, t, gi * 128:(gi + 1) * 128],
                                                        rhs=xw[:, 4 * gi:4 * gi + 4, :].rearrange("p h d -> p (h d)"),
                                                        start=True, stop=True), r=['B_tm', 'xw'], w=[kH])
                      S.op('dve', lambda e: e.tensor_tensor(out=hst[:].rearrange("p (h d) -> p h d", d=64),
                                                            in0=hst[:].rearrange("p (h d) -> p h d", d=64),
                                                            in1=bc(sm[:, 32:40], 2, [128, 8, 64]), op=ALU.mult),
                           r=['hst', 'sm'], w=['hst'])
                      S.op('dve', lambda e: e.tensor_tensor(out=hst[:], in0=hst[:], in1=psH[:, :], op=ALU.add),
                           r=['hst', kH], w=['hst'])
                  S.dma('pool', yo[d][:, hf * 512:(hf + 1) * 512].rearrange("(t p) c -> p t c", p=128), y_acc[:], r=['y_acc'],
                        is_out=True)
        S.finish()
    return nc


def flip_seq(a, axis):
    sl_c = [slice(None)] * a.ndim
    sl_x = [slice(None)] * a.ndim
    sl_c[axis] = slice(0, CTX)
    sl_x[axis] = slice(CTX, LT)
    return np.concatenate([np.flip(a[tuple(sl_c)], axis), np.flip(a[tuple(sl_x)], axis)], axis=axis)


def run_C1(px_tm, px_fm, inputs, layer):
    cwl = inputs['ssd_conv_w'][layer]
    cbl = inputs['ssd_conv_b'][layer]
    in_maps = []
    for c in range(NCORES):
        b, d = c // 2, c % 2
        xbc = px_fm[b, 0:2048, :]
        dt = px_tm[b, :, 1024:1040]
        cw_ = cwl
        if d == 1:
            xbc = flip_seq(xbc, 1)
            dt = flip_seq(dt, 0)
            cw_ = cwl[::-1]
        abd = np.stack([inputs['ssd_a_log'][layer, d], inputs['ssd_dt_bias'][layer, d], inputs['ssd_d'][layer, d]], 0)
        in_maps.append({"xbc": np.ascontiguousarray(xbc), "dt": np.ascontiguousarray(dt),
                        "cw": np.ascontiguousarray(cw_.T.reshape(16, 128, 5).transpose(1, 0, 2)).reshape(128, 80),
                        "cb": np.ascontiguousarray(cbl.reshape(16, 128).T),
                        "abd": np.ascontiguousarray(np.broadcast_to(abd.reshape(1, 48), (128, 48)))})
    res = run_bass_kernel_spmd(build_C1(), in_maps, core_ids=list(range(NCORES)))
    out = np.zeros((NB, 2, LT, 1024), np.float32)
    for c in range(NCORES):
        b, d = c // 2, c % 2
        yv = res.results[c]["y"]
        out[b, d] = flip_seq(yv, 0) if d == 1 else yv
    return out


def rstd_cols(S, ss, cols, inv_ns, key):
    c0 = cols[0]
    for (a, b), inv in zip(cols[1], inv_ns):
        S.op('dve', lambda e: e.tensor_scalar(out=ss[:, a:b], in0=ss[:, a:b], scalar1=inv, scalar2=EPS, op0=ALU.mult,
                                              op1=ALU.add), r=[key], w=[key])
    a, b = c0
    S.op('act', lambda e: e.activation(out=ss[:, a:b], in_=ss[:, a:b], func=AF.Sqrt), r=[key], w=[key])
    S.op('dve', lambda e: e.reciprocal(out=ss[:, a:b], in_=ss[:, a:b]), r=[key], w=[key])


def rope_ops(S, src3, dst3, cos, sin, nh, tmp, rkeys, wkey, tkey):
    s4 = src3.rearrange("p h (i two) -> p h i two", two=2)
    d4 = dst3.rearrange("p h (i two) -> p h i two", two=2)
    x1, x2 = s4[:, :, :, 0], s4[:, :, :, 1]
    cb_, sb_ = bc(cos, 1, [128, nh, 16]), bc(sin, 1, [128, nh, 16])
    for i, (xa, tb) in enumerate([(x1, cb_), (x2, sb_), (x1, sb_), (x2, cb_)]):
        S.op('dve', lambda e: e.tensor_tensor(out=tmp[:, i, :, :], in0=xa, in1=tb, op=ALU.mult), r=rkeys, w=[tkey])
    S.op('dve', lambda e: e.tensor_tensor(out=d4[:, :, :, 0], in0=tmp[:, 0, :, :], in1=tmp[:, 1, :, :], op=ALU.subtract),
         r=[tkey], w=[wkey])
    S.op('dve', lambda e: e.tensor_tensor(out=d4[:, :, :, 1], in0=tmp[:, 2, :, :], in1=tmp[:, 3, :, :], op=ALU.add),
         r=[tkey], w=[wkey])


def build_C2():
    nc = new_nc()
    mla = din(nc, "mla", [LT, 800])
    gqa = din(nc, "gqa", [128, 4])
    gkv = din(nc, "gkv", [128, 2])
    wq = din(nc, "wq", [512, 768])
    wkv = din(nc, "wkv", [256, 1024])
    qg = din(nc, "qg", [128, 96])
    kg = din(nc, "kg", [128, 96])
    cs = din(nc, "cs", [LT, 32])
    oT = dout(nc, "oT", [512, LT])
    SCALE = 96.0 ** -0.5
    with ExitStack() as ctx:
        S = get_sched(nc, ctx)
        S.nrot = 4
        ident = make_ident(S)
        ones = S.sb('ones', [128, 64])
        ones_f = S.sb('ones_f', [128, 64])
        S.op('pool', lambda e: e.memset(ones_f[:], 1.0), w=['ones_f'])
        S.op('dve', lambda e: e.tensor_copy(out=R(ones[:]), in_=ones_f[:]), r=['ones_f'], w=['ones'])
        gqat = S.sb('gqat', [128, 4]); gkvt = S.sb('gkvt', [128, 2])
        wqt = S.sb('wqt', [128, 4, 768]); wkvt = S.sb('wkvt', [128, 2, 1024])
        qgt = S.sb('qgt', [128, 96]); kgt = S.sb('kgt', [128, 96])
        cst = S.sb('cst', [128, NTL, 32])
        S.dma('sp', gqat[:], gqa[:, :], w=['gqat'])
        S.dma('sp', gkvt[:], gkv[:, :], w=['gkvt'])
        S.dma('pool', R(wqt[:]), wq.rearrange("(k p) n -> p k n", p=128), w=['wqt'])
        S.dma('pool', R(wkvt[:]), wkv.rearrange("(k p) n -> p k n", p=128), w=['wkvt'])
        S.dma('sp', qgt[:], qg[:, :], w=['qgt'])
        S.dma('sp', kgt[:], kg[:, :], w=['kgt'])
        S.dma('sp', cst[:], cs.rearrange("(t p) c -> p t c", p=128), w=['cst'])
        qaT = S.sb('qaT', [128, 4, LT])
        kvT = S.sb('kvT', [128, 2, LT])
        kpr = S.sb('kpr', [128, NTL, 32])
        mtb = [S.sb(f'mt{i}', [128, 800]) for i in range(2)]
        junk = S.sb('junk', [128, 512])
        ss = S.sb('ss', [128, 16])
        kpn = S.sb('kpn', [128, 1, 32])
        rtmp = S.sb('rtmp', [128, 4, 4, 16])
        for t in range(NTL):
            mt = mtb[t % 2]
            mk = ('mt', t % 2)
            S.dma('sp', mt[:], mla[t * 128:(t + 1) * 128, :], w=[mk])
            S.op('dve', lambda e: e.memset(ss[:, 0:3], 0.0), w=['ss'])
            for i, (a, b) in enumerate([(0, 512), (512, 768), (768, 800)]):
                S.op('act', lambda e: e.activation(out=junk[:, 0:b - a], in_=mt[:, a:b], func=AF.Square,
                                                   accum_out=ss[:, i:i + 1]), r=[mk], w=['junk', 'ss'])
            rstd_cols(S, ss, ((0, 3), [(0, 1), (1, 2), (2, 3)]), [1.0 / 512, 1.0 / 256, 1.0 / 32], 'ss')
            S.op('dve', lambda e: e.tensor_scalar(out=mt[:, 0:512], in0=mt[:, 0:512], scalar1=ss[:, 0:1], scalar2=None,
                                                  op0=ALU.mult), r=[mk, 'ss'], w=[mk])
            S.op('dve', lambda e: e.tensor_scalar(out=mt[:, 512:768], in0=mt[:, 512:768], scalar1=ss[:, 1:2], scalar2=None,
                                                  op0=ALU.mult), r=[mk, 'ss'], w=[mk])
            S.op('dve', lambda e: e.scalar_tensor_tensor(out=kpn[:, 0, :], in0=mt[:, 768:800], scalar=ss[:, 2:3],
                                                         in1=kgt[:, 64:96], op0=ALU.mult, op1=ALU.mult),
                 r=[mk, 'ss', 'kgt'], w=['kpn'])
            rope_ops(S, kpn[:], kpr[:, t:t + 1, :], cst[:, t, 0:16], cst[:, t, 16:32], 1, rtmp[:, :, 0:1, :],
                     ['kpn', 'cst'], 'kpr', 'rtmp')
            ps, pk = S.next_ps()
            for k in range(4):
                S.op('pe', lambda e: e.transpose(out=ps[:, k * 128:(k + 1) * 128], in_=mt[:, k * 128:(k + 1) * 128],
                                                 identity=ident[:]), r=[mk, 'ident'], w=[pk])
            for k in range(4):
                S.op('act', lambda e: e.activation(out=R(qaT[:, k, t * 128:(t + 1) * 128]), in_=ps[:, k * 128:(k + 1) * 128],
                                                   func=AF.Identity, scale=gqat[:, k:k + 1]), r=[pk, 'gqat'], w=['qaT'])
            ps, pk = S.next_ps()
            for k in range(2):
                S.op('pe', lambda e: e.transpose(out=ps[:, k * 128:(k + 1) * 128], in_=mt[:, 512 + k * 128:640 + k * 128],
                                                 identity=ident[:]), r=[mk, 'ident'], w=[pk])
            for k in range(2):
                S.op('act', lambda e: e.activation(out=R(kvT[:, k, t * 128:(t + 1) * 128]), in_=ps[:, k * 128:(k + 1) * 128],
                                                   func=AF.Identity, scale=gkvt[:, k:k + 1]), r=[pk, 'gkvt'], w=['kvT'])
        kT = S.sb('kT', [128, 4, LT])
        vv = S.sb('vv', [128, NTL, 4, 64])
        kfull = S.sb('kfull', [128, 4, 96])
        sq = S.sb('sq', [128, 4, 96])
        t1 = S.sb('t1', [128, 4, 64])
        qn = S.sb('qn', [128, 4, 96])
        qp = S.sb('qp', [128, 4, 32])
        qT = S.sb('qT', [128, 4, 512])
        ptb = [S.sb(f'pt{i}', [128, 512]) for i in range(3)]
        rden = S.sb('rden', [64, 512])
        osb = [S.sb(f'osb{i}', [64, 512]) for i in range(2)]
        groups = [[0, 1]] + [list(range(2 + 4 * g, 6 + 4 * g)) for g in range(4)]
        pti = 0
        oi = 0
        hcount = 0
        for hp in range(2):
            for t in range(NTL):
                tsl = slice(t * 128, (t + 1) * 128)
                psK, kK = S.next_ps()
                for k in range(2):
                    S.op('pe', lambda e: e.matmul(psK[:, :], lhsT=R(kvT[:, k, tsl]), rhs=R(wkvt[:, k, hp * 512:(hp + 1) * 512]),
                                                  start=(k == 0), stop=(k == 1)), r=['kvT', 'wkvt'], w=[kK])
                pk3 = psK[:, :].rearrange("p (h c) -> p h c", c=128)
                S.op('act', lambda e: e.activation(out=sq[:, :, 0:64], in_=pk3[:, :, 0:64], func=AF.Square), r=[kK], w=['sq'])
                S.op('dve', lambda e: e.tensor_reduce(out=ss[:, 4:8], in_=sq[:, :, 0:64], axis=AX.X, op=ALU.add),
                     r=['sq'], w=['ss'])
                rstd_cols(S, ss, ((4, 8), [(4, 8)]), [1.0 / 64], 'ss')
                S.op('dve', lambda e: e.tensor_tensor(out=t1[:], in0=pk3[:, :, 0:64], in1=bc(ss[:, 4:8], 2, [128, 4, 64]),
                                                      op=ALU.mult), r=[kK, 'ss'], w=['t1'])
                S.op('dve', lambda e: e.tensor_tensor(out=kfull[:, :, 0:64], in0=t1[:], in1=bc(kgt[:, 0:64], 1, [128, 4, 64]),
                                                      op=ALU.mult), r=['t1', 'kgt'], w=['kfull'])
                S.op('dve', lambda e: e.tensor_copy(out=kfull[:, :, 64:96], in_=bc(kpr[:, t, :], 1, [128, 4, 32])),
                     r=['kpr'], w=['kfull'])
                S.op('act', lambda e: e.activation(out=R(vv[:, t, :, :]), in_=pk3[:, :, 64:128], func=AF.Copy), r=[kK], w=['vv'])
                psT, kTk = S.next_ps()
                for hd in range(4):
                    S.op('pe', lambda e: e.transpose(out=psT[0:96, hd * 128:(hd + 1) * 128], in_=kfull[:, hd, :],
                                                     identity=ident[:]), r=['kfull', 'ident'], w=[kTk])
                S.op('act', lambda e: e.activation(out=R(kT[0:96, :, tsl]), in_=psT[0:96, :].rearrange("p (h c) -> p h c", c=128),
                                                   func=AF.Copy), r=[kTk], w=['kT'])
            for gi, tiles in enumerate(groups):
                nq = len(tiles) * 128
                key_tiles = [0, 1] if gi == 0 else list(range(NTL))
                for li, t in enumerate(tiles):
                    tsl = slice(t * 128, (t + 1) * 128)
                    psQ, kQ = S.next_ps()
                    for k in range(4):
                        S.op('pe', lambda e: e.matmul(psQ[:, 0:384], lhsT=R(qaT[:, k, tsl]), rhs=R(wqt[:, k, hp * 384:(hp + 1) * 384]),
                                                      start=(k == 0), stop=(k == 3)), r=['qaT', 'wqt'], w=[kQ])
                    pq3 = psQ[:, 0:384].rearrange("p (h c) -> p h c", c=96)
                    S.op('act', lambda e: e.activation(out=sq[:], in_=pq3, func=AF.Square), r=[kQ], w=['sq'])
                    S.op('dve', lambda e: e.tensor_reduce(out=ss[:, 8:12], in_=sq[:, :, 0:64], axis=AX.X, op=ALU.add),
                         r=['sq'], w=['ss'])
                    S.op('dve', lambda e: e.tensor_reduce(out=ss[:, 12:16], in_=sq[:, :, 64:96], axis=AX.X, op=ALU.add),
                         r=['sq'], w=['ss'])
                    rstd_cols(S, ss, ((8, 16), [(8, 12), (12, 16)]), [1.0 / 64, 1.0 / 32], 'ss')
                    S.op('dve', lambda e: e.tensor_tensor(out=t1[:], in0=pq3[:, :, 0:64], in1=bc(ss[:, 8:12], 2, [128, 4, 64]),
                                                          op=ALU.mult), r=[kQ, 'ss'], w=['t1'])
                    S.op('dve', lambda e: e.tensor_tensor(out=qn[:, :, 0:64], in0=t1[:], in1=bc(qgt[:, 0:64], 1, [128, 4, 64]),
                                                          op=ALU.mult), r=['t1', 'qgt'], w=['qn'])
                    S.op('dve', lambda e: e.tensor_tensor(out=qp[:], in0=pq3[:, :, 64:96], in1=bc(ss[:, 12:16], 2, [128, 4, 32]),
                                                          op=ALU.mult), r=[kQ, 'ss'], w=['qp'])
                    S.op('dve', lambda e: e.tensor_tensor(out=qp[:], in0=qp[:], in1=bc(qgt[:, 64:96], 1, [128, 4, 32]),
                                                          op=ALU.mult), r=['qp', 'qgt'], w=['qp'])
                    rope_ops(S, qp[:], qn[:, :, 64:96], cst[:, t, 0:16], cst[:, t, 16:32], 4, rtmp[:], ['qp', 'cst'], 'qn', 'rtmp')
                    psT, kTk = S.next_ps()
                    for hd in range(4):
                        S.op('pe', lambda e: e.transpose(out=psT[0:96, hd * 128:(hd + 1) * 128], in_=qn[:, hd, :],
                                                         identity=ident[:]), r=['qn', 'ident'], w=[kTk])
                    S.op('act', lambda e: e.activation(out=R(qT[0:96, :, li * 128:(li + 1) * 128]),
                                                       in_=psT[0:96, :].rearrange("p (h c) -> p h c", c=128), func=AF.Copy),
                         r=[kTk], w=['qT'])
                for hd in range(4):
                    ao, ad = (4, 5) if hcount % 2 == 0 else (6, 7)
                    hcount += 1
                    pso, psd = S.ps[ao], S.ps[ad]
                    ko, kd = ('ps', ao), ('ps', ad)
                    for ki, kc in enumerate(key_tiles):
                        ksl = slice(kc * 128, (kc + 1) * 128)
                        psS, kS = S.next_ps()
                        S.op('pe', lambda e: e.matmul(psS[:, 0:nq], lhsT=R(kT[0:96, hd, ksl]), rhs=R(qT[0:96, hd, 0:nq]),
                                                      start=True, stop=True), r=['kT', 'qT'], w=[kS])
                        pt = ptb[pti % 3]
                        ptk = ('pt', pti % 3)
                        pti += 1
                        S.op('act', lambda e: e.activation(out=R(pt[:, 0:nq]), in_=psS[:, 0:nq], func=AF.Exp, scale=SCALE),
                             r=[kS], w=[ptk])
                        S.op('pe', lambda e: e.matmul(pso[0:64, 0:nq], lhsT=R(vv[:, kc, hd, :]), rhs=R(pt[:, 0:nq]),
                                                      start=(ki == 0), stop=(ki == len(key_tiles) - 1)), r=['vv', ptk], w=[ko])
                        S.op('pe', lambda e: e.matmul(psd[0:64, 0:nq], lhsT=R(ones[:, :]), rhs=R(pt[:, 0:nq]),
                                                      start=(ki == 0), stop=(ki == len(key_tiles) - 1)), r=['ones', ptk], w=[kd])
                    S.op('dve', lambda e: e.reciprocal(out=rden[:, 0:nq], in_=psd[0:64, 0:nq]), r=[kd], w=['rden'])
                    ob = osb[oi % 2]
                    obk = ('osb', oi % 2)
                    oi += 1
                    S.op('dve', lambda e: e.tensor_tensor(out=ob[:, 0:nq], in0=pso[0:64, 0:nq], in1=rden[:, 0:nq], op=ALU.mult),
                         r=[ko, 'rden'], w=[obk])
                    hrow = (hp * 4 + hd) * 64
                    S.dma('sp', oT[hrow:hrow + 64, tiles[0] * 128:tiles[0] * 128 + nq], ob[:, 0:nq], r=[obk], is_out=True)
        S.finish()
    return nc


def rope_table():
    n_freq = 8
    inv = (10000.0 ** (-np.arange(n_freq, dtype=np.float32) / n_freq)).astype(np.float32)
    rows = np.repeat(np.arange(SEQ // 64, dtype=np.float32), 64)
    cols = np.tile(np.arange(64, dtype=np.float32), SEQ // 64)
    ang = np.concatenate([rows[:, None] * inv, cols[:, None] * inv], axis=-1).astype(np.float32)
    cs = np.zeros((LT, 32), np.float32)
    cs[:CTX, 0:16] = 1.0
    cs[CTX:, 0:16] = np.cos(ang)
    cs[CTX:, 16:32] = np.sin(ang)
    return cs


def rep128(v):
    return np.ascontiguousarray(np.broadcast_to(np.asarray(v, np.float32).reshape(1, -1), (128, v.size)))


def run_C2(px_tm, inputs, layer):
    cs = rope_table()
    in_maps = []
    for c in range(NCORES):
        b, hf = c // 2, c % 2
        in_maps.append({
            "mla": np.ascontiguousarray(px_tm[b, :, 1040:1840]),
            "gqa": np.ascontiguousarray(inputs['mla_q_a_gain'][layer].reshape(4, 128).T),
            "gkv": np.ascontiguousarray(inputs['mla_kv_a_gain'][layer].reshape(2, 128).T),
            "wq": np.ascontiguousarray(inputs['mla_w_q_b'][layer][:, hf * 768:(hf + 1) * 768]),
            "wkv": np.ascontiguousarray(inputs['mla_w_kv_b'][layer][:, hf * 1024:(hf + 1) * 1024]),
            "qg": rep128(inputs['mla_q_gain'][layer]), "kg": rep128(inputs['mla_k_gain'][layer]), "cs": cs})
    res = run_bass_kernel_spmd(build_C2(), in_maps, core_ids=list(range(NCORES)))
    out = np.zeros((NB, 1024, LT), np.float32)
    for c in range(NCORES):
        b, hf = c // 2, c % 2
        out[b, hf * 512:(hf + 1) * 512] = res.results[c]["oT"]
    return out


PI = float(np.pi)
S5_CH = [(0, 512), (512, 1024), (1024, 1536), (1536, 2048), (2048, 2304)]


def sin_reduced(S, dst, src, offset, sign, tmps, rkeys, wkey, pref):
    ki, kf, r, g = tmps
    kk = [pref + n for n in ('ki', 'kf', 'r', 'g')]
    S.op('dve', lambda e: e.tensor_scalar(out=r, in0=src, scalar1=offset + 8.0 * PI, scalar2=None, op0=ALU.add),
         r=rkeys, w=[kk[2]])
    S.op('dve', lambda e: e.tensor_scalar(out=ki, in0=r, scalar1=1.0 / (2.0 * PI), scalar2=None, op0=ALU.mult),
         r=[kk[2]], w=[kk[0]])
    S.op('dve', lambda e: e.tensor_copy(out=kf, in_=ki), r=[kk[0]], w=[kk[1]])
    S.op('dve', lambda e: e.scalar_tensor_tensor(out=r, in0=kf, scalar=-2.0 * PI, in1=r, op0=ALU.mult, op1=ALU.add),
         r=[kk[1], kk[2]], w=[kk[2]])
    S.op('dve', lambda e: e.tensor_scalar(out=g, in0=r, scalar1=PI, scalar2=-2.0 * PI, op0=ALU.is_gt, op1=ALU.mult),
         r=[kk[2]], w=[kk[3]])
    S.op('dve', lambda e: e.tensor_tensor(out=r, in0=r, in1=g, op=ALU.add), r=[kk[2], kk[3]], w=[kk[2]])
    S.op('dve', lambda e: e.tensor_scalar(out=r, in0=r, scalar1=-PI, scalar2=PI, op0=ALU.max, op1=ALU.min),
         r=[kk[2]], w=[kk[2]])
    S.op('act', lambda e: e.activation(out=dst, in_=r, func=AF.Sin, scale=float(sign)), r=[kk[2]], w=[wkey])


def build_C3():
    nc = new_nc()
    u = din(nc, "u", [1024, LT])
    lamp_ = din(nc, "lamp", [128, 2 * 96])
    lamr_ = din(nc, "lamr", [32, 2 * 3 * 4096])
    bT_ = din(nc, "bT", [32, 2 * 2 * 4096])
    cbd_ = din(nc, "cbd", [128, 2 * 2 * 1024])
    yTo = [dout(nc, "yT0", [1024, LT]), dout(nc, "yT1", [1024, LT])]
    with ExitStack() as ctx:
        S = get_sched(nc, ctx)
        lp = S.sb('lp', [128, 3, 32])
        cb_ = S.sb('cbd_sb', [128, 2, 32, 32])
        dtp = S.sb('dtp', [128, 32]); magp = S.sb('magp', [128, 32]); thp = S.sb('thp', [128, 32])
        BbT = S.sb('BbT', [32, 2, 32, 128])
        W = 512
        names = ['lr', 'li', 'ld', 'br', 'bi', 'mag', 'th', 'sn', 'cs', 'm', 'abr', 'abi', 'den', 'fr', 'fi', 'ta', 'tb']
        T = {n: S.sb('r_' + n, [32, W]) for n in names}
        T['ki'] = S.sb('r_ki', [32, W], I32)
        T['kf'] = S.sb('r_kf', [32, W])
        T['g'] = S.sb('r_g', [32, W])
        io_i = S.sb('io_i', [128, 512], I32)
        io_f = S.sb('io_f', [128, 512])
        S.op('pool', lambda e: e.iota(io_i[:], pattern=[[1, 512]], base=1, channel_multiplier=0), w=['io_i'])
        S.op('dve', lambda e: e.tensor_copy(out=io_f[:], in_=io_i[:]), r=['io_i'], w=['io_f'])
        Er = S.sb('Er', [128, 512]); Ei = S.sb('Ei', [128, 512]); amag = S.sb('amag', [128, 512])
        phi = S.sb('phi', [128, 512]); mm = S.sb('mm', [128, 512])
        pki = S.sb('pki', [128, 512], I32); pkf = S.sb('pkf', [128, 512]); pg = S.sb('pg', [128, 512])
        ub = [S.sb(f'ub{i}', [32, LT]) for i in range(2)]
        wre = S.sb('wre', [128, 512]); wim = S.sb('wim', [128, 512]); ta = S.sb('ta', [128, 512]); tb = S.sb('tb', [128, 512])
        zre = S.sb('zre', [128, 512]); zim = S.sb('zim', [128, 512])
        xre = [S.sb(f'xre{i}', [128, 512]) for i in range(2)]
        xim = [S.sb(f'xim{i}', [128, 512]) for i in range(2)]
        tc_ = S.sb('tc', [128, 512]); td = S.sb('td', [128, 512])
        yb = [S.sb(f'yb{i}', [32, 512]) for i in range(2)]

        def tt(o, a_, b_, op, eng='dve'):
            S.op(eng, lambda e: e.tensor_tensor(out=T[o][:], in0=T[a_][:], in1=T[b_][:], op=op), r=[a_, b_], w=[o])

        yi = 0
        ubi = 0
        for d in range(2):
            S.dma('sp', lp[:].rearrange("p a j -> p (a j)"), lamp_[:, d * 96:(d + 1) * 96], w=['lp'])
            S.dma('sp', cb_[:].rearrange("p a j s -> p (a j s)"), cbd_[:, d * 2048:(d + 1) * 2048], w=['cbd'])
            S.op('dve', lambda e: e.tensor_scalar(out=cb_[:, 1], in0=cb_[:, 1], scalar1=-1.0, scalar2=None, op0=ALU.mult),
                 r=['cbd'], w=['cbd'])
            S.op('act', lambda e: e.activation(out=dtp[:], in_=lp[:, 2, :], func=AF.Exp), r=['lp'], w=['dtp'])
            S.op('dve', lambda e: e.tensor_tensor(out=magp[:], in0=lp[:, 0, :], in1=dtp[:], op=ALU.mult), r=['lp', 'dtp'], w=['magp'])
            S.op('act', lambda e: e.activation(out=magp[:], in_=magp[:], func=AF.Exp), r=['magp'], w=['magp'])
            S.op('dve', lambda e: e.tensor_tensor(out=thp[:], in0=lp[:, 1, :], in1=dtp[:], op=ALU.mult), r=['lp', 'dtp'], w=['thp'])
            for pc in range(8):
                for i, n in enumerate(['lr', 'li', 'ld']):
                    o0 = d * 3 * 4096 + i * 4096 + pc * W
                    S.dma('sp', T[n][:], lamr_[:, o0:o0 + W], w=[n])
                for i, n in enumerate(['br', 'bi']):
                    o0 = d * 2 * 4096 + i * 4096 + pc * W
                    S.dma('sp', T[n][:], bT_[:, o0:o0 + W], w=[n])
                S.op('act', lambda e: e.activation(out=T['ld'][:], in_=T['ld'][:], func=AF.Exp), r=['ld'], w=['ld'])
                tt('mag', 'lr', 'ld', ALU.mult)
                S.op('act', lambda e: e.activation(out=T['mag'][:], in_=T['mag'][:], func=AF.Exp), r=['mag'], w=['mag'])
                tt('th', 'li', 'ld', ALU.mult)
                rt = (T['ki'][:], T['kf'][:], T['m'][:], T['g'][:])
                sin_reduced(S, T['sn'][:], T['th'][:], 0.0, 1.0, rt, ['th'], 'sn', 'R')
                sin_reduced(S, T['cs'][:], T['th'][:], 0.5 * PI, 1.0, rt, ['th'], 'cs', 'R')
                tt('abr', 'mag', 'cs', ALU.mult)
                tt('abi', 'mag', 'sn', ALU.mult)
                S.op('dve', lambda e: e.tensor_scalar(out=T['abr'][:], in0=T['abr'][:], scalar1=-1.0, scalar2=None, op0=ALU.add),
                     r=['abr'], w=['abr'])
                tt('den', 'lr', 'lr', ALU.mult)
                tt('ta', 'li', 'li', ALU.mult)
                tt('den', 'den', 'ta', ALU.add)
                S.op('dve', lambda e: e.reciprocal(out=T['den'][:], in_=T['den'][:]), r=['den'], w=['den'])
                tt('fr', 'abr', 'lr', ALU.mult)
                tt('ta', 'abi', 'li', ALU.mult)
                tt('fr', 'fr', 'ta', ALU.add)
                tt('fr', 'fr', 'den', ALU.mult)
                tt('fi', 'abi', 'lr', ALU.mult)
                tt('ta', 'abr', 'li', ALU.mult)
                tt('fi', 'fi', 'ta', ALU.subtract)
                tt('fi', 'fi', 'den', ALU.mult)
                o_re = BbT[:, 0, 4 * pc:4 * pc + 4, :].rearrange("p j c -> p (j c)")
                o_im = BbT[:, 1, 4 * pc:4 * pc + 4, :].rearrange("p j c -> p (j c)")
                tt('ta', 'br', 'fr', ALU.mult)
                tt('tb', 'bi', 'fi', ALU.mult)
                S.op('dve', lambda e: e.tensor_tensor(out=o_re, in0=T['ta'][:], in1=T['tb'][:], op=ALU.subtract),
                     r=['ta', 'tb'], w=['BbT'])
                tt('ta', 'br', 'fi', ALU.mult)
                tt('tb', 'bi', 'fr', ALU.mult)
                S.op('dve', lambda e: e.tensor_tensor(out=o_im, in0=T['ta'][:], in1=T['tb'][:], op=ALU.add),
                     r=['ta', 'tb'], w=['BbT'])
            if d == 0:
                chunks = S5_CH
            else:
                chunks = [(0, 256), (1792, 2304), (1280, 1792), (768, 1280), (256, 768)]

            def V(ap, n):
                v = ap[:, 0:n]
                return v if d == 0 else v[:, ::-1]

            for j in range(32):
                ubj = ub[ubi % 2]
                uk = ('ub', ubi % 2)
                ubi += 1
                S.dma('sp', ubj[:], u[32 * j:32 * j + 32, :], w=[uk])
                S.op('dve', lambda e: e.tensor_scalar(out=phi[:], in0=io_f[:], scalar1=thp[:, j:j + 1], scalar2=None, op0=ALU.mult),
                     r=['io_f', 'thp'], w=['phi'])
                pt_ = (pki[:], pkf[:], mm[:], pg[:])
                sin_reduced(S, Ei[:], phi[:], 0.0, -1.0, pt_, ['phi'], 'Ei', 'P')
                sin_reduced(S, Er[:], phi[:], 0.5 * PI, 1.0, pt_, ['phi'], 'Er', 'P')
                S.op('dve', lambda e: e.tensor_copy(out=amag[:], in_=magp[:, j:j + 1].to_broadcast([128, 512])),
                     r=['magp'], w=['amag'])
                for ci, (c0, c1) in enumerate(chunks):
                    n = c1 - c0
                    Erv, Eiv = V(Er, n), V(Ei, n)
                    psR, kR = S.next_ps()
                    psI, kI = S.next_ps()
                    S.op('pe', lambda e: e.matmul(psR[:, 0:n], lhsT=BbT[:, 0, j, :], rhs=ubj[:, c0:c1], start=True, stop=True),
                         r=['BbT', uk], w=[kR])
                    S.op('pe', lambda e: e.matmul(psI[:, 0:n], lhsT=BbT[:, 1, j, :], rhs=ubj[:, c0:c1], start=True, stop=True),
                         r=['BbT', uk], w=[kI])
                    S.op('dve', lambda e: e.tensor_tensor(out=wre[:, 0:n], in0=psR[:, 0:n], in1=Erv, op=ALU.mult), r=[kR, 'Er'], w=['wre'])
                    S.op('dve', lambda e: e.tensor_tensor(out=ta[:, 0:n], in0=psI[:, 0:n], in1=Eiv, op=ALU.mult), r=[kI, 'Ei'], w=['ta'])
                    S.op('dve', lambda e: e.tensor_tensor(out=wim[:, 0:n], in0=psI[:, 0:n], in1=Erv, op=ALU.mult), r=[kI, 'Er'], w=['wim'])
                    S.op('dve', lambda e: e.tensor_tensor(out=tb[:, 0:n], in0=psR[:, 0:n], in1=Eiv, op=ALU.mult), r=[kR, 'Ei'], w=['tb'])
                    S.op('pool', lambda e: e.tensor_tensor(out=wre[:, 0:n], in0=wre[:, 0:n], in1=ta[:, 0:n], op=ALU.subtract),
                         r=['wre', 'ta'], w=['wre'])
                    S.op('pool', lambda e: e.tensor_tensor(out=wim[:, 0:n], in0=wim[:, 0:n], in1=tb[:, 0:n], op=ALU.add),
                         r=['wim', 'tb'], w=['wim'])
                    xr, xi_ = xre[ci % 2], xim[ci % 2]
                    xrk, xik = ('xre', ci % 2), ('xim', ci % 2)
                    if ci == 0:
                        ini_r, ini_i, rk = 0.0, 0.0, []
                    else:
                        pn = chunks[ci - 1][1] - chunks[ci - 1][0]
                        col = pn - 1 if d == 0 else 0
                        ini_r = xre[(ci - 1) % 2][:, col:col + 1]
                        ini_i = xim[(ci - 1) % 2][:, col:col + 1]
                        rk = [('xre', (ci - 1) % 2), ('xim', (ci - 1) % 2)]
                    S.op('dve', lambda e: e.tensor_tensor_scan(out=V(zre, n), data0=amag[:, 0:n], data1=V(wre, n), initial=ini_r,
                                                               op0=ALU.mult, op1=ALU.add), r=['amag', 'wre'] + rk, w=['zre'])
                    S.op('dve', lambda e: e.tensor_tensor_scan(out=V(zim, n), data0=amag[:, 0:n], data1=V(wim, n), initial=ini_i,
                                                               op0=ALU.mult, op1=ALU.add), r=['amag', 'wim'] + rk, w=['zim'])
                    S.op('dve', lambda e: e.tensor_tensor(out=xr[:, 0:n], in0=zre[:, 0:n], in1=Erv, op=ALU.mult), r=['zre', 'Er'], w=[xrk])
                    S.op('pool', lambda e: e.tensor_tensor(out=tc_[:, 0:n], in0=zim[:, 0:n], in1=Eiv, op=ALU.mult), r=['zim', 'Ei'], w=['tc'])
                    S.op('dve', lambda e: e.tensor_tensor(out=xr[:, 0:n], in0=xr[:, 0:n], in1=tc_[:, 0:n], op=ALU.add), r=[xrk, 'tc'], w=[xrk])
                    S.op('pool', lambda e: e.tensor_tensor(out=xi_[:, 0:n], in0=zim[:, 0:n], in1=Erv, op=ALU.mult), r=['zim', 'Er'], w=[xik])
                    S.op('pool', lambda e: e.tensor_tensor(out=td[:, 0:n], in0=zre[:, 0:n], in1=Eiv, op=ALU.mult), r=['zre', 'Ei'], w=['td'])
                    S.op('pool', lambda e: e.tensor_tensor(out=xi_[:, 0:n], in0=xi_[:, 0:n], in1=td[:, 0:n], op=ALU.subtract), r=[xik, 'td'], w=[xik])
                    psY, kY = S.next_ps()
                    S.op('pe', lambda e: e.matmul(psY[0:32, 0:n], lhsT=cb_[:, 0, j, :], rhs=xr[:, 0:n], start=True, stop=False),
                         r=['cbd', xrk], w=[kY])
                    S.op('pe', lambda e: e.matmul(psY[0:32, 0:n], lhsT=cb_[:, 1, j, :], rhs=xi_[:, 0:n], start=False, stop=True),
                         r=['cbd', xik], w=[kY])
                    ybb = yb[yi % 2]
                    ybk = ('yb', yi % 2)
                    yi += 1
                    S.op('act', lambda e: e.activation(out=ybb[:, 0:n], in_=psY[0:32, 0:n], func=AF.Copy), r=[kY], w=[ybk])
                    S.dma('pool', yTo[d][32 * j:32 * j + 32, c0:c1], ybb[:, 0:n], r=[ybk], is_out=True)
        S.finish()
    return nc


def run_C3(px_fm, inputs, layer):
    in_maps = []
    for c in range(NCORES):
        b, d = c // 2, c % 2
        u = px_fm[b, 2048:3072, :]
        if d == 1:
            u = flip_seq(u, 1)
        lre = inputs['s5_lam_re'][layer, d]
        lim = inputs['s5_lam_im'][layer, d]
        ldt = np.broadcast_to(inputs['s5_log_dt'][layer, d][:, None], (64, 64))
        P = lambda a: a.reshape(32, 128).T
        lamp = np.concatenate([P(lre), P(lim), P(ldt)], axis=1)
        Rl = lambda a: np.broadcast_to(a.reshape(1, 4096), (32, 4096))
        lamr = np.concatenate([Rl(lre), Rl(lim), Rl(ldt)], axis=1)

        def bdT(bm):
            o = np.zeros((2, 16, 32, 2, 64), np.float32)
            bb = bm.reshape(32, 2, 64, 16)
            for gg in range(2):
                o[gg, :, :, gg, :] = bb[:, gg].transpose(2, 0, 1)
            return o.reshape(32, 4096)

        def cbdf(cm):
            o = np.zeros((2, 64, 32, 2, 16), np.float32)
            cc = cm.reshape(32, 2, 16, 64)
            for gg in range(2):
                o[gg, :, :, gg, :] = cc[:, gg].transpose(2, 0, 1)
            return o.reshape(128, 1024)

        in_maps.append({"u": np.ascontiguousarray(u), "lamp": np.ascontiguousarray(lamp, dtype=np.float32),
                        "lamr": np.ascontiguousarray(lamr, dtype=np.float32),
                        "bT": np.concatenate([bdT(inputs['s5_b_re'][layer, d]), bdT(inputs['s5_b_im'][layer, d])], axis=1),
                        "cbd": np.concatenate([cbdf(inputs['s5_c_re'][layer, d]), cbdf(inputs['s5_c_im'][layer, d])], axis=1)})
    res = run_bass_kernel_spmd(build_C3(), in_maps, core_ids=list(range(NCORES)))
    out = np.zeros((NB, 2, 1024, LT), np.float32)
    for c in range(NCORES):
        b, d = c // 2, c % 2
        yv = res.results[c]["yT"]
        out[b, d] = flip_seq(yv, 1) if d == 1 else yv
    return out


TOK_RANGES = [(0, 512), (512, 1024), (1024, 1152)]
GELU_K2 = 2.0 * 0.7978845608028654


def build_D1():
    nc = new_nc()
    NT = NT_B
    NTOK = NT * 128
    y0 = din(nc, "y0", [NTOK, 1024]); y1 = din(nc, "y1", [NTOK, 1024]); z = din(nc, "z", [NTOK, 1024])
    gssd = din(nc, "gssd", [128, 8])
    y5a = din(nc, "y5a", [1024, NTOK]); y5b = din(nc, "y5b", [1024, NTOK]); uT = din(nc, "uT", [1024, NTOK])
    s5d = din(nc, "s5d", [128, 8])
    wglu = din(nc, "wglu", [1024, 1024])
    ysT = dout(nc, "ysT", [1024, NTOK]); y5T = dout(nc, "y5T", [1024, NTOK])
    with ExitStack() as ctx:
        S = get_sched(nc, ctx)
        ident = make_ident(S)
        gs = S.sb('gs', [128, 8]); sd = S.sb('sd', [128, 8]); wg = S.sb('wg', [128, 8, 1024])
        S.dma('sp', gs[:], gssd[:, :], w=['gs'])
        S.dma('sp', sd[:], s5d[:, :], w=['sd'])
        S.dma('sp', wg[:], wglu.rearrange("(k p) n -> p k n", p=128), w=['wg'])
        yb0 = [S.sb(f'yb0_{i}', [128, 1024]) for i in range(2)]
        yb1 = [S.sb(f'yb1_{i}', [128, 1024]) for i in range(2)]
        zb = [S.sb(f'zb_{i}', [128, 1024]) for i in range(2)]
        junk = S.sb('junk', [128, 1024])
        ss = S.sb('ss', [128, NT])
        S.op('dve', lambda e: e.memset(ss[:], 0.0), w=['ss'])
        ob = [S.sb(f'ob{i}', [128, 8, 128]) for i in range(2)]
        ysv = ysT.rearrange("(k p) n -> p k n", p=128)
        for t in range(NT):
            i = t % 2
            rows = slice(t * 128, (t + 1) * 128)
            S.dma('sp', yb0[i][:], y0[rows, :], w=[('yb0', i)])
            S.dma('sp', yb1[i][:], y1[rows, :], w=[('yb1', i)])
            S.dma('sp', zb[i][:], z[rows, :], w=[('zb', i)])
            S.op('dve', lambda e: e.tensor_tensor(out=yb0[i][:], in0=yb0[i][:], in1=yb1[i][:], op=ALU.add),
                 r=[('yb0', i), ('yb1', i)], w=[('yb0', i)])
            S.op('act', lambda e: e.activation(out=zb[i][:], in_=zb[i][:], func=AF.Silu), r=[('zb', i)], w=[('zb', i)])
            S.op('dve', lambda e: e.tensor_tensor(out=yb0[i][:], in0=yb0[i][:], in1=zb[i][:], op=ALU.mult),
                 r=[('yb0', i), ('zb', i)], w=[('yb0', i)])
            S.op('act', lambda e: e.activation(out=junk[:], in_=yb0[i][:], func=AF.Square, accum_out=ss[:, t:t + 1]),
                 r=[('yb0', i)], w=['junk', 'ss'])
            rstd_cols(S, ss, ((t, t + 1), [(t, t + 1)]), [1.0 / 1024], 'ss')
            S.op('dve', lambda e: e.tensor_scalar(out=yb0[i][:], in0=yb0[i][:], scalar1=ss[:, t:t + 1], scalar2=None,
                                                  op0=ALU.mult), r=[('yb0', i), 'ss'], w=[('yb0', i)])
            o = ob[i]
            for kk in range(2):
                ps, pk = S.next_ps()
                for j in range(4):
                    k = kk * 4 + j
                    S.op('pe', lambda e: e.transpose(out=ps[:, j * 128:(j + 1) * 128], in_=yb0[i][:, k * 128:(k + 1) * 128],
                                                     identity=ident[:]), r=[('yb0', i), 'ident'], w=[pk])
                for j in range(4):
                    k = kk * 4 + j
                    S.op('act', lambda e: e.activation(out=o[:, k, :], in_=ps[:, j * 128:(j + 1) * 128], func=AF.Identity,
                                                       scale=gs[:, k:k + 1]), r=[pk, 'gs'], w=[('ob', i)])
            S.dma('pool', ysv[:, :, rows], o[:], r=[('ob', i)], is_out=True)
        va = S.sb('va', [128, 8, 512]); vb = S.sb('vb', [128, 8, 512]); vu = S.sb('vu', [128, 8, 512])
        x2 = S.sb('x2', [128, 8, 512]); sg = S.sb('sg', [128, 512]); o5 = [S.sb(f'o5_{i}', [128, 512]) for i in range(2)]
        y5v = y5T.rearrange("(k p) n -> p k n", p=128)
        oi = 0
        for (n0, n1) in TOK_RANGES:
            n = n1 - n0
            for src, dst, kname in [(y5a, va, 'va'), (y5b, vb, 'vb'), (uT, vu, 'vu')]:
                S.dma('sp', dst[:, :, 0:n], src.rearrange("(k p) n -> p k n", p=128)[:, :, n0:n1], w=[kname])
            S.op('dve', lambda e: e.tensor_tensor(out=vu[:, :, 0:n], in0=vu[:, :, 0:n], in1=bc(sd[:], 2, [128, 8, n]), op=ALU.mult),
                 r=['vu', 'sd'], w=['vu'])
            S.op('dve', lambda e: e.tensor_tensor(out=va[:, :, 0:n], in0=va[:, :, 0:n], in1=vb[:, :, 0:n], op=ALU.add),
                 r=['va', 'vb'], w=['va'])
            S.op('dve', lambda e: e.tensor_tensor(out=va[:, :, 0:n], in0=va[:, :, 0:n], in1=vu[:, :, 0:n], op=ALU.add),
                 r=['va', 'vu'], w=['va'])
            S.op('dve', lambda e: e.tensor_tensor(out=x2[:, :, 0:n], in0=va[:, :, 0:n], in1=va[:, :, 0:n], op=ALU.mult),
                 r=['va'], w=['x2'])
            S.op('dve', lambda e: e.tensor_scalar(out=x2[:, :, 0:n], in0=x2[:, :, 0:n], scalar1=0.044715, scalar2=1.0,
                                                  op0=ALU.mult, op1=ALU.add), r=['x2'], w=['x2'])
            S.op('dve', lambda e: e.tensor_tensor(out=x2[:, :, 0:n], in0=x2[:, :, 0:n], in1=va[:, :, 0:n], op=ALU.mult),
                 r=['x2', 'va'], w=['x2'])
            S.op('act', lambda e: e.activation(out=x2[:, :, 0:n], in_=x2[:, :, 0:n], func=AF.Sigmoid, scale=GELU_K2),
                 r=['x2'], w=['x2'])
            S.op('dve', lambda e: e.tensor_tensor(out=va[:, :, 0:n], in0=va[:, :, 0:n], in1=x2[:, :, 0:n], op=ALU.mult),
                 r=['va', 'x2'], w=['va'])
            for m in range(8):
                ps, pk = S.next_ps()
                for k in range(8):
                    S.op('pe', lambda e: e.matmul(ps[:, 0:n], lhsT=wg[:, k, m * 128:(m + 1) * 128], rhs=va[:, k, 0:n],
                                                  start=(k == 0), stop=(k == 7)), r=['wg', 'va'], w=[pk])
                S.op('act', lambda e: e.activation(out=sg[:, 0:n], in_=ps[:, 0:n], func=AF.Sigmoid), r=[pk], w=['sg'])
                o = o5[oi % 2]
                ok = ('o5', oi % 2)
                oi += 1
                S.op('dve', lambda e: e.tensor_tensor(out=o[:, 0:n], in0=va[:, m, 0:n], in1=sg[:, 0:n], op=ALU.mult),
                     r=['va', 'sg'], w=[ok])
                S.dma('pool', y5v[:, m, n0:n1], o[:, 0:n], r=[ok], is_out=True)
        S.finish()
    return nc


def run_D1(yssd, ys5, px_tm, px_fm, inputs, layer):
    in_maps = []
    for c in range(NCORES):
        b, h = c // 2, c % 2
        r0, r1 = core_rows(b, h)
        ca = np.ascontiguousarray
        in_maps.append({"y0": ca(yssd[b, 0, r0:r1]), "y1": ca(yssd[b, 1, r0:r1]), "z": ca(px_tm[b, r0:r1, 0:1024]),
                        "gssd": ca(inputs['ssd_norm_gain'][layer].reshape(8, 128).T),
                        "y5a": ca(ys5[b, 0, :, r0:r1]), "y5b": ca(ys5[b, 1, :, r0:r1]), "uT": ca(px_fm[b, 2048:3072, r0:r1]),
                        "s5d": ca(inputs['s5_d'][layer].reshape(8, 128).T), "wglu": inputs['s5_w_glu'][layer]})
    res = run_bass_kernel_spmd(build_D1(), in_maps, core_ids=list(range(NCORES)))
    ysT = np.zeros((NB, 1024, LT), np.float32)
    y5T = np.zeros((NB, 1024, LT), np.float32)
    for c in range(NCORES):
        b, h = c // 2, c % 2
        r0, r1 = core_rows(b, h)
        ysT[b, :, r0:r1] = res.results[c]["ysT"]
        y5T[b, :, r0:r1] = res.results[c]["y5T"]
    return ysT, y5T


def tile_gate(S, dst, gx, flags, t, cols, key):
    S.op('dve', lambda e: e.scalar_tensor_tensor(out=dst, in0=gx[:, 1, cols], scalar=flags[:, t:t + 1], in1=gx[:, 0, cols],
                                                 op0=ALU.mult, op1=ALU.add), r=['gx', 'flags'], w=[key])


def load_gx(S, gxr, flg, NT):
    gx = S.sb('gx', [128, 2, D])
    flags = S.sb('flags', [128, NT])
    S.dma('sp', gx[:, 0, :], gxr[0:1, :].to_broadcast([128, D]), w=['gx'])
    S.dma('sp', gx[:, 1, :], gxr[1:2, :].to_broadcast([128, D]), w=['gx'])
    S.dma('sp', flags[:], flg[:, :], w=['flags'])
    S.op('dve', lambda e: e.tensor_tensor(out=gx[:, 1, :], in0=gx[:, 1, :], in1=gx[:, 0, :], op=ALU.subtract),
         r=['gx'], w=['gx'])
    return gx, flags


def build_D2():
    nc = new_nc()
    NT = NT_B
    NTOK = NT * 128
    yin = [din(nc, nm, [1024, NTOK]) for nm in ("ysT", "omT", "y5T")]
    gT = din(nc, "gT", [6144, NTOK])
    wb = [din(nc, nm, [1024, D]) for nm in ("wbs", "wbm", "wb5")]
    wo = din(nc, "wo", [D, D])
    x = din(nc, "x", [NTOK, D])
    gxr = din(nc, "gxr", [2, D])
    flg = din(nc, "flg", [128, NT])
    x1 = dout(nc, "x1", [NTOK, D])
    with ExitStack() as ctx:
        S = get_sched(nc, ctx)
        gx, flags = load_gx(S, gxr, flg, NT)
        mT = S.sb('mT', [128, 16, NTOK])
        big = S.sb('big', [128, 18432])
        yT = [big[:, br * 4096:(br + 1) * 4096].rearrange("p (k n) -> p k n", k=8) for br in range(3)]
        wbt = [big[:, 12288 + br * 2048:12288 + (br + 1) * 2048].rearrange("p (k n) -> p k n", k=8) for br in range(3)]
        wot = [big[:, i * 8192:(i + 1) * 8192].rearrange("p (k n) -> p k n", k=16) for i in range(2)]
        gtl = [S.sb(f'gt{i}', [128, 512]) for i in range(3)]
        tmp = S.sb('tmp', [128, 512])
        gi = 0
        for (n0, n1) in TOK_RANGES:
            n = n1 - n0
            for br in range(3):
                S.dma('pool', R(yT[br][:, :, 0:n]), yin[br].rearrange("(k p) n -> p k n", p=128)[:, :, n0:n1], w=[('yT', br)])
            for db in range(8):
                for br in range(3):
                    S.dma('pool', R(wbt[br][:, :, :]), wb[br].rearrange("(k p) n -> p k n", p=128)[:, :, db * 256:(db + 1) * 256],
                          w=[('wbt', br)])
                for mm in range(2):
                    m = db * 2 + mm
                    for br in range(3):
                        g = gtl[gi % 3]
                        gk = ('gt', gi % 3)
                        gi += 1
                        S.dma('sp', g[:, 0:n], gT[br * 2048 + m * 128:br * 2048 + (m + 1) * 128, n0:n1], w=[gk])
                        S.op('act', lambda e: e.activation(out=g[:, 0:n], in_=g[:, 0:n], func=AF.Sigmoid), r=[gk], w=[gk])
                        ps, pk = S.next_ps()
                        for k in range(8):
                            S.op('pe', lambda e: e.matmul(ps[:, 0:n], lhsT=R(wbt[br][:, k, mm * 128:(mm + 1) * 128]),
                                                          rhs=R(yT[br][:, k, 0:n]), start=(k == 0), stop=(k == 7)),
                                 r=[('wbt', br), ('yT', br)], w=[pk])
                        if br == 0:
                            S.op('dve', lambda e: e.tensor_tensor(out=R(mT[:, m, n0:n1]), in0=ps[:, 0:n], in1=g[:, 0:n], op=ALU.mult),
                                 r=[pk, gk], w=[('mT', m)])
                        else:
                            S.op('dve', lambda e: e.tensor_tensor(out=tmp[:, 0:n], in0=ps[:, 0:n], in1=g[:, 0:n], op=ALU.mult),
                                 r=[pk, gk], w=['tmp'])
                            S.op('pool', lambda e: e.tensor_tensor(out=R(mT[:, m, n0:n1]), in0=mT[:, m, n0:n1], in1=tmp[:, 0:n],
                                                                   op=ALU.add), r=[('mT', m), 'tmp'], w=[('mT', m)])
        allk = [('yT', br) for br in range(3)] + [('wbt', br) for br in range(3)]
        mkeys = [('mT', m) for m in range(16)]
        xt = [S.sb(f'xt{i}', [128, 512]) for i in range(3)]
        gtb = S.sb('gtb', [128, 512])
        xi = 0
        for cb in range(4):
            cols = slice(cb * 512, (cb + 1) * 512)
            w_ = wot[cb % 2]
            wk = ('wot', cb % 2)
            S.dma('pool', R(w_[:, 0:8, :]), wo.rearrange("(k p) n -> p k n", p=128)[:, 0:8, cols], w=[wk] + (allk if cb < 2 else []))
            S.dma('pool', R(w_[:, 8:16, :]), wo.rearrange("(k p) n -> p k n", p=128)[:, 8:16, cols], w=[wk])
            for t in range(NT):
                rows = slice(t * 128, (t + 1) * 128)
                xx = xt[xi % 3]
                xk = ('xt', xi % 3)
                xi += 1
                S.dma('sp', xx[:], x[rows, cols], w=[xk])
                ps, pk = S.next_ps()
                for k in range(16):
                    S.op('pe', lambda e: e.matmul(ps[:, :], lhsT=R(mT[:, k, rows]), rhs=R(w_[:, k, :]), start=(k == 0), stop=(k == 15)),
                         r=mkeys + [wk], w=[pk])
                tile_gate(S, gtb[:], gx, flags, t, cols, 'gtb')
                S.op('dve', lambda e: e.tensor_tensor(out=gtb[:], in0=ps[:, :], in1=gtb[:], op=ALU.mult), r=[pk, 'gtb'], w=['gtb'])
                S.op('pool', lambda e: e.tensor_tensor(out=xx[:], in0=xx[:], in1=gtb[:], op=ALU.add), r=[xk, 'gtb'], w=[xk])
                S.dma('sp', x1[rows, cols], xx[:], r=[xk], is_out=True)
        S.finish()
    return nc


def make_flags(h, NT=9):
    f = np.zeros((128, NT), np.float32)
    for t in range(NT):
        if tile_is_ctx(h, t):
            f[:, t] = 1.0
    return f


def mod_vec(modT, which, r):
    return np.ascontiguousarray(modT[:, which * 16:(which + 1) * 16, r].T).reshape(D)


def run_D2(ysT, omT, y5T, px_fm, xseq, modT, inputs, layer):
    in_maps = []
    ca = np.ascontiguousarray
    for c in range(NCORES):
        b, h = c // 2, c % 2
        r0, r1 = core_rows(b, h)
        in_maps.append({"ysT": ca(ysT[b, :, r0:r1]), "omT": ca(omT[b, :, r0:r1]), "y5T": ca(y5T[b, :, r0:r1]),
                        "gT": ca(px_fm[b, 3072:9216, r0:r1]),
                        "wbs": inputs['w_branch_ssd'][layer], "wbm": inputs['w_branch_mla'][layer],
                        "wb5": inputs['w_branch_s5'][layer], "wo": inputs['w_out'][layer],
                        "x": ca(xseq[b, r0:r1]), "gxr": np.stack([mod_vec(modT, 2, b), mod_vec(modT, 2, 4)], 0),
                        "flg": make_flags(h)})
    res = run_bass_kernel_spmd(build_D2(), in_maps, core_ids=list(range(NCORES)))
    x1 = np.zeros((NB, LT, D), np.float32)
    for c in range(NCORES):
        b, h = c // 2, c % 2
        r0, r1 = core_rows(b, h)
        x1[b, r0:r1] = res.results[c]["x1"]
    return x1


def build_D3():
    nc = new_nc()
    NT = NT_B
    NTOK = NT * 128
    xin = din(nc, "x", [NTOK, D])
    msel = din(nc, "msel", [128, NT * 16 * 2])
    gn = din(nc, "gn", [128, 16])
    wr = din(nc, "wr", [D, 16])
    hx = dout(nc, "hx", [NTOK, D])
    aff = dout(nc, "aff", [NTOK, 16])
    affTo = dout(nc, "affT", [16, NTOK])
    with ExitStack() as ctx:
        S = get_sched(nc, ctx)
        ident = make_ident(S)
        hT = S.sb('hT', [128, 16, NTOK])
        ms, g1 = load_mod(S, msel, gn, NT)
        wrt = S.sb('wrt', [128, 16, 16])
        S.dma('sp', wrt[:], wr.rearrange("(k p) e -> p k e", p=128), w=['wrt'])
        norm_mod_tiles(S, xin, NT, ms, g1, hT, ident, 'n2')
        hb = [S.sb(f'hb{i}', [128, D]) for i in range(2)]
        lg = S.sb('lg', [128, 16]); mx = S.sb('mx', [128, 1]); sm_ = S.sb('sm', [128, 1])
        ab = [S.sb(f'ab{i}', [128, 16]) for i in range(2)]
        atb = [S.sb(f'atb{i}', [16, 128]) for i in range(2)]
        for t in range(NT):
            rows = slice(t * 128, (t + 1) * 128)
            h = hb[t % 2]
            hk = ('hb', t % 2)
            for kk in range(4):
                ps, pk = S.next_ps()
                for j in range(4):
                    k = kk * 4 + j
                    S.op('pe', lambda e: e.transpose(out=ps[:, j * 128:(j + 1) * 128], in_=hT[:, k, rows], identity=ident[:]),
                         r=[('hT', t), 'ident'], w=[pk])
                if kk % 2 == 0:
                    S.op('act', lambda e: e.activation(out=h[:, kk * 512:(kk + 1) * 512], in_=ps[:, :], func=AF.Copy), r=[pk], w=[hk])
                else:
                    S.op('dve', lambda e: e.tensor_copy(out=h[:, kk * 512:(kk + 1) * 512], in_=ps[:, :]), r=[pk], w=[hk])
            S.dma('pool', hx[rows, :], h[:], r=[hk], is_out=True)
            ps, pk = S.next_ps()
            for k in range(16):
                S.op('pe', lambda e: e.matmul(ps[:, 0:16], lhsT=hT[:, k, rows], rhs=wrt[:, k, :], start=(k == 0), stop=(k == 15)),
                     r=[('hT', t), 'wrt'], w=[pk])
            a = ab[t % 2]
            ak = ('ab', t % 2)
            S.op('dve', lambda e: e.tensor_copy(out=lg[:], in_=ps[:, 0:16]), r=[pk], w=['lg'])
            S.op('dve', lambda e: e.tensor_reduce(out=mx[:], in_=lg[:], axis=AX.X, op=ALU.max), r=['lg'], w=['mx'])
            S.op('dve', lambda e: e.tensor_scalar(out=mx[:], in0=mx[:], scalar1=-1.0, scalar2=None, op0=ALU.mult), r=['mx'], w=['mx'])
            S.op('dve', lambda e: e.memset(sm_[:], 0.0), w=['sm'])
            S.op('act', lambda e: e.activation(out=a[:], in_=lg[:], func=AF.Exp, bias=mx[:, 0:1], scale=1.0, accum_out=sm_[:, 0:1]),
                 r=['lg', 'mx', 'sm'], w=[ak, 'sm'])
            S.op('dve', lambda e: e.reciprocal(out=sm_[:], in_=sm_[:]), r=['sm'], w=['sm'])
            S.op('dve', lambda e: e.tensor_scalar(out=a[:], in0=a[:], scalar1=sm_[:, 0:1], scalar2=None, op0=ALU.mult),
                 r=[ak, 'sm'], w=[ak])
            S.dma('pool', aff[rows, :], a[:], r=[ak], is_out=True)
            ps, pk = S.next_ps()
            S.op('pe', lambda e: e.transpose(out=ps[0:16, 0:128], in_=a[:, 0:16], identity=ident[:]), r=[ak, 'ident'], w=[pk])
            at = atb[t % 2]
            atk = ('atb', t % 2)
            S.op('dve', lambda e: e.tensor_copy(out=at[:], in_=ps[0:16, 0:128]), r=[pk], w=[atk])
            S.dma('pool', affTo[:, rows], at[:], r=[atk], is_out=True)
        S.finish()
    return nc


def run_D3(x1, modT, inputs, layer):
    gn = np.ascontiguousarray(inputs['norm2_gain'][layer].reshape(16, 128).T)
    in_maps = []
    for c in range(NCORES):
        b, h = c // 2, c % 2
        r0, r1 = core_rows(b, h)
        in_maps.append({"x": np.ascontiguousarray(x1[b, r0:r1]), "msel": make_msel(modT, b, h, 4, 3), "gn": gn,
                        "wr": inputs['moe_router'][layer]})
    res = run_bass_kernel_spmd(build_D3(), in_maps, core_ids=list(range(NCORES)))
    hx2 = np.zeros((NB, LT, D), np.float32)
    aff = np.zeros((NB, LT, 16), np.float32)
    for c in range(NCORES):
        b, h = c // 2, c % 2
        r0, r1 = core_rows(b, h)
        hx2[b, r0:r1] = res.results[c]["hx"]
        aff[b, r0:r1] = res.results[c]["aff"]
    return hx2, aff


def build_E(with_ctx, do_zero=True):
    nc = new_nc()
    NE = 8
    NSL = 288 if with_ctx else 256
    affT = din(nc, "affT", [NE, LT])
    hx = din(nc, "hx", [LT, D])
    wg = din(nc, "wg", [NE, D, D]); wu = din(nc, "wu", [NE, D, D]); wd = din(nc, "wd", [NE, D, D])
    delta = dout(nc, "delta", [LT, D])
    with ExitStack() as ctx:
        S = get_sched(nc, ctx)
        ident = make_ident(S)
        ysb = [S.sb(f'ys{i}', [128, D]) for i in range(2)]
        if do_zero:
            S.op('pool', lambda e: e.memset(ysb[0][:], 0.0), w=[('ys', 0)])
            for t in range(NTL):
                S.dma('pool', delta[t * 128:(t + 1) * 128, :], ysb[0][:], r=[('ys', 0)], w=['delta'], is_out=True)
        work = S.sb('work', [NE, LT])
        S.dma('sp', work[:], affT[:, :], w=['work'])
        vals = S.sb('vals', [NE, 288]); idxu = S.sb('idxu', [NE, 288], U32); idxf = S.sb('idxf', [NE, 288])
        S.op('dve', lambda e: e.memset(vals[:], 0.0), w=['vals'])
        S.op('dve', lambda e: e.memset(idxf[:], 0.0), w=['idxf'])
        segs = [(CTX, LT, 0, 32, float(CTX))] + ([(0, CTX, 256, 4, 0.0)] if with_ctx else [])
        for (a0, a1, s0, rounds, off) in segs:
            for r in range(rounds):
                sl = slice(s0 + r * 8, s0 + r * 8 + 8)
                S.op('dve', lambda e: e.max(out=vals[:, sl], in_=work[:, a0:a1]), r=['work'], w=['vals'])
                S.op('dve', lambda e: e.max_index(out=idxu[:, sl], in_max=vals[:, sl], in_values=work[:, a0:a1]),
                     r=['work', 'vals'], w=['idxu'])
                S.op('dve', lambda e: e.match_replace(out=work[:, a0:a1], in_to_replace=vals[:, sl], in_values=work[:, a0:a1],
                                                      imm_value=-1.0), r=['work', 'vals'], w=['work'])
            S.op('dve', lambda e: e.tensor_copy(out=idxf[:, s0:s0 + rounds * 8], in_=idxu[:, s0:s0 + rounds * 8]),
                 r=['idxu'], w=['idxf'])
            if off != 0.0:
                S.op('dve', lambda e: e.tensor_scalar(out=idxf[:, s0:s0 + rounds * 8], in0=idxf[:, s0:s0 + rounds * 8],
                                                      scalar1=off, scalar2=None, op0=ALU.add), r=['idxf'], w=['idxf'])
        gTt = S.sb('gTt', [128, 3, NE]); iTf = S.sb('iTf', [128, 3, NE]); iTu = S.sb('iTu', [128, 3, NE], U32)
        S.op('dve', lambda e: e.memset(iTf[:], 0.0), w=['iTf'])
        S.op('dve', lambda e: e.memset(gTt[:], 0.0), w=['gTt'])
        tiles = [(0, 128), (128, 128)] + ([(256, 32)] if with_ctx else [])
        for st, (c0, nr) in enumerate(tiles):
            for src, dst, dk in [(vals, gTt, 'gTt'), (idxf, iTf, 'iTf')]:
                ps, pk = S.next_ps()
                S.op('pe', lambda e: e.transpose(out=ps[0:nr, 0:NE], in_=src[:, c0:c0 + nr], identity=ident[0:NE, 0:NE]),
                     r=['vals', 'idxf', 'ident'], w=[pk])
                S.op('dve', lambda e: e.tensor_copy(out=dst[0:nr, st, :], in_=ps[0:nr, 0:NE]), r=[pk], w=[dk])
        S.op('dve', lambda e: e.tensor_copy(out=iTu[:], in_=iTf[:]), r=['iTf'], w=['iTu'])
        NWB = 4
        wbuf = [S.sb(f'wbuf{i}', [128, 16, 512]) for i in range(NWB)]
        wcnt = [0]

        def load_w(src2d, cols):
            i = wcnt[0] % NWB
            wcnt[0] += 1
            v = src2d.rearrange("(k p) n -> p k n", p=128)
            S.dma('pool', R(wbuf[i][:, 0:8, :]), v[:, 0:8, cols], w=[('wbuf', i)])
            S.dma('pool', R(wbuf[i][:, 8:16, :]), v[:, 8:16, cols], w=[('wbuf', i)])
            return wbuf[i], ('wbuf', i)

        xs = [S.sb(f'xs{i}', [128, D]) for i in range(1)]
        xsT = S.sb('xsT', [128, 16, NSL])
        hidT = S.sb('hidT', [128, 16, NSL])
        sg = S.sb('sg', [128, NSL])
        xi = 0
        yi = 0
        for e_ in range(NE):
            for st, (c0, nr) in enumerate(tiles):
                xx = xs[0]
                xk = ('xs', 0)
                xi += 1
                S._deps('pool', ['iTu'], [xk])
                S._guard_dma('pool')
                ins = nc.gpsimd.indirect_dma_start(out=xx[0:nr, :], out_offset=None, in_=hx[:, :],
                                                   in_offset=bass.IndirectOffsetOnAxis(ap=iTu[0:nr, st, e_:e_ + 1], axis=0))
                tok = S._finish_dma('pool', ins, ['iTu'], [xk])
                for kk in range(4):
                    ps, pk = S.next_ps()
                    for j in range(4):
                        k = kk * 4 + j
                        S.op('pe', lambda e: e.transpose(out=ps[:, j * 128:j * 128 + nr], in_=xx[0:nr, k * 128:(k + 1) * 128],
                                                         identity=ident[0:nr, 0:nr]), r=[xk, 'ident'], w=[pk])
                    S.op('act', lambda e: e.activation(out=R(xsT[:, kk * 4:kk * 4 + 4, c0:c0 + nr]),
                                                       in_=ps[:, :].rearrange("p (j c) -> p j c", c=128)[:, :, 0:nr], func=AF.Copy),
                         r=[pk], w=['xsT'])
            for fb in range(4):
                fcols = slice(fb * 512, (fb + 1) * 512)
                wgt, wgk = load_w(wg[e_], fcols)
                wut, wuk = load_w(wu[e_], fcols)
                for ff in range(4):
                    f = fb * 4 + ff
                    psg, kg_ = S.next_ps()
                    psu, ku_ = S.next_ps()
                    for k in range(16):
                        S.op('pe', lambda e: e.matmul(psg[:, 0:NSL], lhsT=R(wgt[:, k, ff * 128:(ff + 1) * 128]), rhs=R(xsT[:, k, :]),
                                                      start=(k == 0), stop=(k == 15)), r=[wgk, 'xsT'], w=[kg_])
                    for k in range(16):
                        S.op('pe', lambda e: e.matmul(psu[:, 0:NSL], lhsT=R(wut[:, k, ff * 128:(ff + 1) * 128]), rhs=R(xsT[:, k, :]),
                                                      start=(k == 0), stop=(k == 15)), r=[wuk, 'xsT'], w=[ku_])
                    S.op('act', lambda e: e.activation(out=sg[:, :], in_=psg[:, 0:NSL], func=AF.Silu), r=[kg_], w=['sg'])
                    S.op('dve', lambda e: e.tensor_tensor(out=R(hidT[:, f, :]), in0=psu[:, 0:NSL], in1=sg[:, :], op=ALU.mult),
                         r=[ku_, 'sg'], w=['hidT'])
            yts = []
            for st, (c0, nr) in enumerate(tiles):
                yts.append((ysb[yi % 2] if st < 2 else xs[0], ('ys', yi % 2) if st < 2 else ('xs', 0)))
                if st < 2:
                    yi += 1
            for cb in range(4):
                cols = slice(cb * 512, (cb + 1) * 512)
                wdt, wdk = load_w(wd[e_], cols)
                for st, (c0, nr) in enumerate(tiles):
                    yt, yk = yts[st]
                    ps, pk = S.next_ps()
                    for f in range(16):
                        S.op('pe', lambda e: e.matmul(ps[0:nr, :], lhsT=R(hidT[:, f, c0:c0 + nr]), rhs=R(wdt[:, f, :]),
                                                      start=(f == 0), stop=(f == 15)), r=['hidT', wdk], w=[pk])
                    S.op('act', lambda e: e.activation(out=yt[0:nr, cols], in_=ps[0:nr, :], func=AF.Copy,
                                                       scale=gTt[0:nr, st, e_:e_ + 1]), r=[pk, 'gTt'], w=[yk])
            for st, (c0, nr) in enumerate(tiles):
                yt, yk = yts[st]
                S._deps('pool', [yk, 'iTu'], ['delta'])
                S._guard_dma('pool')
                ins = nc.gpsimd.indirect_dma_start(out=delta[:, :], out_offset=bass.IndirectOffsetOnAxis(ap=iTu[0:nr, st, e_:e_ + 1], axis=0),
                                                   in_=yt[0:nr, :], in_offset=None, compute_op=ALU.add)
                tok = S._finish_dma('pool', ins, [yk, 'iTu'], ['delta'])
                S.out_tokens.append(tok)
        S.finish()
    return nc


def run_E(aff, hx2, inputs, layer, with_ctx, batches=(0, 1, 2, 3)):
    in_maps = []
    cores = []
    for b in batches:
        for hf in range(2):
            cores.append((b, hf))
            es = slice(8 * hf, 8 * hf + 8)
            in_maps.append({"affT": np.ascontiguousarray(aff[b].T[es]), "hx": np.ascontiguousarray(hx2[b]),
                            "wg": inputs['moe_w_gate'][layer, es], "wu": inputs['moe_w_up'][layer, es],
                            "wd": inputs['moe_w_down'][layer, es]})
    res = run_bass_kernel_spmd(build_E(with_ctx), in_maps, core_ids=list(range(len(cores))))
    out = np.zeros((NB, 2, LT, D), np.float32)
    for i, (b, hf) in enumerate(cores):
        out[b, hf] = res.results[i]["delta"]
    return out


def build_F(single=False):
    nc = new_nc()
    NT = NT_B
    NTOK = NT * 128
    x1 = din(nc, "x1", [NTOK, D]); da = din(nc, "da", [NTOK, D])
    db = None if single else din(nc, "db", [NTOK, D])
    gxr = din(nc, "gxr", [2, D]); flg = din(nc, "flg", [128, NT])
    x2 = dout(nc, "x2", [NTOK, D])
    with ExitStack() as ctx:
        S = get_sched(nc, ctx)
        gx, flags = load_gx(S, gxr, flg, NT)
        xa = [S.sb(f'xa{i}', [128, D]) for i in range(2)]
        ta = [S.sb(f'ta{i}', [128, D]) for i in range(2)]
        tb = [S.sb(f'tb{i}', [128, D]) for i in range(2)]
        gt = S.sb('gt', [128, D])
        for t in range(NT):
            i = t % 2
            rows = slice(t * 128, (t + 1) * 128)
            S.dma('sp', xa[i][:], x1[rows, :], w=[('xa', i)])
            S.dma('sp', ta[i][:], da[rows, :], w=[('ta', i)])
            if not single:
                S.dma('act', tb[i][:], db[rows, :], w=[('tb', i)])
            tile_gate(S, gt[:], gx, flags, t, slice(0, D), 'gt')
            if not single:
                S.op('pool', lambda e: e.tensor_tensor(out=ta[i][:], in0=ta[i][:], in1=tb[i][:], op=ALU.add),
                     r=[('ta', i), ('tb', i)], w=[('ta', i)])
            S.op('dve', lambda e: e.tensor_tensor(out=ta[i][:], in0=ta[i][:], in1=gt[:], op=ALU.mult), r=[('ta', i), 'gt'], w=[('ta', i)])
            S.op('pool', lambda e: e.tensor_tensor(out=xa[i][:], in0=xa[i][:], in1=ta[i][:], op=ALU.add),
                 r=[('xa', i), ('ta', i)], w=[('xa', i)])
            S.dma('pool', x2[rows, :], xa[i][:], r=[('xa', i)], is_out=True)
        S.finish()
    return nc


def run_F(x1, dl, modT):
    in_maps = []
    ca = np.ascontiguousarray
    for c in range(NCORES):
        b, h = c // 2, c % 2
        r0, r1 = core_rows(b, h)
        in_maps.append({"x1": ca(x1[b, r0:r1]), "da": ca(dl[b, 0, r0:r1]), "db": ca(dl[b, 1, r0:r1]),
                        "gxr": np.stack([mod_vec(modT, 5, b), mod_vec(modT, 5, 4)], 0), "flg": make_flags(h)})
    res = run_bass_kernel_spmd(build_F(), in_maps, core_ids=list(range(NCORES)))
    x2 = np.zeros((NB, LT, D), np.float32)
    for c in range(NCORES):
        b, h = c // 2, c % 2
        r0, r1 = core_rows(b, h)
        x2[b, r0:r1] = res.results[c]["x2"]
    return x2


def build_A2():
    nc = new_nc()
    cT = din(nc, "cT2", [128, 32])
    w = din(nc, "wmod", [D, 12288])
    b = din(nc, "bmod", [128, 96])
    modT_d = dout(nc, "modT", [128, 2 * 96])
    gvec_d = dout(nc, "gvec", [2, 12288])
    with ExitStack() as ctx:
        S = get_sched(nc, ctx)
        ident = make_ident(S)
        ct = S.sb('ct', [128, 16, 2]); ca = S.sb('ca', [128, 16, 2]); bt = S.sb('bt', [128, 96])
        mt = S.sb('mt', [128, 2, 96])
        wt = [S.sb(f'wt{i}', [128, 16, 768]) for i in range(2)]
        S.dma('sp', ct[:].rearrange("p k r -> p (k r)"), cT[:, :], w=['ct'])
        S.dma('sp', bt[:], b[:, :], w=['bt'])
        S.op('act', lambda e: e.activation(out=ca[:], in_=ct[:], func=AF.Silu), r=['ct'], w=['ca'])
        wv = w.rearrange("(k p) n -> p k n", p=128)
        for j in range(16):
            wtj = wt[j % 2]
            for g in range(4):
                S.dma('sp' if g % 2 == 0 else 'act', wtj[:, 4 * g:4 * g + 4, :], wv[:, 4 * g:4 * g + 4, j * 768:(j + 1) * 768],
                      w=[('wt', j % 2, g)])
            ps, pk = S.next_ps()
            for m in range(6):
                for k in range(16):
                    S.op('pe', lambda e: e.matmul(ps[:, m * 2:m * 2 + 2], lhsT=wtj[:, k, m * 128:(m + 1) * 128],
                                                  rhs=ca[:, k, :], start=(k == 0), stop=(k == 15)),
                         r=['ca', ('wt', j % 2, k // 4)], w=[pk])
            for m in range(6):
                c = j * 6 + m
                S.op('act', lambda e: e.activation(out=mt[:, :, c], in_=ps[:, m * 2:m * 2 + 2], func=AF.Identity,
                                                   bias=bt[:, c:c + 1], scale=1.0), r=[pk, 'bt'], w=['mt'])
        S.dma('pool', modT_d[:, :], mt[:].rearrange("p r c -> p (r c)"), r=['mt'], is_out=True)
        gv = S.sb('gv', [96, 2, 128])
        for r in range(2):
            ps, pk = S.next_ps()
            S.op('pe', lambda e: e.transpose(out=ps[0:96, 0:128], in_=mt[:, r, :], identity=ident[:]), r=['mt', 'ident'], w=[pk])
            S.op('dve', lambda e: e.tensor_copy(out=gv[:, r, :], in_=ps[0:96, 0:128]), r=[pk], w=['gv'])
            S.dma('pool', gvec_d[r, :].rearrange("(c p) -> c p", p=128), gv[:, r, :], r=['gv'], is_out=True)
        S.finish()
    return nc


def build_msel(modT_d, out_d, rows, sc_i, sh_i):
    nc = new_nc()
    with ExitStack() as ctx:
        S = get_sched(nc, ctx)
        mv = modT_d.rearrange("p (r c) -> p r c", r=2)
        ov = out_d.rearrange("p (t s k) -> p t s k", s=2, k=16)
        for t, r in enumerate(rows):
            S.dma('sp', ov[:, t, 0, :], mv[:, r, sc_i * 16:(sc_i + 1) * 16])
            S.dma('act', ov[:, t, 1, :], mv[:, r, sh_i * 16:(sh_i + 1) * 16])
        S.finish()


FUSED_INPUT_SPECS = None


_DBG = {'export': (), 'stop': None}


def build_fused():
    nc = bass.Bass("TRN2", target_bir_lowering=False)
    ext = {}

    def EI(name, shape, dt=F32):
        ext[name] = nc.dram_tensor(name, list(shape), dt, kind="ExternalInput").ap()
        return ext[name]

    def SC(name, shape, dt=F32):
        kind = "ExternalOutput" if name in _DBG['export'] else "Internal"
        return nc.dram_tensor(name, list(shape), dt, kind=kind).ap()

    class _Stop(Exception):
        pass

    nst = [0]

    def chk():
        nst[0] += 1
        if _DBG['stop'] is not None and nst[0] >= _DBG['stop']:
            raise _Stop()

    xs0 = EI("xseq", [LT, D])
    out = nc.dram_tensor("out", [SEQ, D], F32, kind="ExternalOutput").ap()
    cs = EI("cs", [LT, 32])
    flg = [EI("flg0", [128, 9]), EI("flg1", [128, 9])]
    WSPEC = dict(cT2=[128, 32], wmod=[D, 12288], bmod=[128, 96], gn1=[128, 16], gn2=[128, 16], win=[D, PROJ_IN],
                 cw=[128, 80], cb=[128, 16], abd=[128, 96], gqa=[128, 4], gkv=[128, 2], wq=[512, 1536], wkv=[256, 2048],
                 qg=[128, 96], kg=[128, 96], lamp=[128, 192], lamr=[32, 6 * 4096], bT=[32, 4 * 4096], cbd=[128, 4096],
                 gssd=[128, 8], s5d=[128, 8], wglu=[1024, 1024], wbs=[1024, D], wbm=[1024, D], wb5=[1024, D], wo=[D, D],
                 wr=[D, 16], wg=[16, D, D], wu=[16, D, D], wd=[16, D, D])

    class LazyW(dict):
        def __init__(self, l):
            super().__init__()
            self.l = l

        def __missing__(self, k):
            self[k] = EI(f"l{self.l}_{k}", WSPEC[k])
            return self[k]

    L = [LazyW(0), LazyW(1)]
    modT = SC("modT", [128, 192]); gvec = SC("gvec", [2, 12288])
    msel1 = [SC(f"msel1_{h}", [128, 9 * 32]) for h in range(2)]
    msel2 = [SC(f"msel2_{h}", [128, 9 * 32]) for h in range(2)]
    px_tm = SC("px_tm", [LT, 1840]); px_fm = SC("px_fm", [9216, LT])
    y0 = SC("y0", [LT, 1024]); y1 = SC("y1", [LT, 1024]); omT = SC("omT", [1024, LT])
    yT0 = SC("yT0", [1024, LT]); yT1 = SC("yT1", [1024, LT])
    ysT = SC("ysT", [1024, LT]); y5T = SC("y5T", [1024, LT])
    x1 = SC("x1", [LT, D]); hx = SC("hx", [LT, D]); aff = SC("aff", [LT, 16]); affT = SC("affT", [16, LT])
    delta = SC("delta", [LT, D])
    xs1 = SC("xs1", [LT, D]); xs2 = SC("xs2", [LT, D])
    with ExitStack() as gctx:
        S = Sched(nc, gctx)
        S.fused = True
        _F['nc'], _F['S'] = nc, S
        try:
            xcur = xs0
            halves = [(0, 1152), (1152, 2304)]
            rows_h = [[1, 1] + [0] * 7, [0] * 9]
            try:
                for l in range(2):
                    if nst[0] < 0:
                        break
                    W = L[l]
                    _F['io'] = dict(cT2=W['cT2'], wmod=W['wmod'], bmod=W['bmod'], modT=modT, gvec=gvec)
                    build_A2()
                    chk()
                    for h in range(2):
                        build_msel(modT, msel1[h], rows_h[h], 1, 0)
                        build_msel(modT, msel2[h], rows_h[h], 4, 3)
                    for h, (r0, r1) in enumerate(halves * _DBG.get('brep', 1)):
                        h = h % 2
                        if 'B' in _DBG.get('skip', ()):
                            continue
                        _F['io'] = dict(x=xcur[r0:r1, :], msel=msel1[h], gn=W['gn1'], w=W['win'], otm=px_tm[r0:r1, :], ofm=px_fm[:, r0:r1])
                        build_B()
                        chk()
                    _F['io'] = dict(xbc=px_fm[0:2048, :], dt=px_tm[:, 1024:1040], cw=W['cw'], cb=W['cb'], abd=W['abd'], y0=y0, y1=y1)
                    c1io = _F['io']
                    if 'C1' not in _DBG.get('skip', ()) and not _DBG.get('c1late'):
                        build_C1()
                    chk()
                    for hf in _DBG.get('c2', (0, 1)):
                        _F['io'] = dict(mla=px_tm[:, 1040:1840], gqa=W['gqa'], gkv=W['gkv'], wq=W['wq'][:, hf * 768:(hf + 1) * 768],
                                        wkv=W['wkv'][:, hf * 1024:(hf + 1) * 1024], qg=W['qg'], kg=W['kg'], cs=cs,
                                        oT=omT[hf * 512:(hf + 1) * 512, :])
                        build_C2()
                        chk()
                    if _DBG.get('c1late'):
                        _F['io'] = c1io
                        build_C1()
                        chk()
                    _F['io'] = dict(u=px_fm[2048:3072, :], lamp=W['lamp'], lamr=W['lamr'], bT=W['bT'], cbd=W['cbd'], yT0=yT0, yT1=yT1)
                    build_C3()
                    chk()
                    for h, (r0, r1) in enumerate(halves):
                        _F['io'] = dict(y0=y0[r0:r1, :], y1=y1[r0:r1, :], z=px_tm[r0:r1, 0:1024], gssd=W['gssd'],
                                        y5a=yT0[:, r0:r1], y5b=yT1[:, r0:r1], uT=px_fm[2048:3072, r0:r1], s5d=W['s5d'], wglu=W['wglu'],
                                        ysT=ysT[:, r0:r1], y5T=y5T[:, r0:r1])
                        build_D1()
                        chk()
                    for h, (r0, r1) in enumerate(halves):
                        _F['io'] = dict(ysT=ysT[:, r0:r1], omT=omT[:, r0:r1], y5T=y5T[:, r0:r1], gT=px_fm[3072:9216, r0:r1],
                                        wbs=W['wbs'], wbm=W['wbm'], wb5=W['wb5'], wo=W['wo'], x=xcur[r0:r1, :],
                                        gxr=gvec[:, 2 * D:3 * D], flg=flg[h], x1=x1[r0:r1, :])
                        build_D2()
                        chk()
                    for h, (r0, r1) in enumerate(halves):
                        _F['io'] = dict(x=x1[r0:r1, :], msel=msel2[h], gn=W['gn2'], wr=W['wr'], hx=hx[r0:r1, :], aff=aff[r0:r1, :],
                                        affT=affT[:, r0:r1])
                        build_D3()
                        chk()
                    for hf in range(2):
                        es = slice(8 * hf, 8 * hf + 8)
                        _F['io'] = dict(affT=affT[es, :], hx=hx, wg=W['wg'][es], wu=W['wu'][es], wd=W['wd'][es], delta=delta)
                        build_E(with_ctx=(l == 0), do_zero=(hf == 0))
                        chk()
                    xnext = xs1 if l == 0 else xs2
                    for h, (r0, r1) in enumerate(halves):
                        if l == 1:
                            pass
                        _F['io'] = dict(x1=x1[r0:r1, :], da=delta[r0:r1, :], gxr=gvec[:, 5 * D:6 * D], flg=flg[h],
                                        x2=xnext[r0:r1, :])
                        build_F(single=True)
                        chk()
                    xcur = xnext
                nc_ = new_nc()
                with ExitStack() as c3:
                    S3 = get_sched(nc_, c3)
                    S3.fused = False
                    for i in range(4):
                        S3.dma('sp' if i % 2 == 0 else 'act', out[i * 512:(i + 1) * 512, :], xcur[CTX + i * 512:CTX + (i + 1) * 512, :],
                               is_out=True)
                    S3.finish()
            except _Stop:
                pass
        finally:
            _F['nc'], _F['S'], _F['io'] = None, None, {}
    global FUSED_INPUT_SPECS
    FUSED_INPUT_SPECS = set(ext.keys())
    return nc


def fused_inputs(inputs, b):
    ca = np.ascontiguousarray
    m = {}
    m["xseq"] = ca(np.concatenate([inputs['ctx'][b], inputs['x'][b]], axis=0))
    m["cs"] = rope_table()
    m["flg0"] = make_flags(0)
    m["flg1"] = make_flags(1)
    for l in range(2):
        p = f"l{l}_"
        c2 = np.stack([inputs['c'][b], inputs['c_ctx']], axis=0)
        m[p + "cT2"] = ca(c2.reshape(2, 16, 128).transpose(2, 1, 0)).reshape(128, 32)
        m[p + "wmod"] = inputs['w_mod'][l]
        m[p + "bmod"] = ca(inputs['b_mod'][l].reshape(96, 128).T)
        m[p + "gn1"] = ca(inputs['norm1_gain'][l].reshape(16, 128).T)
        m[p + "gn2"] = ca(inputs['norm2_gain'][l].reshape(16, 128).T)
        m[p + "win"] = inputs['w_in'][l]
        m[p + "cw"] = ca(inputs['ssd_conv_w'][l].T.reshape(16, 128, 5).transpose(1, 0, 2)).reshape(128, 80)
        m[p + "cb"] = ca(inputs['ssd_conv_b'][l].reshape(16, 128).T)
        abd = np.stack([inputs['ssd_a_log'][l], inputs['ssd_dt_bias'][l], inputs['ssd_d'][l]], 0)
        m[p + "abd"] = rep128(abd.reshape(-1))
        m[p + "gqa"] = ca(inputs['mla_q_a_gain'][l].reshape(4, 128).T)
        m[p + "gkv"] = ca(inputs['mla_kv_a_gain'][l].reshape(2, 128).T)
        m[p + "wq"] = inputs['mla_w_q_b'][l]
        m[p + "wkv"] = inputs['mla_w_kv_b'][l]
        m[p + "qg"] = rep128(inputs['mla_q_gain'][l])
        m[p + "kg"] = rep128(inputs['mla_k_gain'][l])
        lamp, lamr, bTs, cbds = [], [], [], []
        for d in range(2):
            lre = inputs['s5_lam_re'][l, d]
            lim = inputs['s5_lam_im'][l, d]
            ldt = np.broadcast_to(inputs['s5_log_dt'][l, d][:, None], (64, 64))
            P = lambda a: a.reshape(32, 128).T
            Rl = lambda a: np.broadcast_to(a.reshape(1, 4096), (32, 4096))
            lamp += [P(lre), P(lim), P(ldt)]
            lamr += [Rl(lre), Rl(lim), Rl(ldt)]
            for bm in (inputs['s5_b_re'][l, d], inputs['s5_b_im'][l, d]):
                o = np.zeros((2, 16, 32, 2, 64), np.float32)
                bb = bm.reshape(32, 2, 64, 16)
                for gg in range(2):
                    o[gg, :, :, gg, :] = bb[:, gg].transpose(2, 0, 1)
                bTs.append(o.reshape(32, 4096))
            for cm in (inputs['s5_c_re'][l, d], inputs['s5_c_im'][l, d]):
                o = np.zeros((2, 64, 32, 2, 16), np.float32)
                cc = cm.reshape(32, 2, 16, 64)
                for gg in range(2):
                    o[gg, :, :, gg, :] = cc[:, gg].transpose(2, 0, 1)
                cbds.append(o.reshape(128, 1024))
        m[p + "lamp"] = ca(np.concatenate(lamp, axis=1), dtype=np.float32)
        m[p + "lamr"] = ca(np.concatenate(lamr, axis=1), dtype=np.float32)
        m[p + "bT"] = ca(np.concatenate(bTs, axis=1))
        m[p + "cbd"] = ca(np.concatenate(cbds, axis=1))
        m[p + "gssd"] = ca(inputs['ssd_norm_gain'][l].reshape(8, 128).T)
        m[p + "s5d"] = ca(inputs['s5_d'][l].reshape(8, 128).T)
        m[p + "wglu"] = inputs['s5_w_glu'][l]
        m[p + "wbs"] = inputs['w_branch_ssd'][l]
        m[p + "wbm"] = inputs['w_branch_mla'][l]
        m[p + "wb5"] = inputs['w_branch_s5'][l]
        m[p + "wo"] = inputs['w_out'][l]
        m[p + "wr"] = inputs['moe_router'][l]
        m[p + "wg"] = inputs['moe_w_gate'][l]
        m[p + "wu"] = inputs['moe_w_up'][l]
        m[p + "wd"] = inputs['moe_w_down'][l]
    return {k: np.ascontiguousarray(v, dtype=np.float32) for k, v in m.items() if k in FUSED_INPUT_SPECS}


def kernel(**inputs):
    inputs = {k: np.asarray(v, dtype=np.float32) for k, v in inputs.items()}
    nc = build_fused()
    maps = [fused_inputs(inputs, b) for b in range(NB)]
    in_maps = [maps[c % NB] for c in range(NCORES_F)]
    res = run_bass_kernel_spmd(nc, in_maps, core_ids=list(range(NCORES_F)))
    return np.stack([res.results[b]["out"] for b in range(NB)], axis=0).astype(np.float32)
```

```python
from contextlib import ExitStack
import numpy as np
import concourse.bass as bass
import concourse.mybir as mybir
from concourse.bass_utils import run_bass_kernel_spmd

F32 = mybir.dt.float32
F32R = mybir.dt.float32r


def R(ap):
    return ap.bitcast(F32R)
I32 = mybir.dt.int32
U32 = mybir.dt.uint32
AF = mybir.ActivationFunctionType
ALU = mybir.AluOpType
AX = mybir.AxisListType

D = 2048
NB = 4
SEQ = 2048
CTX = 256
LT = SEQ + CTX
EPS = 1e-6
PROJ_IN = 11056
NCORES = 8
NCORES_F = 4


_QMAP = {}
SAME_ENGINE_WAITS = False


class Sched:
    NDS = 8

    def __init__(self, nc, ctx):
        self.nc = nc
        self.ctx = ctx
        self.E = {'pe': nc.tensor, 'act': nc.scalar, 'dve': nc.vector, 'pool': nc.gpsimd, 'sp': nc.sync}
        self.sem = {k: ctx.enter_context(nc.semaphore('s_' + k)) for k in ['pe', 'act', 'dve', 'pool']}
        self.cnt = {k: 0 for k in self.sem}
        self.seen = {e: {} for e in self.E}
        self.dsem = {q: [ctx.enter_context(nc.semaphore(f'd_{q}{i}')) for i in range(self.NDS)]
                     for q in ['sp', 'pool', 'act']}
        self.dcnt = {q: 0 for q in self.dsem}
        self.last_w = {}
        self.readers = {}
        self.ps = [ctx.enter_context(nc.psum_tensor(f'ps{i}', [128, 512], F32)) for i in range(8)]
        self.psi = 0
        self.nrot = 8
        self.stage_id = 0
        self.qmap = dict(_QMAP)
        self.fused = False
        self.out_tokens = []

    def sb(self, name, shape, dt=F32):
        return self.ctx.enter_context(self.nc.sbuf_tensor(f"s{self.stage_id}_{name}", list(shape), dt))

    def round_r(self, eng, ap, keys):
        if eng == 'act':
            self.op('act', lambda e: e.activation(out=R(ap), in_=ap, func=AF.Copy), r=keys, w=keys)
        else:
            self.op(eng, lambda e: e.tensor_copy(out=R(ap), in_=ap), r=keys, w=keys)

    def barrier(self):
        for e in self.E:
            for f in self.sem:
                if f != e and self.cnt[f] > 0:
                    self._wait(e, (self.sem[f], self.cnt[f], f))
            for q in self.dsem:
                n = self.dcnt[q]
                for i in range(self.NDS):
                    k = (n - i + self.NDS - 1) // self.NDS if n > i else 0
                    if k > 0:
                        self._wait(e, (self.dsem[q][i], 16 * k, 'dma_' + q))
        self.last_w = {}
        self.readers = {}
        self.out_tokens = []

    def next_ps(self):
        i = self.psi
        self.psi = (self.psi + 1) % self.nrot
        return self.ps[i], ('ps', i)

    def _wait(self, e, tok):
        sem, val, src = tok
        if src == e and e == 'pe':
            return
        sid = id(sem)
        if self.seen[e].get(sid, 0) >= val:
            return
        self.E[e].wait_ge(sem, val)
        self.seen[e][sid] = val

    def _deps(self, e, r, w):
        toks = []
        for k in r:
            if k in self.last_w:
                toks.append(self.last_w[k])
        for k in w:
            if k in self.last_w:
                toks.append(self.last_w[k])
            toks.extend(self.readers.get(k, {}).values())
        for t in toks:
            self._wait(e, t)

    def _record(self, tok, r, w):
        for k in r:
            d = self.readers.setdefault(k, {})
            sid = id(tok[0])
            if sid not in d or d[sid][1] < tok[1]:
                d[sid] = tok
        for k in w:
            self.last_w[k] = tok
            self.readers[k] = {}

    def op(self, e, fn, r=(), w=()):
        self._deps(e, r, w)
        ins = fn(self.E[e])
        self.cnt[e] += 1
        ins.then_inc(self.sem[e], 1)
        tok = (self.sem[e], self.cnt[e], e)
        self._record(tok, r, w)
        return tok

    def _guard_dma(self, q):
        n = self.dcnt[q]
        sem = self.dsem[q][n % self.NDS]
        prev = 16 * (n // self.NDS)
        if prev > 0 and self.seen[q].get(id(sem), 0) < prev:
            self.E[q].wait_ge(sem, prev)
            self.seen[q][id(sem)] = prev
        return sem, prev

    def _finish_dma(self, q, ins, r, w):
        n = self.dcnt[q]
        sem = self.dsem[q][n % self.NDS]
        prev = 16 * (n // self.NDS)
        ins.then_inc(sem, 16)
        self.dcnt[q] += 1
        tok = (sem, prev + 16, 'dma_' + q)
        self._record(tok, r, w)
        return tok

    def dma(self, q, out, in_, r=(), w=(), is_out=False, **kw):
        q = self.qmap.get(q, q)
        self._deps(q, r, w)
        self._guard_dma(q)
        ins = self.E[q].dma_start(out=out, in_=in_, **kw)
        tok = self._finish_dma(q, ins, r, w)
        if is_out:
            self.out_tokens.append(tok)
        return tok

    def finish(self):
        if self.fused:
            self.barrier()
            return
        for t in self.out_tokens:
            self._wait('sp', t)


_F = {'nc': None, 'S': None, 'io': {}}


def new_nc():
    if _F['nc'] is not None:
        return _F['nc']
    return bass.Bass("TRN2", target_bir_lowering=False)


def _io(nc, name, shape, dt, kind):
    if _F['nc'] is not None:
        ap = _F['io'][name]
        assert tuple(ap.shape) == tuple(shape), (name, ap.shape, shape)
        return ap
    return nc.dram_tensor(name, list(shape), dt, kind=kind).ap()


def din(nc, name, shape, dt=F32):
    return _io(nc, name, shape, dt, "ExternalInput")


def dout(nc, name, shape, dt=F32):
    return _io(nc, name, shape, dt, "ExternalOutput")


def get_sched(nc, ctx):
    if _F['S'] is None:
        return Sched(nc, ctx)
    S = _F['S']
    S.ctx = ctx
    S.stage_id += 1
    S.nrot = 8
    S.psi = 0
    return S


def make_ident(S, name='ident'):
    nc = S.nc
    idt = S.sb(name, [128, 128])
    S.op('pool', lambda e: e.memset(idt[:], 1.0), w=[name])
    S.op('pool', lambda e: e.affine_select(out=idt[:], in_=idt[:], pattern=[[-1, 128]], compare_op=ALU.is_equal,
                                           fill=0.0, base=0, channel_multiplier=1), r=[name], w=[name])
    return idt


def build_A():
    nc = new_nc()
    cT = din(nc, "cT", [128, 16 * 5])
    w = din(nc, "w", [D, 1536])
    b = din(nc, "b", [128, 12])
    o = dout(nc, "o", [128, 60])
    with ExitStack() as ctx:
        S = get_sched(nc, ctx)
        ct = S.sb('ct', [128, 16, 5])
        ca = S.sb('ca', [128, 16, 5])
        bt = S.sb('bt', [128, 12])
        wt = S.sb('wt', [128, 16, 1536])
        ot = S.sb('ot', [128, 12, 5])
        S.dma('sp', ct[:].rearrange("p k r -> p (k r)"), cT[:, :], w=['ct'])
        S.dma('sp', bt[:], b[:, :], w=['bt'])
        wv = w.rearrange("(k p) n -> p k n", p=128)
        for g in range(8):
            q = 'sp' if g % 2 == 0 else 'pool'
            S.dma(q, wt[:, 2 * g:2 * g + 2, :], wv[:, 2 * g:2 * g + 2, :], w=[('wt', g)])
        S.op('act', lambda e: e.activation(out=ca[:], in_=ct[:], func=AF.Silu), r=['ct'], w=['ca'])
        ps, pk = S.next_ps()
        for m in range(12):
            for k in range(16):
                S.op('pe', lambda e: e.matmul(ps[:, m * 5:m * 5 + 5], lhsT=wt[:, k, m * 128:(m + 1) * 128],
                                              rhs=ca[:, k, :], start=(k == 0), stop=(k == 15)),
                     r=['ca', ('wt', k // 2)], w=[pk])
        for m in range(12):
            S.op('act', lambda e: e.activation(out=ot[:, m, :], in_=ps[:, m * 5:m * 5 + 5], func=AF.Identity,
                                               bias=bt[:, m:m + 1], scale=1.0), r=[pk, 'bt'], w=['ot'])
        S.dma('sp', o[:, :], ot[:].rearrange("p m r -> p (m r)"), r=['ot'], is_out=True)
        S.finish()
    return nc


def run_A(inputs, layer):
    c5 = np.concatenate([inputs['c'], inputs['c_ctx'][None, :]], axis=0)
    cT = np.ascontiguousarray(c5.reshape(5, 16, 128).transpose(2, 1, 0)).reshape(128, 80)
    wm = inputs['w_mod'][layer]
    bm = inputs['b_mod'][layer]
    in_maps = []
    for j in range(NCORES):
        in_maps.append({"cT": cT, "w": np.ascontiguousarray(wm[:, j * 1536:(j + 1) * 1536]),
                        "b": np.ascontiguousarray(bm[j * 1536:(j + 1) * 1536].reshape(12, 128).T)})
    res = run_bass_kernel_spmd(build_A(), in_maps, core_ids=list(range(NCORES)))
    modT = np.concatenate([r["o"].reshape(128, 12, 5) for r in res.results], axis=1)
    return modT


NT_B = 9
B_BLOCKS = ([(0, 512, 'tm'), (512, 1024, 'tm')] + [(1024 + 512 * i, 1536 + 512 * i, 'fm') for i in range(4)]
            + [(3072, 3584, 'tm'), (3584, 3888, 'tm'), (3888, 4400, 'fm'), (4400, 4912, 'fm')]
            + [(4912 + 512 * i, 5424 + 512 * i, 'fm') for i in range(12)])


def tm_col(c):
    return c if c < 1024 else c - 2048


def fm_row(c):
    return c - 1024 if c < 3072 else c - 1840


def norm_mod_tiles(S, xin, NT, ms, g1, hT, ident, pref, rr=False):
    xb = [S.sb(f'{pref}xb{i}', [128, D]) for i in range(2)]
    junk = S.sb(pref + 'junk', [128, D])
    ss = S.sb(pref + 'ss', [128, NT])
    rs = S.sb(pref + 'rs', [128, NT])
    S.op('dve', lambda e: e.memset(ss[:], 0.0), w=[pref + 'ss'])
    for t in range(NT):
        xt = xb[t % 2]
        xk = (pref + 'xb', t % 2)
        S.dma('sp', xt[:], xin[t * 128:(t + 1) * 128, :], w=[xk])
        S.op('act', lambda e: e.activation(out=junk[:], in_=xt[:], func=AF.Square, accum_out=ss[:, t:t + 1]),
             r=[xk], w=[pref + 'junk', pref + 'ss'])
        S.op('dve', lambda e: e.tensor_scalar(out=rs[:, t:t + 1], in0=ss[:, t:t + 1], scalar1=1.0 / D, scalar2=EPS,
                                              op0=ALU.mult, op1=ALU.add), r=[pref + 'ss'], w=[pref + 'rs'])
        S.op('act', lambda e: e.activation(out=rs[:, t:t + 1], in_=rs[:, t:t + 1], func=AF.Sqrt),
             r=[pref + 'rs'], w=[pref + 'rs'])
        S.op('dve', lambda e: e.reciprocal(out=rs[:, t:t + 1], in_=rs[:, t:t + 1]), r=[pref + 'rs'], w=[pref + 'rs'])
        S.op('dve', lambda e: e.tensor_scalar(out=xt[:], in0=xt[:], scalar1=rs[:, t:t + 1], scalar2=None,
                                              op0=ALU.mult), r=[xk, pref + 'rs'], w=[xk])
        for kk in range(4):
            ps, pk = S.next_ps()
            for j in range(4):
                k = kk * 4 + j
                S.op('pe', lambda e: e.transpose(out=ps[:, j * 128:(j + 1) * 128], in_=xt[:, k * 128:(k + 1) * 128],
                                                 identity=ident[:]), r=[xk, 'ident'], w=[pk])
            for j in range(4):
                k = kk * 4 + j
                S.op('act', lambda e: e.activation(out=(R(hT[:, k, t * 128:(t + 1) * 128]) if rr else hT[:, k, t * 128:(t + 1) * 128]), in_=ps[:, j * 128:(j + 1) * 128],
                                                   func=AF.Identity, scale=g1[:, t, k:k + 1], bias=ms[:, t, 1, k:k + 1]),
                     r=[pk, 'g1', 'ms'], w=[('hT', t)])


def load_mod(S, msel, gn, NT):
    ms = S.sb('ms', [128, NT, 2, 16])
    gnt = S.sb('gnt', [128, 16])
    g1 = S.sb('g1', [128, NT, 16])
    S.dma('sp', ms[:].rearrange("p t s k -> p (t s k)"), msel[:, :], w=['ms'])
    S.dma('sp', gnt[:], gn[:, :], w=['gnt'])
    for t in range(NT):
        S.op('dve', lambda e: e.scalar_tensor_tensor(out=g1[:, t, :], in0=ms[:, t, 0, :], scalar=1.0, in1=gnt[:],
                                                     op0=ALU.add, op1=ALU.mult), r=['ms', 'gnt'], w=['g1'])
    return ms, g1


def build_B():
    nc = new_nc()
    NT = NT_B
    NTOK = NT * 128
    xin = din(nc, "x", [NTOK, D])
    msel = din(nc, "msel", [128, NT * 16 * 2])
    gn = din(nc, "gn", [128, 16])
    w = din(nc, "w", [D, PROJ_IN])
    otm = dout(nc, "otm", [NTOK, 1840])
    ofm = dout(nc, "ofm", [9216, NTOK])
    with ExitStack() as ctx:
        S = get_sched(nc, ctx)
        ident = make_ident(S)
        hT = S.sb('hT', [128, 16, NTOK])
        ms, g1 = load_mod(S, msel, gn, NT)
        norm_mod_tiles(S, xin, NT, ms, g1, hT, ident, 'n1', rr=True)
        wbuf = [S.sb(f'wb{i}', [128, 16, 512]) for i in range(2)]
        obuf = [S.sb(f'ob{i}', [128, 512]) for i in range(4)]
        wv = w.rearrange("(k p) n -> p k n", p=128)
        oi = 0
        hkeys = [('hT', t) for t in range(NT)]
        for bi, (c0, c1, kind) in enumerate(B_BLOCKS):
            nw = c1 - c0
            wb = wbuf[bi % 2]
            S.dma('pool', R(wb[:, 0:8, :nw]), wv[:, 0:8, c0:c1], w=[('wb', bi % 2, 0)])
            S.dma('pool', R(wb[:, 8:16, :nw]), wv[:, 8:16, c0:c1], w=[('wb', bi % 2, 1)])
            if kind == 'tm':
                jobs = [('tm', t, None) for t in range(NT)]
            else:
                jobs = [('fm', m, rng) for m in range(nw // 128) for rng in [(0, 512), (512, 1024), (1024, NTOK)]]
            for kind_, a, rng in jobs:
                ps, pk = S.next_ps()
                if kind_ == 'tm':
                    t = a
                    n = nw
                    for k in range(16):
                        S.op('pe', lambda e: e.matmul(ps[:, :nw], lhsT=R(hT[:, k, t * 128:(t + 1) * 128]), rhs=R(wb[:, k, :nw]),
                                                      start=(k == 0), stop=(k == 15)),
                             r=[('hT', t), ('wb', bi % 2, k // 8)], w=[pk])
                    dst = otm[t * 128:(t + 1) * 128, tm_col(c0):tm_col(c0) + nw]
                else:
                    m = a
                    n0, n1 = rng
                    n = n1 - n0
                    for k in range(16):
                        S.op('pe', lambda e: e.matmul(ps[:, :n], lhsT=R(wb[:, k, m * 128:(m + 1) * 128]), rhs=R(hT[:, k, n0:n1]),
                                                      start=(k == 0), stop=(k == 15)),
                             r=hkeys + [('wb', bi % 2, k // 8)], w=[pk])
                    fr = fm_row(c0) + m * 128
                    dst = ofm[fr:fr + 128, n0:n1]
                ob = obuf[oi % 4]
                ok = ('ob', oi % 4)
                if oi % 2 == 0:
                    S.op('act', lambda e: e.activation(out=ob[:, :n], in_=ps[:, :n], func=AF.Copy), r=[pk], w=[ok])
                else:
                    S.op('dve', lambda e: e.tensor_copy(out=ob[:, :n], in_=ps[:, :n]), r=[pk], w=[ok])
                S.dma('sp', dst, ob[:, :n], r=[ok], is_out=True)
                oi += 1
        S.finish()
    return nc


def core_rows(b, h):
    return (0, 1152) if h == 0 else (1152, 2304)


def mod_rows(modT, b, which):
    sl = modT[:, which * 16:(which + 1) * 16, :]
    return sl[:, :, b], sl[:, :, 4]


def tile_is_ctx(h, t):
    return h == 0 and t < 2


def make_msel(modT, b, h, sc_i, sh_i, NT=9):
    scx, scc = mod_rows(modT, b, sc_i)
    shx, shc = mod_rows(modT, b, sh_i)
    ms = np.zeros((128, NT, 2, 16), np.float32)
    for t in range(NT):
        if tile_is_ctx(h, t):
            ms[:, t, 0, :] = scc
            ms[:, t, 1, :] = shc
        else:
            ms[:, t, 0, :] = scx
            ms[:, t, 1, :] = shx
    return ms.reshape(128, -1)


def run_B(xseq, modT, inputs, layer):
    gn = np.ascontiguousarray(inputs['norm1_gain'][layer].reshape(16, 128).T)
    w = inputs['w_in'][layer]
    in_maps = []
    for c in range(NCORES):
        b, h = c // 2, c % 2
        r0, r1 = core_rows(b, h)
        in_maps.append({"x": np.ascontiguousarray(xseq[b, r0:r1]), "msel": make_msel(modT, b, h, 1, 0),
                        "gn": gn, "w": w})
    res = run_bass_kernel_spmd(build_B(), in_maps, core_ids=list(range(NCORES)))
    px_tm = np.zeros((NB, LT, 1840), np.float32)
    px_fm = np.zeros((NB, 9216, LT), np.float32)
    for c in range(NCORES):
        b, h = c // 2, c % 2
        r0, r1 = core_rows(b, h)
        px_tm[b, r0:r1] = res.results[c]["otm"]
        px_fm[b, :, r0:r1] = res.results[c]["ofm"]
    return px_tm, px_fm


def bc(ap, axis, shape):
    return ap.unsqueeze(axis).to_broadcast(list(shape))


def make_masks(S, transposed=False):
    sfx = 'T' if transposed else ''
    cm, pm = (1, -1) if transposed else (-1, 1)
    U8 = S.sb('U8' + sfx, [128, 8, 128])
    ones = S.sb('ones' + sfx, [128, 128])
    nm8 = S.sb('nm8' + sfx, [128, 8, 128])
    S.op('pool', lambda e: e.memset(ones[:], 1.0), w=['ones' + sfx])
    S.op('pool', lambda e: e.memset(U8[:], 1.0), w=['U8' + sfx])
    S.op('pool', lambda e: e.affine_select(out=U8[:], in_=U8[:], pattern=[[0, 8], [pm, 128]], compare_op=ALU.is_ge,
                                           fill=0.0, base=0, channel_multiplier=cm), r=['U8' + sfx], w=['U8' + sfx])
    S.op('pool', lambda e: e.memset(nm8[:], 0.0), w=['nm8' + sfx])
    S.op('pool', lambda e: e.affine_select(out=nm8[:], in_=nm8[:], pattern=[[0, 8], [pm, 128]], compare_op=ALU.is_ge,
                                           fill=-30000.0, base=0, channel_multiplier=cm), r=['nm8' + sfx], w=['nm8' + sfx])
    return U8, ones, nm8


NTL = LT // 128


def build_C1():
    nc = new_nc()
    xbc = din(nc, "xbc", [2048, LT])
    dtin = din(nc, "dt", [LT, 16])
    cw = din(nc, "cw", [128, 16 * 5])
    cb = din(nc, "cb", [128, 16])
    abd = din(nc, "abd", [128, 96])
    yo = [dout(nc, "y0", [LT, 1024]), dout(nc, "y1", [LT, 1024])]
    with ExitStack() as ctx:
        S = get_sched(nc, ctx)
        ident = make_ident(S)
        U8, ones, nm8 = make_masks(S)
        U8T, _, nm8T = make_masks(S, transposed=True)
        cwt = S.sb('cwt', [128, 16, 5])
        cbt = S.sb('cbt', [128, 16])
        abt = S.sb('abt', [128, 3, 2, 16])
        S.dma('sp', cwt[:].rearrange("p c j -> p (c j)"), cw[:, :], w=['cwt'])
        S.dma('sp', cbt[:], cb[:, :], w=['cbt'])
        S.dma('sp', abt[:].rearrange("p a d h -> p (a d h)"), abd[:, :], w=['abt'])
        dt_all = S.sb('dt_all', [128, NTL, 16])
        dtv = S.sb('dtv', [128, 2, NTL, 16])
        a_all = S.sb('a_all', [128, 2, NTL, 16])
        aneg = S.sb('aneg', [128, 2, 16])
        S.dma('sp', dt_all[:], dtin.rearrange("(t p) h -> p t h", p=128), w=['dt_all'])
        S.op('act', lambda e: e.activation(out=aneg[:], in_=abt[:, 0, :, :], func=AF.Exp), r=['abt'], w=['aneg'])
        S.op('dve', lambda e: e.tensor_scalar(out=aneg[:], in0=aneg[:], scalar1=-1.0, scalar2=None, op0=ALU.mult),
             r=['aneg'], w=['aneg'])
        for d in range(2):
            S.op('dve', lambda e: e.tensor_tensor(out=dtv[:, d], in0=dt_all[:], in1=bc(abt[:, 1, d, :], 1, [128, NTL, 16]),
                                                  op=ALU.add), r=['dt_all', 'abt'], w=['dtv'])
            S.op('act', lambda e: e.activation(out=dtv[:, d], in_=dtv[:, d], func=AF.Exp), r=['dtv'], w=['dtv'])
            S.op('act', lambda e: e.activation(out=dtv[:, d], in_=dtv[:, d], func=AF.Ln, bias=1.0, scale=1.0), r=['dtv'], w=['dtv'])
            S.op('dve', lambda e: e.tensor_tensor(out=a_all[:, d], in0=dtv[:, d], in1=bc(aneg[:, d, :], 1, [128, NTL, 16]), op=ALU.mult),
                 r=['dtv', 'aneg'], w=['a_all'])

        pb = [S.sb(f'pb{i}', [128, LT + 8]) for i in range(2)]
        for i in range(2):
            S.op('pool', lambda e: e.memset(pb[i][:], 0.0), w=[('pb', i)])
        acc = S.sb('acc', [128, LT])
        tmpx = S.sb('tmpx', [128, LT])
        BT = S.sb('BT', [128, 2, LT])
        CT = S.sb('CT', [128, 2, LT])
        x_tm = S.sb('x_tm', [128, NTL, 512])
        B_tm = S.sb('B_tm', [128, NTL, 256])
        y_acc = S.sb('y_acc', [128, NTL, 512])
        hst = S.sb('hst', [128, 512])
        sm = S.sb('sm', [128, 64])
        aU = S.sb('aU', [128, 8, 128])
        tmp = S.sb('tmp', [128, 8, 128])
        MT = S.sb('MT', [128, 8, 128])
        xdt = S.sb('xdt', [128, 8, 64])
        xw = S.sb('xw', [128, 8, 64])
        ydsb = S.sb('ydsb', [128, 8, 64])
        segs = [(0, 0, 256), (260, 256, 2048)]
        ci_glob = 0
        for hf in range(2):
            chunks = ([('x', i, 4 * hf + i) for i in range(4)] + [('B', i, 8 + 2 * hf + i) for i in range(2)]
                      + [('C', i, 12 + 2 * hf + i) for i in range(2)])
            for kind, i, ch in chunks:
                p = pb[ci_glob % 2]
                pk = ('pb', ci_glob % 2)
                ci_glob += 1
                S.dma('sp', p[:, 2:258], xbc[ch * 128:(ch + 1) * 128, 0:256], w=[pk])
                S.dma('sp', p[:, 262:2310], xbc[ch * 128:(ch + 1) * 128, 256:LT], w=[pk])
                for (oi, oo, n) in segs:
                    S.op('dve', lambda e: e.tensor_scalar(out=acc[:, oo:oo + n], in0=p[:, oi:oi + n],
                                                          scalar1=cwt[:, ch, 0:1], scalar2=None, op0=ALU.mult),
                         r=[pk, 'cwt'], w=['acc'])
                    for j in range(1, 5):
                        S.op('dve', lambda e: e.scalar_tensor_tensor(out=acc[:, oo:oo + n], in0=p[:, oi + j:oi + j + n],
                                                                     scalar=cwt[:, ch, j:j + 1], in1=acc[:, oo:oo + n],
                                                                     op0=ALU.mult, op1=ALU.add),
                             r=[pk, 'cwt', 'acc'], w=['acc'])
                if kind == 'x':
                    dst, dk = tmpx[:], 'tmpx'
                elif kind == 'B':
                    dst, dk = BT[:, i, :], ('BT', i)
                else:
                    dst, dk = CT[:, i, :], ('CT', i)
                S.op('act', lambda e: e.activation(out=dst, in_=acc[:], func=AF.Silu, bias=cbt[:, ch:ch + 1], scale=1.0),
                     r=['acc', 'cbt'], w=[dk])
                if kind in ('x', 'B'):
                    for t0 in range(0, NTL, 4):
                        nt = min(4, NTL - t0)
                        ps, pk2 = S.next_ps()
                        for tt in range(nt):
                            t = t0 + tt
                            S.op('pe', lambda e: e.transpose(out=ps[:, tt * 128:(tt + 1) * 128],
                                                             in_=dst[:, t * 128:(t + 1) * 128], identity=ident[:]),
                                 r=[dk, 'ident'], w=[pk2])
                        if kind == 'x':
                            o_ap = x_tm[:, t0:t0 + nt, i * 128:(i + 1) * 128]
                            ok = 'x_tm'
                        else:
                            o_ap = B_tm[:, t0:t0 + nt, i * 128:(i + 1) * 128]
                            ok = 'B_tm'
                        S.op('act', lambda e: e.activation(out=o_ap, in_=ps[:, 0:nt * 128].rearrange("p (t c) -> p t c", c=128),
                                                           func=AF.Copy), r=[pk2], w=[ok])
            h0 = hf * 8
            for d in range(2):
              Um, nmm = (U8, nm8) if d == 0 else (U8T, nm8T)
              U = Um[:, 0, :]
              order = list(range(NTL)) if d == 0 else [1, 0] + list(range(NTL - 1, 1, -1))
              if True:
                  S.op('dve', lambda e: e.tensor_tensor(
                      out=y_acc[:].rearrange("p t (h d) -> p t h d", d=64), in0=x_tm[:].rearrange("p t (h d) -> p t h d", d=64),
                      in1=abt[:, 2, d, h0:h0 + 8].unsqueeze(1).unsqueeze(3).to_broadcast([128, NTL, 8, 64]), op=ALU.mult),
                      r=['x_tm', 'abt'], w=['y_acc'])
                  S.op('dve', lambda e: e.memset(hst[:], 0.0), w=['hst'])
                  for t in order:
                      tsl = slice(t * 128, (t + 1) * 128)
                      a_t = a_all[:, d, t, h0:h0 + 8]
                      dtv_t = dtv[:, d, t, h0:h0 + 8]
                      psA, kA = S.next_ps()
                      S.op('pe', lambda e: e.matmul(psA[:, 0:8], lhsT=U, rhs=a_t, start=True, stop=True),
                           r=['U8', 'U8T', 'a_all'], w=[kA])
                      S.op('pe', lambda e: e.matmul(psA[:, 8:16], lhsT=ones[:], rhs=a_t, start=True, stop=True),
                           r=['ones', 'a_all'], w=[kA])
                      S.op('dve', lambda e: e.tensor_copy(out=sm[:, 0:16], in_=psA[:, 0:16]), r=[kA], w=['sm'])
                      S.op('dve', lambda e: e.tensor_tensor(out=sm[:, 24:32], in0=sm[:, 8:16], in1=sm[:, 0:8], op=ALU.subtract),
                           r=['sm'], w=['sm'])
                      S.op('act', lambda e: e.activation(out=sm[:, 16:24], in_=sm[:, 0:8], func=AF.Exp), r=['sm'], w=['sm'])
                      S.op('act', lambda e: e.activation(out=sm[:, 24:32], in_=sm[:, 24:32], func=AF.Exp), r=['sm'], w=['sm'])
                      S.op('act', lambda e: e.activation(out=sm[:, 32:40], in_=sm[:, 8:16], func=AF.Exp), r=['sm'], w=['sm'])
                      S.op('dve', lambda e: e.tensor_tensor(out=sm[:, 24:32], in0=sm[:, 24:32], in1=dtv_t, op=ALU.mult),
                           r=['sm', 'dtv'], w=['sm'])
                      S.op('dve', lambda e: e.tensor_tensor(out=aU[:], in0=Um[:], in1=bc(a_t, 2, [128, 8, 128]), op=ALU.mult),
                           r=['U8', 'U8T', 'a_all'], w=['aU'])
                      psB = []
                      for q in range(2):
                          pq, kq = S.next_ps()
                          S.op('pe', lambda e: e.matmul(pq[:, :], lhsT=ones[:], rhs=aU[:, 4 * q:4 * q + 4, :].rearrange("p h l -> p (h l)"),
                                                        start=True, stop=True), r=['ones', 'aU'], w=[kq])
                          psB.append((pq, kq))
                      for q in range(2):
                          pq, kq = psB[q]
                          S.op('dve', lambda e: e.tensor_tensor(out=tmp[:, 4 * q:4 * q + 4, :],
                                                                in0=pq[:, :].rearrange("p (h l) -> p h l", l=128),
                                                                in1=bc(sm[:, 4 * q:4 * q + 4], 2, [128, 4, 128]), op=ALU.subtract),
                               r=[kq, 'sm'], w=['tmp'])
                      S.op('dve', lambda e: e.tensor_tensor(out=tmp[:], in0=tmp[:], in1=nmm[:], op=ALU.add),
                           r=['tmp', 'nm8', 'nm8T'], w=['tmp'])
                      S.op('act', lambda e: e.activation(out=tmp[:], in_=tmp[:], func=AF.Exp), r=['tmp'], w=['tmp'])
                      psS, kS = S.next_ps()
                      for gi in range(2):
                          S.op('pe', lambda e: e.matmul(psS[:, gi * 128:(gi + 1) * 128], lhsT=BT[:, gi, tsl], rhs=CT[:, gi, tsl],
                                                        start=True, stop=True), r=[('BT', gi), ('CT', gi)], w=[kS])
                      for gi in range(2):
                          S.op('dve', lambda e: e.tensor_tensor(out=MT[:, 4 * gi:4 * gi + 4, :], in0=tmp[:, 4 * gi:4 * gi + 4, :],
                                                                in1=bc(psS[:, gi * 128:(gi + 1) * 128], 1, [128, 4, 128]),
                                                                op=ALU.mult), r=['tmp', kS], w=['MT'])
                      xt3 = x_tm[:, t, :].rearrange("p (h d) -> p h d", d=64)
                      S.op('dve', lambda e: e.tensor_tensor(out=xdt[:], in0=xt3, in1=bc(dtv_t, 2, [128, 8, 64]), op=ALU.mult),
                           r=['x_tm', 'dtv'], w=['xdt'])
                      S.op('dve', lambda e: e.tensor_tensor(out=xw[:], in0=xt3, in1=bc(sm[:, 24:32], 2, [128, 8, 64]), op=ALU.mult),
                           r=['x_tm', 'sm'], w=['xw'])
                      psY, kY = S.next_ps()
                      for hh in range(8):
                          S.op('pe', lambda e: e.matmul(psY[:, hh * 64:(hh + 1) * 64], lhsT=MT[:, hh, :], rhs=xdt[:, hh, :],
                                                        start=True, stop=True), r=['MT', 'xdt'], w=[kY])
                      psO, kO = S.next_ps()
                      for gi in range(2):
                          S.op('pe', lambda e: e.matmul(psO[:, gi * 256:(gi + 1) * 256], lhsT=CT[:, gi, tsl],
                                                        rhs=hst[:, gi * 256:(gi + 1) * 256], start=True, stop=True),
                               r=[('CT', gi), 'hst'], w=[kO])
                      S.op('act', lambda e: e.activation(out=ydsb[:].rearrange("p h d -> p (h d)"), in_=psY[:, :], func=AF.Copy),
                           r=[kY], w=['ydsb'])
                      S.op('dve', lambda e: e.tensor_tensor(out=xdt[:], in0=psO[:, :].rearrange("p (h d) -> p h d", d=64),
                                                            in1=bc(sm[:, 16:24], 2, [128, 8, 64]), op=ALU.mult),
                           r=[kO, 'sm'], w=['xdt'])
                      S.op('pool', lambda e: e.tensor_tensor(out=ydsb[:], in0=ydsb[:], in1=xdt[:], op=ALU.add),
                           r=['ydsb', 'xdt'], w=['ydsb'])
                      S.op('pool', lambda e: e.tensor_tensor(out=y_acc[:, t, :], in0=y_acc[:, t, :],
                                                             in1=ydsb[:].rearrange("p h d -> p (h d)"), op=ALU.add),
                           r=['ydsb', 'y_acc'], w=['y_acc'])
                      psH, kH = S.next_ps()
                      for gi in range(2):
                          S.op('pe', lambda e: e.matmul(psH[:, gi * 256:(gi + 1) * 256], lhsT=B_tm[:, t, gi * 128:(gi + 1) * 128],
                                                        rhs=xw[:, 4 * gi:4 * gi + 4, :].rearrange("p h d -> p (h d)"),
                                                        start=True, stop=True), r=['B_tm', 'xw'], w=[kH])
                      S.op('dve', lambda e: e.tensor_tensor(out=hst[:].rearrange("p (h d) -> p h d", d=64),
                                                            in0=hst[:].rearrange("p (h d) -> p h d", d=64),
                                                            in1=bc(sm[:, 32:40], 2, [128, 8, 64]), op=ALU.mult),
                           r=['hst', 'sm'], w=['hst'])
                      S.op('dve', lambda e: e.tensor_tensor(out=hst[:], in0=hst[:], in1=psH[:, :], op=ALU.add),
                           r=['hst', kH], w=['hst'])
                  S.dma('pool', yo[d][:, hf * 512:(hf + 1) * 512].rearrange("(t p) c -> p t c", p=128), y_acc[:], r=['y_acc'],
                        is_out=True)
        S.finish()
    return nc


def flip_seq(a, axis):
    sl_c = [slice(None)] * a.ndim
    sl_x = [slice(None)] * a.ndim
    sl_c[axis] = slice(0, CTX)
    sl_x[axis] = slice(CTX, LT)
    return np.concatenate([np.flip(a[tuple(sl_c)], axis), np.flip(a[tuple(sl_x)], axis)], axis=axis)


def run_C1(px_tm, px_fm, inputs, layer):
    cwl = inputs['ssd_conv_w'][layer]
    cbl = inputs['ssd_conv_b'][layer]
    in_maps = []
    for c in range(NCORES):
        b, d = c // 2, c % 2
        xbc = px_fm[b, 0:2048, :]
        dt = px_tm[b, :, 1024:1040]
        cw_ = cwl
        if d == 1:
            xbc = flip_seq(xbc, 1)
            dt = flip_seq(dt, 0)
            cw_ = cwl[::-1]
        abd = np.stack([inputs['ssd_a_log'][layer, d], inputs['ssd_dt_bias'][layer, d], inputs['ssd_d'][layer, d]], 0)
        in_maps.append({"xbc": np.ascontiguousarray(xbc), "dt": np.ascontiguousarray(dt),
                        "cw": np.ascontiguousarray(cw_.T.reshape(16, 128, 5).transpose(1, 0, 2)).reshape(128, 80),
                        "cb": np.ascontiguousarray(cbl.reshape(16, 128).T),
                        "abd": np.ascontiguousarray(np.broadcast_to(abd.reshape(1, 48), (128, 48)))})
    res = run_bass_kernel_spmd(build_C1(), in_maps, core_ids=list(range(NCORES)))
    out = np.zeros((NB, 2, LT, 1024), np.float32)
    for c in range(NCORES):
        b, d = c // 2, c % 2
        yv = res.results[c]["y"]
        out[b, d] = flip_seq(yv, 0) if d == 1 else yv
    return out


def rstd_cols(S, ss, cols, inv_ns, key):
    c0 = cols[0]
    for (a, b), inv in zip(cols[1], inv_ns):
        S.op('dve', lambda e: e.tensor_scalar(out=ss[:, a:b], in0=ss[:, a:b], scalar1=inv, scalar2=EPS, op0=ALU.mult,
                                              op1=ALU.add), r=[key], w=[key])
    a, b = c0
    S.op('act', lambda e: e.activation(out=ss[:, a:b], in_=ss[:, a:b], func=AF.Sqrt), r=[key], w=[key])
    S.op('dve', lambda e: e.reciprocal(out=ss[:, a:b], in_=ss[:, a:b]), r=[key], w=[key])


def rope_ops(S, src3, dst3, cos, sin, nh, tmp, rkeys, wkey, tkey):
    s4 = src3.rearrange("p h (i two) -> p h i two", two=2)
    d4 = dst3.rearrange("p h (i two) -> p h i two", two=2)
    x1, x2 = s4[:, :, :, 0], s4[:, :, :, 1]
    cb_, sb_ = bc(cos, 1, [128, nh, 16]), bc(sin, 1, [128, nh, 16])
    for i, (xa, tb) in enumerate([(x1, cb_), (x2, sb_), (x1, sb_), (x2, cb_)]):
        S.op('dve', lambda e: e.tensor_tensor(out=tmp[:, i, :, :], in0=xa, in1=tb, op=ALU.mult), r=rkeys, w=[tkey])
    S.op('dve', lambda e: e.tensor_tensor(out=d4[:, :, :, 0], in0=tmp[:, 0, :, :], in1=tmp[:, 1, :, :], op=ALU.subtract),
         r=[tkey], w=[wkey])
    S.op('dve', lambda e: e.tensor_tensor(out=d4[:, :, :, 1], in0=tmp[:, 2, :, :], in1=tmp[:, 3, :, :], op=ALU.add),
         r=[tkey], w=[wkey])


def build_C2():
    nc = new_nc()
    mla = din(nc, "mla", [LT, 800])
    gqa = din(nc, "gqa", [128, 4])
    gkv = din(nc, "gkv", [128, 2])
    wq = din(nc, "wq", [512, 768])
    wkv = din(nc, "wkv", [256, 1024])
    qg = din(nc, "qg", [128, 96])
    kg = din(nc, "kg", [128, 96])
    cs = din(nc, "cs", [LT, 32])
    oT = dout(nc, "oT", [512, LT])
    SCALE = 96.0 ** -0.5
    with ExitStack() as ctx:
        S = get_sched(nc, ctx)
        S.nrot = 4
        ident = make_ident(S)
        ones = S.sb('ones', [128, 64])
        ones_f = S.sb('ones_f', [128, 64])
        S.op('pool', lambda e: e.memset(ones_f[:], 1.0), w=['ones_f'])
        S.op('dve', lambda e: e.tensor_copy(out=R(ones[:]), in_=ones_f[:]), r=['ones_f'], w=['ones'])
        gqat = S.sb('gqat', [128, 4]); gkvt = S.sb('gkvt', [128, 2])
        wqt = S.sb('wqt', [128, 4, 768]); wkvt = S.sb('wkvt', [128, 2, 1024])
        qgt = S.sb('qgt', [128, 96]); kgt = S.sb('kgt', [128, 96])
        cst = S.sb('cst', [128, NTL, 32])
        S.dma('sp', gqat[:], gqa[:, :], w=['gqat'])
        S.dma('sp', gkvt[:], gkv[:, :], w=['gkvt'])
        S.dma('pool', R(wqt[:]), wq.rearrange("(k p) n -> p k n", p=128), w=['wqt'])
        S.dma('pool', R(wkvt[:]), wkv.rearrange("(k p) n -> p k n", p=128), w=['wkvt'])
        S.dma('sp', qgt[:], qg[:, :], w=['qgt'])
        S.dma('sp', kgt[:], kg[:, :], w=['kgt'])
        S.dma('sp', cst[:], cs.rearrange("(t p) c -> p t c", p=128), w=['cst'])
        qaT = S.sb('qaT', [128, 4, LT])
        kvT = S.sb('kvT', [128, 2, LT])
        kpr = S.sb('kpr', [128, NTL, 32])
        mtb = [S.sb(f'mt{i}', [128, 800]) for i in range(2)]
        junk = S.sb('junk', [128, 512])
        ss = S.sb('ss', [128, 16])
        kpn = S.sb('kpn', [128, 1, 32])
        rtmp = S.sb('rtmp', [128, 4, 4, 16])
        for t in range(NTL):
            mt = mtb[t % 2]
            mk = ('mt', t % 2)
            S.dma('sp', mt[:], mla[t * 128:(t + 1) * 128, :], w=[mk])
            S.op('dve', lambda e: e.memset(ss[:, 0:3], 0.0), w=['ss'])
            for i, (a, b) in enumerate([(0, 512), (512, 768), (768, 800)]):
                S.op('act', lambda e: e.activation(out=junk[:, 0:b - a], in_=mt[:, a:b], func=AF.Square,
                                                   accum_out=ss[:, i:i + 1]), r=[mk], w=['junk', 'ss'])
            rstd_cols(S, ss, ((0, 3), [(0, 1), (1, 2), (2, 3)]), [1.0 / 512, 1.0 / 256, 1.0 / 32], 'ss')
            S.op('dve', lambda e: e.tensor_scalar(out=mt[:, 0:512], in0=mt[:, 0:512], scalar1=ss[:, 0:1], scalar2=None,
                                                  op0=ALU.mult), r=[mk, 'ss'], w=[mk])
            S.op('dve', lambda e: e.tensor_scalar(out=mt[:, 512:768], in0=mt[:, 512:768], scalar1=ss[:, 1:2], scalar2=None,
                                                  op0=ALU.mult), r=[mk, 'ss'], w=[mk])
            S.op('dve', lambda e: e.scalar_tensor_tensor(out=kpn[:, 0, :], in0=mt[:, 768:800], scalar=ss[:, 2:3],
                                                         in1=kgt[:, 64:96], op0=ALU.mult, op1=ALU.mult),
                 r=[mk, 'ss', 'kgt'], w=['kpn'])
            rope_ops(S, kpn[:], kpr[:, t:t + 1, :], cst[:, t, 0:16], cst[:, t, 16:32], 1, rtmp[:, :, 0:1, :],
                     ['kpn', 'cst'], 'kpr', 'rtmp')
            ps, pk = S.next_ps()
            for k in range(4):
                S.op('pe', lambda e: e.transpose(out=ps[:, k * 128:(k + 1) * 128], in_=mt[:, k * 128:(k + 1) * 128],
                                                 identity=ident[:]), r=[mk, 'ident'], w=[pk])
            for k in range(4):
                S.op('act', lambda e: e.activation(out=R(qaT[:, k, t * 128:(t + 1) * 128]), in_=ps[:, k * 128:(k + 1) * 128],
                                                   func=AF.Identity, scale=gqat[:, k:k + 1]), r=[pk, 'gqat'], w=['qaT'])
            ps, pk = S.next_ps()
            for k in range(2):
                S.op('pe', lambda e: e.transpose(out=ps[:, k * 128:(k + 1) * 128], in_=mt[:, 512 + k * 128:640 + k * 128],
                                                 identity=ident[:]), r=[mk, 'ident'], w=[pk])
            for k in range(2):
                S.op('act', lambda e: e.activation(out=R(kvT[:, k, t * 128:(t + 1) * 128]), in_=ps[:, k * 128:(k + 1) * 128],
                                                   func=AF.Identity, scale=gkvt[:, k:k + 1]), r=[pk, 'gkvt'], w=['kvT'])
        kT = S.sb('kT', [128, 4, LT])
        vv = S.sb('vv', [128, NTL, 4, 64])
        kfull = S.sb('kfull', [128, 4, 96])
        sq = S.sb('sq', [128, 4, 96])
        t1 = S.sb('t1', [128, 4, 64])
        qn = S.sb('qn', [128, 4, 96])
        qp = S.sb('qp', [128, 4, 32])
        qT = S.sb('qT', [128, 4, 512])
        ptb = [S.sb(f'pt{i}', [128, 512]) for i in range(3)]
        rden = S.sb('rden', [64, 512])
        osb = [S.sb(f'osb{i}', [64, 512]) for i in range(2)]
        groups = [[0, 1]] + [list(range(2 + 4 * g, 6 + 4 * g)) for g in range(4)]
        pti = 0
        oi = 0
        hcount = 0
        for hp in range(2):
            for t in range(NTL):
                tsl = slice(t * 128, (t + 1) * 128)
                psK, kK = S.next_ps()
                for k in range(2):
                    S.op('pe', lambda e: e.matmul(psK[:, :], lhsT=R(kvT[:, k, tsl]), rhs=R(wkvt[:, k, hp * 512:(hp + 1) * 512]),
                                                  start=(k == 0), stop=(k == 1)), r=['kvT', 'wkvt'], w=[kK])
                pk3 = psK[:, :].rearrange("p (h c) -> p h c", c=128)
                S.op('act', lambda e: e.activation(out=sq[:, :, 0:64], in_=pk3[:, :, 0:64], func=AF.Square), r=[kK], w=['sq'])
                S.op('dve', lambda e: e.tensor_reduce(out=ss[:, 4:8], in_=sq[:, :, 0:64], axis=AX.X, op=ALU.add),
                     r=['sq'], w=['ss'])
                rstd_cols(S, ss, ((4, 8), [(4, 8)]), [1.0 / 64], 'ss')
                S.op('dve', lambda e: e.tensor_tensor(out=t1[:], in0=pk3[:, :, 0:64], in1=bc(ss[:, 4:8], 2, [128, 4, 64]),
                                                      op=ALU.mult), r=[kK, 'ss'], w=['t1'])
                S.op('dve', lambda e: e.tensor_tensor(out=kfull[:, :, 0:64], in0=t1[:], in1=bc(kgt[:, 0:64], 1, [128, 4, 64]),
                                                      op=ALU.mult), r=['t1', 'kgt'], w=['kfull'])
                S.op('dve', lambda e: e.tensor_copy(out=kfull[:, :, 64:96], in_=bc(kpr[:, t, :], 1, [128, 4, 32])),
                     r=['kpr'], w=['kfull'])
                S.op('act', lambda e: e.activation(out=R(vv[:, t, :, :]), in_=pk3[:, :, 64:128], func=AF.Copy), r=[kK], w=['vv'])
                psT, kTk = S.next_ps()
                for hd in range(4):
                    S.op('pe', lambda e: e.transpose(out=psT[0:96, hd * 128:(hd + 1) * 128], in_=kfull[:, hd, :],
                                                     identity=ident[:]), r=['kfull', 'ident'], w=[kTk])
                S.op('act', lambda e: e.activation(out=R(kT[0:96, :, tsl]), in_=psT[0:96, :].rearrange("p (h c) -> p h c", c=128),
                                                   func=AF.Copy), r=[kTk], w=['kT'])
            for gi, tiles in enumerate(groups):
                nq = len(tiles) * 128
                key_tiles = [0, 1] if gi == 0 else list(range(NTL))
                for li, t in enumerate(tiles):
                    tsl = slice(t * 128, (t + 1) * 128)
                    psQ, kQ = S.next_ps()
                    for k in range(4):
                        S.op('pe', lambda e: e.matmul(psQ[:, 0:384], lhsT=R(qaT[:, k, tsl]), rhs=R(wqt[:, k, hp * 384:(hp + 1) * 384]),
                                                      start=(k == 0), stop=(k == 3)), r=['qaT', 'wqt'], w=[kQ])
                    pq3 = psQ[:, 0:384].rearrange("p (h c) -> p h c", c=96)
                    S.op('act', lambda e: e.activation(out=sq[:], in_=pq3, func=AF.Square), r=[kQ], w=['sq'])
                    S.op('dve', lambda e: e.tensor_reduce(out=ss[:, 8:12], in_=sq[:, :, 0:64], axis=AX.X, op=ALU.add),
                         r=['sq'], w=['ss'])
                    S.op('dve', lambda e: e.tensor_reduce(out=ss[:, 12:16], in_=sq[:, :, 64:96], axis=AX.X, op=ALU.add),
                         r=['sq'], w=['ss'])
                    rstd_cols(S, ss, ((8, 16), [(8, 12), (12, 16)]), [1.0 / 64, 1.0 / 32], 'ss')
                    S.op('dve', lambda e: e.tensor_tensor(out=t1[:], in0=pq3[:, :, 0:64], in1=bc(ss[:, 8:12], 2, [128, 4, 64]),
                                                          op=ALU.mult), r=[kQ, 'ss'], w=['t1'])
                    S.op('dve', lambda e: e.tensor_tensor(out=qn[:, :, 0:64], in0=t1[:], in1=bc(qgt[:, 0:64], 1, [128, 4, 64]),
                                                          op=ALU.mult), r=['t1', 'qgt'], w=['qn'])
                    S.op('dve', lambda e: e.tensor_tensor(out=qp[:], in0=pq3[:, :, 64:96], in1=bc(ss[:, 12:16], 2, [128, 4, 32]),
                                                          op=ALU.mult), r=[kQ, 'ss'], w=['qp'])
                    S.op('dve', lambda e: e.tensor_tensor(out=qp[:], in0=qp[:], in1=bc(qgt[:, 64:96], 1, [128, 4, 32]),
                                                          op=ALU.mult), r=['qp', 'qgt'], w=['qp'])
                    rope_ops(S, qp[:], qn[:, :, 64:96], cst[:, t, 0:16], cst[:, t, 16:32], 4, rtmp[:], ['qp', 'cst'], 'qn', 'rtmp')
                    psT, kTk = S.next_ps()
                    for hd in range(4):
                        S.op('pe', lambda e: e.transpose(out=psT[0:96, hd * 128:(hd + 1) * 128], in_=qn[:, hd, :],
                                                         identity=ident[:]), r=['qn', 'ident'], w=[kTk])
                    S.op('act', lambda e: e.activation(out=R(qT[0:96, :, li * 128:(li + 1) * 128]),
                                                       in_=psT[0:96, :].rearrange("p (h c) -> p h c", c=128), func=AF.Copy),
                         r=[kTk], w=['qT'])
                for hd in range(4):
                    ao, ad = (4, 5) if hcount % 2 == 0 else (6, 7)
                    hcount += 1
                    pso, psd = S.ps[ao], S.ps[ad]
                    ko, kd = ('ps', ao), ('ps', ad)
                    for ki, kc in enumerate(key_tiles):
                        ksl = slice(kc * 128, (kc + 1) * 128)
                        psS, kS = S.next_ps()
                        S.op('pe', lambda e: e.matmul(psS[:, 0:nq], lhsT=R(kT[0:96, hd, ksl]), rhs=R(qT[0:96, hd, 0:nq]),
                                                      start=True, stop=True), r=['kT', 'qT'], w=[kS])
                        pt = ptb[pti % 3]
                        ptk = ('pt', pti % 3)
                        pti += 1
                        S.op('act', lambda e: e.activation(out=R(pt[:, 0:nq]), in_=psS[:, 0:nq], func=AF.Exp, scale=SCALE),
                             r=[kS], w=[ptk])
                        S.op('pe', lambda e: e.matmul(pso[0:64, 0:nq], lhsT=R(vv[:, kc, hd, :]), rhs=R(pt[:, 0:nq]),
                                                      start=(ki == 0), stop=(ki == len(key_tiles) - 1)), r=['vv', ptk], w=[ko])
                        S.op('pe', lambda e: e.matmul(psd[0:64, 0:nq], lhsT=R(ones[:, :]), rhs=R(pt[:, 0:nq]),
                                                      start=(ki == 0), stop=(ki == len(key_tiles) - 1)), r=['ones', ptk], w=[kd])
                    S.op('dve', lambda e: e.reciprocal(out=rden[:, 0:nq], in_=psd[0:64, 0:nq]), r=[kd], w=['rden'])
                    ob = osb[oi % 2]
                    obk = ('osb', oi % 2)
                    oi += 1
                    S.op('dve', lambda e: e.tensor_tensor(out=ob[:, 0:nq], in0=pso[0:64, 0:nq], in1=rden[:, 0:nq], op=ALU.mult),
                         r=[ko, 'rden'], w=[obk])
                    hrow = (hp * 4 + hd) * 64
                    S.dma('sp', oT[hrow:hrow + 64, tiles[0] * 128:tiles[0] * 128 + nq], ob[:, 0:nq], r=[obk], is_out=True)
        S.finish()
    return nc


def rope_table():
    n_freq = 8
    inv = (10000.0 ** (-np.arange(n_freq, dtype=np.float32) / n_freq)).astype(np.float32)
    rows = np.repeat(np.arange(SEQ // 64, dtype=np.float32), 64)
    cols = np.tile(np.arange(64, dtype=np.float32), SEQ // 64)
    ang = np.concatenate([rows[:, None] * inv, cols[:, None] * inv], axis=-1).astype(np.float32)
    cs = np.zeros((LT, 32), np.float32)
    cs[:CTX, 0:16] = 1.0
    cs[CTX:, 0:16] = np.cos(ang)
    cs[CTX:, 16:32] = np.sin(ang)
    return cs


def rep128(v):
    return np.ascontiguousarray(np.broadcast_to(np.asarray(v, np.float32).reshape(1, -1), (128, v.size)))


def run_C2(px_tm, inputs, layer):
    cs = rope_table()
    in_maps = []
    for c in range(NCORES):
        b, hf = c // 2, c % 2
        in_maps.append({
            "mla": np.ascontiguousarray(px_tm[b, :, 1040:1840]),
            "gqa": np.ascontiguousarray(inputs['mla_q_a_gain'][layer].reshape(4, 128).T),
            "gkv": np.ascontiguousarray(inputs['mla_kv_a_gain'][layer].reshape(2, 128).T),
            "wq": np.ascontiguousarray(inputs['mla_w_q_b'][layer][:, hf * 768:(hf + 1) * 768]),
            "wkv": np.ascontiguousarray(inputs['mla_w_kv_b'][layer][:, hf * 1024:(hf + 1) * 1024]),
            "qg": rep128(inputs['mla_q_gain'][layer]), "kg": rep128(inputs['mla_k_gain'][layer]), "cs": cs})
    res = run_bass_kernel_spmd(build_C2(), in_maps, core_ids=list(range(NCORES)))
    out = np.zeros((NB, 1024, LT), np.float32)
    for c in range(NCORES):
        b, hf = c // 2, c % 2
        out[b, hf * 512:(hf + 1) * 512] = res.results[c]["oT"]
    return out


PI = float(np.pi)
S5_CH = [(0, 512), (512, 1024), (1024, 1536), (1536, 2048), (2048, 2304)]


def sin_reduced(S, dst, src, offset, sign, tmps, rkeys, wkey, pref):
    ki, kf, r, g = tmps
    kk = [pref + n for n in ('ki', 'kf', 'r', 'g')]
    S.op('dve', lambda e: e.tensor_scalar(out=r, in0=src, scalar1=offset + 8.0 * PI, scalar2=None, op0=ALU.add),
         r=rkeys, w=[kk[2]])
    S.op('dve', lambda e: e.tensor_scalar(out=ki, in0=r, scalar1=1.0 / (2.0 * PI), scalar2=None, op0=ALU.mult),
         r=[kk[2]], w=[kk[0]])
    S.op('dve', lambda e: e.tensor_copy(out=kf, in_=ki), r=[kk[0]], w=[kk[1]])
    S.op('dve', lambda e: e.scalar_tensor_tensor(out=r, in0=kf, scalar=-2.0 * PI, in1=r, op0=ALU.mult, op1=ALU.add),
         r=[kk[1], kk[2]], w=[kk[2]])
    S.op('dve', lambda e: e.tensor_scalar(out=g, in0=r, scalar1=PI, scalar2=-2.0 * PI, op0=ALU.is_gt, op1=ALU.mult),
         r=[kk[2]], w=[kk[3]])
    S.op('dve', lambda e: e.tensor_tensor(out=r, in0=r, in1=g, op=ALU.add), r=[kk[2], kk[3]], w=[kk[2]])
    S.op('dve', lambda e: e.tensor_scalar(out=r, in0=r, scalar1=-PI, scalar2=PI, op0=ALU.max, op1=ALU.min),
         r=[kk[2]], w=[kk[2]])
    S.op('act', lambda e: e.activation(out=dst, in_=r, func=AF.Sin, scale=float(sign)), r=[kk[2]], w=[wkey])


def build_C3():
    nc = new_nc()
    u = din(nc, "u", [1024, LT])
    lamp_ = din(nc, "lamp", [128, 2 * 96])
    lamr_ = din(nc, "lamr", [32, 2 * 3 * 4096])
    bT_ = din(nc, "bT", [32, 2 * 2 * 4096])
    cbd_ = din(nc, "cbd", [128, 2 * 2 * 1024])
    yTo = [dout(nc, "yT0", [1024, LT]), dout(nc, "yT1", [1024, LT])]
    with ExitStack() as ctx:
        S = get_sched(nc, ctx)
        lp = S.sb('lp', [128, 3, 32])
        cb_ = S.sb('cbd_sb', [128, 2, 32, 32])
        dtp = S.sb('dtp', [128, 32]); magp = S.sb('magp', [128, 32]); thp = S.sb('thp', [128, 32])
        BbT = S.sb('BbT', [32, 2, 32, 128])
        W = 512
        names = ['lr', 'li', 'ld', 'br', 'bi', 'mag', 'th', 'sn', 'cs', 'm', 'abr', 'abi', 'den', 'fr', 'fi', 'ta', 'tb']
        T = {n: S.sb('r_' + n, [32, W]) for n in names}
        T['ki'] = S.sb('r_ki', [32, W], I32)
        T['kf'] = S.sb('r_kf', [32, W])
        T['g'] = S.sb('r_g', [32, W])
        io_i = S.sb('io_i', [128, 512], I32)
        io_f = S.sb('io_f', [128, 512])
        S.op('pool', lambda e: e.iota(io_i[:], pattern=[[1, 512]], base=1, channel_multiplier=0), w=['io_i'])
        S.op('dve', lambda e: e.tensor_copy(out=io_f[:], in_=io_i[:]), r=['io_i'], w=['io_f'])
        P2 = range(2)
        Er = [S.sb(f'Er{p}', [128, 512]) for p in P2]; Ei = [S.sb(f'Ei{p}', [128, 512]) for p in P2]
        amag = [S.sb(f'amag{p}', [128, 512]) for p in P2]
        phi = [S.sb(f'phi{p}', [128, 512]) for p in P2]; mm = [S.sb(f'mm{p}', [128, 512]) for p in P2]
        pki = [S.sb(f'pki{p}', [128, 512], I32) for p in P2]; pkf = [S.sb(f'pkf{p}', [128, 512]) for p in P2]
        pg = [S.sb(f'pg{p}', [128, 512]) for p in P2]
        ub = [S.sb(f'ub{i}', [32, LT]) for i in range(2)]
        wre = [S.sb(f'wre{p}', [128, 512]) for p in P2]; wim = [S.sb(f'wim{p}', [128, 512]) for p in P2]
        ta = [S.sb(f'ta{p}', [128, 512]) for p in P2]; tb = [S.sb(f'tb{p}', [128, 512]) for p in P2]
        zre = [S.sb(f'zre{p}', [128, 512]) for p in P2]; zim = [S.sb(f'zim{p}', [128, 512]) for p in P2]
        xre = [[S.sb(f'xre{p}_{i}', [128, 512]) for i in range(2)] for p in P2]
        xim = [[S.sb(f'xim{p}_{i}', [128, 512]) for i in range(2)] for p in P2]
        tc_ = [S.sb(f'tc{p}', [128, 512]) for p in P2]; td = [S.sb(f'td{p}', [128, 512]) for p in P2]
        yb = [S.sb(f'yb{i}', [32, 512]) for i in range(4)]

        def tt(o, a_, b_, op, eng='dve'):
            S.op(eng, lambda e: e.tensor_tensor(out=T[o][:], in0=T[a_][:], in1=T[b_][:], op=op), r=[a_, b_], w=[o])

        yi = 0
        for d in range(2):
            S.dma('sp', lp[:].rearrange("p a j -> p (a j)"), lamp_[:, d * 96:(d + 1) * 96], w=['lp'])
            S.dma('sp', cb_[:].rearrange("p a j s -> p (a j s)"), cbd_[:, d * 2048:(d + 1) * 2048], w=['cbd'])
            S.op('dve', lambda e: e.tensor_scalar(out=cb_[:, 1], in0=cb_[:, 1], scalar1=-1.0, scalar2=None, op0=ALU.mult),
                 r=['cbd'], w=['cbd'])
            S.op('act', lambda e: e.activation(out=dtp[:], in_=lp[:, 2, :], func=AF.Exp), r=['lp'], w=['dtp'])
            S.op('dve', lambda e: e.tensor_tensor(out=magp[:], in0=lp[:, 0, :], in1=dtp[:], op=ALU.mult), r=['lp', 'dtp'], w=['magp'])
            S.op('act', lambda e: e.activation(out=magp[:], in_=magp[:], func=AF.Exp), r=['magp'], w=['magp'])
            S.op('dve', lambda e: e.tensor_tensor(out=thp[:], in0=lp[:, 1, :], in1=dtp[:], op=ALU.mult), r=['lp', 'dtp'], w=['thp'])
            for pc in range(8):
                for i, n in enumerate(['lr', 'li', 'ld']):
                    o0 = d * 3 * 4096 + i * 4096 + pc * W
                    S.dma('sp', T[n][:], lamr_[:, o0:o0 + W], w=[n])
                for i, n in enumerate(['br', 'bi']):
                    o0 = d * 2 * 4096 + i * 4096 + pc * W
                    S.dma('sp', T[n][:], bT_[:, o0:o0 + W], w=[n])
                S.op('act', lambda e: e.activation(out=T['ld'][:], in_=T['ld'][:], func=AF.Exp), r=['ld'], w=['ld'])
                tt('mag', 'lr', 'ld', ALU.mult)
                S.op('act', lambda e: e.activation(out=T['mag'][:], in_=T['mag'][:], func=AF.Exp), r=['mag'], w=['mag'])
                tt('th', 'li', 'ld', ALU.mult)
                rt = (T['ki'][:], T['kf'][:], T['m'][:], T['g'][:])
                sin_reduced(S, T['sn'][:], T['th'][:], 0.0, 1.0, rt, ['th'], 'sn', 'R')
                sin_reduced(S, T['cs'][:], T['th'][:], 0.5 * PI, 1.0, rt, ['th'], 'cs', 'R')
                tt('abr', 'mag', 'cs', ALU.mult)
                tt('abi', 'mag', 'sn', ALU.mult)
                S.op('dve', lambda e: e.tensor_scalar(out=T['abr'][:], in0=T['abr'][:], scalar1=-1.0, scalar2=None, op0=ALU.add),
                     r=['abr'], w=['abr'])
                tt('den', 'lr', 'lr', ALU.mult)
                tt('ta', 'li', 'li', ALU.mult)
                tt('den', 'den', 'ta', ALU.add)
                S.op('dve', lambda e: e.reciprocal(out=T['den'][:], in_=T['den'][:]), r=['den'], w=['den'])
                tt('fr', 'abr', 'lr', ALU.mult)
                tt('ta', 'abi', 'li', ALU.mult)
                tt('fr', 'fr', 'ta', ALU.add)
                tt('fr', 'fr', 'den', ALU.mult)
                tt('fi', 'abi', 'lr', ALU.mult)
                tt('ta', 'abr', 'li', ALU.mult)
                tt('fi', 'fi', 'ta', ALU.subtract)
                tt('fi', 'fi', 'den', ALU.mult)
                o_re = BbT[:, 0, 4 * pc:4 * pc + 4, :].rearrange("p j c -> p (j c)")
                o_im = BbT[:, 1, 4 * pc:4 * pc + 4, :].rearrange("p j c -> p (j c)")
                tt('ta', 'br', 'fr', ALU.mult)
                tt('tb', 'bi', 'fi', ALU.mult)
                S.op('dve', lambda e: e.tensor_tensor(out=o_re, in0=T['ta'][:], in1=T['tb'][:], op=ALU.subtract),
                     r=['ta', 'tb'], w=['BbT'])
                tt('ta', 'br', 'fi', ALU.mult)
                tt('tb', 'bi', 'fr', ALU.mult)
                S.op('dve', lambda e: e.tensor_tensor(out=o_im, in0=T['ta'][:], in1=T['tb'][:], op=ALU.add),
                     r=['ta', 'tb'], w=['BbT'])
            if d == 0:
                chunks = S5_CH
            else:
                chunks = [(0, 256), (1792, 2304), (1280, 1792), (768, 1280), (256, 768)]

            def V(ap, n):
                v = ap[:, 0:n]
                return v if d == 0 else v[:, ::-1]

            def tables(j, p):
                S.dma('sp', ub[p][:], u[32 * j:32 * j + 32, :], w=[('ub', p)])
                S.op('dve', lambda e: e.tensor_scalar(out=phi[p][:], in0=io_f[:], scalar1=thp[:, j:j + 1], scalar2=None, op0=ALU.mult),
                     r=['io_f', 'thp'], w=[('phi', p)])
                pt_ = (pki[p][:], pkf[p][:], mm[p][:], pg[p][:])
                sin_reduced(S, Ei[p][:], phi[p][:], 0.0, -1.0, pt_, [('phi', p)], ('Ei', p), f'P{p}')
                sin_reduced(S, Er[p][:], phi[p][:], 0.5 * PI, 1.0, pt_, [('phi', p)], ('Er', p), f'P{p}')
                S.op('pool', lambda e: e.tensor_copy(out=amag[p][:], in_=magp[:, j:j + 1].to_broadcast([128, 512])),
                     r=['magp'], w=[('amag', p)])

            def ph_w(j, p, ci, st):
                c0, c1 = chunks[ci]
                n = c1 - c0
                Erv, Eiv = V(Er[p], n), V(Ei[p], n)
                psR, kR = S.next_ps()
                psI, kI = S.next_ps()
                st['ps'] = (psR, kR, psI, kI)
                uk = ('ub', p)
                S.op('pe', lambda e: e.matmul(psR[:, 0:n], lhsT=BbT[:, 0, j, :], rhs=ub[p][:, c0:c1], start=True, stop=True),
                     r=['BbT', uk], w=[kR])
                S.op('pe', lambda e: e.matmul(psI[:, 0:n], lhsT=BbT[:, 1, j, :], rhs=ub[p][:, c0:c1], start=True, stop=True),
                     r=['BbT', uk], w=[kI])
                S.op('dve', lambda e: e.tensor_tensor(out=wre[p][:, 0:n], in0=psR[:, 0:n], in1=Erv, op=ALU.mult), r=[kR, ('Er', p)], w=[('wre', p)])
                S.op('dve', lambda e: e.tensor_tensor(out=ta[p][:, 0:n], in0=psI[:, 0:n], in1=Eiv, op=ALU.mult), r=[kI, ('Ei', p)], w=[('ta', p)])
                S.op('dve', lambda e: e.tensor_tensor(out=wim[p][:, 0:n], in0=psI[:, 0:n], in1=Erv, op=ALU.mult), r=[kI, ('Er', p)], w=[('wim', p)])
                S.op('dve', lambda e: e.tensor_tensor(out=tb[p][:, 0:n], in0=psR[:, 0:n], in1=Eiv, op=ALU.mult), r=[kR, ('Ei', p)], w=[('tb', p)])
                S.op('pool', lambda e: e.tensor_tensor(out=wre[p][:, 0:n], in0=wre[p][:, 0:n], in1=ta[p][:, 0:n], op=ALU.subtract),
                     r=[('wre', p), ('ta', p)], w=[('wre', p)])
                S.op('pool', lambda e: e.tensor_tensor(out=wim[p][:, 0:n], in0=wim[p][:, 0:n], in1=tb[p][:, 0:n], op=ALU.add),
                     r=[('wim', p), ('tb', p)], w=[('wim', p)])

            def ph_scan(j, p, ci, st):
                c0, c1 = chunks[ci]
                n = c1 - c0
                if ci == 0:
                    ini_r, ini_i, rk = 0.0, 0.0, []
                else:
                    pn = chunks[ci - 1][1] - chunks[ci - 1][0]
                    col = pn - 1 if d == 0 else 0
                    ini_r = xre[p][(ci - 1) % 2][:, col:col + 1]
                    ini_i = xim[p][(ci - 1) % 2][:, col:col + 1]
                    rk = [('xre', p, (ci - 1) % 2), ('xim', p, (ci - 1) % 2)]
                S.op('dve', lambda e: e.tensor_tensor_scan(out=V(zre[p], n), data0=amag[p][:, 0:n], data1=V(wre[p], n), initial=ini_r,
                                                           op0=ALU.mult, op1=ALU.add), r=[('amag', p), ('wre', p)] + rk, w=[('zre', p)])
                S.op('dve', lambda e: e.tensor_tensor_scan(out=V(zim[p], n), data0=amag[p][:, 0:n], data1=V(wim[p], n), initial=ini_i,
                                                           op0=ALU.mult, op1=ALU.add), r=[('amag', p), ('wim', p)] + rk, w=[('zim', p)])

            def ph_x(j, p, ci, st):
                nonlocal yi
                c0, c1 = chunks[ci]
                n = c1 - c0
                Erv, Eiv = V(Er[p], n), V(Ei[p], n)
                xr, xi_ = xre[p][ci % 2], xim[p][ci % 2]
                xrk, xik = ('xre', p, ci % 2), ('xim', p, ci % 2)
                S.op('dve', lambda e: e.tensor_tensor(out=xr[:, 0:n], in0=zre[p][:, 0:n], in1=Erv, op=ALU.mult), r=[('zre', p), ('Er', p)], w=[xrk])
                S.op('pool', lambda e: e.tensor_tensor(out=tc_[p][:, 0:n], in0=zim[p][:, 0:n], in1=Eiv, op=ALU.mult), r=[('zim', p), ('Ei', p)], w=[('tc', p)])
                S.op('dve', lambda e: e.tensor_tensor(out=xr[:, 0:n], in0=xr[:, 0:n], in1=tc_[p][:, 0:n], op=ALU.add), r=[xrk, ('tc', p)], w=[xrk])
                S.op('pool', lambda e: e.tensor_tensor(out=xi_[:, 0:n], in0=zim[p][:, 0:n], in1=Erv, op=ALU.mult), r=[('zim', p), ('Er', p)], w=[xik])
                S.op('pool', lambda e: e.tensor_tensor(out=td[p][:, 0:n], in0=zre[p][:, 0:n], in1=Eiv, op=ALU.mult), r=[('zre', p), ('Ei', p)], w=[('td', p)])
                S.op('pool', lambda e: e.tensor_tensor(out=xi_[:, 0:n], in0=xi_[:, 0:n], in1=td[p][:, 0:n], op=ALU.subtract), r=[xik, ('td', p)], w=[xik])
                psY, kY = S.next_ps()
                S.op('pe', lambda e: e.matmul(psY[0:32, 0:n], lhsT=cb_[:, 0, j, :], rhs=xr[:, 0:n], start=True, stop=False),
                     r=['cbd', xrk], w=[kY])
                S.op('pe', lambda e: e.matmul(psY[0:32, 0:n], lhsT=cb_[:, 1, j, :], rhs=xi_[:, 0:n], start=False, stop=True),
                     r=['cbd', xik], w=[kY])
                ybb = yb[yi % 4]
                ybk = ('yb', yi % 4)
                yi += 1
                S.op('act', lambda e: e.activation(out=ybb[:, 0:n], in_=psY[0:32, 0:n], func=AF.Copy), r=[kY], w=[ybk])
                S.dma('sp', yTo[d][32 * j:32 * j + 32, c0:c1], ybb[:, 0:n], r=[ybk], is_out=True)

            for jp in range(16):
                js = [(2 * jp, 0), (2 * jp + 1, 1)]
                for j, p in js:
                    tables(j, p)
                for ci in range(len(chunks)):
                    sts = [dict(), dict()]
                    for ph in (ph_w, ph_scan, ph_x):
                        for j, p in js:
                            ph(j, p, ci, sts[p])
        S.finish()
    return nc


def run_C3(px_fm, inputs, layer):
    in_maps = []
    for c in range(NCORES):
        b, d = c // 2, c % 2
        u = px_fm[b, 2048:3072, :]
        if d == 1:
            u = flip_seq(u, 1)
        lre = inputs['s5_lam_re'][layer, d]
        lim = inputs['s5_lam_im'][layer, d]
        ldt = np.broadcast_to(inputs['s5_log_dt'][layer, d][:, None], (64, 64))
        P = lambda a: a.reshape(32, 128).T
        lamp = np.concatenate([P(lre), P(lim), P(ldt)], axis=1)
        Rl = lambda a: np.broadcast_to(a.reshape(1, 4096), (32, 4096))
        lamr = np.concatenate([Rl(lre), Rl(lim), Rl(ldt)], axis=1)

        def bdT(bm):
            o = np.zeros((2, 16, 32, 2, 64), np.float32)
            bb = bm.reshape(32, 2, 64, 16)
            for gg in range(2):
                o[gg, :, :, gg, :] = bb[:, gg].transpose(2, 0, 1)
            return o.reshape(32, 4096)

        def cbdf(cm):
            o = np.zeros((2, 64, 32, 2, 16), np.float32)
            cc = cm.reshape(32, 2, 16, 64)
            for gg in range(2):
                o[gg, :, :, gg, :] = cc[:, gg].transpose(2, 0, 1)
            return o.reshape(128, 1024)

        in_maps.append({"u": np.ascontiguousarray(u), "lamp": np.ascontiguousarray(lamp, dtype=np.float32),
                        "lamr": np.ascontiguousarray(lamr, dtype=np.float32),
                        "bT": np.concatenate([bdT(inputs['s5_b_re'][layer, d]), bdT(inputs['s5_b_im'][layer, d])], axis=1),
                        "cbd": np.concatenate([cbdf(inputs['s5_c_re'][layer, d]), cbdf(inputs['s5_c_im'][layer, d])], axis=1)})
    res = run_bass_kernel_spmd(build_C3(), in_maps, core_ids=list(range(NCORES)))
    out = np.zeros((NB, 2, 1024, LT), np.float32)
    for c in range(NCORES):
        b, d = c // 2, c % 2
        yv = res.results[c]["yT"]
        out[b, d] = flip_seq(yv, 1) if d == 1 else yv
    return out


TOK_RANGES = [(0, 512), (512, 1024), (1024, 1152)]
GELU_K2 = 2.0 * 0.7978845608028654


def build_D1():
    nc = new_nc()
    NT = NT_B
    NTOK = NT * 128
    y0 = din(nc, "y0", [NTOK, 1024]); y1 = din(nc, "y1", [NTOK, 1024]); z = din(nc, "z", [NTOK, 1024])
    gssd = din(nc, "gssd", [128, 8])
    y5a = din(nc, "y5a", [1024, NTOK]); y5b = din(nc, "y5b", [1024, NTOK]); uT = din(nc, "uT", [1024, NTOK])
    s5d = din(nc, "s5d", [128, 8])
    wglu = din(nc, "wglu", [1024, 1024])
    ysT = dout(nc, "ysT", [1024, NTOK]); y5T = dout(nc, "y5T", [1024, NTOK])
    with ExitStack() as ctx:
        S = get_sched(nc, ctx)
        ident = make_ident(S)
        gs = S.sb('gs', [128, 8]); sd = S.sb('sd', [128, 8]); wg = S.sb('wg', [128, 8, 1024])
        S.dma('sp', gs[:], gssd[:, :], w=['gs'])
        S.dma('sp', sd[:], s5d[:, :], w=['sd'])
        S.dma('sp', wg[:], wglu.rearrange("(k p) n -> p k n", p=128), w=['wg'])
        yb0 = [S.sb(f'yb0_{i}', [128, 1024]) for i in range(2)]
        yb1 = [S.sb(f'yb1_{i}', [128, 1024]) for i in range(2)]
        zb = [S.sb(f'zb_{i}', [128, 1024]) for i in range(2)]
        junk = S.sb('junk', [128, 1024])
        ss = S.sb('ss', [128, NT])
        S.op('dve', lambda e: e.memset(ss[:], 0.0), w=['ss'])
        ob = [S.sb(f'ob{i}', [128, 8, 128]) for i in range(2)]
        ysv = ysT.rearrange("(k p) n -> p k n", p=128)
        for t in range(NT):
            i = t % 2
            rows = slice(t * 128, (t + 1) * 128)
            S.dma('sp', yb0[i][:], y0[rows, :], w=[('yb0', i)])
            S.dma('sp', yb1[i][:], y1[rows, :], w=[('yb1', i)])
            S.dma('sp', zb[i][:], z[rows, :], w=[('zb', i)])
            S.op('dve', lambda e: e.tensor_tensor(out=yb0[i][:], in0=yb0[i][:], in1=yb1[i][:], op=ALU.add),
                 r=[('yb0', i), ('yb1', i)], w=[('yb0', i)])
            S.op('act', lambda e: e.activation(out=zb[i][:], in_=zb[i][:], func=AF.Silu), r=[('zb', i)], w=[('zb', i)])
            S.op('dve', lambda e: e.tensor_tensor(out=yb0[i][:], in0=yb0[i][:], in1=zb[i][:], op=ALU.mult),
                 r=[('yb0', i), ('zb', i)], w=[('yb0', i)])
            S.op('act', lambda e: e.activation(out=junk[:], in_=yb0[i][:], func=AF.Square, accum_out=ss[:, t:t + 1]),
                 r=[('yb0', i)], w=['junk', 'ss'])
            rstd_cols(S, ss, ((t, t + 1), [(t, t + 1)]), [1.0 / 1024], 'ss')
            S.op('dve', lambda e: e.tensor_scalar(out=yb0[i][:], in0=yb0[i][:], scalar1=ss[:, t:t + 1], scalar2=None,
                                                  op0=ALU.mult), r=[('yb0', i), 'ss'], w=[('yb0', i)])
            o = ob[i]
            for kk in range(2):
                ps, pk = S.next_ps()
                for j in range(4):
                    k = kk * 4 + j
                    S.op('pe', lambda e: e.transpose(out=ps[:, j * 128:(j + 1) * 128], in_=yb0[i][:, k * 128:(k + 1) * 128],
                                                     identity=ident[:]), r=[('yb0', i), 'ident'], w=[pk])
                for j in range(4):
                    k = kk * 4 + j
                    S.op('act', lambda e: e.activation(out=o[:, k, :], in_=ps[:, j * 128:(j + 1) * 128], func=AF.Identity,
                                                       scale=gs[:, k:k + 1]), r=[pk, 'gs'], w=[('ob', i)])
            S.dma('pool', ysv[:, :, rows], o[:], r=[('ob', i)], is_out=True)
        va = S.sb('va', [128, 8, 512]); vb = S.sb('vb', [128, 8, 512]); vu = S.sb('vu', [128, 8, 512])
        x2 = S.sb('x2', [128, 8, 512]); sg = S.sb('sg', [128, 512]); o5 = [S.sb(f'o5_{i}', [128, 512]) for i in range(2)]
        y5v = y5T.rearrange("(k p) n -> p k n", p=128)
        oi = 0
        for (n0, n1) in TOK_RANGES:
            n = n1 - n0
            for src, dst, kname in [(y5a, va, 'va'), (y5b, vb, 'vb'), (uT, vu, 'vu')]:
                S.dma('sp', dst[:, :, 0:n], src.rearrange("(k p) n -> p k n", p=128)[:, :, n0:n1], w=[kname])
            S.op('dve', lambda e: e.tensor_tensor(out=vu[:, :, 0:n], in0=vu[:, :, 0:n], in1=bc(sd[:], 2, [128, 8, n]), op=ALU.mult),
                 r=['vu', 'sd'], w=['vu'])
            S.op('dve', lambda e: e.tensor_tensor(out=va[:, :, 0:n], in0=va[:, :, 0:n], in1=vb[:, :, 0:n], op=ALU.add),
                 r=['va', 'vb'], w=['va'])
            S.op('dve', lambda e: e.tensor_tensor(out=va[:, :, 0:n], in0=va[:, :, 0:n], in1=vu[:, :, 0:n], op=ALU.add),
                 r=['va', 'vu'], w=['va'])
            S.op('dve', lambda e: e.tensor_tensor(out=x2[:, :, 0:n], in0=va[:, :, 0:n], in1=va[:, :, 0:n], op=ALU.mult),
                 r=['va'], w=['x2'])
            S.op('dve', lambda e: e.tensor_scalar(out=x2[:, :, 0:n], in0=x2[:, :, 0:n], scalar1=0.044715, scalar2=1.0,
                                                  op0=ALU.mult, op1=ALU.add), r=['x2'], w=['x2'])
            S.op('dve', lambda e: e.tensor_tensor(out=x2[:, :, 0:n], in0=x2[:, :, 0:n], in1=va[:, :, 0:n], op=ALU.mult),
                 r=['x2', 'va'], w=['x2'])
            S.op('act', lambda e: e.activation(out=x2[:, :, 0:n], in_=x2[:, :, 0:n], func=AF.Sigmoid, scale=GELU_K2),
                 r=['x2'], w=['x2'])
            S.op('dve', lambda e: e.tensor_tensor(out=va[:, :, 0:n], in0=va[:, :, 0:n], in1=x2[:, :, 0:n], op=ALU.mult),
                 r=['va', 'x2'], w=['va'])
            for m in range(8):
                ps, pk = S.next_ps()
                for k in range(8):
                    S.op('pe', lambda e: e.matmul(ps[:, 0:n], lhsT=wg[:, k, m * 128:(m + 1) * 128], rhs=va[:, k, 0:n],
                                                  start=(k == 0), stop=(k == 7)), r=['wg', 'va'], w=[pk])
                S.op('act', lambda e: e.activation(out=sg[:, 0:n], in_=ps[:, 0:n], func=AF.Sigmoid), r=[pk], w=['sg'])
                o = o5[oi % 2]
                ok = ('o5', oi % 2)
                oi += 1
                S.op('dve', lambda e: e.tensor_tensor(out=o[:, 0:n], in0=va[:, m, 0:n], in1=sg[:, 0:n], op=ALU.mult),
                     r=['va', 'sg'], w=[ok])
                S.dma('pool', y5v[:, m, n0:n1], o[:, 0:n], r=[ok], is_out=True)
        S.finish()
    return nc


def run_D1(yssd, ys5, px_tm, px_fm, inputs, layer):
    in_maps = []
    for c in range(NCORES):
        b, h = c // 2, c % 2
        r0, r1 = core_rows(b, h)
        ca = np.ascontiguousarray
        in_maps.append({"y0": ca(yssd[b, 0, r0:r1]), "y1": ca(yssd[b, 1, r0:r1]), "z": ca(px_tm[b, r0:r1, 0:1024]),
                        "gssd": ca(inputs['ssd_norm_gain'][layer].reshape(8, 128).T),
                        "y5a": ca(ys5[b, 0, :, r0:r1]), "y5b": ca(ys5[b, 1, :, r0:r1]), "uT": ca(px_fm[b, 2048:3072, r0:r1]),
                        "s5d": ca(inputs['s5_d'][layer].reshape(8, 128).T), "wglu": inputs['s5_w_glu'][layer]})
    res = run_bass_kernel_spmd(build_D1(), in_maps, core_ids=list(range(NCORES)))
    ysT = np.zeros((NB, 1024, LT), np.float32)
    y5T = np.zeros((NB, 1024, LT), np.float32)
    for c in range(NCORES):
        b, h = c // 2, c % 2
        r0, r1 = core_rows(b, h)
        ysT[b, :, r0:r1] = res.results[c]["ysT"]
        y5T[b, :, r0:r1] = res.results[c]["y5T"]
    return ysT, y5T


def tile_gate(S, dst, gx, flags, t, cols, key):
    S.op('dve', lambda e: e.scalar_tensor_tensor(out=dst, in0=gx[:, 1, cols], scalar=flags[:, t:t + 1], in1=gx[:, 0, cols],
                                                 op0=ALU.mult, op1=ALU.add), r=['gx', 'flags'], w=[key])


def load_gx(S, gxr, flg, NT):
    gx = S.sb('gx', [128, 2, D])
    flags = S.sb('flags', [128, NT])
    S.dma('sp', gx[:, 0, :], gxr[0:1, :].to_broadcast([128, D]), w=['gx'])
    S.dma('sp', gx[:, 1, :], gxr[1:2, :].to_broadcast([128, D]), w=['gx'])
    S.dma('sp', flags[:], flg[:, :], w=['flags'])
    S.op('dve', lambda e: e.tensor_tensor(out=gx[:, 1, :], in0=gx[:, 1, :], in1=gx[:, 0, :], op=ALU.subtract),
         r=['gx'], w=['gx'])
    return gx, flags


def build_D2():
    nc = new_nc()
    NT = NT_B
    NTOK = NT * 128
    yin = [din(nc, nm, [1024, NTOK]) for nm in ("ysT", "omT", "y5T")]
    gT = din(nc, "gT", [6144, NTOK])
    wb = [din(nc, nm, [1024, D]) for nm in ("wbs", "wbm", "wb5")]
    wo = din(nc, "wo", [D, D])
    x = din(nc, "x", [NTOK, D])
    gxr = din(nc, "gxr", [2, D])
    flg = din(nc, "flg", [128, NT])
    x1 = dout(nc, "x1", [NTOK, D])
    with ExitStack() as ctx:
        S = get_sched(nc, ctx)
        gx, flags = load_gx(S, gxr, flg, NT)
        mT = S.sb('mT', [128, 16, NTOK])
        big = S.sb('big', [128, 18432])
        yT = [big[:, br * 4096:(br + 1) * 4096].rearrange("p (k n) -> p k n", k=8) for br in range(3)]
        wbt = [big[:, 12288 + br * 2048:12288 + (br + 1) * 2048].rearrange("p (k n) -> p k n", k=8) for br in range(3)]
        wot = [big[:, i * 8192:(i + 1) * 8192].rearrange("p (k n) -> p k n", k=16) for i in range(2)]
        gtl = [S.sb(f'gt{i}', [128, 512]) for i in range(3)]
        tmp = S.sb('tmp', [128, 512])
        gi = 0
        for (n0, n1) in TOK_RANGES:
            n = n1 - n0
            for br in range(3):
                S.dma('pool', R(yT[br][:, :, 0:n]), yin[br].rearrange("(k p) n -> p k n", p=128)[:, :, n0:n1], w=[('yT', br)])
            for db in range(8):
                for br in range(3):
                    S.dma('pool', R(wbt[br][:, :, :]), wb[br].rearrange("(k p) n -> p k n", p=128)[:, :, db * 256:(db + 1) * 256],
                          w=[('wbt', br)])
                for mm in range(2):
                    m = db * 2 + mm
                    for br in range(3):
                        g = gtl[gi % 3]
                        gk = ('gt', gi % 3)
                        gi += 1
                        S.dma('sp', g[:, 0:n], gT[br * 2048 + m * 128:br * 2048 + (m + 1) * 128, n0:n1], w=[gk])
                        S.op('act', lambda e: e.activation(out=g[:, 0:n], in_=g[:, 0:n], func=AF.Sigmoid), r=[gk], w=[gk])
                        ps, pk = S.next_ps()
                        for k in range(8):
                            S.op('pe', lambda e: e.matmul(ps[:, 0:n], lhsT=R(wbt[br][:, k, mm * 128:(mm + 1) * 128]),
                                                          rhs=R(yT[br][:, k, 0:n]), start=(k == 0), stop=(k == 7)),
                                 r=[('wbt', br), ('yT', br)], w=[pk])
                        if br == 0:
                            S.op('dve', lambda e: e.tensor_tensor(out=R(mT[:, m, n0:n1]), in0=ps[:, 0:n], in1=g[:, 0:n], op=ALU.mult),
                                 r=[pk, gk], w=[('mT', m)])
                        else:
                            S.op('dve', lambda e: e.tensor_tensor(out=tmp[:, 0:n], in0=ps[:, 0:n], in1=g[:, 0:n], op=ALU.mult),
                                 r=[pk, gk], w=['tmp'])
                            S.op('pool', lambda e: e.tensor_tensor(out=R(mT[:, m, n0:n1]), in0=mT[:, m, n0:n1], in1=tmp[:, 0:n],
                                                                   op=ALU.add), r=[('mT', m), 'tmp'], w=[('mT', m)])
        allk = [('yT', br) for br in range(3)] + [('wbt', br) for br in range(3)]
        mkeys = [('mT', m) for m in range(16)]
        xt = [S.sb(f'xt{i}', [128, 512]) for i in range(3)]
        gtb = S.sb('gtb', [128, 512])
        xi = 0
        for cb in range(4):
            cols = slice(cb * 512, (cb + 1) * 512)
            w_ = wot[cb % 2]
            wk = ('wot', cb % 2)
            S.dma('pool', R(w_[:, 0:8, :]), wo.rearrange("(k p) n -> p k n", p=128)[:, 0:8, cols], w=[wk] + (allk if cb < 2 else []))
            S.dma('pool', R(w_[:, 8:16, :]), wo.rearrange("(k p) n -> p k n", p=128)[:, 8:16, cols], w=[wk])
            for t in range(NT):
                rows = slice(t * 128, (t + 1) * 128)
                xx = xt[xi % 3]
                xk = ('xt', xi % 3)
                xi += 1
                S.dma('sp', xx[:], x[rows, cols], w=[xk])
                ps, pk = S.next_ps()
                for k in range(16):
                    S.op('pe', lambda e: e.matmul(ps[:, :], lhsT=R(mT[:, k, rows]), rhs=R(w_[:, k, :]), start=(k == 0), stop=(k == 15)),
                         r=mkeys + [wk], w=[pk])
                tile_gate(S, gtb[:], gx, flags, t, cols, 'gtb')
                S.op('dve', lambda e: e.tensor_tensor(out=gtb[:], in0=ps[:, :], in1=gtb[:], op=ALU.mult), r=[pk, 'gtb'], w=['gtb'])
                S.op('pool', lambda e: e.tensor_tensor(out=xx[:], in0=xx[:], in1=gtb[:], op=ALU.add), r=[xk, 'gtb'], w=[xk])
                S.dma('sp', x1[rows, cols], xx[:], r=[xk], is_out=True)
        S.finish()
    return nc


def make_flags(h, NT=9):
    f = np.zeros((128, NT), np.float32)
    for t in range(NT):
        if tile_is_ctx(h, t):
            f[:, t] = 1.0
    return f


def mod_vec(modT, which, r):
    return np.ascontiguousarray(modT[:, which * 16:(which + 1) * 16, r].T).reshape(D)


def run_D2(ysT, omT, y5T, px_fm, xseq, modT, inputs, layer):
    in_maps = []
    ca = np.ascontiguousarray
    for c in range(NCORES):
        b, h = c // 2, c % 2
        r0, r1 = core_rows(b, h)
        in_maps.append({"ysT": ca(ysT[b, :, r0:r1]), "omT": ca(omT[b, :, r0:r1]), "y5T": ca(y5T[b, :, r0:r1]),
                        "gT": ca(px_fm[b, 3072:9216, r0:r1]),
                        "wbs": inputs['w_branch_ssd'][layer], "wbm": inputs['w_branch_mla'][layer],
                        "wb5": inputs['w_branch_s5'][layer], "wo": inputs['w_out'][layer],
                        "x": ca(xseq[b, r0:r1]), "gxr": np.stack([mod_vec(modT, 2, b), mod_vec(modT, 2, 4)], 0),
                        "flg": make_flags(h)})
    res = run_bass_kernel_spmd(build_D2(), in_maps, core_ids=list(range(NCORES)))
    x1 = np.zeros((NB, LT, D), np.float32)
    for c in range(NCORES):
        b, h = c // 2, c % 2
        r0, r1 = core_rows(b, h)
        x1[b, r0:r1] = res.results[c]["x1"]
    return x1


def build_D3():
    nc = new_nc()
    NT = NT_B
    NTOK = NT * 128
    xin = din(nc, "x", [NTOK, D])
    msel = din(nc, "msel", [128, NT * 16 * 2])
    gn = din(nc, "gn", [128, 16])
    wr = din(nc, "wr", [D, 16])
    hx = dout(nc, "hx", [NTOK, D])
    aff = dout(nc, "aff", [NTOK, 16])
    affTo = dout(nc, "affT", [16, NTOK])
    with ExitStack() as ctx:
        S = get_sched(nc, ctx)
        ident = make_ident(S)
        hT = S.sb('hT', [128, 16, NTOK])
        ms, g1 = load_mod(S, msel, gn, NT)
        wrt = S.sb('wrt', [128, 16, 16])
        S.dma('sp', wrt[:], wr.rearrange("(k p) e -> p k e", p=128), w=['wrt'])
        norm_mod_tiles(S, xin, NT, ms, g1, hT, ident, 'n2')
        hb = [S.sb(f'hb{i}', [128, D]) for i in range(2)]
        lg = S.sb('lg', [128, 16]); mx = S.sb('mx', [128, 1]); sm_ = S.sb('sm', [128, 1])
        ab = [S.sb(f'ab{i}', [128, 16]) for i in range(2)]
        atb = [S.sb(f'atb{i}', [16, 128]) for i in range(2)]
        for t in range(NT):
            rows = slice(t * 128, (t + 1) * 128)
            h = hb[t % 2]
            hk = ('hb', t % 2)
            for kk in range(4):
                ps, pk = S.next_ps()
                for j in range(4):
                    k = kk * 4 + j
                    S.op('pe', lambda e: e.transpose(out=ps[:, j * 128:(j + 1) * 128], in_=hT[:, k, rows], identity=ident[:]),
                         r=[('hT', t), 'ident'], w=[pk])
                if kk % 2 == 0:
                    S.op('act', lambda e: e.activation(out=h[:, kk * 512:(kk + 1) * 512], in_=ps[:, :], func=AF.Copy), r=[pk], w=[hk])
                else:
                    S.op('dve', lambda e: e.tensor_copy(out=h[:, kk * 512:(kk + 1) * 512], in_=ps[:, :]), r=[pk], w=[hk])
            S.dma('pool', hx[rows, :], h[:], r=[hk], is_out=True)
            ps, pk = S.next_ps()
            for k in range(16):
                S.op('pe', lambda e: e.matmul(ps[:, 0:16], lhsT=hT[:, k, rows], rhs=wrt[:, k, :], start=(k == 0), stop=(k == 15)),
                     r=[('hT', t), 'wrt'], w=[pk])
            a = ab[t % 2]
            ak = ('ab', t % 2)
            S.op('dve', lambda e: e.tensor_copy(out=lg[:], in_=ps[:, 0:16]), r=[pk], w=['lg'])
            S.op('dve', lambda e: e.tensor_reduce(out=mx[:], in_=lg[:], axis=AX.X, op=ALU.max), r=['lg'], w=['mx'])
            S.op('dve', lambda e: e.tensor_scalar(out=mx[:], in0=mx[:], scalar1=-1.0, scalar2=None, op0=ALU.mult), r=['mx'], w=['mx'])
            S.op('dve', lambda e: e.memset(sm_[:], 0.0), w=['sm'])
            S.op('act', lambda e: e.activation(out=a[:], in_=lg[:], func=AF.Exp, bias=mx[:, 0:1], scale=1.0, accum_out=sm_[:, 0:1]),
                 r=['lg', 'mx', 'sm'], w=[ak, 'sm'])
            S.op('dve', lambda e: e.reciprocal(out=sm_[:], in_=sm_[:]), r=['sm'], w=['sm'])
            S.op('dve', lambda e: e.tensor_scalar(out=a[:], in0=a[:], scalar1=sm_[:, 0:1], scalar2=None, op0=ALU.mult),
                 r=[ak, 'sm'], w=[ak])
            S.dma('pool', aff[rows, :], a[:], r=[ak], is_out=True)
            ps, pk = S.next_ps()
            S.op('pe', lambda e: e.transpose(out=ps[0:16, 0:128], in_=a[:, 0:16], identity=ident[:]), r=[ak, 'ident'], w=[pk])
            at = atb[t % 2]
            atk = ('atb', t % 2)
            S.op('dve', lambda e: e.tensor_copy(out=at[:], in_=ps[0:16, 0:128]), r=[pk], w=[atk])
            S.dma('pool', affTo[:, rows], at[:], r=[atk], is_out=True)
        S.finish()
    return nc


def run_D3(x1, modT, inputs, layer):
    gn = np.ascontiguousarray(inputs['norm2_gain'][layer].reshape(16, 128).T)
    in_maps = []
    for c in range(NCORES):
        b, h = c // 2, c % 2
        r0, r1 = core_rows(b, h)
        in_maps.append({"x": np.ascontiguousarray(x1[b, r0:r1]), "msel": make_msel(modT, b, h, 4, 3), "gn": gn,
                        "wr": inputs['moe_router'][layer]})
    res = run_bass_kernel_spmd(build_D3(), in_maps, core_ids=list(range(NCORES)))
    hx2 = np.zeros((NB, LT, D), np.float32)
    aff = np.zeros((NB, LT, 16), np.float32)
    for c in range(NCORES):
        b, h = c // 2, c % 2
        r0, r1 = core_rows(b, h)
        hx2[b, r0:r1] = res.results[c]["hx"]
        aff[b, r0:r1] = res.results[c]["aff"]
    return hx2, aff


def build_E(with_ctx, do_zero=True):
    nc = new_nc()
    NE = 8
    NSL = 288 if with_ctx else 256
    affT = din(nc, "affT", [NE, LT])
    hx = din(nc, "hx", [LT, D])
    wg = din(nc, "wg", [NE, D, D]); wu = din(nc, "wu", [NE, D, D]); wd = din(nc, "wd", [NE, D, D])
    delta = dout(nc, "delta", [LT, D])
    with ExitStack() as ctx:
        S = get_sched(nc, ctx)
        ident = make_ident(S)
        ysb = [S.sb(f'ys{i}', [128, D]) for i in range(2)]
        if do_zero:
            S.op('pool', lambda e: e.memset(ysb[0][:], 0.0), w=[('ys', 0)])
            for t in range(NTL):
                S.dma('pool', delta[t * 128:(t + 1) * 128, :], ysb[0][:], r=[('ys', 0)], w=['delta'], is_out=True)
        work = S.sb('work', [NE, LT])
        S.dma('sp', work[:], affT[:, :], w=['work'])
        vals = S.sb('vals', [NE, 288]); idxu = S.sb('idxu', [NE, 288], U32); idxf = S.sb('idxf', [NE, 288])
        S.op('dve', lambda e: e.memset(vals[:], 0.0), w=['vals'])
        S.op('dve', lambda e: e.memset(idxf[:], 0.0), w=['idxf'])
        segs = [(CTX, LT, 0, 32, float(CTX))] + ([(0, CTX, 256, 4, 0.0)] if with_ctx else [])
        for (a0, a1, s0, rounds, off) in segs:
            for r in range(rounds):
                sl = slice(s0 + r * 8, s0 + r * 8 + 8)
                S.op('dve', lambda e: e.max(out=vals[:, sl], in_=work[:, a0:a1]), r=['work'], w=['vals'])
                S.op('dve', lambda e: e.max_index(out=idxu[:, sl], in_max=vals[:, sl], in_values=work[:, a0:a1]),
                     r=['work', 'vals'], w=['idxu'])
                S.op('dve', lambda e: e.match_replace(out=work[:, a0:a1], in_to_replace=vals[:, sl], in_values=work[:, a0:a1],
                                                      imm_value=-1.0), r=['work', 'vals'], w=['work'])
            S.op('dve', lambda e: e.tensor_copy(out=idxf[:, s0:s0 + rounds * 8], in_=idxu[:, s0:s0 + rounds * 8]),
                 r=['idxu'], w=['idxf'])
            if off != 0.0:
                S.op('dve', lambda e: e.tensor_scalar(out=idxf[:, s0:s0 + rounds * 8], in0=idxf[:, s0:s0 + rounds * 8],
                                                      scalar1=off, scalar2=None, op0=ALU.add), r=['idxf'], w=['idxf'])
        gTt = S.sb('gTt', [128, 3, NE]); iTf = S.sb('iTf', [128, 3, NE]); iTu = S.sb('iTu', [128, 3, NE], U32)
        S.op('dve', lambda e: e.memset(iTf[:], 0.0), w=['iTf'])
        S.op('dve', lambda e: e.memset(gTt[:], 0.0), w=['gTt'])
        tiles = [(0, 128), (128, 128)] + ([(256, 32)] if with_ctx else [])
        for st, (c0, nr) in enumerate(tiles):
            for src, dst, dk in [(vals, gTt, 'gTt'), (idxf, iTf, 'iTf')]:
                ps, pk = S.next_ps()
                S.op('pe', lambda e: e.transpose(out=ps[0:nr, 0:NE], in_=src[:, c0:c0 + nr], identity=ident[0:NE, 0:NE]),
                     r=['vals', 'idxf', 'ident'], w=[pk])
                S.op('dve', lambda e: e.tensor_copy(out=dst[0:nr, st, :], in_=ps[0:nr, 0:NE]), r=[pk], w=[dk])
        S.op('dve', lambda e: e.tensor_copy(out=iTu[:], in_=iTf[:]), r=['iTf'], w=['iTu'])
        NWB = 4
        wbuf = [S.sb(f'wbuf{i}', [128, 16, 512]) for i in range(NWB)]
        wcnt = [0]

        def load_w(src2d, cols):
            i = wcnt[0] % NWB
            wcnt[0] += 1
            v = src2d.rearrange("(k p) n -> p k n", p=128)
            S.dma('pool', R(wbuf[i][:, 0:8, :]), v[:, 0:8, cols], w=[('wbuf', i)])
            S.dma('pool', R(wbuf[i][:, 8:16, :]), v[:, 8:16, cols], w=[('wbuf', i)])
            return wbuf[i], ('wbuf', i)

        xs = [S.sb(f'xs{i}', [128, D]) for i in range(1)]
        xsT = S.sb('xsT', [128, 16, NSL])
        hidT = S.sb('hidT', [128, 16, NSL])
        sg = S.sb('sg', [128, NSL])
        xi = 0
        yi = 0
        for e_ in range(NE):
            for st, (c0, nr) in enumerate(tiles):
                xx = xs[0]
                xk = ('xs', 0)
                xi += 1
                S._deps('pool', ['iTu'], [xk])
                S._guard_dma('pool')
                ins = nc.gpsimd.indirect_dma_start(out=xx[0:nr, :], out_offset=None, in_=hx[:, :],
                                                   in_offset=bass.IndirectOffsetOnAxis(ap=iTu[0:nr, st, e_:e_ + 1], axis=0))
                tok = S._finish_dma('pool', ins, ['iTu'], [xk])
                for kk in range(4):
                    ps, pk = S.next_ps()
                    for j in range(4):
                        k = kk * 4 + j
                        S.op('pe', lambda e: e.transpose(out=ps[:, j * 128:j * 128 + nr], in_=xx[0:nr, k * 128:(k + 1) * 128],
                                                         identity=ident[0:nr, 0:nr]), r=[xk, 'ident'], w=[pk])
                    S.op('act', lambda e: e.activation(out=R(xsT[:, kk * 4:kk * 4 + 4, c0:c0 + nr]),
                                                       in_=ps[:, :].rearrange("p (j c) -> p j c", c=128)[:, :, 0:nr], func=AF.Copy),
                         r=[pk], w=['xsT'])
            for fb in range(4):
                fcols = slice(fb * 512, (fb + 1) * 512)
                wgt, wgk = load_w(wg[e_], fcols)
                wut, wuk = load_w(wu[e_], fcols)
                for ff in range(4):
                    f = fb * 4 + ff
                    psg, kg_ = S.next_ps()
                    psu, ku_ = S.next_ps()
                    for k in range(16):
                        S.op('pe', lambda e: e.matmul(psg[:, 0:NSL], lhsT=R(wgt[:, k, ff * 128:(ff + 1) * 128]), rhs=R(xsT[:, k, :]),
                                                      start=(k == 0), stop=(k == 15)), r=[wgk, 'xsT'], w=[kg_])
                    for k in range(16):
                        S.op('pe', lambda e: e.matmul(psu[:, 0:NSL], lhsT=R(wut[:, k, ff * 128:(ff + 1) * 128]), rhs=R(xsT[:, k, :]),
                                                      start=(k == 0), stop=(k == 15)), r=[wuk, 'xsT'], w=[ku_])
                    S.op('act', lambda e: e.activation(out=sg[:, :], in_=psg[:, 0:NSL], func=AF.Silu), r=[kg_], w=['sg'])
                    S.op('dve', lambda e: e.tensor_tensor(out=R(hidT[:, f, :]), in0=psu[:, 0:NSL], in1=sg[:, :], op=ALU.mult),
                         r=[ku_, 'sg'], w=['hidT'])
            yts = []
            for st, (c0, nr) in enumerate(tiles):
                yts.append((ysb[yi % 2] if st < 2 else xs[0], ('ys', yi % 2) if st < 2 else ('xs', 0)))
                if st < 2:
                    yi += 1
            for cb in range(4):
                cols = slice(cb * 512, (cb + 1) * 512)
                wdt, wdk = load_w(wd[e_], cols)
                for st, (c0, nr) in enumerate(tiles):
                    yt, yk = yts[st]
                    ps, pk = S.next_ps()
                    for f in range(16):
                        S.op('pe', lambda e: e.matmul(ps[0:nr, :], lhsT=R(hidT[:, f, c0:c0 + nr]), rhs=R(wdt[:, f, :]),
                                                      start=(f == 0), stop=(f == 15)), r=['hidT', wdk], w=[pk])
                    S.op('act', lambda e: e.activation(out=yt[0:nr, cols], in_=ps[0:nr, :], func=AF.Copy,
                                                       scale=gTt[0:nr, st, e_:e_ + 1]), r=[pk, 'gTt'], w=[yk])
            for st, (c0, nr) in enumerate(tiles):
                yt, yk = yts[st]
                S._deps('pool', [yk, 'iTu'], ['delta'])
                S._guard_dma('pool')
                ins = nc.gpsimd.indirect_dma_start(out=delta[:, :], out_offset=bass.IndirectOffsetOnAxis(ap=iTu[0:nr, st, e_:e_ + 1], axis=0),
                                                   in_=yt[0:nr, :], in_offset=None, compute_op=ALU.add)
                tok = S._finish_dma('pool', ins, [yk, 'iTu'], ['delta'])
                S.out_tokens.append(tok)
        S.finish()
    return nc


def run_E(aff, hx2, inputs, layer, with_ctx, batches=(0, 1, 2, 3)):
    in_maps = []
    cores = []
    for b in batches:
        for hf in range(2):
            cores.append((b, hf))
            es = slice(8 * hf, 8 * hf + 8)
            in_maps.append({"affT": np.ascontiguousarray(aff[b].T[es]), "hx": np.ascontiguousarray(hx2[b]),
                            "wg": inputs['moe_w_gate'][layer, es], "wu": inputs['moe_w_up'][layer, es],
                            "wd": inputs['moe_w_down'][layer, es]})
    res = run_bass_kernel_spmd(build_E(with_ctx), in_maps, core_ids=list(range(len(cores))))
    out = np.zeros((NB, 2, LT, D), np.float32)
    for i, (b, hf) in enumerate(cores):
        out[b, hf] = res.results[i]["delta"]
    return out


def build_F(single=False):
    nc = new_nc()
    NT = NT_B
    NTOK = NT * 128
    x1 = din(nc, "x1", [NTOK, D]); da = din(nc, "da", [NTOK, D])
    db = None if single else din(nc, "db", [NTOK, D])
    gxr = din(nc, "gxr", [2, D]); flg = din(nc, "flg", [128, NT])
    x2 = dout(nc, "x2", [NTOK, D])
    with ExitStack() as ctx:
        S = get_sched(nc, ctx)
        gx, flags = load_gx(S, gxr, flg, NT)
        xa = [S.sb(f'xa{i}', [128, D]) for i in range(2)]
        ta = [S.sb(f'ta{i}', [128, D]) for i in range(2)]
        tb = [S.sb(f'tb{i}', [128, D]) for i in range(2)]
        gt = S.sb('gt', [128, D])
        for t in range(NT):
            i = t % 2
            rows = slice(t * 128, (t + 1) * 128)
            S.dma('sp', xa[i][:], x1[rows, :], w=[('xa', i)])
            S.dma('sp', ta[i][:], da[rows, :], w=[('ta', i)])
            if not single:
                S.dma('act', tb[i][:], db[rows, :], w=[('tb', i)])
            tile_gate(S, gt[:], gx, flags, t, slice(0, D), 'gt')
            if not single:
                S.op('pool', lambda e: e.tensor_tensor(out=ta[i][:], in0=ta[i][:], in1=tb[i][:], op=ALU.add),
                     r=[('ta', i), ('tb', i)], w=[('ta', i)])
            S.op('dve', lambda e: e.tensor_tensor(out=ta[i][:], in0=ta[i][:], in1=gt[:], op=ALU.mult), r=[('ta', i), 'gt'], w=[('ta', i)])
            S.op('pool', lambda e: e.tensor_tensor(out=xa[i][:], in0=xa[i][:], in1=ta[i][:], op=ALU.add),
                 r=[('xa', i), ('ta', i)], w=[('xa', i)])
            S.dma('pool', x2[rows, :], xa[i][:], r=[('xa', i)], is_out=True)
        S.finish()
    return nc


def run_F(x1, dl, modT):
    in_maps = []
    ca = np.ascontiguousarray
    for c in range(NCORES):
        b, h = c // 2, c % 2
        r0, r1 = core_rows(b, h)
        in_maps.append({"x1": ca(x1[b, r0:r1]), "da": ca(dl[b, 0, r0:r1]), "db": ca(dl[b, 1, r0:r1]),
                        "gxr": np.stack([mod_vec(modT, 5, b), mod_vec(modT, 5, 4)], 0), "flg": make_flags(h)})
    res = run_bass_kernel_spmd(build_F(), in_maps, core_ids=list(range(NCORES)))
    x2 = np.zeros((NB, LT, D), np.float32)
    for c in range(NCORES):
        b, h = c // 2, c % 2
        r0, r1 = core_rows(b, h)
        x2[b, r0:r1] = res.results[c]["x2"]
    return x2


def build_A2():
    nc = new_nc()
    cT = din(nc, "cT2", [128, 32])
    w = din(nc, "wmod", [D, 12288])
    b = din(nc, "bmod", [128, 96])
    modT_d = dout(nc, "modT", [128, 2 * 96])
    gvec_d = dout(nc, "gvec", [2, 12288])
    with ExitStack() as ctx:
        S = get_sched(nc, ctx)
        ident = make_ident(S)
        ct = S.sb('ct', [128, 16, 2]); ca = S.sb('ca', [128, 16, 2]); bt = S.sb('bt', [128, 96])
        mt = S.sb('mt', [128, 2, 96])
        wt = [S.sb(f'wt{i}', [128, 16, 768]) for i in range(2)]
        S.dma('sp', ct[:].rearrange("p k r -> p (k r)"), cT[:, :], w=['ct'])
        S.dma('sp', bt[:], b[:, :], w=['bt'])
        S.op('act', lambda e: e.activation(out=ca[:], in_=ct[:], func=AF.Silu), r=['ct'], w=['ca'])
        wv = w.rearrange("(k p) n -> p k n", p=128)
        for j in range(16):
            wtj = wt[j % 2]
            for g in range(4):
                S.dma('sp' if g % 2 == 0 else 'act', wtj[:, 4 * g:4 * g + 4, :], wv[:, 4 * g:4 * g + 4, j * 768:(j + 1) * 768],
                      w=[('wt', j % 2, g)])
            ps, pk = S.next_ps()
            for m in range(6):
                for k in range(16):
                    S.op('pe', lambda e: e.matmul(ps[:, m * 2:m * 2 + 2], lhsT=wtj[:, k, m * 128:(m + 1) * 128],
                                                  rhs=ca[:, k, :], start=(k == 0), stop=(k == 15)),
                         r=['ca', ('wt', j % 2, k // 4)], w=[pk])
            for m in range(6):
                c = j * 6 + m
                S.op('act', lambda e: e.activation(out=mt[:, :, c], in_=ps[:, m * 2:m * 2 + 2], func=AF.Identity,
                                                   bias=bt[:, c:c + 1], scale=1.0), r=[pk, 'bt'], w=['mt'])
        S.dma('pool', modT_d[:, :], mt[:].rearrange("p r c -> p (r c)"), r=['mt'], is_out=True)
        gv = S.sb('gv', [96, 2, 128])
        for r in range(2):
            ps, pk = S.next_ps()
            S.op('pe', lambda e: e.transpose(out=ps[0:96, 0:128], in_=mt[:, r, :], identity=ident[:]), r=['mt', 'ident'], w=[pk])
            S.op('dve', lambda e: e.tensor_copy(out=gv[:, r, :], in_=ps[0:96, 0:128]), r=[pk], w=['gv'])
            S.dma('pool', gvec_d[r, :].rearrange("(c p) -> c p", p=128), gv[:, r, :], r=['gv'], is_out=True)
        S.finish()
    return nc


def build_msel(modT_d, out_d, rows, sc_i, sh_i):
    nc = new_nc()
    with ExitStack() as ctx:
        S = get_sched(nc, ctx)
        mv = modT_d.rearrange("p (r c) -> p r c", r=2)
        ov = out_d.rearrange("p (t s k) -> p t s k", s=2, k=16)
        for t, r in enumerate(rows):
            S.dma('sp', ov[:, t, 0, :], mv[:, r, sc_i * 16:(sc_i + 1) * 16])
            S.dma('act', ov[:, t, 1, :], mv[:, r, sh_i * 16:(sh_i + 1) * 16])
        S.finish()


FUSED_INPUT_SPECS = None


_DBG = {'export': (), 'stop': None}


def build_fused():
    nc = bass.Bass("TRN2", target_bir_lowering=False)
    ext = {}

    def EI(name, shape, dt=F32):
        ext[name] = nc.dram_tensor(name, list(shape), dt, kind="ExternalInput").ap()
        return ext[name]

    def SC(name, shape, dt=F32):
        kind = "ExternalOutput" if name in _DBG['export'] else "Internal"
        return nc.dram_tensor(name, list(shape), dt, kind=kind).ap()

    class _Stop(Exception):
        pass

    nst = [0]

    def chk():
        nst[0] += 1
        if _DBG['stop'] is not None and nst[0] >= _DBG['stop']:
            raise _Stop()

    xs0 = EI("xseq", [LT, D])
    out = nc.dram_tensor("out", [SEQ, D], F32, kind="ExternalOutput").ap()
    cs = EI("cs", [LT, 32])
    flg = [EI("flg0", [128, 9]), EI("flg1", [128, 9])]
    WSPEC = dict(cT2=[128, 32], wmod=[D, 12288], bmod=[128, 96], gn1=[128, 16], gn2=[128, 16], win=[D, PROJ_IN],
                 cw=[128, 80], cb=[128, 16], abd=[128, 96], gqa=[128, 4], gkv=[128, 2], wq=[512, 1536], wkv=[256, 2048],
                 qg=[128, 96], kg=[128, 96], lamp=[128, 192], lamr=[32, 6 * 4096], bT=[32, 4 * 4096], cbd=[128, 4096],
                 gssd=[128, 8], s5d=[128, 8], wglu=[1024, 1024], wbs=[1024, D], wbm=[1024, D], wb5=[1024, D], wo=[D, D],
                 wr=[D, 16], wg=[16, D, D], wu=[16, D, D], wd=[16, D, D])

    class LazyW(dict):
        def __init__(self, l):
            super().__init__()
            self.l = l

        def __missing__(self, k):
            self[k] = EI(f"l{self.l}_{k}", WSPEC[k])
            return self[k]

    L = [LazyW(0), LazyW(1)]
    modT = SC("modT", [128, 192]); gvec = SC("gvec", [2, 12288])
    msel1 = [SC(f"msel1_{h}", [128, 9 * 32]) for h in range(2)]
    msel2 = [SC(f"msel2_{h}", [128, 9 * 32]) for h in range(2)]
    px_tm = SC("px_tm", [LT, 1840]); px_fm = SC("px_fm", [9216, LT])
    y0 = SC("y0", [LT, 1024]); y1 = SC("y1", [LT, 1024]); omT = SC("omT", [1024, LT])
    yT0 = SC("yT0", [1024, LT]); yT1 = SC("yT1", [1024, LT])
    ysT = SC("ysT", [1024, LT]); y5T = SC("y5T", [1024, LT])
    x1 = SC("x1", [LT, D]); hx = SC("hx", [LT, D]); aff = SC("aff", [LT, 16]); affT = SC("affT", [16, LT])
    delta = SC("delta", [LT, D])
    xs1 = SC("xs1", [LT, D]); xs2 = SC("xs2", [LT, D])
    with ExitStack() as gctx:
        S = Sched(nc, gctx)
        S.fused = True
        _F['nc'], _F['S'] = nc, S
        try:
            xcur = xs0
            halves = [(0, 1152), (1152, 2304)]
            rows_h = [[1, 1] + [0] * 7, [0] * 9]
            try:
                for l in range(2):
                    if nst[0] < 0:
                        break
                    W = L[l]
                    _F['io'] = dict(cT2=W['cT2'], wmod=W['wmod'], bmod=W['bmod'], modT=modT, gvec=gvec)
                    build_A2()
                    chk()
                    for h in range(2):
                        build_msel(modT, msel1[h], rows_h[h], 1, 0)
                        build_msel(modT, msel2[h], rows_h[h], 4, 3)
                    for h, (r0, r1) in enumerate(halves * _DBG.get('brep', 1)):
                        h = h % 2
                        if 'B' in _DBG.get('skip', ()):
                            continue
                        _F['io'] = dict(x=xcur[r0:r1, :], msel=msel1[h], gn=W['gn1'], w=W['win'], otm=px_tm[r0:r1, :], ofm=px_fm[:, r0:r1])
                        build_B()
                        chk()
                    _F['io'] = dict(xbc=px_fm[0:2048, :], dt=px_tm[:, 1024:1040], cw=W['cw'], cb=W['cb'], abd=W['abd'], y0=y0, y1=y1)
                    c1io = _F['io']
                    if 'C1' not in _DBG.get('skip', ()) and not _DBG.get('c1late'):
                        build_C1()
                    chk()
                    for hf in _DBG.get('c2', (0, 1)):
                        _F['io'] = dict(mla=px_tm[:, 1040:1840], gqa=W['gqa'], gkv=W['gkv'], wq=W['wq'][:, hf * 768:(hf + 1) * 768],
                                        wkv=W['wkv'][:, hf * 1024:(hf + 1) * 1024], qg=W['qg'], kg=W['kg'], cs=cs,
                                        oT=omT[hf * 512:(hf + 1) * 512, :])
                        build_C2()
                        chk()
                    if _DBG.get('c1late'):
                        _F['io'] = c1io
                        build_C1()
                        chk()
                    _F['io'] = dict(u=px_fm[2048:3072, :], lamp=W['lamp'], lamr=W['lamr'], bT=W['bT'], cbd=W['cbd'], yT0=yT0, yT1=yT1)
                    build_C3()
                    chk()
                    for h, (r0, r1) in enumerate(halves):
                        _F['io'] = dict(y0=y0[r0:r1, :], y1=y1[r0:r1, :], z=px_tm[r0:r1, 0:1024], gssd=W['gssd'],
                                        y5a=yT0[:, r0:r1], y5b=yT1[:, r0:r1], uT=px_fm[2048:3072, r0:r1], s5d=W['s5d'], wglu=W['wglu'],
                                        ysT=ysT[:, r0:r1], y5T=y5T[:, r0:r1])
                        build_D1()
                        chk()
                    for h, (r0, r1) in enumerate(halves):
                        _F['io'] = dict(ysT=ysT[:, r0:r1], omT=omT[:, r0:r1], y5T=y5T[:, r0:r1], gT=px_fm[3072:9216, r0:r1],
                                        wbs=W['wbs'], wbm=W['wbm'], wb5=W['wb5'], wo=W['wo'], x=xcur[r0:r1, :],
                                        gxr=gvec[:, 2 * D:3 * D], flg=flg[h], x1=x1[r0:r1, :])
                        build_D2()
                        chk()
                    for h, (r0, r1) in enumerate(halves):
                        _F['io'] = dict(x=x1[r0:r1, :], msel=msel2[h], gn=W['gn2'], wr=W['wr'], hx=hx[r0:r1, :], aff=aff[r0:r1, :],
                                        affT=affT[:, r0:r1])
                        build_D3()
                        chk()
                    for hf in range(2):
                        es = slice(8 * hf, 8 * hf + 8)
                        _F['io'] = dict(affT=affT[es, :], hx=hx, wg=W['wg'][es], wu=W['wu'][es], wd=W['wd'][es], delta=delta)
                        build_E(with_ctx=(l == 0), do_zero=(hf == 0))
                        chk()
                    xnext = xs1 if l == 0 else xs2
                    for h, (r0, r1) in enumerate(halves):
                        if l == 1:
                            pass
                        _F['io'] = dict(x1=x1[r0:r1, :], da=delta[r0:r1, :], gxr=gvec[:, 5 * D:6 * D], flg=flg[h],
                                        x2=xnext[r0:r1, :])
                        build_F(single=True)
                        chk()
                    xcur = xnext
                nc_ = new_nc()
                with ExitStack() as c3:
                    S3 = get_sched(nc_, c3)
                    S3.fused = False
                    for i in range(4):
                        S3.dma('sp' if i % 2 == 0 else 'act', out[i * 512:(i + 1) * 512, :], xcur[CTX + i * 512:CTX + (i + 1) * 512, :],
                               is_out=True)
                    S3.finish()
            except _Stop:
                pass
        finally:
            _F['nc'], _F['S'], _F['io'] = None, None, {}
    global FUSED_INPUT_SPECS
    FUSED_INPUT_SPECS = set(ext.keys())
    return nc


def fused_inputs(inputs, b):
    ca = np.ascontiguousarray
    m = {}
    m["xseq"] = ca(np.concatenate([inputs['ctx'][b], inputs['x'][b]], axis=0))
    m["cs"] = rope_table()
    m["flg0"] = make_flags(0)
    m["flg1"] = make_flags(1)
    for l in range(2):
        p = f"l{l}_"
        c2 = np.stack([inputs['c'][b], inputs['c_ctx']], axis=0)
        m[p + "cT2"] = ca(c2.reshape(2, 16, 128).transpose(2, 1, 0)).reshape(128, 32)
        m[p + "wmod"] = inputs['w_mod'][l]
        m[p + "bmod"] = ca(inputs['b_mod'][l].reshape(96, 128).T)
        m[p + "gn1"] = ca(inputs['norm1_gain'][l].reshape(16, 128).T)
        m[p + "gn2"] = ca(inputs['norm2_gain'][l].reshape(16, 128).T)
        m[p + "win"] = inputs['w_in'][l]
        m[p + "cw"] = ca(inputs['ssd_conv_w'][l].T.reshape(16, 128, 5).transpose(1, 0, 2)).reshape(128, 80)
        m[p + "cb"] = ca(inputs['ssd_conv_b'][l].reshape(16, 128).T)
        abd = np.stack([inputs['ssd_a_log'][l], inputs['ssd_dt_bias'][l], inputs['ssd_d'][l]], 0)
        m[p + "abd"] = rep128(abd.reshape(-1))
        m[p + "gqa"] = ca(inputs['mla_q_a_gain'][l].reshape(4, 128).T)
        m[p + "gkv"] = ca(inputs['mla_kv_a_gain'][l].reshape(2, 128).T)
        m[p + "wq"] = inputs['mla_w_q_b'][l]
        m[p + "wkv"] = inputs['mla_w_kv_b'][l]
        m[p + "qg"] = rep128(inputs['mla_q_gain'][l])
        m[p + "kg"] = rep128(inputs['mla_k_gain'][l])
        lamp, lamr, bTs, cbds = [], [], [], []
        for d in range(2):
            lre = inputs['s5_lam_re'][l, d]
            lim = inputs['s5_lam_im'][l, d]
            ldt = np.broadcast_to(inputs['s5_log_dt'][l, d][:, None], (64, 64))
            P = lambda a: a.reshape(32, 128).T
            Rl = lambda a: np.broadcast_to(a.reshape(1, 4096), (32, 4096))
            lamp += [P(lre), P(lim), P(ldt)]
            lamr += [Rl(lre), Rl(lim), Rl(ldt)]
            for bm in (inputs['s5_b_re'][l, d], inputs['s5_b_im'][l, d]):
                o = np.zeros((2, 16, 32, 2, 64), np.float32)
                bb = bm.reshape(32, 2, 64, 16)
                for gg in range(2):
                    o[gg, :, :, gg, :] = bb[:, gg].transpose(2, 0, 1)
                bTs.append(o.reshape(32, 4096))
            for cm in (inputs['s5_c_re'][l, d], inputs['s5_c_im'][l, d]):
                o = np.zeros((2, 64, 32, 2, 16), np.float32)
                cc = cm.reshape(32, 2, 16, 64)
                for gg in range(2):
                    o[gg, :, :, gg, :] = cc[:, gg].transpose(2, 0, 1)
                cbds.append(o.reshape(128, 1024))
        m[p + "lamp"] = ca(np.concatenate(lamp, axis=1), dtype=np.float32)
        m[p + "lamr"] = ca(np.concatenate(lamr, axis=1), dtype=np.float32)
        m[p + "bT"] = ca(np.concatenate(bTs, axis=1))
        m[p + "cbd"] = ca(np.concatenate(cbds, axis=1))
        m[p + "gssd"] = ca(inputs['ssd_norm_gain'][l].reshape(8, 128).T)
        m[p + "s5d"] = ca(inputs['s5_d'][l].reshape(8, 128).T)
        m[p + "wglu"] = inputs['s5_w_glu'][l]
        m[p + "wbs"] = inputs['w_branch_ssd'][l]
        m[p + "wbm"] = inputs['w_branch_mla'][l]
        m[p + "wb5"] = inputs['w_branch_s5'][l]
        m[p + "wo"] = inputs['w_out'][l]
        m[p + "wr"] = inputs['moe_router'][l]
        m[p + "wg"] = inputs['moe_w_gate'][l]
        m[p + "wu"] = inputs['moe_w_up'][l]
        m[p + "wd"] = inputs['moe_w_down'][l]
    return {k: np.ascontiguousarray(v, dtype=np.float32) for k, v in m.items() if k in FUSED_INPUT_SPECS}


def kernel(**inputs):
    inputs = {k: np.asarray(v, dtype=np.float32) for k, v in inputs.items()}
    nc = build_fused()
    maps = [fused_inputs(inputs, b) for b in range(NB)]
    in_maps = [maps[c % NB] for c in range(NCORES_F)]
    res = run_bass_kernel_spmd(nc, in_maps, core_ids=list(range(NCORES_F)))
    return np.stack([res.results[b]["out"] for b in range(NB)], axis=0).astype(np.float32)
```

```python
from contextlib import ExitStack
import numpy as np
import concourse.bass as bass
import concourse.mybir as mybir
from concourse.bass_utils import run_bass_kernel_spmd

F32 = mybir.dt.float32
F32R = mybir.dt.float32r


def R(ap):
    return ap.bitcast(F32R)
I32 = mybir.dt.int32
U32 = mybir.dt.uint32
AF = mybir.ActivationFunctionType
ALU = mybir.AluOpType
AX = mybir.AxisListType

D = 2048
NB = 4
SEQ = 2048
CTX = 256
LT = SEQ + CTX
EPS = 1e-6
PROJ_IN = 11056
NCORES = 8
NCORES_F = 4


_QMAP = {}
SAME_ENGINE_WAITS = False


class Sched:
    NDS = 8

    def __init__(self, nc, ctx):
        self.nc = nc
        self.ctx = ctx
        self.E = {'pe': nc.tensor, 'act': nc.scalar, 'dve': nc.vector, 'pool': nc.gpsimd, 'sp': nc.sync}
        self.sem = {k: ctx.enter_context(nc.semaphore('s_' + k)) for k in ['pe', 'act', 'dve', 'pool']}
        self.cnt = {k: 0 for k in self.sem}
        self.seen = {e: {} for e in self.E}
        self.dsem = {q: [ctx.enter_context(nc.semaphore(f'd_{q}{i}')) for i in range(self.NDS)]
                     for q in ['sp', 'pool', 'act']}
        self.dcnt = {q: 0 for q in self.dsem}
        self.last_w = {}
        self.readers = {}
        self.ps = [ctx.enter_context(nc.psum_tensor(f'ps{i}', [128, 512], F32)) for i in range(8)]
        self.psi = 0
        self.nrot = 8
        self.stage_id = 0
        self.rec = None
        self.lane = 0
        self.qmap = dict(_QMAP)
        self.fused = False
        self.out_tokens = []

    def sb(self, name, shape, dt=F32):
        return self.ctx.enter_context(self.nc.sbuf_tensor(f"s{self.stage_id}_{name}", list(shape), dt))

    def round_r(self, eng, ap, keys):
        if eng == 'act':
            self.op('act', lambda e: e.activation(out=R(ap), in_=ap, func=AF.Copy), r=keys, w=keys)
        else:
            self.op(eng, lambda e: e.tensor_copy(out=R(ap), in_=ap), r=keys, w=keys)

    def barrier(self):
        for e in self.E:
            for f in self.sem:
                if f != e and self.cnt[f] > 0:
                    self._wait(e, (self.sem[f], self.cnt[f], f))
            for q in self.dsem:
                n = self.dcnt[q]
                for i in range(self.NDS):
                    k = (n - i + self.NDS - 1) // self.NDS if n > i else 0
                    if k > 0:
                        self._wait(e, (self.dsem[q][i], 16 * k, 'dma_' + q))
        self.last_w = {}
        self.readers = {}
        self.out_tokens = []

    def next_ps(self):
        i = self.psi
        self.psi = (self.psi + 1) % self.nrot
        return self.ps[i], ('ps', i)

    def _wait(self, e, tok):
        sem, val, src = tok
        if src == e and e == 'pe':
            return
        sid = id(sem)
        if self.seen[e].get(sid, 0) >= val:
            return
        self.E[e].wait_ge(sem, val)
        self.seen[e][sid] = val

    def _deps(self, e, r, w):
        toks = []
        for k in r:
            if k in self.last_w:
                toks.append(self.last_w[k])
            if isinstance(k, tuple) and k[0] == 'ps':
                toks.extend(t for t in self.readers.get(k, {}).values() if t[2] != e)
        for k in w:
            if k in self.last_w:
                toks.append(self.last_w[k])
            toks.extend(self.readers.get(k, {}).values())
        for t in toks:
            self._wait(e, t)

    def _record(self, tok, r, w):
        for k in r:
            d = self.readers.setdefault(k, {})
            sid = id(tok[0])
            if sid not in d or d[sid][1] < tok[1]:
                d[sid] = tok
        for k in w:
            self.last_w[k] = tok
            self.readers[k] = {}

    def rec_lane(self, lane):
        if self.rec is None:
            self.rec = {}
        self.lane = lane
        self.rec.setdefault(lane, [])

    def rec_flush(self):
        rec, self.rec = self.rec, None
        if not rec:
            return
        lanes = [rec[k] for k in sorted(rec)]
        for i in range(max(len(l) for l in lanes)):
            for l in lanes:
                if i < len(l):
                    it = l[i]
                    if it[0] == 'op':
                        _, e, name, a, kw, r, w = it
                        self.op(e, lambda eng: getattr(eng, name)(*a, **kw), r=r, w=w)
                    else:
                        _, q, out, in_, r, w, is_out, kw = it
                        self.dma(q, out, in_, r=r, w=w, is_out=is_out, **kw)

    def op(self, e, fn, r=(), w=()):
        if self.rec is not None:
            cap = []

            class _P:
                def __getattr__(self_, name):
                    def f(*a, **kw):
                        cap.append((name, a, kw))
                        return None
                    return f
            fn(_P())
            assert len(cap) == 1
            name, a, kw = cap[0]
            self.rec[self.lane].append(('op', e, name, a, kw, list(r), list(w)))
            return None
        self._deps(e, r, w)
        ins = fn(self.E[e])
        self.cnt[e] += 1
        ins.then_inc(self.sem[e], 1)
        tok = (self.sem[e], self.cnt[e], e)
        self._record(tok, r, w)
        return tok

    def _guard_dma(self, q):
        n = self.dcnt[q]
        sem = self.dsem[q][n % self.NDS]
        prev = 16 * (n // self.NDS)
        if prev > 0 and self.seen[q].get(id(sem), 0) < prev:
            self.E[q].wait_ge(sem, prev)
            self.seen[q][id(sem)] = prev
        return sem, prev

    def _finish_dma(self, q, ins, r, w):
        n = self.dcnt[q]
        sem = self.dsem[q][n % self.NDS]
        prev = 16 * (n // self.NDS)
        ins.then_inc(sem, 16)
        self.dcnt[q] += 1
        tok = (sem, prev + 16, 'dma_' + q)
        self._record(tok, r, w)
        return tok

    def dma(self, q, out, in_, r=(), w=(), is_out=False, **kw):
        if self.rec is not None:
            self.rec[self.lane].append(('dma', q, out, in_, list(r), list(w), is_out, kw))
            return None
        q = self.qmap.get(q, q)
        self._deps(q, r, w)
        self._guard_dma(q)
        ins = self.E[q].dma_start(out=out, in_=in_, **kw)
        tok = self._finish_dma(q, ins, r, w)
        if is_out:
            self.out_tokens.append(tok)
        return tok

    def finish(self):
        if self.fused:
            self.barrier()
            return
        for t in self.out_tokens:
            self._wait('sp', t)


_F = {'nc': None, 'S': None, 'io': {}}


def new_nc():
    if _F['nc'] is not None:
        return _F['nc']
    return bass.Bass("TRN2", target_bir_lowering=False)


def _io(nc, name, shape, dt, kind):
    if _F['nc'] is not None:
        ap = _F['io'][name]
        assert tuple(ap.shape) == tuple(shape), (name, ap.shape, shape)
        return ap
    return nc.dram_tensor(name, list(shape), dt, kind=kind).ap()


def din(nc, name, shape, dt=F32):
    return _io(nc, name, shape, dt, "ExternalInput")


def dout(nc, name, shape, dt=F32):
    return _io(nc, name, shape, dt, "ExternalOutput")


def get_sched(nc, ctx):
    if _F['S'] is None:
        return Sched(nc, ctx)
    S = _F['S']
    S.ctx = ctx
    S.stage_id += 1
    S.nrot = 8
    S.psi = 0
    return S


def make_ident(S, name='ident'):
    nc = S.nc
    idt = S.sb(name, [128, 128])
    S.op('pool', lambda e: e.memset(idt[:], 1.0), w=[name])
    S.op('pool', lambda e: e.affine_select(out=idt[:], in_=idt[:], pattern=[[-1, 128]], compare_op=ALU.is_equal,
                                           fill=0.0, base=0, channel_multiplier=1), r=[name], w=[name])
    return idt


def build_A():
    nc = new_nc()
    cT = din(nc, "cT", [128, 16 * 5])
    w = din(nc, "w", [D, 1536])
    b = din(nc, "b", [128, 12])
    o = dout(nc, "o", [128, 60])
    with ExitStack() as ctx:
        S = get_sched(nc, ctx)
        ct = S.sb('ct', [128, 16, 5])
        ca = S.sb('ca', [128, 16, 5])
        bt = S.sb('bt', [128, 12])
        wt = S.sb('wt', [128, 16, 1536])
        ot = S.sb('ot', [128, 12, 5])
        S.dma('sp', ct[:].rearrange("p k r -> p (k r)"), cT[:, :], w=['ct'])
        S.dma('sp', bt[:], b[:, :], w=['bt'])
        wv = w.rearrange("(k p) n -> p k n", p=128)
        for g in range(8):
            q = 'sp' if g % 2 == 0 else 'pool'
            S.dma(q, wt[:, 2 * g:2 * g + 2, :], wv[:, 2 * g:2 * g + 2, :], w=[('wt', g)])
        S.op('act', lambda e: e.activation(out=ca[:], in_=ct[:], func=AF.Silu), r=['ct'], w=['ca'])
        ps, pk = S.next_ps()
        for m in range(12):
            for k in range(16):
                S.op('pe', lambda e: e.matmul(ps[:, m * 5:m * 5 + 5], lhsT=wt[:, k, m * 128:(m + 1) * 128],
                                              rhs=ca[:, k, :], start=(k == 0), stop=(k == 15)),
                     r=['ca', ('wt', k // 2)], w=[pk])
        for m in range(12):
            S.op('act', lambda e: e.activation(out=ot[:, m, :], in_=ps[:, m * 5:m * 5 + 5], func=AF.Identity,
                                               bias=bt[:, m:m + 1], scale=1.0), r=[pk, 'bt'], w=['ot'])
        S.dma('sp', o[:, :], ot[:].rearrange("p m r -> p (m r)"), r=['ot'], is_out=True)
        S.finish()
    return nc


def run_A(inputs, layer):
    c5 = np.concatenate([inputs['c'], inputs['c_ctx'][None, :]], axis=0)
    cT = np.ascontiguousarray(c5.reshape(5, 16, 128).transpose(2, 1, 0)).reshape(128, 80)
    wm = inputs['w_mod'][layer]
    bm = inputs['b_mod'][layer]
    in_maps = []
    for j in range(NCORES):
        in_maps.append({"cT": cT, "w": np.ascontiguousarray(wm[:, j * 1536:(j + 1) * 1536]),
                        "b": np.ascontiguousarray(bm[j * 1536:(j + 1) * 1536].reshape(12, 128).T)})
    res = run_bass_kernel_spmd(build_A(), in_maps, core_ids=list(range(NCORES)))
    modT = np.concatenate([r["o"].reshape(128, 12, 5) for r in res.results], axis=1)
    return modT


NT_B = 9
B_BLOCKS = ([(0, 512, 'tm'), (512, 1024, 'tm')] + [(1024 + 512 * i, 1536 + 512 * i, 'fm') for i in range(4)]
            + [(3072, 3584, 'tm'), (3584, 3888, 'tm'), (3888, 4400, 'fm'), (4400, 4912, 'fm')]
            + [(4912 + 512 * i, 5424 + 512 * i, 'fm') for i in range(12)])


def tm_col(c):
    return c if c < 1024 else c - 2048


def fm_row(c):
    return c - 1024 if c < 3072 else c - 1840


def norm_mod_tiles(S, xin, NT, ms, g1, hT, ident, pref, rr=False):
    xb = [S.sb(f'{pref}xb{i}', [128, D]) for i in range(2)]
    junk = S.sb(pref + 'junk', [128, D])
    ss = S.sb(pref + 'ss', [128, NT])
    rs = S.sb(pref + 'rs', [128, NT])
    S.op('dve', lambda e: e.memset(ss[:], 0.0), w=[pref + 'ss'])
    for t in range(NT):
        xt = xb[t % 2]
        xk = (pref + 'xb', t % 2)
        S.dma('sp', xt[:], xin[t * 128:(t + 1) * 128, :], w=[xk])
        S.op('act', lambda e: e.activation(out=junk[:], in_=xt[:], func=AF.Square, accum_out=ss[:, t:t + 1]),
             r=[xk], w=[pref + 'junk', pref + 'ss'])
        S.op('dve', lambda e: e.tensor_scalar(out=rs[:, t:t + 1], in0=ss[:, t:t + 1], scalar1=1.0 / D, scalar2=EPS,
                                              op0=ALU.mult, op1=ALU.add), r=[pref + 'ss'], w=[pref + 'rs'])
        S.op('act', lambda e: e.activation(out=rs[:, t:t + 1], in_=rs[:, t:t + 1], func=AF.Sqrt),
             r=[pref + 'rs'], w=[pref + 'rs'])
        S.op('dve', lambda e: e.reciprocal(out=rs[:, t:t + 1], in_=rs[:, t:t + 1]), r=[pref + 'rs'], w=[pref + 'rs'])
        S.op('dve', lambda e: e.tensor_scalar(out=xt[:], in0=xt[:], scalar1=rs[:, t:t + 1], scalar2=None,
                                              op0=ALU.mult), r=[xk, pref + 'rs'], w=[xk])
        for kk in range(4):
            ps, pk = S.next_ps()
            for j in range(4):
                k = kk * 4 + j
                S.op('pe', lambda e: e.transpose(out=ps[:, j * 128:(j + 1) * 128], in_=xt[:, k * 128:(k + 1) * 128],
                                                 identity=ident[:]), r=[xk, 'ident'], w=[pk])
            for j in range(4):
                k = kk * 4 + j
                S.op('act', lambda e: e.activation(out=(R(hT[:, k, t * 128:(t + 1) * 128]) if rr else hT[:, k, t * 128:(t + 1) * 128]), in_=ps[:, j * 128:(j + 1) * 128],
                                                   func=AF.Identity, scale=g1[:, t, k:k + 1], bias=ms[:, t, 1, k:k + 1]),
                     r=[pk, 'g1', 'ms'], w=[('hT', t)])


def load_mod(S, msel, gn, NT):
    ms = S.sb('ms', [128, NT, 2, 16])
    gnt = S.sb('gnt', [128, 16])
    g1 = S.sb('g1', [128, NT, 16])
    S.dma('sp', ms[:].rearrange("p t s k -> p (t s k)"), msel[:, :], w=['ms'])
    S.dma('sp', gnt[:], gn[:, :], w=['gnt'])
    for t in range(NT):
        S.op('dve', lambda e: e.scalar_tensor_tensor(out=g1[:, t, :], in0=ms[:, t, 0, :], scalar=1.0, in1=gnt[:],
                                                     op0=ALU.add, op1=ALU.mult), r=['ms', 'gnt'], w=['g1'])
    return ms, g1


def build_B():
    nc = new_nc()
    NT = NT_B
    NTOK = NT * 128
    xin = din(nc, "x", [NTOK, D])
    msel = din(nc, "msel", [128, NT * 16 * 2])
    gn = din(nc, "gn", [128, 16])
    w = din(nc, "w", [D, PROJ_IN])
    otm = dout(nc, "otm", [NTOK, 1840])
    ofm = dout(nc, "ofm", [9216, NTOK])
    with ExitStack() as ctx:
        S = get_sched(nc, ctx)
        ident = make_ident(S)
        hT = S.sb('hT', [128, 16, NTOK])
        ms, g1 = load_mod(S, msel, gn, NT)
        norm_mod_tiles(S, xin, NT, ms, g1, hT, ident, 'n1', rr=True)
        wbuf = [S.sb(f'wb{i}', [128, 16, 512]) for i in range(2)]
        obuf = [S.sb(f'ob{i}', [128, 512]) for i in range(4)]
        wv = w.rearrange("(k p) n -> p k n", p=128)
        oi = 0
        hkeys = [('hT', t) for t in range(NT)]
        for bi, (c0, c1, kind) in enumerate(B_BLOCKS):
            nw = c1 - c0
            wb = wbuf[bi % 2]
            S.dma('pool', R(wb[:, 0:8, :nw]), wv[:, 0:8, c0:c1], w=[('wb', bi % 2, 0)])
            S.dma('pool', R(wb[:, 8:16, :nw]), wv[:, 8:16, c0:c1], w=[('wb', bi % 2, 1)])
            if kind == 'tm':
                jobs = [('tm', t, None) for t in range(NT)]
            else:
                jobs = [('fm', m, rng) for m in range(nw // 128) for rng in [(0, 512), (512, 1024), (1024, NTOK)]]
            for kind_, a, rng in jobs:
                ps, pk = S.next_ps()
                if kind_ == 'tm':
                    t = a
                    n = nw
                    for k in range(16):
                        S.op('pe', lambda e: e.matmul(ps[:, :nw], lhsT=R(hT[:, k, t * 128:(t + 1) * 128]), rhs=R(wb[:, k, :nw]),
                                                      start=(k == 0), stop=(k == 15)),
                             r=[('hT', t), ('wb', bi % 2, k // 8)], w=[pk])
                    dst = otm[t * 128:(t + 1) * 128, tm_col(c0):tm_col(c0) + nw]
                else:
                    m = a
                    n0, n1 = rng
                    n = n1 - n0
                    for k in range(16):
                        S.op('pe', lambda e: e.matmul(ps[:, :n], lhsT=R(wb[:, k, m * 128:(m + 1) * 128]), rhs=R(hT[:, k, n0:n1]),
                                                      start=(k == 0), stop=(k == 15)),
                             r=hkeys + [('wb', bi % 2, k // 8)], w=[pk])
                    fr = fm_row(c0) + m * 128
                    dst = ofm[fr:fr + 128, n0:n1]
                ob = obuf[oi % 4]
                ok = ('ob', oi % 4)
                if oi % 2 == 0:
                    S.op('act', lambda e: e.activation(out=ob[:, :n], in_=ps[:, :n], func=AF.Copy), r=[pk], w=[ok])
                else:
                    S.op('dve', lambda e: e.tensor_copy(out=ob[:, :n], in_=ps[:, :n]), r=[pk], w=[ok])
                S.dma('sp', dst, ob[:, :n], r=[ok], is_out=True)
                oi += 1
        S.finish()
    return nc


def core_rows(b, h):
    return (0, 1152) if h == 0 else (1152, 2304)


def mod_rows(modT, b, which):
    sl = modT[:, which * 16:(which + 1) * 16, :]
    return sl[:, :, b], sl[:, :, 4]


def tile_is_ctx(h, t):
    return h == 0 and t < 2


def make_msel(modT, b, h, sc_i, sh_i, NT=9):
    scx, scc = mod_rows(modT, b, sc_i)
    shx, shc = mod_rows(modT, b, sh_i)
    ms = np.zeros((128, NT, 2, 16), np.float32)
    for t in range(NT):
        if tile_is_ctx(h, t):
            ms[:, t, 0, :] = scc
            ms[:, t, 1, :] = shc
        else:
            ms[:, t, 0, :] = scx
            ms[:, t, 1, :] = shx
    return ms.reshape(128, -1)


def run_B(xseq, modT, inputs, layer):
    gn = np.ascontiguousarray(inputs['norm1_gain'][layer].reshape(16, 128).T)
    w = inputs['w_in'][layer]
    in_maps = []
    for c in range(NCORES):
        b, h = c // 2, c % 2
        r0, r1 = core_rows(b, h)
        in_maps.append({"x": np.ascontiguousarray(xseq[b, r0:r1]), "msel": make_msel(modT, b, h, 1, 0),
                        "gn": gn, "w": w})
    res = run_bass_kernel_spmd(build_B(), in_maps, core_ids=list(range(NCORES)))
    px_tm = np.zeros((NB, LT, 1840), np.float32)
    px_fm = np.zeros((NB, 9216, LT), np.float32)
    for c in range(NCORES):
        b, h = c // 2, c % 2
        r0, r1 = core_rows(b, h)
        px_tm[b, r0:r1] = res.results[c]["otm"]
        px_fm[b, :, r0:r1] = res.results[c]["ofm"]
    return px_tm, px_fm


def bc(ap, axis, shape):
    return ap.unsqueeze(axis).to_broadcast(list(shape))


def make_masks(S, transposed=False):
    sfx = 'T' if transposed else ''
    cm, pm = (1, -1) if transposed else (-1, 1)
    U8 = S.sb('U8' + sfx, [128, 8, 128])
    ones = S.sb('ones' + sfx, [128, 128])
    nm8 = S.sb('nm8' + sfx, [128, 8, 128])
    S.op('pool', lambda e: e.memset(ones[:], 1.0), w=['ones' + sfx])
    S.op('pool', lambda e: e.memset(U8[:], 1.0), w=['U8' + sfx])
    S.op('pool', lambda e: e.affine_select(out=U8[:], in_=U8[:], pattern=[[0, 8], [pm, 128]], compare_op=ALU.is_ge,
                                           fill=0.0, base=0, channel_multiplier=cm), r=['U8' + sfx], w=['U8' + sfx])
    S.op('pool', lambda e: e.memset(nm8[:], 0.0), w=['nm8' + sfx])
    S.op('pool', lambda e: e.affine_select(out=nm8[:], in_=nm8[:], pattern=[[0, 8], [pm, 128]], compare_op=ALU.is_ge,
                                           fill=-30000.0, base=0, channel_multiplier=cm), r=['nm8' + sfx], w=['nm8' + sfx])
    return U8, ones, nm8


NTL = LT // 128


def build_C1():
    nc = new_nc()
    xbc = din(nc, "xbc", [2048, LT])
    dtin = din(nc, "dt", [LT, 16])
    cw = din(nc, "cw", [128, 16 * 5])
    cb = din(nc, "cb", [128, 16])
    abd = din(nc, "abd", [128, 96])
    yo = [dout(nc, "y0", [LT, 1024]), dout(nc, "y1", [LT, 1024])]
    with ExitStack() as ctx:
        S = get_sched(nc, ctx)
        ident = make_ident(S)
        U8, ones, nm8 = make_masks(S)
        U8T, _, nm8T = make_masks(S, transposed=True)
        cwt = S.sb('cwt', [128, 16, 5])
        cbt = S.sb('cbt', [128, 16])
        abt = S.sb('abt', [128, 3, 2, 16])
        S.dma('sp', cwt[:].rearrange("p c j -> p (c j)"), cw[:, :], w=['cwt'])
        S.dma('sp', cbt[:], cb[:, :], w=['cbt'])
        S.dma('sp', abt[:].rearrange("p a d h -> p (a d h)"), abd[:, :], w=['abt'])
        dt_all = S.sb('dt_all', [128, NTL, 16])
        dtv = S.sb('dtv', [128, 2, NTL, 16])
        a_all = S.sb('a_all', [128, 2, NTL, 16])
        aneg = S.sb('aneg', [128, 2, 16])
        S.dma('sp', dt_all[:], dtin.rearrange("(t p) h -> p t h", p=128), w=['dt_all'])
        S.op('act', lambda e: e.activation(out=aneg[:], in_=abt[:, 0, :, :], func=AF.Exp), r=['abt'], w=['aneg'])
        S.op('dve', lambda e: e.tensor_scalar(out=aneg[:], in0=aneg[:], scalar1=-1.0, scalar2=None, op0=ALU.mult),
             r=['aneg'], w=['aneg'])
        for d in range(2):
            S.op('dve', lambda e: e.tensor_tensor(out=dtv[:, d], in0=dt_all[:], in1=bc(abt[:, 1, d, :], 1, [128, NTL, 16]),
                                                  op=ALU.add), r=['dt_all', 'abt'], w=['dtv'])
            S.op('act', lambda e: e.activation(out=dtv[:, d], in_=dtv[:, d], func=AF.Exp), r=['dtv'], w=['dtv'])
            S.op('act', lambda e: e.activation(out=dtv[:, d], in_=dtv[:, d], func=AF.Ln, bias=1.0, scale=1.0), r=['dtv'], w=['dtv'])
            S.op('dve', lambda e: e.tensor_tensor(out=a_all[:, d], in0=dtv[:, d], in1=bc(aneg[:, d, :], 1, [128, NTL, 16]), op=ALU.mult),
                 r=['dtv', 'aneg'], w=['a_all'])

        pb = [S.sb(f'pb{i}', [128, LT + 8]) for i in range(2)]
        for i in range(2):
            S.op('pool', lambda e: e.memset(pb[i][:], 0.0), w=[('pb', i)])
        acc = S.sb('acc', [128, LT])
        tmpx = S.sb('tmpx', [128, LT])
        BT = S.sb('BT', [128, 2, LT])
        CT = S.sb('CT', [128, 2, LT])
        x_tm = S.sb('x_tm', [128, NTL, 512])
        B_tm = S.sb('B_tm', [128, NTL, 256])
        y_acc = S.sb('y_acc', [128, NTL, 512])
        hst = S.sb('hst', [128, 512])
        sm = S.sb('sm', [128, 64])
        aU = S.sb('aU', [128, 8, 128])
        tmp = S.sb('tmp', [128, 8, 128])
        MT = S.sb('MT', [128, 8, 128])
        xdt = S.sb('xdt', [128, 8, 64])
        xw = S.sb('xw', [128, 8, 64])
        ydsb = S.sb('ydsb', [128, 8, 64])
        segs = [(0, 0, 256), (260, 256, 2048)]
        ci_glob = 0
        for hf in range(2):
            chunks = ([('x', i, 4 * hf + i) for i in range(4)] + [('B', i, 8 + 2 * hf + i) for i in range(2)]
                      + [('C', i, 12 + 2 * hf + i) for i in range(2)])
            for kind, i, ch in chunks:
                p = pb[ci_glob % 2]
                pk = ('pb', ci_glob % 2)
                ci_glob += 1
                S.dma('sp', p[:, 2:258], xbc[ch * 128:(ch + 1) * 128, 0:256], w=[pk])
                S.dma('sp', p[:, 262:2310], xbc[ch * 128:(ch + 1) * 128, 256:LT], w=[pk])
                for (oi, oo, n) in segs:
                    S.op('dve', lambda e: e.tensor_scalar(out=acc[:, oo:oo + n], in0=p[:, oi:oi + n],
                                                          scalar1=cwt[:, ch, 0:1], scalar2=None, op0=ALU.mult),
                         r=[pk, 'cwt'], w=['acc'])
                    for j in range(1, 5):
                        S.op('dve', lambda e: e.scalar_tensor_tensor(out=acc[:, oo:oo + n], in0=p[:, oi + j:oi + j + n],
                                                                     scalar=cwt[:, ch, j:j + 1], in1=acc[:, oo:oo + n],
                                                                     op0=ALU.mult, op1=ALU.add),
                             r=[pk, 'cwt', 'acc'], w=['acc'])
                if kind == 'x':
                    dst, dk = tmpx[:], 'tmpx'
                elif kind == 'B':
                    dst, dk = BT[:, i, :], ('BT', i)
                else:
                    dst, dk = CT[:, i, :], ('CT', i)
                S.op('act', lambda e: e.activation(out=dst, in_=acc[:], func=AF.Silu, bias=cbt[:, ch:ch + 1], scale=1.0),
                     r=['acc', 'cbt'], w=[dk])
                if kind in ('x', 'B'):
                    for t0 in range(0, NTL, 4):
                        nt = min(4, NTL - t0)
                        ps, pk2 = S.next_ps()
                        for tt in range(nt):
                            t = t0 + tt
                            S.op('pe', lambda e: e.transpose(out=ps[:, tt * 128:(tt + 1) * 128],
                                                             in_=dst[:, t * 128:(t + 1) * 128], identity=ident[:]),
                                 r=[dk, 'ident'], w=[pk2])
                        if kind == 'x':
                            o_ap = x_tm[:, t0:t0 + nt, i * 128:(i + 1) * 128]
                            ok = 'x_tm'
                        else:
                            o_ap = B_tm[:, t0:t0 + nt, i * 128:(i + 1) * 128]
                            ok = 'B_tm'
                        S.op('act', lambda e: e.activation(out=o_ap, in_=ps[:, 0:nt * 128].rearrange("p (t c) -> p t c", c=128),
                                                           func=AF.Copy), r=[pk2], w=[ok])
            h0 = hf * 8
            for d in range(2):
              Um, nmm = (U8, nm8) if d == 0 else (U8T, nm8T)
              U = Um[:, 0, :]
              order = list(range(NTL)) if d == 0 else [1, 0] + list(range(NTL - 1, 1, -1))
              if True:
                  S.op('dve', lambda e: e.tensor_tensor(
                      out=y_acc[:].rearrange("p t (h d) -> p t h d", d=64), in0=x_tm[:].rearrange("p t (h d) -> p t h d", d=64),
                      in1=abt[:, 2, d, h0:h0 + 8].unsqueeze(1).unsqueeze(3).to_broadcast([128, NTL, 8, 64]), op=ALU.mult),
                      r=['x_tm', 'abt'], w=['y_acc'])
                  S.op('dve', lambda e: e.memset(hst[:], 0.0), w=['hst'])
                  for t in order:
                      tsl = slice(t * 128, (t + 1) * 128)
                      a_t = a_all[:, d, t, h0:h0 + 8]
                      dtv_t = dtv[:, d, t, h0:h0 + 8]
                      psA, kA = S.next_ps()
                      S.op('pe', lambda e: e.matmul(psA[:, 0:8], lhsT=U, rhs=a_t, start=True, stop=True),
                           r=['U8', 'U8T', 'a_all'], w=[kA])
                      S.op('pe', lambda e: e.matmul(psA[:, 8:16], lhsT=ones[:], rhs=a_t, start=True, stop=True),
                           r=['ones', 'a_all'], w=[kA])
                      S.op('dve', lambda e: e.tensor_copy(out=sm[:, 0:16], in_=psA[:, 0:16]), r=[kA], w=['sm'])
                      S.op('dve', lambda e: e.tensor_tensor(out=sm[:, 24:32], in0=sm[:, 8:16], in1=sm[:, 0:8], op=ALU.subtract),
                           r=['sm'], w=['sm'])
                      S.op('act', lambda e: e.activation(out=sm[:, 16:24], in_=sm[:, 0:8], func=AF.Exp), r=['sm'], w=['sm'])
                      S.op('act', lambda e: e.activation(out=sm[:, 24:32], in_=sm[:, 24:32], func=AF.Exp), r=['sm'], w=['sm'])
                      S.op('act', lambda e: e.activation(out=sm[:, 32:40], in_=sm[:, 8:16], func=AF.Exp), r=['sm'], w=['sm'])
                      S.op('dve', lambda e: e.tensor_tensor(out=sm[:, 24:32], in0=sm[:, 24:32], in1=dtv_t, op=ALU.mult),
                           r=['sm', 'dtv'], w=['sm'])
                      S.op('dve', lambda e: e.tensor_tensor(out=aU[:], in0=Um[:], in1=bc(a_t, 2, [128, 8, 128]), op=ALU.mult),
                           r=['U8', 'U8T', 'a_all'], w=['aU'])
                      psB = []
                      for q in range(2):
                          pq, kq = S.next_ps()
                          S.op('pe', lambda e: e.matmul(pq[:, :], lhsT=ones[:], rhs=aU[:, 4 * q:4 * q + 4, :].rearrange("p h l -> p (h l)"),
                                                        start=True, stop=True), r=['ones', 'aU'], w=[kq])
                          psB.append((pq, kq))
                      for q in range(2):
                          pq, kq = psB[q]
                          S.op('dve', lambda e: e.tensor_tensor(out=tmp[:, 4 * q:4 * q + 4, :],
                                                                in0=pq[:, :].rearrange("p (h l) -> p h l", l=128),
                                                                in1=bc(sm[:, 4 * q:4 * q + 4], 2, [128, 4, 128]), op=ALU.subtract),
                               r=[kq, 'sm'], w=['tmp'])
                      S.op('dve', lambda e: e.tensor_tensor(out=tmp[:], in0=tmp[:], in1=nmm[:], op=ALU.add),
                           r=['tmp', 'nm8', 'nm8T'], w=['tmp'])
                      S.op('act', lambda e: e.activation(out=tmp[:], in_=tmp[:], func=AF.Exp), r=['tmp'], w=['tmp'])
                      psS, kS = S.next_ps()
                      for gi in range(2):
                          S.op('pe', lambda e: e.matmul(psS[:, gi * 128:(gi + 1) * 128], lhsT=BT[:, gi, tsl], rhs=CT[:, gi, tsl],
                                                        start=True, stop=True), r=[('BT', gi), ('CT', gi)], w=[kS])
                      for gi in range(2):
                          S.op('dve', lambda e: e.tensor_tensor(out=MT[:, 4 * gi:4 * gi + 4, :], in0=tmp[:, 4 * gi:4 * gi + 4, :],
                                                                in1=bc(psS[:, gi * 128:(gi + 1) * 128], 1, [128, 4, 128]),
                                                                op=ALU.mult), r=['tmp', kS], w=['MT'])
                      xt3 = x_tm[:, t, :].rearrange("p (h d) -> p h d", d=64)
                      S.op('dve', lambda e: e.tensor_tensor(out=xdt[:], in0=xt3, in1=bc(dtv_t, 2, [128, 8, 64]), op=ALU.mult),
                           r=['x_tm', 'dtv'], w=['xdt'])
                      S.op('dve', lambda e: e.tensor_tensor(out=xw[:], in0=xt3, in1=bc(sm[:, 24:32], 2, [128, 8, 64]), op=ALU.mult),
                           r=['x_tm', 'sm'], w=['xw'])
                      psY, kY = S.next_ps()
                      for hh in range(8):
                          S.op('pe', lambda e: e.matmul(psY[:, hh * 64:(hh + 1) * 64], lhsT=MT[:, hh, :], rhs=xdt[:, hh, :],
                                                        start=True, stop=True), r=['MT', 'xdt'], w=[kY])
                      psO, kO = S.next_ps()
                      for gi in range(2):
                          S.op('pe', lambda e: e.matmul(psO[:, gi * 256:(gi + 1) * 256], lhsT=CT[:, gi, tsl],
                                                        rhs=hst[:, gi * 256:(gi + 1) * 256], start=True, stop=True),
                               r=[('CT', gi), 'hst'], w=[kO])
                      S.op('act', lambda e: e.activation(out=ydsb[:].rearrange("p h d -> p (h d)"), in_=psY[:, :], func=AF.Copy),
                           r=[kY], w=['ydsb'])
                      S.op('dve', lambda e: e.tensor_tensor(out=xdt[:], in0=psO[:, :].rearrange("p (h d) -> p h d", d=64),
                                                            in1=bc(sm[:, 16:24], 2, [128, 8, 64]), op=ALU.mult),
                           r=[kO, 'sm'], w=['xdt'])
                      S.op('pool', lambda e: e.tensor_tensor(out=ydsb[:], in0=ydsb[:], in1=xdt[:], op=ALU.add),
                           r=['ydsb', 'xdt'], w=['ydsb'])
                      S.op('pool', lambda e: e.tensor_tensor(out=y_acc[:, t, :], in0=y_acc[:, t, :],
                                                             in1=ydsb[:].rearrange("p h d -> p (h d)"), op=ALU.add),
                           r=['ydsb', 'y_acc'], w=['y_acc'])
                      psH, kH = S.next_ps()
                      for gi in range(2):
                          S.op('pe', lambda e: e.matmul(psH[:, gi * 256:(gi + 1) * 256], lhsT=B_tm[:, t, gi * 128:(gi + 1) * 128],
                                                        rhs=xw[:, 4 * gi:4 * gi + 4, :].rearrange("p h d -> p (h d)"),
                                                        start=True, stop=True), r=['B_tm', 'xw'], w=[kH])
                      S.op('dve', lambda e: e.tensor_tensor(out=hst[:].rearrange("p (h d) -> p h d", d=64),
                                                            in0=hst[:].rearrange("p (h d) -> p h d", d=64),
                                                            in1=bc(sm[:, 32:40], 2, [128, 8, 64]), op=ALU.mult),
                           r=['hst', 'sm'], w=['hst'])
                      S.op('dve', lambda e: e.tensor_tensor(out=hst[:], in0=hst[:], in1=psH[:, :], op=ALU.add),
                           r=['hst', kH], w=['hst'])
                  S.dma('pool', yo[d][:, hf * 512:(hf + 1) * 512].rearrange("(t p) c -> p t c", p=128), y_acc[:], r=['y_acc'],
                        is_out=True)
        S.finish()
    return nc


def flip_seq(a, axis):
    sl_c = [slice(None)] * a.ndim
    sl_x = [slice(None)] * a.ndim
    sl_c[axis] = slice(0, CTX)
    sl_x[axis] = slice(CTX, LT)
    return np.concatenate([np.flip(a[tuple(sl_c)], axis), np.flip(a[tuple(sl_x)], axis)], axis=axis)


def run_C1(px_tm, px_fm, inputs, layer):
    cwl = inputs['ssd_conv_w'][layer]
    cbl = inputs['ssd_conv_b'][layer]
    in_maps = []
    for c in range(NCORES):
        b, d = c // 2, c % 2
        xbc = px_fm[b, 0:2048, :]
        dt = px_tm[b, :, 1024:1040]
        cw_ = cwl
        if d == 1:
            xbc = flip_seq(xbc, 1)
            dt = flip_seq(dt, 0)
            cw_ = cwl[::-1]
        abd = np.stack([inputs['ssd_a_log'][layer, d], inputs['ssd_dt_bias'][layer, d], inputs['ssd_d'][layer, d]], 0)
        in_maps.append({"xbc": np.ascontiguousarray(xbc), "dt": np.ascontiguousarray(dt),
                        "cw": np.ascontiguousarray(cw_.T.reshape(16, 128, 5).transpose(1, 0, 2)).reshape(128, 80),
                        "cb": np.ascontiguousarray(cbl.reshape(16, 128).T),
                        "abd": np.ascontiguousarray(np.broadcast_to(abd.reshape(1, 48), (128, 48)))})
    res = run_bass_kernel_spmd(build_C1(), in_maps, core_ids=list(range(NCORES)))
    out = np.zeros((NB, 2, LT, 1024), np.float32)
    for c in range(NCORES):
        b, d = c // 2, c % 2
        yv = res.results[c]["y"]
        out[b, d] = flip_seq(yv, 0) if d == 1 else yv
    return out


def rstd_cols(S, ss, cols, inv_ns, key):
    c0 = cols[0]
    for (a, b), inv in zip(cols[1], inv_ns):
        S.op('dve', lambda e: e.tensor_scalar(out=ss[:, a:b], in0=ss[:, a:b], scalar1=inv, scalar2=EPS, op0=ALU.mult,
                                              op1=ALU.add), r=[key], w=[key])
    a, b = c0
    S.op('act', lambda e: e.activation(out=ss[:, a:b], in_=ss[:, a:b], func=AF.Sqrt), r=[key], w=[key])
    S.op('dve', lambda e: e.reciprocal(out=ss[:, a:b], in_=ss[:, a:b]), r=[key], w=[key])


def rope_ops(S, src3, dst3, cos, sin, nh, tmp, rkeys, wkey, tkey):
    s4 = src3.rearrange("p h (i two) -> p h i two", two=2)
    d4 = dst3.rearrange("p h (i two) -> p h i two", two=2)
    x1, x2 = s4[:, :, :, 0], s4[:, :, :, 1]
    cb_, sb_ = bc(cos, 1, [128, nh, 16]), bc(sin, 1, [128, nh, 16])
    for i, (xa, tb) in enumerate([(x1, cb_), (x2, sb_), (x1, sb_), (x2, cb_)]):
        S.op('dve', lambda e: e.tensor_tensor(out=tmp[:, i, :, :], in0=xa, in1=tb, op=ALU.mult), r=rkeys, w=[tkey])
    S.op('dve', lambda e: e.tensor_tensor(out=d4[:, :, :, 0], in0=tmp[:, 0, :, :], in1=tmp[:, 1, :, :], op=ALU.subtract),
         r=[tkey], w=[wkey])
    S.op('dve', lambda e: e.tensor_tensor(out=d4[:, :, :, 1], in0=tmp[:, 2, :, :], in1=tmp[:, 3, :, :], op=ALU.add),
         r=[tkey], w=[wkey])


def build_C2():
    nc = new_nc()
    mla = din(nc, "mla", [LT, 800])
    gqa = din(nc, "gqa", [128, 4])
    gkv = din(nc, "gkv", [128, 2])
    wq = din(nc, "wq", [512, 768])
    wkv = din(nc, "wkv", [256, 1024])
    qg = din(nc, "qg", [128, 96])
    kg = din(nc, "kg", [128, 96])
    cs = din(nc, "cs", [LT, 32])
    oT = dout(nc, "oT", [512, LT])
    SCALE = 96.0 ** -0.5
    with ExitStack() as ctx:
        S = get_sched(nc, ctx)
        S.nrot = 4
        ident = make_ident(S)
        ones = S.sb('ones', [128, 64])
        ones_f = S.sb('ones_f', [128, 64])
        S.op('pool', lambda e: e.memset(ones_f[:], 1.0), w=['ones_f'])
        S.op('dve', lambda e: e.tensor_copy(out=R(ones[:]), in_=ones_f[:]), r=['ones_f'], w=['ones'])
        gqat = S.sb('gqat', [128, 4]); gkvt = S.sb('gkvt', [128, 2])
        wqt = S.sb('wqt', [128, 4, 768]); wkvt = S.sb('wkvt', [128, 2, 1024])
        qgt = S.sb('qgt', [128, 96]); kgt = S.sb('kgt', [128, 96])
        cst = S.sb('cst', [128, NTL, 32])
        S.dma('sp', gqat[:], gqa[:, :], w=['gqat'])
        S.dma('sp', gkvt[:], gkv[:, :], w=['gkvt'])
        S.dma('pool', R(wqt[:]), wq.rearrange("(k p) n -> p k n", p=128), w=['wqt'])
        S.dma('pool', R(wkvt[:]), wkv.rearrange("(k p) n -> p k n", p=128), w=['wkvt'])
        S.dma('sp', qgt[:], qg[:, :], w=['qgt'])
        S.dma('sp', kgt[:], kg[:, :], w=['kgt'])
        S.dma('sp', cst[:], cs.rearrange("(t p) c -> p t c", p=128), w=['cst'])
        qaT = S.sb('qaT', [128, 4, LT])
        kvT = S.sb('kvT', [128, 2, LT])
        kpr = S.sb('kpr', [128, NTL, 32])
        mtb = [S.sb(f'mt{i}', [128, 800]) for i in range(2)]
        junk = [S.sb(f'junk{i}', [128, 512]) for i in range(2)]
        ss = [S.sb(f'ss{i}', [128, 16]) for i in range(2)]
        kpn = [S.sb(f'kpn{i}', [128, 1, 32]) for i in range(2)]
        rtmp = [S.sb(f'rtmp{i}', [128, 4, 4, 16]) for i in range(2)]
        for t in range(NTL):
            p = t % 2
            S.rec_lane(p)
            mt = mtb[t % 2]
            mk = ('mt', t % 2)
            S.dma('sp', mt[:], mla[t * 128:(t + 1) * 128, :], w=[mk])
            S.op('dve', lambda e: e.memset(ss[p][:, 0:3], 0.0), w=[('ss', p)])
            for i, (a, b) in enumerate([(0, 512), (512, 768), (768, 800)]):
                S.op('act', lambda e: e.activation(out=junk[p][:, 0:b - a], in_=mt[:, a:b], func=AF.Square,
                                                   accum_out=ss[p][:, i:i + 1]), r=[mk], w=[('junk', p), ('ss', p)])
            rstd_cols(S, ss[p], ((0, 3), [(0, 1), (1, 2), (2, 3)]), [1.0 / 512, 1.0 / 256, 1.0 / 32], ('ss', p))
            S.op('dve', lambda e: e.tensor_scalar(out=mt[:, 0:512], in0=mt[:, 0:512], scalar1=ss[p][:, 0:1], scalar2=None,
                                                  op0=ALU.mult), r=[mk, ('ss', p)], w=[mk])
            S.op('dve', lambda e: e.tensor_scalar(out=mt[:, 512:768], in0=mt[:, 512:768], scalar1=ss[p][:, 1:2], scalar2=None,
                                                  op0=ALU.mult), r=[mk, ('ss', p)], w=[mk])
            S.op('dve', lambda e: e.scalar_tensor_tensor(out=kpn[p][:, 0, :], in0=mt[:, 768:800], scalar=ss[p][:, 2:3],
                                                         in1=kgt[:, 64:96], op0=ALU.mult, op1=ALU.mult),
                 r=[mk, ('ss', p), 'kgt'], w=[('kpn', p)])
            rope_ops(S, kpn[p][:], kpr[:, t:t + 1, :], cst[:, t, 0:16], cst[:, t, 16:32], 1, rtmp[p][:, :, 0:1, :],
                     [('kpn', p), 'cst'], 'kpr', ('rtmp', p))
            ps, pk = S.next_ps()
            for k in range(4):
                S.op('pe', lambda e: e.transpose(out=ps[:, k * 128:(k + 1) * 128], in_=mt[:, k * 128:(k + 1) * 128],
                                                 identity=ident[:]), r=[mk, 'ident'], w=[pk])
            for k in range(4):
                S.op('act', lambda e: e.activation(out=R(qaT[:, k, t * 128:(t + 1) * 128]), in_=ps[:, k * 128:(k + 1) * 128],
                                                   func=AF.Identity, scale=gqat[:, k:k + 1]), r=[pk, 'gqat'], w=['qaT'])
            ps, pk = S.next_ps()
            for k in range(2):
                S.op('pe', lambda e: e.transpose(out=ps[:, k * 128:(k + 1) * 128], in_=mt[:, 512 + k * 128:640 + k * 128],
                                                 identity=ident[:]), r=[mk, 'ident'], w=[pk])
            for k in range(2):
                S.op('act', lambda e: e.activation(out=R(kvT[:, k, t * 128:(t + 1) * 128]), in_=ps[:, k * 128:(k + 1) * 128],
                                                   func=AF.Identity, scale=gkvt[:, k:k + 1]), r=[pk, 'gkvt'], w=['kvT'])
            if p == 1 or t == NTL - 1:
                S.rec_flush()
        kT = S.sb('kT', [128, 4, LT])
        vv = S.sb('vv', [128, NTL, 4, 64])
        kfull = [S.sb(f'kfull{i}', [128, 4, 96]) for i in range(2)]
        sq = [S.sb(f'sq{i}', [128, 4, 96]) for i in range(2)]
        t1 = [S.sb(f't1{i}', [128, 4, 64]) for i in range(2)]
        qn = [S.sb(f'qn{i}', [128, 4, 96]) for i in range(2)]
        qp = [S.sb(f'qp{i}', [128, 4, 32]) for i in range(2)]
        qT = S.sb('qT', [128, 4, 512])
        ptb = [S.sb(f'pt{i}', [128, 512]) for i in range(3)]
        rden = S.sb('rden', [64, 512])
        osb = [S.sb(f'osb{i}', [64, 512]) for i in range(2)]
        groups = [[0, 1]] + [list(range(2 + 4 * g, 6 + 4 * g)) for g in range(4)]
        pti = 0
        oi = 0
        hcount = 0
        for hp in range(2):
            for t in range(NTL):
                p = t % 2
                S.rec_lane(p)
                tsl = slice(t * 128, (t + 1) * 128)
                psK, kK = S.next_ps()
                for k in range(2):
                    S.op('pe', lambda e: e.matmul(psK[:, :], lhsT=R(kvT[:, k, tsl]), rhs=R(wkvt[:, k, hp * 512:(hp + 1) * 512]),
                                                  start=(k == 0), stop=(k == 1)), r=['kvT', 'wkvt'], w=[kK])
                pk3 = psK[:, :].rearrange("p (h c) -> p h c", c=128)
                S.op('act', lambda e: e.activation(out=sq[p][:, :, 0:64], in_=pk3[:, :, 0:64], func=AF.Square), r=[kK], w=[('sq', p)])
                S.op('dve', lambda e: e.tensor_reduce(out=ss[p][:, 4:8], in_=sq[p][:, :, 0:64], axis=AX.X, op=ALU.add),
                     r=[('sq', p)], w=[('ss', p)])
                rstd_cols(S, ss[p], ((4, 8), [(4, 8)]), [1.0 / 64], ('ss', p))
                S.op('dve', lambda e: e.tensor_tensor(out=t1[p][:], in0=pk3[:, :, 0:64], in1=bc(ss[p][:, 4:8], 2, [128, 4, 64]),
                                                      op=ALU.mult), r=[kK, ('ss', p)], w=[('t1', p)])
                S.op('dve', lambda e: e.tensor_tensor(out=kfull[p][:, :, 0:64], in0=t1[p][:], in1=bc(kgt[:, 0:64], 1, [128, 4, 64]),
                                                      op=ALU.mult), r=[('t1', p), 'kgt'], w=[('kfull', p)])
                S.op('dve', lambda e: e.tensor_copy(out=kfull[p][:, :, 64:96], in_=bc(kpr[:, t, :], 1, [128, 4, 32])),
                     r=['kpr'], w=[('kfull', p)])
                S.op('act', lambda e: e.activation(out=R(vv[:, t, :, :]), in_=pk3[:, :, 64:128], func=AF.Copy), r=[kK], w=['vv'])
                psT, kTk = S.next_ps()
                for hd in range(4):
                    S.op('pe', lambda e: e.transpose(out=psT[0:96, hd * 128:(hd + 1) * 128], in_=kfull[p][:, hd, :],
                                                     identity=ident[:]), r=[('kfull', p), 'ident'], w=[kTk])
                S.op('act', lambda e: e.activation(out=R(kT[0:96, :, tsl]), in_=psT[0:96, :].rearrange("p (h c) -> p h c", c=128),
                                                   func=AF.Copy), r=[kTk], w=['kT'])
                if p == 1 or t == NTL - 1:
                    S.rec_flush()
            for gi, tiles in enumerate(groups):
                nq = len(tiles) * 128
                key_tiles = [0, 1] if gi == 0 else list(range(NTL))
                for li, t in enumerate(tiles):
                    p = li % 2
                    S.rec_lane(p)
                    tsl = slice(t * 128, (t + 1) * 128)
                    psQ, kQ = S.next_ps()
                    for k in range(4):
                        S.op('pe', lambda e: e.matmul(psQ[:, 0:384], lhsT=R(qaT[:, k, tsl]), rhs=R(wqt[:, k, hp * 384:(hp + 1) * 384]),
                                                      start=(k == 0), stop=(k == 3)), r=['qaT', 'wqt'], w=[kQ])
                    pq3 = psQ[:, 0:384].rearrange("p (h c) -> p h c", c=96)
                    S.op('act', lambda e: e.activation(out=sq[p][:], in_=pq3, func=AF.Square), r=[kQ], w=[('sq', p)])
                    S.op('dve', lambda e: e.tensor_reduce(out=ss[p][:, 8:12], in_=sq[p][:, :, 0:64], axis=AX.X, op=ALU.add),
                         r=[('sq', p)], w=[('ss', p)])
                    S.op('dve', lambda e: e.tensor_reduce(out=ss[p][:, 12:16], in_=sq[p][:, :, 64:96], axis=AX.X, op=ALU.add),
                         r=[('sq', p)], w=[('ss', p)])
                    rstd_cols(S, ss[p], ((8, 16), [(8, 12), (12, 16)]), [1.0 / 64, 1.0 / 32], ('ss', p))
                    S.op('dve', lambda e: e.tensor_tensor(out=t1[p][:], in0=pq3[:, :, 0:64], in1=bc(ss[p][:, 8:12], 2, [128, 4, 64]),
                                                          op=ALU.mult), r=[kQ, ('ss', p)], w=[('t1', p)])
                    S.op('dve', lambda e: e.tensor_tensor(out=qn[p][:, :, 0:64], in0=t1[p][:], in1=bc(qgt[:, 0:64], 1, [128, 4, 64]),
                                                          op=ALU.mult), r=[('t1', p), 'qgt'], w=[('qn', p)])
                    S.op('dve', lambda e: e.tensor_tensor(out=qp[p][:], in0=pq3[:, :, 64:96], in1=bc(ss[p][:, 12:16], 2, [128, 4, 32]),
                                                          op=ALU.mult), r=[kQ, ('ss', p)], w=[('qp', p)])
                    S.op('dve', lambda e: e.tensor_tensor(out=qp[p][:], in0=qp[p][:], in1=bc(qgt[:, 64:96], 1, [128, 4, 32]),
                                                          op=ALU.mult), r=[('qp', p), 'qgt'], w=[('qp', p)])
                    rope_ops(S, qp[p][:], qn[p][:, :, 64:96], cst[:, t, 0:16], cst[:, t, 16:32], 4, rtmp[p][:], [('qp', p), 'cst'], ('qn', p), ('rtmp', p))
                    psT, kTk = S.next_ps()
                    for hd in range(4):
                        S.op('pe', lambda e: e.transpose(out=psT[0:96, hd * 128:(hd + 1) * 128], in_=qn[p][:, hd, :],
                                                         identity=ident[:]), r=[('qn', p), 'ident'], w=[kTk])
                    S.op('act', lambda e: e.activation(out=R(qT[0:96, :, li * 128:(li + 1) * 128]),
                                                       in_=psT[0:96, :].rearrange("p (h c) -> p h c", c=128), func=AF.Copy),
                         r=[kTk], w=['qT'])
                    if p == 1 or li == len(tiles) - 1:
                        S.rec_flush()
                for hd in range(4):
                    ao, ad = (4, 5) if hcount % 2 == 0 else (6, 7)
                    hcount += 1
                    pso, psd = S.ps[ao], S.ps[ad]
                    ko, kd = ('ps', ao), ('ps', ad)
                    for ki, kc in enumerate(key_tiles):
                        ksl = slice(kc * 128, (kc + 1) * 128)
                        psS, kS = S.next_ps()
                        S.op('pe', lambda e: e.matmul(psS[:, 0:nq], lhsT=R(kT[0:96, hd, ksl]), rhs=R(qT[0:96, hd, 0:nq]),
                                                      start=True, stop=True), r=['kT', 'qT'], w=[kS])
                        pt = ptb[pti % 3]
                        ptk = ('pt', pti % 3)
                        pti += 1
                        S.op('act', lambda e: e.activation(out=R(pt[:, 0:nq]), in_=psS[:, 0:nq], func=AF.Exp, scale=SCALE),
                             r=[kS], w=[ptk])
                        S.op('pe', lambda e: e.matmul(pso[0:64, 0:nq], lhsT=R(vv[:, kc, hd, :]), rhs=R(pt[:, 0:nq]),
                                                      start=(ki == 0), stop=(ki == len(key_tiles) - 1)), r=['vv', ptk], w=[ko])
                        S.op('pe', lambda e: e.matmul(psd[0:64, 0:nq], lhsT=R(ones[:, :]), rhs=R(pt[:, 0:nq]),
                                                      start=(ki == 0), stop=(ki == len(key_tiles) - 1)), r=['ones', ptk], w=[kd])
                    S.op('dve', lambda e: e.reciprocal(out=rden[:, 0:nq], in_=psd[0:64, 0:nq]), r=[kd], w=['rden'])
                    ob = osb[oi % 2]
                    obk = ('osb', oi % 2)
                    oi += 1
                    S.op('dve', lambda e: e.tensor_tensor(out=ob[:, 0:nq], in0=pso[0:64, 0:nq], in1=rden[:, 0:nq], op=ALU.mult),
                         r=[ko, 'rden'], w=[obk])
                    hrow = (hp * 4 + hd) * 64
                    S.dma('sp', oT[hrow:hrow + 64, tiles[0] * 128:tiles[0] * 128 + nq], ob[:, 0:nq], r=[obk], is_out=True)
        S.finish()
    return nc


def rope_table():
    n_freq = 8
    inv = (10000.0 ** (-np.arange(n_freq, dtype=np.float32) / n_freq)).astype(np.float32)
    rows = np.repeat(np.arange(SEQ // 64, dtype=np.float32), 64)
    cols = np.tile(np.arange(64, dtype=np.float32), SEQ // 64)
    ang = np.concatenate([rows[:, None] * inv, cols[:, None] * inv], axis=-1).astype(np.float32)
    cs = np.zeros((LT, 32), np.float32)
    cs[:CTX, 0:16] = 1.0
    cs[CTX:, 0:16] = np.cos(ang)
    cs[CTX:, 16:32] = np.sin(ang)
    return cs


def rep128(v):
    return np.ascontiguousarray(np.broadcast_to(np.asarray(v, np.float32).reshape(1, -1), (128, v.size)))


def run_C2(px_tm, inputs, layer):
    cs = rope_table()
    in_maps = []
    for c in range(NCORES):
        b, hf = c // 2, c % 2
        in_maps.append({
            "mla": np.ascontiguousarray(px_tm[b, :, 1040:1840]),
            "gqa": np.ascontiguousarray(inputs['mla_q_a_gain'][layer].reshape(4, 128).T),
            "gkv": np.ascontiguousarray(inputs['mla_kv_a_gain'][layer].reshape(2, 128).T),
            "wq": np.ascontiguousarray(inputs['mla_w_q_b'][layer][:, hf * 768:(hf + 1) * 768]),
            "wkv": np.ascontiguousarray(inputs['mla_w_kv_b'][layer][:, hf * 1024:(hf + 1) * 1024]),
            "qg": rep128(inputs['mla_q_gain'][layer]), "kg": rep128(inputs['mla_k_gain'][layer]), "cs": cs})
    res = run_bass_kernel_spmd(build_C2(), in_maps, core_ids=list(range(NCORES)))
    out = np.zeros((NB, 1024, LT), np.float32)
    for c in range(NCORES):
        b, hf = c // 2, c % 2
        out[b, hf * 512:(hf + 1) * 512] = res.results[c]["oT"]
    return out


PI = float(np.pi)
S5_CH = [(0, 512), (512, 1024), (1024, 1536), (1536, 2048), (2048, 2304)]


def sin_reduced(S, dst, src, offset, sign, tmps, rkeys, wkey, pref):
    ki, kf, r, g = tmps
    kk = [pref + n for n in ('ki', 'kf', 'r', 'g')]
    S.op('dve', lambda e: e.tensor_scalar(out=r, in0=src, scalar1=offset + 8.0 * PI, scalar2=None, op0=ALU.add),
         r=rkeys, w=[kk[2]])
    S.op('dve', lambda e: e.tensor_scalar(out=ki, in0=r, scalar1=1.0 / (2.0 * PI), scalar2=None, op0=ALU.mult),
         r=[kk[2]], w=[kk[0]])
    S.op('dve', lambda e: e.tensor_copy(out=kf, in_=ki), r=[kk[0]], w=[kk[1]])
    S.op('dve', lambda e: e.scalar_tensor_tensor(out=r, in0=kf, scalar=-2.0 * PI, in1=r, op0=ALU.mult, op1=ALU.add),
         r=[kk[1], kk[2]], w=[kk[2]])
    S.op('dve', lambda e: e.tensor_scalar(out=g, in0=r, scalar1=PI, scalar2=-2.0 * PI, op0=ALU.is_gt, op1=ALU.mult),
         r=[kk[2]], w=[kk[3]])
    S.op('dve', lambda e: e.tensor_tensor(out=r, in0=r, in1=g, op=ALU.add), r=[kk[2], kk[3]], w=[kk[2]])
    S.op('dve', lambda e: e.tensor_scalar(out=r, in0=r, scalar1=-PI, scalar2=PI, op0=ALU.max, op1=ALU.min),
         r=[kk[2]], w=[kk[2]])
    S.op('act', lambda e: e.activation(out=dst, in_=r, func=AF.Sin, scale=float(sign)), r=[kk[2]], w=[wkey])


def build_C3():
    nc = new_nc()
    u = din(nc, "u", [1024, LT])
    lamp_ = din(nc, "lamp", [128, 2 * 96])
    lamr_ = din(nc, "lamr", [32, 2 * 3 * 4096])
    bT_ = din(nc, "bT", [32, 2 * 2 * 4096])
    cbd_ = din(nc, "cbd", [128, 2 * 2 * 1024])
    yTo = [dout(nc, "yT0", [1024, LT]), dout(nc, "yT1", [1024, LT])]
    with ExitStack() as ctx:
        S = get_sched(nc, ctx)
        lp = S.sb('lp', [128, 3, 32])
        cb_ = S.sb('cbd_sb', [128, 2, 32, 32])
        dtp = S.sb('dtp', [128, 32]); magp = S.sb('magp', [128, 32]); thp = S.sb('thp', [128, 32])
        BbT = S.sb('BbT', [32, 2, 32, 128])
        W = 512
        names = ['lr', 'li', 'ld', 'br', 'bi', 'mag', 'th', 'sn', 'cs', 'm', 'abr', 'abi', 'den', 'fr', 'fi', 'ta', 'tb']
        T = {n: S.sb('r_' + n, [32, W]) for n in names}
        T['ki'] = S.sb('r_ki', [32, W], I32)
        T['kf'] = S.sb('r_kf', [32, W])
        T['g'] = S.sb('r_g', [32, W])
        io_i = S.sb('io_i', [128, 512], I32)
        io_f = S.sb('io_f', [128, 512])
        S.op('pool', lambda e: e.iota(io_i[:], pattern=[[1, 512]], base=1, channel_multiplier=0), w=['io_i'])
        S.op('dve', lambda e: e.tensor_copy(out=io_f[:], in_=io_i[:]), r=['io_i'], w=['io_f'])
        P2 = range(2)
        Er = [S.sb(f'Er{p}', [128, 512]) for p in P2]; Ei = [S.sb(f'Ei{p}', [128, 512]) for p in P2]
        amag = [S.sb(f'amag{p}', [128, 512]) for p in P2]
        phi = [S.sb(f'phi{p}', [128, 512]) for p in P2]; mm = [S.sb(f'mm{p}', [128, 512]) for p in P2]
        pki = [S.sb(f'pki{p}', [128, 512], I32) for p in P2]; pkf = [S.sb(f'pkf{p}', [128, 512]) for p in P2]
        pg = [S.sb(f'pg{p}', [128, 512]) for p in P2]
        ub = [S.sb(f'ub{i}', [32, LT]) for i in range(2)]
        wre = [S.sb(f'wre{p}', [128, 512]) for p in P2]; wim = [S.sb(f'wim{p}', [128, 512]) for p in P2]
        ta = [S.sb(f'ta{p}', [128, 512]) for p in P2]; tb = [S.sb(f'tb{p}', [128, 512]) for p in P2]
        zre = [S.sb(f'zre{p}', [128, 512]) for p in P2]; zim = [S.sb(f'zim{p}', [128, 512]) for p in P2]
        xre = [[S.sb(f'xre{p}_{i}', [128, 512]) for i in range(2)] for p in P2]
        xim = [[S.sb(f'xim{p}_{i}', [128, 512]) for i in range(2)] for p in P2]
        tc_ = [S.sb(f'tc{p}', [128, 512]) for p in P2]; td = [S.sb(f'td{p}', [128, 512]) for p in P2]
        yb = [S.sb(f'yb{i}', [32, 512]) for i in range(4)]

        def tt(o, a_, b_, op, eng='dve'):
            S.op(eng, lambda e: e.tensor_tensor(out=T[o][:], in0=T[a_][:], in1=T[b_][:], op=op), r=[a_, b_], w=[o])

        yi = 0
        for d in range(2):
            S.dma('sp', lp[:].rearrange("p a j -> p (a j)"), lamp_[:, d * 96:(d + 1) * 96], w=['lp'])
            S.dma('sp', cb_[:].rearrange("p a j s -> p (a j s)"), cbd_[:, d * 2048:(d + 1) * 2048], w=['cbd'])
            S.op('dve', lambda e: e.tensor_scalar(out=cb_[:, 1], in0=cb_[:, 1], scalar1=-1.0, scalar2=None, op0=ALU.mult),
                 r=['cbd'], w=['cbd'])
            S.op('act', lambda e: e.activation(out=dtp[:], in_=lp[:, 2, :], func=AF.Exp), r=['lp'], w=['dtp'])
            S.op('dve', lambda e: e.tensor_tensor(out=magp[:], in0=lp[:, 0, :], in1=dtp[:], op=ALU.mult), r=['lp', 'dtp'], w=['magp'])
            S.op('act', lambda e: e.activation(out=magp[:], in_=magp[:], func=AF.Exp), r=['magp'], w=['magp'])
            S.op('dve', lambda e: e.tensor_tensor(out=thp[:], in0=lp[:, 1, :], in1=dtp[:], op=ALU.mult), r=['lp', 'dtp'], w=['thp'])
            for pc in range(8):
                for i, n in enumerate(['lr', 'li', 'ld']):
                    o0 = d * 3 * 4096 + i * 4096 + pc * W
                    S.dma('sp', T[n][:], lamr_[:, o0:o0 + W], w=[n])
                for i, n in enumerate(['br', 'bi']):
                    o0 = d * 2 * 4096 + i * 4096 + pc * W
                    S.dma('sp', T[n][:], bT_[:, o0:o0 + W], w=[n])
                S.op('act', lambda e: e.activation(out=T['ld'][:], in_=T['ld'][:], func=AF.Exp), r=['ld'], w=['ld'])
                tt('mag', 'lr', 'ld', ALU.mult)
                S.op('act', lambda e: e.activation(out=T['mag'][:], in_=T['mag'][:], func=AF.Exp), r=['mag'], w=['mag'])
                tt('th', 'li', 'ld', ALU.mult)
                rt = (T['ki'][:], T['kf'][:], T['m'][:], T['g'][:])
                sin_reduced(S, T['sn'][:], T['th'][:], 0.0, 1.0, rt, ['th'], 'sn', 'R')
                sin_reduced(S, T['cs'][:], T['th'][:], 0.5 * PI, 1.0, rt, ['th'], 'cs', 'R')
                tt('abr', 'mag', 'cs', ALU.mult)
                tt('abi', 'mag', 'sn', ALU.mult)
                S.op('dve', lambda e: e.tensor_scalar(out=T['abr'][:], in0=T['abr'][:], scalar1=-1.0, scalar2=None, op0=ALU.add),
                     r=['abr'], w=['abr'])
                tt('den', 'lr', 'lr', ALU.mult)
                tt('ta', 'li', 'li', ALU.mult)
                tt('den', 'den', 'ta', ALU.add)
                S.op('dve', lambda e: e.reciprocal(out=T['den'][:], in_=T['den'][:]), r=['den'], w=['den'])
                tt('fr', 'abr', 'lr', ALU.mult)
                tt('ta', 'abi', 'li', ALU.mult)
                tt('fr', 'fr', 'ta', ALU.add)
                tt('fr', 'fr', 'den', ALU.mult)
                tt('fi', 'abi', 'lr', ALU.mult)
                tt('ta', 'abr', 'li', ALU.mult)
                tt('fi', 'fi', 'ta', ALU.subtract)
                tt('fi', 'fi', 'den', ALU.mult)
                o_re = BbT[:, 0, 4 * pc:4 * pc + 4, :].rearrange("p j c -> p (j c)")
                o_im = BbT[:, 1, 4 * pc:4 * pc + 4, :].rearrange("p j c -> p (j c)")
                tt('ta', 'br', 'fr', ALU.mult)
                tt('tb', 'bi', 'fi', ALU.mult)
                S.op('dve', lambda e: e.tensor_tensor(out=o_re, in0=T['ta'][:], in1=T['tb'][:], op=ALU.subtract),
                     r=['ta', 'tb'], w=['BbT'])
                tt('ta', 'br', 'fi', ALU.mult)
                tt('tb', 'bi', 'fr', ALU.mult)
                S.op('dve', lambda e: e.tensor_tensor(out=o_im, in0=T['ta'][:], in1=T['tb'][:], op=ALU.add),
                     r=['ta', 'tb'], w=['BbT'])
            if d == 0:
                chunks = S5_CH
            else:
                chunks = [(0, 256), (1792, 2304), (1280, 1792), (768, 1280), (256, 768)]

            def V(ap, n):
                v = ap[:, 0:n]
                return v if d == 0 else v[:, ::-1]

            def tables(j, p):
                S.dma('sp', ub[p][:], u[32 * j:32 * j + 32, :], w=[('ub', p)])
                S.op('dve', lambda e: e.tensor_scalar(out=phi[p][:], in0=io_f[:], scalar1=thp[:, j:j + 1], scalar2=None, op0=ALU.mult),
                     r=['io_f', 'thp'], w=[('phi', p)])
                pt_ = (pki[p][:], pkf[p][:], mm[p][:], pg[p][:])
                sin_reduced(S, Ei[p][:], phi[p][:], 0.0, -1.0, pt_, [('phi', p)], ('Ei', p), f'P{p}')
                sin_reduced(S, Er[p][:], phi[p][:], 0.5 * PI, 1.0, pt_, [('phi', p)], ('Er', p), f'P{p}')
                S.op('pool', lambda e: e.tensor_copy(out=amag[p][:], in_=magp[:, j:j + 1].to_broadcast([128, 512])),
                     r=['magp'], w=[('amag', p)])

            def ph_w(j, p, ci, st):
                c0, c1 = chunks[ci]
                n = c1 - c0
                Erv, Eiv = V(Er[p], n), V(Ei[p], n)
                psR, kR = S.next_ps()
                psI, kI = S.next_ps()
                st['ps'] = (psR, kR, psI, kI)
                uk = ('ub', p)
                S.op('pe', lambda e: e.matmul(psR[:, 0:n], lhsT=BbT[:, 0, j, :], rhs=ub[p][:, c0:c1], start=True, stop=True),
                     r=['BbT', uk], w=[kR])
                S.op('pe', lambda e: e.matmul(psI[:, 0:n], lhsT=BbT[:, 1, j, :], rhs=ub[p][:, c0:c1], start=True, stop=True),
                     r=['BbT', uk], w=[kI])
                S.op('dve', lambda e: e.tensor_tensor(out=wre[p][:, 0:n], in0=psR[:, 0:n], in1=Erv, op=ALU.mult), r=[kR, ('Er', p)], w=[('wre', p)])
                S.op('dve', lambda e: e.tensor_tensor(out=ta[p][:, 0:n], in0=psI[:, 0:n], in1=Eiv, op=ALU.mult), r=[kI, ('Ei', p)], w=[('ta', p)])
                S.op('dve', lambda e: e.tensor_tensor(out=wim[p][:, 0:n], in0=psI[:, 0:n], in1=Erv, op=ALU.mult), r=[kI, ('Er', p)], w=[('wim', p)])
                S.op('dve', lambda e: e.tensor_tensor(out=tb[p][:, 0:n], in0=psR[:, 0:n], in1=Eiv, op=ALU.mult), r=[kR, ('Ei', p)], w=[('tb', p)])
                S.op('pool', lambda e: e.tensor_tensor(out=wre[p][:, 0:n], in0=wre[p][:, 0:n], in1=ta[p][:, 0:n], op=ALU.subtract),
                     r=[('wre', p), ('ta', p)], w=[('wre', p)])
                S.op('pool', lambda e: e.tensor_tensor(out=wim[p][:, 0:n], in0=wim[p][:, 0:n], in1=tb[p][:, 0:n], op=ALU.add),
                     r=[('wim', p), ('tb', p)], w=[('wim', p)])

            def ph_scan(j, p, ci, st):
                c0, c1 = chunks[ci]
                n = c1 - c0
                if ci == 0:
                    ini_r, ini_i, rk = 0.0, 0.0, []
                else:
                    pn = chunks[ci - 1][1] - chunks[ci - 1][0]
                    col = pn - 1 if d == 0 else 0
                    ini_r = xre[p][(ci - 1) % 2][:, col:col + 1]
                    ini_i = xim[p][(ci - 1) % 2][:, col:col + 1]
                    rk = [('xre', p, (ci - 1) % 2), ('xim', p, (ci - 1) % 2)]
                S.op('dve', lambda e: e.tensor_tensor_scan(out=V(zre[p], n), data0=amag[p][:, 0:n], data1=V(wre[p], n), initial=ini_r,
                                                           op0=ALU.mult, op1=ALU.add), r=[('amag', p), ('wre', p)] + rk, w=[('zre', p)])
                S.op('dve', lambda e: e.tensor_tensor_scan(out=V(zim[p], n), data0=amag[p][:, 0:n], data1=V(wim[p], n), initial=ini_i,
                                                           op0=ALU.mult, op1=ALU.add), r=[('amag', p), ('wim', p)] + rk, w=[('zim', p)])

            def ph_x(j, p, ci, st):
                nonlocal yi
                c0, c1 = chunks[ci]
                n = c1 - c0
                Erv, Eiv = V(Er[p], n), V(Ei[p], n)
                xr, xi_ = xre[p][ci % 2], xim[p][ci % 2]
                xrk, xik = ('xre', p, ci % 2), ('xim', p, ci % 2)
                S.op('dve', lambda e: e.tensor_tensor(out=xr[:, 0:n], in0=zre[p][:, 0:n], in1=Erv, op=ALU.mult), r=[('zre', p), ('Er', p)], w=[xrk])
                S.op('pool', lambda e: e.tensor_tensor(out=tc_[p][:, 0:n], in0=zim[p][:, 0:n], in1=Eiv, op=ALU.mult), r=[('zim', p), ('Ei', p)], w=[('tc', p)])
                S.op('dve', lambda e: e.tensor_tensor(out=xr[:, 0:n], in0=xr[:, 0:n], in1=tc_[p][:, 0:n], op=ALU.add), r=[xrk, ('tc', p)], w=[xrk])
                S.op('pool', lambda e: e.tensor_tensor(out=xi_[:, 0:n], in0=zim[p][:, 0:n], in1=Erv, op=ALU.mult), r=[('zim', p), ('Er', p)], w=[xik])
                S.op('pool', lambda e: e.tensor_tensor(out=td[p][:, 0:n], in0=zre[p][:, 0:n], in1=Eiv, op=ALU.mult), r=[('zre', p), ('Ei', p)], w=[('td', p)])
                S.op('pool', lambda e: e.tensor_tensor(out=xi_[:, 0:n], in0=xi_[:, 0:n], in1=td[p][:, 0:n], op=ALU.subtract), r=[xik, ('td', p)], w=[xik])
                psY, kY = S.next_ps()
                S.op('pe', lambda e: e.matmul(psY[0:32, 0:n], lhsT=cb_[:, 0, j, :], rhs=xr[:, 0:n], start=True, stop=False),
                     r=['cbd', xrk], w=[kY])
                S.op('pe', lambda e: e.matmul(psY[0:32, 0:n], lhsT=cb_[:, 1, j, :], rhs=xi_[:, 0:n], start=False, stop=True),
                     r=['cbd', xik], w=[kY])
                ybb = yb[yi % 4]
                ybk = ('yb', yi % 4)
                yi += 1
                S.op('act', lambda e: e.activation(out=ybb[:, 0:n], in_=psY[0:32, 0:n], func=AF.Copy), r=[kY], w=[ybk])
                S.dma('sp', yTo[d][32 * j:32 * j + 32, c0:c1], ybb[:, 0:n], r=[ybk], is_out=True)

            for jp in range(16):
                js = [(2 * jp, 0), (2 * jp + 1, 1)]
                for j, p in js:
                    tables(j, p)
                for ci in range(len(chunks)):
                    sts = [dict(), dict()]
                    for ph in (ph_w, ph_scan, ph_x):
                        for j, p in js:
                            ph(j, p, ci, sts[p])
        S.finish()
    return nc


def run_C3(px_fm, inputs, layer):
    in_maps = []
    for c in range(NCORES):
        b, d = c // 2, c % 2
        u = px_fm[b, 2048:3072, :]
        if d == 1:
            u = flip_seq(u, 1)
        lre = inputs['s5_lam_re'][layer, d]
        lim = inputs['s5_lam_im'][layer, d]
        ldt = np.broadcast_to(inputs['s5_log_dt'][layer, d][:, None], (64, 64))
        P = lambda a: a.reshape(32, 128).T
        lamp = np.concatenate([P(lre), P(lim), P(ldt)], axis=1)
        Rl = lambda a: np.broadcast_to(a.reshape(1, 4096), (32, 4096))
        lamr = np.concatenate([Rl(lre), Rl(lim), Rl(ldt)], axis=1)

        def bdT(bm):
            o = np.zeros((2, 16, 32, 2, 64), np.float32)
            bb = bm.reshape(32, 2, 64, 16)
            for gg in range(2):
                o[gg, :, :, gg, :] = bb[:, gg].transpose(2, 0, 1)
            return o.reshape(32, 4096)

        def cbdf(cm):
            o = np.zeros((2, 64, 32, 2, 16), np.float32)
            cc = cm.reshape(32, 2, 16, 64)
            for gg in range(2):
                o[gg, :, :, gg, :] = cc[:, gg].transpose(2, 0, 1)
            return o.reshape(128, 1024)

        in_maps.append({"u": np.ascontiguousarray(u), "lamp": np.ascontiguousarray(lamp, dtype=np.float32),
                        "lamr": np.ascontiguousarray(lamr, dtype=np.float32),
                        "bT": np.concatenate([bdT(inputs['s5_b_re'][layer, d]), bdT(inputs['s5_b_im'][layer, d])], axis=1),
                        "cbd": np.concatenate([cbdf(inputs['s5_c_re'][layer, d]), cbdf(inputs['s5_c_im'][layer, d])], axis=1)})
    res = run_bass_kernel_spmd(build_C3(), in_maps, core_ids=list(range(NCORES)))
    out = np.zeros((NB, 2, 1024, LT), np.float32)
    for c in range(NCORES):
        b, d = c // 2, c % 2
        yv = res.results[c]["yT"]
        out[b, d] = flip_seq(yv, 1) if d == 1 else yv
    return out


TOK_RANGES = [(0, 512), (512, 1024), (1024, 1152)]
GELU_K2 = 2.0 * 0.7978845608028654


def build_D1():
    nc = new_nc()
    NT = NT_B
    NTOK = NT * 128
    y0 = din(nc, "y0", [NTOK, 1024]); y1 = din(nc, "y1", [NTOK, 1024]); z = din(nc, "z", [NTOK, 1024])
    gssd = din(nc, "gssd", [128, 8])
    y5a = din(nc, "y5a", [1024, NTOK]); y5b = din(nc, "y5b", [1024, NTOK]); uT = din(nc, "uT", [1024, NTOK])
    s5d = din(nc, "s5d", [128, 8])
    wglu = din(nc, "wglu", [1024, 1024])
    ysT = dout(nc, "ysT", [1024, NTOK]); y5T = dout(nc, "y5T", [1024, NTOK])
    with ExitStack() as ctx:
        S = get_sched(nc, ctx)
        ident = make_ident(S)
        gs = S.sb('gs', [128, 8]); sd = S.sb('sd', [128, 8]); wg = S.sb('wg', [128, 8, 1024])
        S.dma('sp', gs[:], gssd[:, :], w=['gs'])
        S.dma('sp', sd[:], s5d[:, :], w=['sd'])
        S.dma('sp', wg[:], wglu.rearrange("(k p) n -> p k n", p=128), w=['wg'])
        yb0 = [S.sb(f'yb0_{i}', [128, 1024]) for i in range(2)]
        yb1 = [S.sb(f'yb1_{i}', [128, 1024]) for i in range(2)]
        zb = [S.sb(f'zb_{i}', [128, 1024]) for i in range(2)]
        junk = S.sb('junk', [128, 1024])
        ss = S.sb('ss', [128, NT])
        S.op('dve', lambda e: e.memset(ss[:], 0.0), w=['ss'])
        ob = [S.sb(f'ob{i}', [128, 8, 128]) for i in range(2)]
        ysv = ysT.rearrange("(k p) n -> p k n", p=128)
        for t in range(NT):
            i = t % 2
            rows = slice(t * 128, (t + 1) * 128)
            S.dma('sp', yb0[i][:], y0[rows, :], w=[('yb0', i)])
            S.dma('sp', yb1[i][:], y1[rows, :], w=[('yb1', i)])
            S.dma('sp', zb[i][:], z[rows, :], w=[('zb', i)])
            S.op('dve', lambda e: e.tensor_tensor(out=yb0[i][:], in0=yb0[i][:], in1=yb1[i][:], op=ALU.add),
                 r=[('yb0', i), ('yb1', i)], w=[('yb0', i)])
            S.op('act', lambda e: e.activation(out=zb[i][:], in_=zb[i][:], func=AF.Silu), r=[('zb', i)], w=[('zb', i)])
            S.op('dve', lambda e: e.tensor_tensor(out=yb0[i][:], in0=yb0[i][:], in1=zb[i][:], op=ALU.mult),
                 r=[('yb0', i), ('zb', i)], w=[('yb0', i)])
            S.op('act', lambda e: e.activation(out=junk[:], in_=yb0[i][:], func=AF.Square, accum_out=ss[:, t:t + 1]),
                 r=[('yb0', i)], w=['junk', 'ss'])
            rstd_cols(S, ss, ((t, t + 1), [(t, t + 1)]), [1.0 / 1024], 'ss')
            S.op('dve', lambda e: e.tensor_scalar(out=yb0[i][:], in0=yb0[i][:], scalar1=ss[:, t:t + 1], scalar2=None,
                                                  op0=ALU.mult), r=[('yb0', i), 'ss'], w=[('yb0', i)])
            o = ob[i]
            for kk in range(2):
                ps, pk = S.next_ps()
                for j in range(4):
                    k = kk * 4 + j
                    S.op('pe', lambda e: e.transpose(out=ps[:, j * 128:(j + 1) * 128], in_=yb0[i][:, k * 128:(k + 1) * 128],
                                                     identity=ident[:]), r=[('yb0', i), 'ident'], w=[pk])
                for j in range(4):
                    k = kk * 4 + j
                    S.op('act', lambda e: e.activation(out=o[:, k, :], in_=ps[:, j * 128:(j + 1) * 128], func=AF.Identity,
                                                       scale=gs[:, k:k + 1]), r=[pk, 'gs'], w=[('ob', i)])
            S.dma('pool', ysv[:, :, rows], o[:], r=[('ob', i)], is_out=True)
        va = S.sb('va', [128, 8, 512]); vb = S.sb('vb', [128, 8, 512]); vu = S.sb('vu', [128, 8, 512])
        x2 = S.sb('x2', [128, 8, 512]); sg = S.sb('sg', [128, 512]); o5 = [S.sb(f'o5_{i}', [128, 512]) for i in range(2)]
        y5v = y5T.rearrange("(k p) n -> p k n", p=128)
        oi = 0
        for (n0, n1) in TOK_RANGES:
            n = n1 - n0
            for src, dst, kname in [(y5a, va, 'va'), (y5b, vb, 'vb'), (uT, vu, 'vu')]:
                S.dma('sp', dst[:, :, 0:n], src.rearrange("(k p) n -> p k n", p=128)[:, :, n0:n1], w=[kname])
            S.op('dve', lambda e: e.tensor_tensor(out=vu[:, :, 0:n], in0=vu[:, :, 0:n], in1=bc(sd[:], 2, [128, 8, n]), op=ALU.mult),
                 r=['vu', 'sd'], w=['vu'])
            S.op('dve', lambda e: e.tensor_tensor(out=va[:, :, 0:n], in0=va[:, :, 0:n], in1=vb[:, :, 0:n], op=ALU.add),
                 r=['va', 'vb'], w=['va'])
            S.op('dve', lambda e: e.tensor_tensor(out=va[:, :, 0:n], in0=va[:, :, 0:n], in1=vu[:, :, 0:n], op=ALU.add),
                 r=['va', 'vu'], w=['va'])
            S.op('dve', lambda e: e.tensor_tensor(out=x2[:, :, 0:n], in0=va[:, :, 0:n], in1=va[:, :, 0:n], op=ALU.mult),
                 r=['va'], w=['x2'])
            S.op('dve', lambda e: e.tensor_scalar(out=x2[:, :, 0:n], in0=x2[:, :, 0:n], scalar1=0.044715, scalar2=1.0,
                                                  op0=ALU.mult, op1=ALU.add), r=['x2'], w=['x2'])
            S.op('dve', lambda e: e.tensor_tensor(out=x2[:, :, 0:n], in0=x2[:, :, 0:n], in1=va[:, :, 0:n], op=ALU.mult),
                 r=['x2', 'va'], w=['x2'])
            S.op('act', lambda e: e.activation(out=x2[:, :, 0:n], in_=x2[:, :, 0:n], func=AF.Sigmoid, scale=GELU_K2),
                 r=['x2'], w=['x2'])
            S.op('dve', lambda e: e.tensor_tensor(out=va[:, :, 0:n], in0=va[:, :, 0:n], in1=x2[:, :, 0:n], op=ALU.mult),
                 r=['va', 'x2'], w=['va'])
            for m in range(8):
                ps, pk = S.next_ps()
                for k in range(8):
                    S.op('pe', lambda e: e.matmul(ps[:, 0:n], lhsT=wg[:, k, m * 128:(m + 1) * 128], rhs=va[:, k, 0:n],
                                                  start=(k == 0), stop=(k == 7)), r=['wg', 'va'], w=[pk])
                S.op('act', lambda e: e.activation(out=sg[:, 0:n], in_=ps[:, 0:n], func=AF.Sigmoid), r=[pk], w=['sg'])
                o = o5[oi % 2]
                ok = ('o5', oi % 2)
                oi += 1
                S.op('dve', lambda e: e.tensor_tensor(out=o[:, 0:n], in0=va[:, m, 0:n], in1=sg[:, 0:n], op=ALU.mult),
                     r=['va', 'sg'], w=[ok])
                S.dma('pool', y5v[:, m, n0:n1], o[:, 0:n], r=[ok], is_out=True)
        S.finish()
    return nc


def run_D1(yssd, ys5, px_tm, px_fm, inputs, layer):
    in_maps = []
    for c in range(NCORES):
        b, h = c // 2, c % 2
        r0, r1 = core_rows(b, h)
        ca = np.ascontiguousarray
        in_maps.append({"y0": ca(yssd[b, 0, r0:r1]), "y1": ca(yssd[b, 1, r0:r1]), "z": ca(px_tm[b, r0:r1, 0:1024]),
                        "gssd": ca(inputs['ssd_norm_gain'][layer].reshape(8, 128).T),
                        "y5a": ca(ys5[b, 0, :, r0:r1]), "y5b": ca(ys5[b, 1, :, r0:r1]), "uT": ca(px_fm[b, 2048:3072, r0:r1]),
                        "s5d": ca(inputs['s5_d'][layer].reshape(8, 128).T), "wglu": inputs['s5_w_glu'][layer]})
    res = run_bass_kernel_spmd(build_D1(), in_maps, core_ids=list(range(NCORES)))
    ysT = np.zeros((NB, 1024, LT), np.float32)
    y5T = np.zeros((NB, 1024, LT), np.float32)
    for c in range(NCORES):
        b, h = c // 2, c % 2
        r0, r1 = core_rows(b, h)
        ysT[b, :, r0:r1] = res.results[c]["ysT"]
        y5T[b, :, r0:r1] = res.results[c]["y5T"]
    return ysT, y5T


def tile_gate(S, dst, gx, flags, t, cols, key):
    S.op('dve', lambda e: e.scalar_tensor_tensor(out=dst, in0=gx[:, 1, cols], scalar=flags[:, t:t + 1], in1=gx[:, 0, cols],
                                                 op0=ALU.mult, op1=ALU.add), r=['gx', 'flags'], w=[key])


def load_gx(S, gxr, flg, NT):
    gx = S.sb('gx', [128, 2, D])
    flags = S.sb('flags', [128, NT])
    S.dma('sp', gx[:, 0, :], gxr[0:1, :].to_broadcast([128, D]), w=['gx'])
    S.dma('sp', gx[:, 1, :], gxr[1:2, :].to_broadcast([128, D]), w=['gx'])
    S.dma('sp', flags[:], flg[:, :], w=['flags'])
    S.op('dve', lambda e: e.tensor_tensor(out=gx[:, 1, :], in0=gx[:, 1, :], in1=gx[:, 0, :], op=ALU.subtract),
         r=['gx'], w=['gx'])
    return gx, flags


def build_D2():
    nc = new_nc()
    NT = NT_B
    NTOK = NT * 128
    yin = [din(nc, nm, [1024, NTOK]) for nm in ("ysT", "omT", "y5T")]
    gT = din(nc, "gT", [6144, NTOK])
    wb = [din(nc, nm, [1024, D]) for nm in ("wbs", "wbm", "wb5")]
    wo = din(nc, "wo", [D, D])
    x = din(nc, "x", [NTOK, D])
    gxr = din(nc, "gxr", [2, D])
    flg = din(nc, "flg", [128, NT])
    x1 = dout(nc, "x1", [NTOK, D])
    with ExitStack() as ctx:
        S = get_sched(nc, ctx)
        gx, flags = load_gx(S, gxr, flg, NT)
        mT = S.sb('mT', [128, 16, NTOK])
        big = S.sb('big', [128, 18432])
        yT = [big[:, br * 4096:(br + 1) * 4096].rearrange("p (k n) -> p k n", k=8) for br in range(3)]
        wbt = [big[:, 12288 + br * 2048:12288 + (br + 1) * 2048].rearrange("p (k n) -> p k n", k=8) for br in range(3)]
        wot = [big[:, i * 8192:(i + 1) * 8192].rearrange("p (k n) -> p k n", k=16) for i in range(2)]
        gtl = [S.sb(f'gt{i}', [128, 512]) for i in range(3)]
        tmp = S.sb('tmp', [128, 512])
        gi = 0
        for (n0, n1) in TOK_RANGES:
            n = n1 - n0
            for br in range(3):
                S.dma('pool', R(yT[br][:, :, 0:n]), yin[br].rearrange("(k p) n -> p k n", p=128)[:, :, n0:n1], w=[('yT', br)])
            for db in range(8):
                for br in range(3):
                    S.dma('pool', R(wbt[br][:, :, :]), wb[br].rearrange("(k p) n -> p k n", p=128)[:, :, db * 256:(db + 1) * 256],
                          w=[('wbt', br)])
                for mm in range(2):
                    m = db * 2 + mm
                    for br in range(3):
                        g = gtl[gi % 3]
                        gk = ('gt', gi % 3)
                        gi += 1
                        S.dma('sp', g[:, 0:n], gT[br * 2048 + m * 128:br * 2048 + (m + 1) * 128, n0:n1], w=[gk])
                        S.op('act', lambda e: e.activation(out=g[:, 0:n], in_=g[:, 0:n], func=AF.Sigmoid), r=[gk], w=[gk])
                        ps, pk = S.next_ps()
                        for k in range(8):
                            S.op('pe', lambda e: e.matmul(ps[:, 0:n], lhsT=R(wbt[br][:, k, mm * 128:(mm + 1) * 128]),
                                                          rhs=R(yT[br][:, k, 0:n]), start=(k == 0), stop=(k == 7)),
                                 r=[('wbt', br), ('yT', br)], w=[pk])
                        if br == 0:
                            S.op('dve', lambda e: e.tensor_tensor(out=R(mT[:, m, n0:n1]), in0=ps[:, 0:n], in1=g[:, 0:n], op=ALU.mult),
                                 r=[pk, gk], w=[('mT', m)])
                        else:
                            S.op('dve', lambda e: e.tensor_tensor(out=tmp[:, 0:n], in0=ps[:, 0:n], in1=g[:, 0:n], op=ALU.mult),
                                 r=[pk, gk], w=['tmp'])
                            S.op('pool', lambda e: e.tensor_tensor(out=R(mT[:, m, n0:n1]), in0=mT[:, m, n0:n1], in1=tmp[:, 0:n],
                                                                   op=ALU.add), r=[('mT', m), 'tmp'], w=[('mT', m)])
        allk = [('yT', br) for br in range(3)] + [('wbt', br) for br in range(3)]
        mkeys = [('mT', m) for m in range(16)]
        xt = [S.sb(f'xt{i}', [128, 512]) for i in range(3)]
        gtb = S.sb('gtb', [128, 512])
        xi = 0
        for cb in range(4):
            cols = slice(cb * 512, (cb + 1) * 512)
            w_ = wot[cb % 2]
            wk = ('wot', cb % 2)
            S.dma('pool', R(w_[:, 0:8, :]), wo.rearrange("(k p) n -> p k n", p=128)[:, 0:8, cols], w=[wk] + (allk if cb < 2 else []))
            S.dma('pool', R(w_[:, 8:16, :]), wo.rearrange("(k p) n -> p k n", p=128)[:, 8:16, cols], w=[wk])
            for t in range(NT):
                rows = slice(t * 128, (t + 1) * 128)
                xx = xt[xi % 3]
                xk = ('xt', xi % 3)
                xi += 1
                S.dma('sp', xx[:], x[rows, cols], w=[xk])
                ps, pk = S.next_ps()
                for k in range(16):
                    S.op('pe', lambda e: e.matmul(ps[:, :], lhsT=R(mT[:, k, rows]), rhs=R(w_[:, k, :]), start=(k == 0), stop=(k == 15)),
                         r=mkeys + [wk], w=[pk])
                tile_gate(S, gtb[:], gx, flags, t, cols, 'gtb')
                S.op('dve', lambda e: e.tensor_tensor(out=gtb[:], in0=ps[:, :], in1=gtb[:], op=ALU.mult), r=[pk, 'gtb'], w=['gtb'])
                S.op('pool', lambda e: e.tensor_tensor(out=xx[:], in0=xx[:], in1=gtb[:], op=ALU.add), r=[xk, 'gtb'], w=[xk])
                S.dma('sp', x1[rows, cols], xx[:], r=[xk], is_out=True)
        S.finish()
    return nc


def make_flags(h, NT=9):
    f = np.zeros((128, NT), np.float32)
    for t in range(NT):
        if tile_is_ctx(h, t):
            f[:, t] = 1.0
    return f


def mod_vec(modT, which, r):
    return np.ascontiguousarray(modT[:, which * 16:(which + 1) * 16, r].T).reshape(D)


def run_D2(ysT, omT, y5T, px_fm, xseq, modT, inputs, layer):
    in_maps = []
    ca = np.ascontiguousarray
    for c in range(NCORES):
        b, h = c // 2, c % 2
        r0, r1 = core_rows(b, h)
        in_maps.append({"ysT": ca(ysT[b, :, r0:r1]), "omT": ca(omT[b, :, r0:r1]), "y5T": ca(y5T[b, :, r0:r1]),
                        "gT": ca(px_fm[b, 3072:9216, r0:r1]),
                        "wbs": inputs['w_branch_ssd'][layer], "wbm": inputs['w_branch_mla'][layer],
                        "wb5": inputs['w_branch_s5'][layer], "wo": inputs['w_out'][layer],
                        "x": ca(xseq[b, r0:r1]), "gxr": np.stack([mod_vec(modT, 2, b), mod_vec(modT, 2, 4)], 0),
                        "flg": make_flags(h)})
    res = run_bass_kernel_spmd(build_D2(), in_maps, core_ids=list(range(NCORES)))
    x1 = np.zeros((NB, LT, D), np.float32)
    for c in range(NCORES):
        b, h = c // 2, c % 2
        r0, r1 = core_rows(b, h)
        x1[b, r0:r1] = res.results[c]["x1"]
    return x1


def build_D3():
    nc = new_nc()
    NT = NT_B
    NTOK = NT * 128
    xin = din(nc, "x", [NTOK, D])
    msel = din(nc, "msel", [128, NT * 16 * 2])
    gn = din(nc, "gn", [128, 16])
    wr = din(nc, "wr", [D, 16])
    hx = dout(nc, "hx", [NTOK, D])
    aff = dout(nc, "aff", [NTOK, 16])
    affTo = dout(nc, "affT", [16, NTOK])
    with ExitStack() as ctx:
        S = get_sched(nc, ctx)
        ident = make_ident(S)
        hT = S.sb('hT', [128, 16, NTOK])
        ms, g1 = load_mod(S, msel, gn, NT)
        wrt = S.sb('wrt', [128, 16, 16])
        S.dma('sp', wrt[:], wr.rearrange("(k p) e -> p k e", p=128), w=['wrt'])
        norm_mod_tiles(S, xin, NT, ms, g1, hT, ident, 'n2')
        hb = [S.sb(f'hb{i}', [128, D]) for i in range(2)]
        lg = S.sb('lg', [128, 16]); mx = S.sb('mx', [128, 1]); sm_ = S.sb('sm', [128, 1])
        ab = [S.sb(f'ab{i}', [128, 16]) for i in range(2)]
        atb = [S.sb(f'atb{i}', [16, 128]) for i in range(2)]
        for t in range(NT):
            rows = slice(t * 128, (t + 1) * 128)
            h = hb[t % 2]
            hk = ('hb', t % 2)
            for kk in range(4):
                ps, pk = S.next_ps()
                for j in range(4):
                    k = kk * 4 + j
                    S.op('pe', lambda e: e.transpose(out=ps[:, j * 128:(j + 1) * 128], in_=hT[:, k, rows], identity=ident[:]),
                         r=[('hT', t), 'ident'], w=[pk])
                if kk % 2 == 0:
                    S.op('act', lambda e: e.activation(out=h[:, kk * 512:(kk + 1) * 512], in_=ps[:, :], func=AF.Copy), r=[pk], w=[hk])
                else:
                    S.op('dve', lambda e: e.tensor_copy(out=h[:, kk * 512:(kk + 1) * 512], in_=ps[:, :]), r=[pk], w=[hk])
            S.dma('pool', hx[rows, :], h[:], r=[hk], is_out=True)
            ps, pk = S.next_ps()
            for k in range(16):
                S.op('pe', lambda e: e.matmul(ps[:, 0:16], lhsT=hT[:, k, rows], rhs=wrt[:, k, :], start=(k == 0), stop=(k == 15)),
                     r=[('hT', t), 'wrt'], w=[pk])
            a = ab[t % 2]
            ak = ('ab', t % 2)
            S.op('dve', lambda e: e.tensor_copy(out=lg[:], in_=ps[:, 0:16]), r=[pk], w=['lg'])
            S.op('dve', lambda e: e.tensor_reduce(out=mx[:], in_=lg[:], axis=AX.X, op=ALU.max), r=['lg'], w=['mx'])
            S.op('dve', lambda e: e.tensor_scalar(out=mx[:], in0=mx[:], scalar1=-1.0, scalar2=None, op0=ALU.mult), r=['mx'], w=['mx'])
            S.op('dve', lambda e: e.memset(sm_[:], 0.0), w=['sm'])
            S.op('act', lambda e: e.activation(out=a[:], in_=lg[:], func=AF.Exp, bias=mx[:, 0:1], scale=1.0, accum_out=sm_[:, 0:1]),
                 r=['lg', 'mx', 'sm'], w=[ak, 'sm'])
            S.op('dve', lambda e: e.reciprocal(out=sm_[:], in_=sm_[:]), r=['sm'], w=['sm'])
            S.op('dve', lambda e: e.tensor_scalar(out=a[:], in0=a[:], scalar1=sm_[:, 0:1], scalar2=None, op0=ALU.mult),
                 r=[ak, 'sm'], w=[ak])
            S.dma('pool', aff[rows, :], a[:], r=[ak], is_out=True)
            ps, pk = S.next_ps()
            S.op('pe', lambda e: e.transpose(out=ps[0:16, 0:128], in_=a[:, 0:16], identity=ident[:]), r=[ak, 'ident'], w=[pk])
            at = atb[t % 2]
            atk = ('atb', t % 2)
            S.op('dve', lambda e: e.tensor_copy(out=at[:], in_=ps[0:16, 0:128]), r=[pk], w=[atk])
            S.dma('pool', affTo[:, rows], at[:], r=[atk], is_out=True)
        S.finish()
    return nc


def run_D3(x1, modT, inputs, layer):
    gn = np.ascontiguousarray(inputs['norm2_gain'][layer].reshape(16, 128).T)
    in_maps = []
    for c in range(NCORES):
        b, h = c // 2, c % 2
        r0, r1 = core_rows(b, h)
        in_maps.append({"x": np.ascontiguousarray(x1[b, r0:r1]), "msel": make_msel(modT, b, h, 4, 3), "gn": gn,
                        "wr": inputs['moe_router'][layer]})
    res = run_bass_kernel_spmd(build_D3(), in_maps, core_ids=list(range(NCORES)))
    hx2 = np.zeros((NB, LT, D), np.float32)
    aff = np.zeros((NB, LT, 16), np.float32)
    for c in range(NCORES):
        b, h = c // 2, c % 2
        r0, r1 = core_rows(b, h)
        hx2[b, r0:r1] = res.results[c]["hx"]
        aff[b, r0:r1] = res.results[c]["aff"]
    return hx2, aff


def build_E(with_ctx, do_zero=True):
    nc = new_nc()
    NE = 8
    NSL = 288 if with_ctx else 256
    affT = din(nc, "affT", [NE, LT])
    hx = din(nc, "hx", [LT, D])
    wg = din(nc, "wg", [NE, D, D]); wu = din(nc, "wu", [NE, D, D]); wd = din(nc, "wd", [NE, D, D])
    delta = dout(nc, "delta", [LT, D])
    with ExitStack() as ctx:
        S = get_sched(nc, ctx)
        ident = make_ident(S)
        ysb = [S.sb(f'ys{i}', [128, D]) for i in range(2)]
        if do_zero:
            S.op('pool', lambda e: e.memset(ysb[0][:], 0.0), w=[('ys', 0)])
            for t in range(NTL):
                S.dma('pool', delta[t * 128:(t + 1) * 128, :], ysb[0][:], r=[('ys', 0)], w=['delta'], is_out=True)
        work = S.sb('work', [NE, LT])
        S.dma('sp', work[:], affT[:, :], w=['work'])
        vals = S.sb('vals', [NE, 288]); idxu = S.sb('idxu', [NE, 288], U32); idxf = S.sb('idxf', [NE, 288])
        S.op('dve', lambda e: e.memset(vals[:], 0.0), w=['vals'])
        S.op('dve', lambda e: e.memset(idxf[:], 0.0), w=['idxf'])
        segs = [(CTX, LT, 0, 32, float(CTX))] + ([(0, CTX, 256, 4, 0.0)] if with_ctx else [])
        for (a0, a1, s0, rounds, off) in segs:
            for r in range(rounds):
                sl = slice(s0 + r * 8, s0 + r * 8 + 8)
                S.op('dve', lambda e: e.max(out=vals[:, sl], in_=work[:, a0:a1]), r=['work'], w=['vals'])
                S.op('dve', lambda e: e.max_index(out=idxu[:, sl], in_max=vals[:, sl], in_values=work[:, a0:a1]),
                     r=['work', 'vals'], w=['idxu'])
                S.op('dve', lambda e: e.match_replace(out=work[:, a0:a1], in_to_replace=vals[:, sl], in_values=work[:, a0:a1],
                                                      imm_value=-1.0), r=['work', 'vals'], w=['work'])
            S.op('dve', lambda e: e.tensor_copy(out=idxf[:, s0:s0 + rounds * 8], in_=idxu[:, s0:s0 + rounds * 8]),
                 r=['idxu'], w=['idxf'])
            if off != 0.0:
                S.op('dve', lambda e: e.tensor_scalar(out=idxf[:, s0:s0 + rounds * 8], in0=idxf[:, s0:s0 + rounds * 8],
                                                      scalar1=off, scalar2=None, op0=ALU.add), r=['idxf'], w=['idxf'])
        gTt = S.sb('gTt', [128, 3, NE]); iTf = S.sb('iTf', [128, 3, NE]); iTu = S.sb('iTu', [128, 3, NE], U32)
        S.op('dve', lambda e: e.memset(iTf[:], 0.0), w=['iTf'])
        S.op('dve', lambda e: e.memset(gTt[:], 0.0), w=['gTt'])
        tiles = [(0, 128), (128, 128)] + ([(256, 32)] if with_ctx else [])
        for st, (c0, nr) in enumerate(tiles):
            for src, dst, dk in [(vals, gTt, 'gTt'), (idxf, iTf, 'iTf')]:
                ps, pk = S.next_ps()
                S.op('pe', lambda e: e.transpose(out=ps[0:nr, 0:NE], in_=src[:, c0:c0 + nr], identity=ident[0:NE, 0:NE]),
                     r=['vals', 'idxf', 'ident'], w=[pk])
                S.op('dve', lambda e: e.tensor_copy(out=dst[0:nr, st, :], in_=ps[0:nr, 0:NE]), r=[pk], w=[dk])
        S.op('dve', lambda e: e.tensor_copy(out=iTu[:], in_=iTf[:]), r=['iTf'], w=['iTu'])
        NWB = 4
        wbuf = [S.sb(f'wbuf{i}', [128, 16, 512]) for i in range(NWB)]
        wcnt = [0]

        def load_w(src2d, cols):
            i = wcnt[0] % NWB
            wcnt[0] += 1
            v = src2d.rearrange("(k p) n -> p k n", p=128)
            S.dma('pool', R(wbuf[i][:, 0:8, :]), v[:, 0:8, cols], w=[('wbuf', i)])
            S.dma('pool', R(wbuf[i][:, 8:16, :]), v[:, 8:16, cols], w=[('wbuf', i)])
            return wbuf[i], ('wbuf', i)

        xs = [S.sb(f'xs{i}', [128, D]) for i in range(1)]
        xsT = S.sb('xsT', [128, 16, NSL])
        hidT = S.sb('hidT', [128, 16, NSL])
        sg = S.sb('sg', [128, NSL])
        xi = 0
        yi = 0
        for e_ in range(NE):
            for st, (c0, nr) in enumerate(tiles):
                xx = xs[0]
                xk = ('xs', 0)
                xi += 1
                S._deps('pool', ['iTu'], [xk])
                S._guard_dma('pool')
                ins = nc.gpsimd.indirect_dma_start(out=xx[0:nr, :], out_offset=None, in_=hx[:, :],
                                                   in_offset=bass.IndirectOffsetOnAxis(ap=iTu[0:nr, st, e_:e_ + 1], axis=0))
                tok = S._finish_dma('pool', ins, ['iTu'], [xk])
                for kk in range(4):
                    ps, pk = S.next_ps()
                    for j in range(4):
                        k = kk * 4 + j
                        S.op('pe', lambda e: e.transpose(out=ps[:, j * 128:j * 128 + nr], in_=xx[0:nr, k * 128:(k + 1) * 128],
                                                         identity=ident[0:nr, 0:nr]), r=[xk, 'ident'], w=[pk])
                    S.op('act', lambda e: e.activation(out=R(xsT[:, kk * 4:kk * 4 + 4, c0:c0 + nr]),
                                                       in_=ps[:, :].rearrange("p (j c) -> p j c", c=128)[:, :, 0:nr], func=AF.Copy),
                         r=[pk], w=['xsT'])
            for fb in range(4):
                fcols = slice(fb * 512, (fb + 1) * 512)
                wgt, wgk = load_w(wg[e_], fcols)
                wut, wuk = load_w(wu[e_], fcols)
                for ff in range(4):
                    f = fb * 4 + ff
                    psg, kg_ = S.next_ps()
                    psu, ku_ = S.next_ps()
                    for k in range(16):
                        S.op('pe', lambda e: e.matmul(psg[:, 0:NSL], lhsT=R(wgt[:, k, ff * 128:(ff + 1) * 128]), rhs=R(xsT[:, k, :]),
                                                      start=(k == 0), stop=(k == 15)), r=[wgk, 'xsT'], w=[kg_])
                    for k in range(16):
                        S.op('pe', lambda e: e.matmul(psu[:, 0:NSL], lhsT=R(wut[:, k, ff * 128:(ff + 1) * 128]), rhs=R(xsT[:, k, :]),
                                                      start=(k == 0), stop=(k == 15)), r=[wuk, 'xsT'], w=[ku_])
                    S.op('act', lambda e: e.activation(out=sg[:, :], in_=psg[:, 0:NSL], func=AF.Silu), r=[kg_], w=['sg'])
                    S.op('dve', lambda e: e.tensor_tensor(out=R(hidT[:, f, :]), in0=psu[:, 0:NSL], in1=sg[:, :], op=ALU.mult),
                         r=[ku_, 'sg'], w=['hidT'])
            yts = []
            for st, (c0, nr) in enumerate(tiles):
                yts.append((ysb[yi % 2] if st < 2 else xs[0], ('ys', yi % 2) if st < 2 else ('xs', 0)))
                if st < 2:
                    yi += 1
            for cb in range(4):
                cols = slice(cb * 512, (cb + 1) * 512)
                wdt, wdk = load_w(wd[e_], cols)
                for st, (c0, nr) in enumerate(tiles):
                    yt, yk = yts[st]
                    ps, pk = S.next_ps()
                    for f in range(16):
                        S.op('pe', lambda e: e.matmul(ps[0:nr, :], lhsT=R(hidT[:, f, c0:c0 + nr]), rhs=R(wdt[:, f, :]),
                                                      start=(f == 0), stop=(f == 15)), r=['hidT', wdk], w=[pk])
                    S.op('act', lambda e: e.activation(out=yt[0:nr, cols], in_=ps[0:nr, :], func=AF.Copy,
                                                       scale=gTt[0:nr, st, e_:e_ + 1]), r=[pk, 'gTt'], w=[yk])
            for st, (c0, nr) in enumerate(tiles):
                yt, yk = yts[st]
                S._deps('pool', [yk, 'iTu'], ['delta'])
                S._guard_dma('pool')
                ins = nc.gpsimd.indirect_dma_start(out=delta[:, :], out_offset=bass.IndirectOffsetOnAxis(ap=iTu[0:nr, st, e_:e_ + 1], axis=0),
                                                   in_=yt[0:nr, :], in_offset=None, compute_op=ALU.add)
                tok = S._finish_dma('pool', ins, [yk, 'iTu'], ['delta'])
                S.out_tokens.append(tok)
        S.finish()
    return nc


def run_E(aff, hx2, inputs, layer, with_ctx, batches=(0, 1, 2, 3)):
    in_maps = []
    cores = []
    for b in batches:
        for hf in range(2):
            cores.append((b, hf))
            es = slice(8 * hf, 8 * hf + 8)
            in_maps.append({"affT": np.ascontiguousarray(aff[b].T[es]), "hx": np.ascontiguousarray(hx2[b]),
                            "wg": inputs['moe_w_gate'][layer, es], "wu": inputs['moe_w_up'][layer, es],
                            "wd": inputs['moe_w_down'][layer, es]})
    res = run_bass_kernel_spmd(build_E(with_ctx), in_maps, core_ids=list(range(len(cores))))
    out = np.zeros((NB, 2, LT, D), np.float32)
    for i, (b, hf) in enumerate(cores):
        out[b, hf] = res.results[i]["delta"]
    return out


def build_F(single=False):
    nc = new_nc()
    NT = NT_B
    NTOK = NT * 128
    x1 = din(nc, "x1", [NTOK, D]); da = din(nc, "da", [NTOK, D])
    db = None if single else din(nc, "db", [NTOK, D])
    gxr = din(nc, "gxr", [2, D]); flg = din(nc, "flg", [128, NT])
    x2 = dout(nc, "x2", [NTOK, D])
    with ExitStack() as ctx:
        S = get_sched(nc, ctx)
        gx, flags = load_gx(S, gxr, flg, NT)
        xa = [S.sb(f'xa{i}', [128, D]) for i in range(2)]
        ta = [S.sb(f'ta{i}', [128, D]) for i in range(2)]
        tb = [S.sb(f'tb{i}', [128, D]) for i in range(2)]
        gt = S.sb('gt', [128, D])
        for t in range(NT):
            i = t % 2
            rows = slice(t * 128, (t + 1) * 128)
            S.dma('sp', xa[i][:], x1[rows, :], w=[('xa', i)])
            S.dma('sp', ta[i][:], da[rows, :], w=[('ta', i)])
            if not single:
                S.dma('act', tb[i][:], db[rows, :], w=[('tb', i)])
            tile_gate(S, gt[:], gx, flags, t, slice(0, D), 'gt')
            if not single:
                S.op('pool', lambda e: e.tensor_tensor(out=ta[i][:], in0=ta[i][:], in1=tb[i][:], op=ALU.add),
                     r=[('ta', i), ('tb', i)], w=[('ta', i)])
            S.op('dve', lambda e: e.tensor_tensor(out=ta[i][:], in0=ta[i][:], in1=gt[:], op=ALU.mult), r=[('ta', i), 'gt'], w=[('ta', i)])
            S.op('pool', lambda e: e.tensor_tensor(out=xa[i][:], in0=xa[i][:], in1=ta[i][:], op=ALU.add),
                 r=[('xa', i), ('ta', i)], w=[('xa', i)])
            S.dma('pool', x2[rows, :], xa[i][:], r=[('xa', i)], is_out=True)
        S.finish()
    return nc


def run_F(x1, dl, modT):
    in_maps = []
    ca = np.ascontiguousarray
    for c in range(NCORES):
        b, h = c // 2, c % 2
        r0, r1 = core_rows(b, h)
        in_maps.append({"x1": ca(x1[b, r0:r1]), "da": ca(dl[b, 0, r0:r1]), "db": ca(dl[b, 1, r0:r1]),
                        "gxr": np.stack([mod_vec(modT, 5, b), mod_vec(modT, 5, 4)], 0), "flg": make_flags(h)})
    res = run_bass_kernel_spmd(build_F(), in_maps, core_ids=list(range(NCORES)))
    x2 = np.zeros((NB, LT, D), np.float32)
    for c in range(NCORES):
        b, h = c // 2, c % 2
        r0, r1 = core_rows(b, h)
        x2[b, r0:r1] = res.results[c]["x2"]
    return x2


def build_A2():
    nc = new_nc()
    cT = din(nc, "cT2", [128, 32])
    w = din(nc, "wmod", [D, 12288])
    b = din(nc, "bmod", [128, 96])
    modT_d = dout(nc, "modT", [128, 2 * 96])
    gvec_d = dout(nc, "gvec", [2, 12288])
    with ExitStack() as ctx:
        S = get_sched(nc, ctx)
        ident = make_ident(S)
        ct = S.sb('ct', [128, 16, 2]); ca = S.sb('ca', [128, 16, 2]); bt = S.sb('bt', [128, 96])
        mt = S.sb('mt', [128, 2, 96])
        wt = [S.sb(f'wt{i}', [128, 16, 768]) for i in range(2)]
        S.dma('sp', ct[:].rearrange("p k r -> p (k r)"), cT[:, :], w=['ct'])
        S.dma('sp', bt[:], b[:, :], w=['bt'])
        S.op('act', lambda e: e.activation(out=ca[:], in_=ct[:], func=AF.Silu), r=['ct'], w=['ca'])
        wv = w.rearrange("(k p) n -> p k n", p=128)
        for j in range(16):
            wtj = wt[j % 2]
            for g in range(4):
                S.dma('sp' if g % 2 == 0 else 'act', wtj[:, 4 * g:4 * g + 4, :], wv[:, 4 * g:4 * g + 4, j * 768:(j + 1) * 768],
                      w=[('wt', j % 2, g)])
            ps, pk = S.next_ps()
            for m in range(6):
                for k in range(16):
                    S.op('pe', lambda e: e.matmul(ps[:, m * 2:m * 2 + 2], lhsT=wtj[:, k, m * 128:(m + 1) * 128],
                                                  rhs=ca[:, k, :], start=(k == 0), stop=(k == 15)),
                         r=['ca', ('wt', j % 2, k // 4)], w=[pk])
            for m in range(6):
                c = j * 6 + m
                S.op('act', lambda e: e.activation(out=mt[:, :, c], in_=ps[:, m * 2:m * 2 + 2], func=AF.Identity,
                                                   bias=bt[:, c:c + 1], scale=1.0), r=[pk, 'bt'], w=['mt'])
        S.dma('pool', modT_d[:, :], mt[:].rearrange("p r c -> p (r c)"), r=['mt'], is_out=True)
        gv = S.sb('gv', [96, 2, 128])
        for r in range(2):
            ps, pk = S.next_ps()
            S.op('pe', lambda e: e.transpose(out=ps[0:96, 0:128], in_=mt[:, r, :], identity=ident[:]), r=['mt', 'ident'], w=[pk])
            S.op('dve', lambda e: e.tensor_copy(out=gv[:, r, :], in_=ps[0:96, 0:128]), r=[pk], w=['gv'])
            S.dma('pool', gvec_d[r, :].rearrange("(c p) -> c p", p=128), gv[:, r, :], r=['gv'], is_out=True)
        S.finish()
    return nc


def build_msel(modT_d, out_d, rows, sc_i, sh_i):
    nc = new_nc()
    with ExitStack() as ctx:
        S = get_sched(nc, ctx)
        mv = modT_d.rearrange("p (r c) -> p r c", r=2)
        ov = out_d.rearrange("p (t s k) -> p t s k", s=2, k=16)
        for t, r in enumerate(rows):
            S.dma('sp', ov[:, t, 0, :], mv[:, r, sc_i * 16:(sc_i + 1) * 16])
            S.dma('act', ov[:, t, 1, :], mv[:, r, sh_i * 16:(sh_i + 1) * 16])
        S.finish()


FUSED_INPUT_SPECS = None


_DBG = {'export': (), 'stop': None}


def build_fused():
    nc = bass.Bass("TRN2", target_bir_lowering=False)
    ext = {}

    def EI(name, shape, dt=F32):
        ext[name] = nc.dram_tensor(name, list(shape), dt, kind="ExternalInput").ap()
        return ext[name]

    def SC(name, shape, dt=F32):
        kind = "ExternalOutput" if name in _DBG['export'] else "Internal"
        return nc.dram_tensor(name, list(shape), dt, kind=kind).ap()

    class _Stop(Exception):
        pass

    nst = [0]

    def chk():
        nst[0] += 1
        if _DBG['stop'] is not None and nst[0] >= _DBG['stop']:
            raise _Stop()

    xs0 = EI("xseq", [LT, D])
    out = nc.dram_tensor("out", [SEQ, D], F32, kind="ExternalOutput").ap()
    cs = EI("cs", [LT, 32])
    flg = [EI("flg0", [128, 9]), EI("flg1", [128, 9])]
    WSPEC = dict(cT2=[128, 32], wmod=[D, 12288], bmod=[128, 96], gn1=[128, 16], gn2=[128, 16], win=[D, PROJ_IN],
                 cw=[128, 80], cb=[128, 16], abd=[128, 96], gqa=[128, 4], gkv=[128, 2], wq=[512, 1536], wkv=[256, 2048],
                 qg=[128, 96], kg=[128, 96], lamp=[128, 192], lamr=[32, 6 * 4096], bT=[32, 4 * 4096], cbd=[128, 4096],
                 gssd=[128, 8], s5d=[128, 8], wglu=[1024, 1024], wbs=[1024, D], wbm=[1024, D], wb5=[1024, D], wo=[D, D],
                 wr=[D, 16], wg=[16, D, D], wu=[16, D, D], wd=[16, D, D])

    class LazyW(dict):
        def __init__(self, l):
            super().__init__()
            self.l = l

        def __missing__(self, k):
            self[k] = EI(f"l{self.l}_{k}", WSPEC[k])
            return self[k]

    L = [LazyW(0), LazyW(1)]
    modT = SC("modT", [128, 192]); gvec = SC("gvec", [2, 12288])
    msel1 = [SC(f"msel1_{h}", [128, 9 * 32]) for h in range(2)]
    msel2 = [SC(f"msel2_{h}", [128, 9 * 32]) for h in range(2)]
    px_tm = SC("px_tm", [LT, 1840]); px_fm = SC("px_fm", [9216, LT])
    y0 = SC("y0", [LT, 1024]); y1 = SC("y1", [LT, 1024]); omT = SC("omT", [1024, LT])
    yT0 = SC("yT0", [1024, LT]); yT1 = SC("yT1", [1024, LT])
    ysT = SC("ysT", [1024, LT]); y5T = SC("y5T", [1024, LT])
    x1 = SC("x1", [LT, D]); hx = SC("hx", [LT, D]); aff = SC("aff", [LT, 16]); affT = SC("affT", [16, LT])
    delta = SC("delta", [LT, D])
    xs1 = SC("xs1", [LT, D]); xs2 = SC("xs2", [LT, D])
    with ExitStack() as gctx:
        S = Sched(nc, gctx)
        S.fused = True
        _F['nc'], _F['S'] = nc, S
        try:
            xcur = xs0
            halves = [(0, 1152), (1152, 2304)]
            rows_h = [[1, 1] + [0] * 7, [0] * 9]
            try:
                for l in range(2):
                    if nst[0] < 0:
                        break
                    W = L[l]
                    _F['io'] = dict(cT2=W['cT2'], wmod=W['wmod'], bmod=W['bmod'], modT=modT, gvec=gvec)
                    build_A2()
                    chk()
                    for h in range(2):
                        build_msel(modT, msel1[h], rows_h[h], 1, 0)
                        build_msel(modT, msel2[h], rows_h[h], 4, 3)
                    for h, (r0, r1) in enumerate(halves * _DBG.get('brep', 1)):
                        h = h % 2
                        if 'B' in _DBG.get('skip', ()):
                            continue
                        _F['io'] = dict(x=xcur[r0:r1, :], msel=msel1[h], gn=W['gn1'], w=W['win'], otm=px_tm[r0:r1, :], ofm=px_fm[:, r0:r1])
                        build_B()
                        chk()
                    _F['io'] = dict(xbc=px_fm[0:2048, :], dt=px_tm[:, 1024:1040], cw=W['cw'], cb=W['cb'], abd=W['abd'], y0=y0, y1=y1)
                    c1io = _F['io']
                    if 'C1' not in _DBG.get('skip', ()) and not _DBG.get('c1late'):
                        build_C1()
                    chk()
                    for hf in _DBG.get('c2', (0, 1)):
                        _F['io'] = dict(mla=px_tm[:, 1040:1840], gqa=W['gqa'], gkv=W['gkv'], wq=W['wq'][:, hf * 768:(hf + 1) * 768],
                                        wkv=W['wkv'][:, hf * 1024:(hf + 1) * 1024], qg=W['qg'], kg=W['kg'], cs=cs,
                                        oT=omT[hf * 512:(hf + 1) * 512, :])
                        build_C2()
                        chk()
                    if _DBG.get('c1late'):
                        _F['io'] = c1io
                        build_C1()
                        chk()
                    _F['io'] = dict(u=px_fm[2048:3072, :], lamp=W['lamp'], lamr=W['lamr'], bT=W['bT'], cbd=W['cbd'], yT0=yT0, yT1=yT1)
                    build_C3()
                    chk()
                    for h, (r0, r1) in enumerate(halves):
                        _F['io'] = dict(y0=y0[r0:r1, :], y1=y1[r0:r1, :], z=px_tm[r0:r1, 0:1024], gssd=W['gssd'],
                                        y5a=yT0[:, r0:r1], y5b=yT1[:, r0:r1], uT=px_fm[2048:3072, r0:r1], s5d=W['s5d'], wglu=W['wglu'],
                                        ysT=ysT[:, r0:r1], y5T=y5T[:, r0:r1])
                        build_D1()
                        chk()
                    for h, (r0, r1) in enumerate(halves):
                        _F['io'] = dict(ysT=ysT[:, r0:r1], omT=omT[:, r0:r1], y5T=y5T[:, r0:r1], gT=px_fm[3072:9216, r0:r1],
                                        wbs=W['wbs'], wbm=W['wbm'], wb5=W['wb5'], wo=W['wo'], x=xcur[r0:r1, :],
                                        gxr=gvec[:, 2 * D:3 * D], flg=flg[h], x1=x1[r0:r1, :])
                        build_D2()
                        chk()
                    for h, (r0, r1) in enumerate(halves):
                        _F['io'] = dict(x=x1[r0:r1, :], msel=msel2[h], gn=W['gn2'], wr=W['wr'], hx=hx[r0:r1, :], aff=aff[r0:r1, :],
                                        affT=affT[:, r0:r1])
                        build_D3()
                        chk()
                    for hf in range(2):
                        es = slice(8 * hf, 8 * hf + 8)
                        _F['io'] = dict(affT=affT[es, :], hx=hx, wg=W['wg'][es], wu=W['wu'][es], wd=W['wd'][es], delta=delta)
                        build_E(with_ctx=(l == 0), do_zero=(hf == 0))
                        chk()
                    xnext = xs1 if l == 0 else xs2
                    for h, (r0, r1) in enumerate(halves):
                        if l == 1:
                            pass
                        _F['io'] = dict(x1=x1[r0:r1, :], da=delta[r0:r1, :], gxr=gvec[:, 5 * D:6 * D], flg=flg[h],
                                        x2=xnext[r0:r1, :])
                        build_F(single=True)
                        chk()
                    xcur = xnext
                nc_ = new_nc()
                with ExitStack() as c3:
                    S3 = get_sched(nc_, c3)
                    S3.fused = False
                    for i in range(4):
                        S3.dma('sp' if i % 2 == 0 else 'act', out[i * 512:(i + 1) * 512, :], xcur[CTX + i * 512:CTX + (i + 1) * 512, :],
                               is_out=True)
                    S3.finish()
            except _Stop:
                pass
        finally:
            _F['nc'], _F['S'], _F['io'] = None, None, {}
    global FUSED_INPUT_SPECS
    FUSED_INPUT_SPECS = set(ext.keys())
    return nc


def fused_inputs(inputs, b):
    ca = np.ascontiguousarray
    m = {}
    m["xseq"] = ca(np.concatenate([inputs['ctx'][b], inputs['x'][b]], axis=0))
    m["cs"] = rope_table()
    m["flg0"] = make_flags(0)
    m["flg1"] = make_flags(1)
    for l in range(2):
        p = f"l{l}_"
        c2 = np.stack([inputs['c'][b], inputs['c_ctx']], axis=0)
        m[p + "cT2"] = ca(c2.reshape(2, 16, 128).transpose(2, 1, 0)).reshape(128, 32)
        m[p + "wmod"] = inputs['w_mod'][l]
        m[p + "bmod"] = ca(inputs['b_mod'][l].reshape(96, 128).T)
        m[p + "gn1"] = ca(inputs['norm1_gain'][l].reshape(16, 128).T)
        m[p + "gn2"] = ca(inputs['norm2_gain'][l].reshape(16, 128).T)
        m[p + "win"] = inputs['w_in'][l]
        m[p + "cw"] = ca(inputs['ssd_conv_w'][l].T.reshape(16, 128, 5).transpose(1, 0, 2)).reshape(128, 80)
        m[p + "cb"] = ca(inputs['ssd_conv_b'][l].reshape(16, 128).T)
        abd = np.stack([inputs['ssd_a_log'][l], inputs['ssd_dt_bias'][l], inputs['ssd_d'][l]], 0)
        m[p + "abd"] = rep128(abd.reshape(-1))
        m[p + "gqa"] = ca(inputs['mla_q_a_gain'][l].reshape(4, 128).T)
        m[p + "gkv"] = ca(inputs['mla_kv_a_gain'][l].reshape(2, 128).T)
        m[p + "wq"] = inputs['mla_w_q_b'][l]
        m[p + "wkv"] = inputs['mla_w_kv_b'][l]
        m[p + "qg"] = rep128(inputs['mla_q_gain'][l])
        m[p + "kg"] = rep128(inputs['mla_k_gain'][l])
        lamp, lamr, bTs, cbds = [], [], [], []
        for d in range(2):
            lre = inputs['s5_lam_re'][l, d]
            lim = inputs['s5_lam_im'][l, d]
            ldt = np.broadcast_to(inputs['s5_log_dt'][l, d][:, None], (64, 64))
            P = lambda a: a.reshape(32, 128).T
            Rl = lambda a: np.broadcast_to(a.reshape(1, 4096), (32, 4096))
            lamp += [P(lre), P(lim), P(ldt)]
            lamr += [Rl(lre), Rl(lim), Rl(ldt)]
            for bm in (inputs['s5_b_re'][l, d], inputs['s5_b_im'][l, d]):
                o = np.zeros((2, 16, 32, 2, 64), np.float32)
                bb = bm.reshape(32, 2, 64, 16)
                for gg in range(2):
                    o[gg, :, :, gg, :] = bb[:, gg].transpose(2, 0, 1)
                bTs.append(o.reshape(32, 4096))
            for cm in (inputs['s5_c_re'][l, d], inputs['s5_c_im'][l, d]):
                o = np.zeros((2, 64, 32, 2, 16), np.float32)
                cc = cm.reshape(32, 2, 16, 64)
                for gg in range(2):
                    o[gg, :, :, gg, :] = cc[:, gg].transpose(2, 0, 1)
                cbds.append(o.reshape(128, 1024))
        m[p + "lamp"] = ca(np.concatenate(lamp, axis=1), dtype=np.float32)
        m[p + "lamr"] = ca(np.concatenate(lamr, axis=1), dtype=np.float32)
        m[p + "bT"] = ca(np.concatenate(bTs, axis=1))
        m[p + "cbd"] = ca(np.concatenate(cbds, axis=1))
        m[p + "gssd"] = ca(inputs['ssd_norm_gain'][l].reshape(8, 128).T)
        m[p + "s5d"] = ca(inputs['s5_d'][l].reshape(8, 128).T)
        m[p + "wglu"] = inputs['s5_w_glu'][l]
        m[p + "wbs"] = inputs['w_branch_ssd'][l]
        m[p + "wbm"] = inputs['w_branch_mla'][l]
        m[p + "wb5"] = inputs['w_branch_s5'][l]
        m[p + "wo"] = inputs['w_out'][l]
        m[p + "wr"] = inputs['moe_router'][l]
        m[p + "wg"] = inputs['moe_w_gate'][l]
        m[p + "wu"] = inputs['moe_w_up'][l]
        m[p + "wd"] = inputs['moe_w_down'][l]
    return {k: np.ascontiguousarray(v, dtype=np.float32) for k, v in m.items() if k in FUSED_INPUT_SPECS}


def kernel(**inputs):
    inputs = {k: np.asarray(v, dtype=np.float32) for k, v in inputs.items()}
    nc = build_fused()
    maps = [fused_inputs(inputs, b) for b in range(NB)]
    in_maps = [maps[c % NB] for c in range(NCORES_F)]
    res = run_bass_kernel_spmd(nc, in_maps, core_ids=list(range(NCORES_F)))
    return np.stack([res.results[b]["out"] for b in range(NB)], axis=0).astype(np.float32)
```

```python
from contextlib import ExitStack
import numpy as np
import concourse.bass as bass
import concourse.mybir as mybir
from concourse.bass_utils import run_bass_kernel_spmd

F32 = mybir.dt.float32
F32R = mybir.dt.float32r


def R(ap):
    return ap.bitcast(F32R)
I32 = mybir.dt.int32
U32 = mybir.dt.uint32
AF = mybir.ActivationFunctionType
ALU = mybir.AluOpType
AX = mybir.AxisListType

D = 2048
NB = 4
SEQ = 2048
CTX = 256
LT = SEQ + CTX
EPS = 1e-6
PROJ_IN = 11056
NCORES = 8
NCORES_F = 4


_QMAP = {}
SAME_ENGINE_WAITS = False


class Sched:
    NDS = 8

    def __init__(self, nc, ctx):
        self.nc = nc
        self.ctx = ctx
        self.E = {'pe': nc.tensor, 'act': nc.scalar, 'dve': nc.vector, 'pool': nc.gpsimd, 'sp': nc.sync}
        self.sem = {k: ctx.enter_context(nc.semaphore('s_' + k)) for k in ['pe', 'act', 'dve', 'pool']}
        self.cnt = {k: 0 for k in self.sem}
        self.seen = {e: {} for e in self.E}
        self.dsem = {q: [ctx.enter_context(nc.semaphore(f'd_{q}{i}')) for i in range(self.NDS)]
                     for q in ['sp', 'pool', 'act']}
        self.dcnt = {q: 0 for q in self.dsem}
        self.last_w = {}
        self.readers = {}
        self.ps = [ctx.enter_context(nc.psum_tensor(f'ps{i}', [128, 512], F32)) for i in range(8)]
        self.psi = 0
        self.nrot = 8
        self.stage_id = 0
        self.rec = None
        self.lane = 0
        self.qmap = dict(_QMAP)
        self.fused = False
        self.out_tokens = []

    def sb(self, name, shape, dt=F32):
        return self.ctx.enter_context(self.nc.sbuf_tensor(f"s{self.stage_id}_{name}", list(shape), dt))

    def round_r(self, eng, ap, keys):
        if eng == 'act':
            self.op('act', lambda e: e.activation(out=R(ap), in_=ap, func=AF.Copy), r=keys, w=keys)
        else:
            self.op(eng, lambda e: e.tensor_copy(out=R(ap), in_=ap), r=keys, w=keys)

    def barrier(self):
        for e in self.E:
            for f in self.sem:
                if f != e and self.cnt[f] > 0:
                    self._wait(e, (self.sem[f], self.cnt[f], f))
            for q in self.dsem:
                n = self.dcnt[q]
                for i in range(self.NDS):
                    k = (n - i + self.NDS - 1) // self.NDS if n > i else 0
                    if k > 0:
                        self._wait(e, (self.dsem[q][i], 16 * k, 'dma_' + q))
        self.last_w = {}
        self.readers = {}
        self.out_tokens = []

    def next_ps(self):
        i = self.psi
        self.psi = (self.psi + 1) % self.nrot
        return self.ps[i], ('ps', i)

    def _wait(self, e, tok):
        sem, val, src = tok
        if src == e and e == 'pe':
            return
        sid = id(sem)
        if self.seen[e].get(sid, 0) >= val:
            return
        self.E[e].wait_ge(sem, val)
        self.seen[e][sid] = val

    def _deps(self, e, r, w):
        toks = []
        for k in r:
            if k in self.last_w:
                toks.append(self.last_w[k])
            if isinstance(k, tuple) and k[0] == 'ps':
                toks.extend(t for t in self.readers.get(k, {}).values() if t[2] != e)
        for k in w:
            if k in self.last_w:
                toks.append(self.last_w[k])
            toks.extend(self.readers.get(k, {}).values())
        for t in toks:
            self._wait(e, t)

    def _record(self, tok, r, w):
        for k in r:
            d = self.readers.setdefault(k, {})
            sid = id(tok[0])
            if sid not in d or d[sid][1] < tok[1]:
                d[sid] = tok
        for k in w:
            self.last_w[k] = tok
            self.readers[k] = {}

    def rec_lane(self, lane):
        if self.rec is None:
            self.rec = {}
        self.lane = lane
        self.rec.setdefault(lane, [])

    def rec_flush(self):
        rec, self.rec = self.rec, None
        if not rec:
            return
        lanes = [rec[k] for k in sorted(rec)]
        for i in range(max(len(l) for l in lanes)):
            for l in lanes:
                if i < len(l):
                    it = l[i]
                    if it[0] == 'op':
                        _, e, name, a, kw, r, w = it
                        self.op(e, lambda eng: getattr(eng, name)(*a, **kw), r=r, w=w)
                    else:
                        _, q, out, in_, r, w, is_out, kw = it
                        self.dma(q, out, in_, r=r, w=w, is_out=is_out, **kw)

    def op(self, e, fn, r=(), w=()):
        if self.rec is not None:
            cap = []

            class _P:
                def __getattr__(self_, name):
                    def f(*a, **kw):
                        cap.append((name, a, kw))
                        return None
                    return f
            fn(_P())
            assert len(cap) == 1
            name, a, kw = cap[0]
            self.rec[self.lane].append(('op', e, name, a, kw, list(r), list(w)))
            return None
        self._deps(e, r, w)
        ins = fn(self.E[e])
        self.cnt[e] += 1
        ins.then_inc(self.sem[e], 1)
        tok = (self.sem[e], self.cnt[e], e)
        self._record(tok, r, w)
        return tok

    def _guard_dma(self, q):
        n = self.dcnt[q]
        sem = self.dsem[q][n % self.NDS]
        prev = 16 * (n // self.NDS)
        if prev > 0 and self.seen[q].get(id(sem), 0) < prev:
            self.E[q].wait_ge(sem, prev)
            self.seen[q][id(sem)] = prev
        return sem, prev

    def _finish_dma(self, q, ins, r, w):
        n = self.dcnt[q]
        sem = self.dsem[q][n % self.NDS]
        prev = 16 * (n // self.NDS)
        ins.then_inc(sem, 16)
        self.dcnt[q] += 1
        tok = (sem, prev + 16, 'dma_' + q)
        self._record(tok, r, w)
        return tok

    def dma(self, q, out, in_, r=(), w=(), is_out=False, **kw):
        if self.rec is not None:
            self.rec[self.lane].append(('dma', q, out, in_, list(r), list(w), is_out, kw))
            return None
        q = self.qmap.get(q, q)
        self._deps(q, r, w)
        self._guard_dma(q)
        ins = self.E[q].dma_start(out=out, in_=in_, **kw)
        tok = self._finish_dma(q, ins, r, w)
        if is_out:
            self.out_tokens.append(tok)
        return tok

    def finish(self):
        if self.fused:
            self.barrier()
            return
        for t in self.out_tokens:
            self._wait('sp', t)


_F = {'nc': None, 'S': None, 'io': {}}


def new_nc():
    if _F['nc'] is not None:
        return _F['nc']
    return bass.Bass("TRN2", target_bir_lowering=False)


def _io(nc, name, shape, dt, kind):
    if _F['nc'] is not None:
        ap = _F['io'][name]
        assert tuple(ap.shape) == tuple(shape), (name, ap.shape, shape)
        return ap
    return nc.dram_tensor(name, list(shape), dt, kind=kind).ap()


def din(nc, name, shape, dt=F32):
    return _io(nc, name, shape, dt, "ExternalInput")


def dout(nc, name, shape, dt=F32):
    return _io(nc, name, shape, dt, "ExternalOutput")


def get_sched(nc, ctx):
    if _F['S'] is None:
        return Sched(nc, ctx)
    S = _F['S']
    S.ctx = ctx
    S.stage_id += 1
    S.nrot = 8
    S.psi = 0
    return S


def make_ident(S, name='ident'):
    nc = S.nc
    idt = S.sb(name, [128, 128])
    S.op('pool', lambda e: e.memset(idt[:], 1.0), w=[name])
    S.op('pool', lambda e: e.affine_select(out=idt[:], in_=idt[:], pattern=[[-1, 128]], compare_op=ALU.is_equal,
                                           fill=0.0, base=0, channel_multiplier=1), r=[name], w=[name])
    return idt


def build_A():
    nc = new_nc()
    cT = din(nc, "cT", [128, 16 * 5])
    w = din(nc, "w", [D, 1536])
    b = din(nc, "b", [128, 12])
    o = dout(nc, "o", [128, 60])
    with ExitStack() as ctx:
        S = get_sched(nc, ctx)
        ct = S.sb('ct', [128, 16, 5])
        ca = S.sb('ca', [128, 16, 5])
        bt = S.sb('bt', [128, 12])
        wt = S.sb('wt', [128, 16, 1536])
        ot = S.sb('ot', [128, 12, 5])
        S.dma('sp', ct[:].rearrange("p k r -> p (k r)"), cT[:, :], w=['ct'])
        S.dma('sp', bt[:], b[:, :], w=['bt'])
        wv = w.rearrange("(k p) n -> p k n", p=128)
        for g in range(8):
            q = 'sp' if g % 2 == 0 else 'pool'
            S.dma(q, wt[:, 2 * g:2 * g + 2, :], wv[:, 2 * g:2 * g + 2, :], w=[('wt', g)])
        S.op('act', lambda e: e.activation(out=ca[:], in_=ct[:], func=AF.Silu), r=['ct'], w=['ca'])
        ps, pk = S.next_ps()
        for m in range(12):
            for k in range(16):
                S.op('pe', lambda e: e.matmul(ps[:, m * 5:m * 5 + 5], lhsT=wt[:, k, m * 128:(m + 1) * 128],
                                              rhs=ca[:, k, :], start=(k == 0), stop=(k == 15)),
                     r=['ca', ('wt', k // 2)], w=[pk])
        for m in range(12):
            S.op('act', lambda e: e.activation(out=ot[:, m, :], in_=ps[:, m * 5:m * 5 + 5], func=AF.Identity,
                                               bias=bt[:, m:m + 1], scale=1.0), r=[pk, 'bt'], w=['ot'])
        S.dma('sp', o[:, :], ot[:].rearrange("p m r -> p (m r)"), r=['ot'], is_out=True)
        S.finish()
    return nc


def run_A(inputs, layer):
    c5 = np.concatenate([inputs['c'], inputs['c_ctx'][None, :]], axis=0)
    cT = np.ascontiguousarray(c5.reshape(5, 16, 128).transpose(2, 1, 0)).reshape(128, 80)
    wm = inputs['w_mod'][layer]
    bm = inputs['b_mod'][layer]
    in_maps = []
    for j in range(NCORES):
        in_maps.append({"cT": cT, "w": np.ascontiguousarray(wm[:, j * 1536:(j + 1) * 1536]),
                        "b": np.ascontiguousarray(bm[j * 1536:(j + 1) * 1536].reshape(12, 128).T)})
    res = run_bass_kernel_spmd(build_A(), in_maps, core_ids=list(range(NCORES)))
    modT = np.concatenate([r["o"].reshape(128, 12, 5) for r in res.results], axis=1)
    return modT


NT_B = 9
B_BLOCKS = ([(0, 512, 'tm'), (512, 1024, 'tm')] + [(1024 + 512 * i, 1536 + 512 * i, 'fm') for i in range(4)]
            + [(3072, 3584, 'tm'), (3584, 3888, 'tm'), (3888, 4400, 'fm'), (4400, 4912, 'fm')]
            + [(4912 + 512 * i, 5424 + 512 * i, 'fm') for i in range(12)])


def tm_col(c):
    return c if c < 1024 else c - 2048


def fm_row(c):
    return c - 1024 if c < 3072 else c - 1840


def norm_mod_tiles(S, xin, NT, ms, g1, hT, ident, pref, rr=False):
    xb = [S.sb(f'{pref}xb{i}', [128, D]) for i in range(2)]
    junk = S.sb(pref + 'junk', [128, D])
    ss = S.sb(pref + 'ss', [128, NT])
    rs = S.sb(pref + 'rs', [128, NT])
    S.op('dve', lambda e: e.memset(ss[:], 0.0), w=[pref + 'ss'])
    for t in range(NT):
        xt = xb[t % 2]
        xk = (pref + 'xb', t % 2)
        S.dma('sp', xt[:], xin[t * 128:(t + 1) * 128, :], w=[xk])
        S.op('act', lambda e: e.activation(out=junk[:], in_=xt[:], func=AF.Square, accum_out=ss[:, t:t + 1]),
             r=[xk], w=[pref + 'junk', pref + 'ss'])
        S.op('dve', lambda e: e.tensor_scalar(out=rs[:, t:t + 1], in0=ss[:, t:t + 1], scalar1=1.0 / D, scalar2=EPS,
                                              op0=ALU.mult, op1=ALU.add), r=[pref + 'ss'], w=[pref + 'rs'])
        S.op('act', lambda e: e.activation(out=rs[:, t:t + 1], in_=rs[:, t:t + 1], func=AF.Sqrt),
             r=[pref + 'rs'], w=[pref + 'rs'])
        S.op('dve', lambda e: e.reciprocal(out=rs[:, t:t + 1], in_=rs[:, t:t + 1]), r=[pref + 'rs'], w=[pref + 'rs'])
        S.op('dve', lambda e: e.tensor_scalar(out=xt[:], in0=xt[:], scalar1=rs[:, t:t + 1], scalar2=None,
                                              op0=ALU.mult), r=[xk, pref + 'rs'], w=[xk])
        for kk in range(4):
            ps, pk = S.next_ps()
            for j in range(4):
                k = kk * 4 + j
                S.op('pe', lambda e: e.transpose(out=ps[:, j * 128:(j + 1) * 128], in_=xt[:, k * 128:(k + 1) * 128],
                                                 identity=ident[:]), r=[xk, 'ident'], w=[pk])
            for j in range(4):
                k = kk * 4 + j
                S.op('act', lambda e: e.activation(out=(R(hT[:, k, t * 128:(t + 1) * 128]) if rr else hT[:, k, t * 128:(t + 1) * 128]), in_=ps[:, j * 128:(j + 1) * 128],
                                                   func=AF.Identity, scale=g1[:, t, k:k + 1], bias=ms[:, t, 1, k:k + 1]),
                     r=[pk, 'g1', 'ms'], w=[('hT', t)])


def load_mod(S, msel, gn, NT):
    ms = S.sb('ms', [128, NT, 2, 16])
    gnt = S.sb('gnt', [128, 16])
    g1 = S.sb('g1', [128, NT, 16])
    S.dma('sp', ms[:].rearrange("p t s k -> p (t s k)"), msel[:, :], w=['ms'])
    S.dma('sp', gnt[:], gn[:, :], w=['gnt'])
    for t in range(NT):
        S.op('dve', lambda e: e.scalar_tensor_tensor(out=g1[:, t, :], in0=ms[:, t, 0, :], scalar=1.0, in1=gnt[:],
                                                     op0=ALU.add, op1=ALU.mult), r=['ms', 'gnt'], w=['g1'])
    return ms, g1


def build_B():
    nc = new_nc()
    NT = NT_B
    NTOK = NT * 128
    xin = din(nc, "x", [NTOK, D])
    msel = din(nc, "msel", [128, NT * 16 * 2])
    gn = din(nc, "gn", [128, 16])
    w = din(nc, "w", [D, PROJ_IN])
    otm = dout(nc, "otm", [NTOK, 1840])
    ofm = dout(nc, "ofm", [9216, NTOK])
    with ExitStack() as ctx:
        S = get_sched(nc, ctx)
        ident = make_ident(S)
        hT = S.sb('hT', [128, 16, NTOK])
        ms, g1 = load_mod(S, msel, gn, NT)
        norm_mod_tiles(S, xin, NT, ms, g1, hT, ident, 'n1', rr=True)
        wbuf = [S.sb(f'wb{i}', [128, 16, 512]) for i in range(2)]
        obuf = [S.sb(f'ob{i}', [128, 512]) for i in range(4)]
        wv = w.rearrange("(k p) n -> p k n", p=128)
        oi = 0
        hkeys = [('hT', t) for t in range(NT)]
        for bi, (c0, c1, kind) in enumerate(B_BLOCKS):
            nw = c1 - c0
            wb = wbuf[bi % 2]
            S.dma('pool', R(wb[:, 0:8, :nw]), wv[:, 0:8, c0:c1], w=[('wb', bi % 2, 0)])
            S.dma('pool', R(wb[:, 8:16, :nw]), wv[:, 8:16, c0:c1], w=[('wb', bi % 2, 1)])
            if kind == 'tm':
                jobs = [('tm', t, None) for t in range(NT)]
            else:
                jobs = [('fm', m, rng) for m in range(nw // 128) for rng in [(0, 512), (512, 1024), (1024, NTOK)]]
            for kind_, a, rng in jobs:
                ps, pk = S.next_ps()
                if kind_ == 'tm':
                    t = a
                    n = nw
                    for k in range(16):
                        S.op('pe', lambda e: e.matmul(ps[:, :nw], lhsT=R(hT[:, k, t * 128:(t + 1) * 128]), rhs=R(wb[:, k, :nw]),
                                                      start=(k == 0), stop=(k == 15)),
                             r=[('hT', t), ('wb', bi % 2, k // 8)], w=[pk])
                    dst = otm[t * 128:(t + 1) * 128, tm_col(c0):tm_col(c0) + nw]
                else:
                    m = a
                    n0, n1 = rng
                    n = n1 - n0
                    for k in range(16):
                        S.op('pe', lambda e: e.matmul(ps[:, :n], lhsT=R(wb[:, k, m * 128:(m + 1) * 128]), rhs=R(hT[:, k, n0:n1]),
                                                      start=(k == 0), stop=(k == 15)),
                             r=hkeys + [('wb', bi % 2, k // 8)], w=[pk])
                    fr = fm_row(c0) + m * 128
                    dst = ofm[fr:fr + 128, n0:n1]
                ob = obuf[oi % 4]
                ok = ('ob', oi % 4)
                if oi % 2 == 0:
                    S.op('act', lambda e: e.activation(out=ob[:, :n], in_=ps[:, :n], func=AF.Copy), r=[pk], w=[ok])
                else:
                    S.op('dve', lambda e: e.tensor_copy(out=ob[:, :n], in_=ps[:, :n]), r=[pk], w=[ok])
                S.dma('sp', dst, ob[:, :n], r=[ok], is_out=True)
                oi += 1
        S.finish()
    return nc


def core_rows(b, h):
    return (0, 1152) if h == 0 else (1152, 2304)


def mod_rows(modT, b, which):
    sl = modT[:, which * 16:(which + 1) * 16, :]
    return sl[:, :, b], sl[:, :, 4]


def tile_is_ctx(h, t):
    return h == 0 and t < 2


def make_msel(modT, b, h, sc_i, sh_i, NT=9):
    scx, scc = mod_rows(modT, b, sc_i)
    shx, shc = mod_rows(modT, b, sh_i)
    ms = np.zeros((128, NT, 2, 16), np.float32)
    for t in range(NT):
        if tile_is_ctx(h, t):
            ms[:, t, 0, :] = scc
            ms[:, t, 1, :] = shc
        else:
            ms[:, t, 0, :] = scx
            ms[:, t, 1, :] = shx
    return ms.reshape(128, -1)


def run_B(xseq, modT, inputs, layer):
    gn = np.ascontiguousarray(inputs['norm1_gain'][layer].reshape(16, 128).T)
    w = inputs['w_in'][layer]
    in_maps = []
    for c in range(NCORES):
        b, h = c // 2, c % 2
        r0, r1 = core_rows(b, h)
        in_maps.append({"x": np.ascontiguousarray(xseq[b, r0:r1]), "msel": make_msel(modT, b, h, 1, 0),
                        "gn": gn, "w": w})
    res = run_bass_kernel_spmd(build_B(), in_maps, core_ids=list(range(NCORES)))
    px_tm = np.zeros((NB, LT, 1840), np.float32)
    px_fm = np.zeros((NB, 9216, LT), np.float32)
    for c in range(NCORES):
        b, h = c // 2, c % 2
        r0, r1 = core_rows(b, h)
        px_tm[b, r0:r1] = res.results[c]["otm"]
        px_fm[b, :, r0:r1] = res.results[c]["ofm"]
    return px_tm, px_fm


def bc(ap, axis, shape):
    return ap.unsqueeze(axis).to_broadcast(list(shape))


def make_masks(S, transposed=False):
    sfx = 'T' if transposed else ''
    cm, pm = (1, -1) if transposed else (-1, 1)
    U8 = S.sb('U8' + sfx, [128, 8, 128])
    ones = S.sb('ones' + sfx, [128, 128])
    nm8 = S.sb('nm8' + sfx, [128, 8, 128])
    S.op('pool', lambda e: e.memset(ones[:], 1.0), w=['ones' + sfx])
    S.op('pool', lambda e: e.memset(U8[:], 1.0), w=['U8' + sfx])
    S.op('pool', lambda e: e.affine_select(out=U8[:], in_=U8[:], pattern=[[0, 8], [pm, 128]], compare_op=ALU.is_ge,
                                           fill=0.0, base=0, channel_multiplier=cm), r=['U8' + sfx], w=['U8' + sfx])
    S.op('pool', lambda e: e.memset(nm8[:], 0.0), w=['nm8' + sfx])
    S.op('pool', lambda e: e.affine_select(out=nm8[:], in_=nm8[:], pattern=[[0, 8], [pm, 128]], compare_op=ALU.is_ge,
                                           fill=-30000.0, base=0, channel_multiplier=cm), r=['nm8' + sfx], w=['nm8' + sfx])
    return U8, ones, nm8


NTL = LT // 128


def build_C1():
    nc = new_nc()
    xbc = din(nc, "xbc", [2048, LT])
    dtin = din(nc, "dt", [LT, 16])
    cw = din(nc, "cw", [128, 16 * 5])
    cb = din(nc, "cb", [128, 16])
    abd = din(nc, "abd", [128, 96])
    yo = [dout(nc, "y0", [LT, 1024]), dout(nc, "y1", [LT, 1024])]
    with ExitStack() as ctx:
        S = get_sched(nc, ctx)
        ident = make_ident(S)
        U8, ones, nm8 = make_masks(S)
        U8T, _, nm8T = make_masks(S, transposed=True)
        cwt = S.sb('cwt', [128, 16, 5])
        cbt = S.sb('cbt', [128, 16])
        abt = S.sb('abt', [128, 3, 2, 16])
        S.dma('sp', cwt[:].rearrange("p c j -> p (c j)"), cw[:, :], w=['cwt'])
        S.dma('sp', cbt[:], cb[:, :], w=['cbt'])
        S.dma('sp', abt[:].rearrange("p a d h -> p (a d h)"), abd[:, :], w=['abt'])
        dt_all = S.sb('dt_all', [128, NTL, 16])
        dtv = S.sb('dtv', [128, 2, NTL, 16])
        a_all = S.sb('a_all', [128, 2, NTL, 16])
        aneg = S.sb('aneg', [128, 2, 16])
        S.dma('sp', dt_all[:], dtin.rearrange("(t p) h -> p t h", p=128), w=['dt_all'])
        S.op('act', lambda e: e.activation(out=aneg[:], in_=abt[:, 0, :, :], func=AF.Exp), r=['abt'], w=['aneg'])
        S.op('dve', lambda e: e.tensor_scalar(out=aneg[:], in0=aneg[:], scalar1=-1.0, scalar2=None, op0=ALU.mult),
             r=['aneg'], w=['aneg'])
        for d in range(2):
            S.op('dve', lambda e: e.tensor_tensor(out=dtv[:, d], in0=dt_all[:], in1=bc(abt[:, 1, d, :], 1, [128, NTL, 16]),
                                                  op=ALU.add), r=['dt_all', 'abt'], w=['dtv'])
            S.op('act', lambda e: e.activation(out=dtv[:, d], in_=dtv[:, d], func=AF.Exp), r=['dtv'], w=['dtv'])
            S.op('act', lambda e: e.activation(out=dtv[:, d], in_=dtv[:, d], func=AF.Ln, bias=1.0, scale=1.0), r=['dtv'], w=['dtv'])
            S.op('dve', lambda e: e.tensor_tensor(out=a_all[:, d], in0=dtv[:, d], in1=bc(aneg[:, d, :], 1, [128, NTL, 16]), op=ALU.mult),
                 r=['dtv', 'aneg'], w=['a_all'])

        pb = [S.sb(f'pb{i}', [128, LT + 8]) for i in range(2)]
        for i in range(2):
            S.op('pool', lambda e: e.memset(pb[i][:], 0.0), w=[('pb', i)])
        acc = S.sb('acc', [128, LT])
        tmpx = S.sb('tmpx', [128, LT])
        BT = S.sb('BT', [128, 2, LT])
        CT = S.sb('CT', [128, 2, LT])
        x_tm = S.sb('x_tm', [128, NTL, 512])
        B_tm = S.sb('B_tm', [128, NTL, 256])
        y_acc = S.sb('y_acc', [128, NTL, 512])
        hst = S.sb('hst', [128, 512])
        sm = S.sb('sm', [128, 64])
        aU = S.sb('aU', [128, 8, 128])
        tmp = S.sb('tmp', [128, 8, 128])
        MT = S.sb('MT', [128, 8, 128])
        xdt = S.sb('xdt', [128, 8, 64])
        xw = S.sb('xw', [128, 8, 64])
        ydsb = S.sb('ydsb', [128, 8, 64])
        segs = [(0, 0, 256), (260, 256, 2048)]
        ci_glob = 0
        for hf in range(2):
            chunks = ([('x', i, 4 * hf + i) for i in range(4)] + [('B', i, 8 + 2 * hf + i) for i in range(2)]
                      + [('C', i, 12 + 2 * hf + i) for i in range(2)])
            for kind, i, ch in chunks:
                p = pb[ci_glob % 2]
                pk = ('pb', ci_glob % 2)
                ci_glob += 1
                S.dma('sp', p[:, 2:258], xbc[ch * 128:(ch + 1) * 128, 0:256], w=[pk])
                S.dma('sp', p[:, 262:2310], xbc[ch * 128:(ch + 1) * 128, 256:LT], w=[pk])
                for (oi, oo, n) in segs:
                    S.op('dve', lambda e: e.tensor_scalar(out=acc[:, oo:oo + n], in0=p[:, oi:oi + n],
                                                          scalar1=cwt[:, ch, 0:1], scalar2=None, op0=ALU.mult),
                         r=[pk, 'cwt'], w=['acc'])
                    for j in range(1, 5):
                        S.op('dve', lambda e: e.scalar_tensor_tensor(out=acc[:, oo:oo + n], in0=p[:, oi + j:oi + j + n],
                                                                     scalar=cwt[:, ch, j:j + 1], in1=acc[:, oo:oo + n],
                                                                     op0=ALU.mult, op1=ALU.add),
                             r=[pk, 'cwt', 'acc'], w=['acc'])
                if kind == 'x':
                    dst, dk = tmpx[:], 'tmpx'
                elif kind == 'B':
                    dst, dk = BT[:, i, :], ('BT', i)
                else:
                    dst, dk = CT[:, i, :], ('CT', i)
                S.op('act', lambda e: e.activation(out=dst, in_=acc[:], func=AF.Silu, bias=cbt[:, ch:ch + 1], scale=1.0),
                     r=['acc', 'cbt'], w=[dk])
                if kind in ('x', 'B'):
                    for t0 in range(0, NTL, 4):
                        nt = min(4, NTL - t0)
                        ps, pk2 = S.next_ps()
                        for tt in range(nt):
                            t = t0 + tt
                            S.op('pe', lambda e: e.transpose(out=ps[:, tt * 128:(tt + 1) * 128],
                                                             in_=dst[:, t * 128:(t + 1) * 128], identity=ident[:]),
                                 r=[dk, 'ident'], w=[pk2])
                        if kind == 'x':
                            o_ap = x_tm[:, t0:t0 + nt, i * 128:(i + 1) * 128]
                            ok = 'x_tm'
                        else:
                            o_ap = B_tm[:, t0:t0 + nt, i * 128:(i + 1) * 128]
                            ok = 'B_tm'
                        S.op('act', lambda e: e.activation(out=o_ap, in_=ps[:, 0:nt * 128].rearrange("p (t c) -> p t c", c=128),
                                                           func=AF.Copy), r=[pk2], w=[ok])
            h0 = hf * 8
            for d in range(2):
              Um, nmm = (U8, nm8) if d == 0 else (U8T, nm8T)
              U = Um[:, 0, :]
              order = list(range(NTL)) if d == 0 else [1, 0] + list(range(NTL - 1, 1, -1))
              if True:
                  S.op('dve', lambda e: e.tensor_tensor(
                      out=y_acc[:].rearrange("p t (h d) -> p t h d", d=64), in0=x_tm[:].rearrange("p t (h d) -> p t h d", d=64),
                      in1=abt[:, 2, d, h0:h0 + 8].unsqueeze(1).unsqueeze(3).to_broadcast([128, NTL, 8, 64]), op=ALU.mult),
                      r=['x_tm', 'abt'], w=['y_acc'])
                  S.op('dve', lambda e: e.memset(hst[:], 0.0), w=['hst'])
                  for t in order:
                      tsl = slice(t * 128, (t + 1) * 128)
                      a_t = a_all[:, d, t, h0:h0 + 8]
                      dtv_t = dtv[:, d, t, h0:h0 + 8]
                      psA, kA = S.next_ps()
                      S.op('pe', lambda e: e.matmul(psA[:, 0:8], lhsT=U, rhs=a_t, start=True, stop=True),
                           r=['U8', 'U8T', 'a_all'], w=[kA])
                      S.op('pe', lambda e: e.matmul(psA[:, 8:16], lhsT=ones[:], rhs=a_t, start=True, stop=True),
                           r=['ones', 'a_all'], w=[kA])
                      S.op('dve', lambda e: e.tensor_copy(out=sm[:, 0:16], in_=psA[:, 0:16]), r=[kA], w=['sm'])
                      S.op('dve', lambda e: e.tensor_tensor(out=sm[:, 24:32], in0=sm[:, 8:16], in1=sm[:, 0:8], op=ALU.subtract),
                           r=['sm'], w=['sm'])
                      S.op('act', lambda e: e.activation(out=sm[:, 16:24], in_=sm[:, 0:8], func=AF.Exp), r=['sm'], w=['sm'])
                      S.op('act', lambda e: e.activation(out=sm[:, 24:32], in_=sm[:, 24:32], func=AF.Exp), r=['sm'], w=['sm'])
                      S.op('act', lambda e: e.activation(out=sm[:, 32:40], in_=sm[:, 8:16], func=AF.Exp), r=['sm'], w=['sm'])
                      S.op('dve', lambda e: e.tensor_tensor(out=sm[:, 24:32], in0=sm[:, 24:32], in1=dtv_t, op=ALU.mult),
                           r=['sm', 'dtv'], w=['sm'])
                      S.op('dve', lambda e: e.tensor_tensor(out=aU[:], in0=Um[:], in1=bc(a_t, 2, [128, 8, 128]), op=ALU.mult),
                           r=['U8', 'U8T', 'a_all'], w=['aU'])
                      psB = []
                      for q in range(2):
                          pq, kq = S.next_ps()
                          S.op('pe', lambda e: e.matmul(pq[:, :], lhsT=ones[:], rhs=aU[:, 4 * q:4 * q + 4, :].rearrange("p h l -> p (h l)"),
                                                        start=True, stop=True), r=['ones', 'aU'], w=[kq])
                          psB.append((pq, kq))
                      for q in range(2):
                          pq, kq = psB[q]
                          S.op('dve', lambda e: e.tensor_tensor(out=tmp[:, 4 * q:4 * q + 4, :],
                                                                in0=pq[:, :].rearrange("p (h l) -> p h l", l=128),
                                                                in1=bc(sm[:, 4 * q:4 * q + 4], 2, [128, 4, 128]), op=ALU.subtract),
                               r=[kq, 'sm'], w=['tmp'])
                      S.op('dve', lambda e: e.tensor_tensor(out=tmp[:], in0=tmp[:], in1=nmm[:], op=ALU.add),
                           r=['tmp', 'nm8', 'nm8T'], w=['tmp'])
                      S.op('act', lambda e: e.activation(out=tmp[:], in_=tmp[:], func=AF.Exp), r=['tmp'], w=['tmp'])
                      psS, kS = S.next_ps()
                      for gi in range(2):
                          S.op('pe', lambda e: e.matmul(psS[:, gi * 128:(gi + 1) * 128], lhsT=BT[:, gi, tsl], rhs=CT[:, gi, tsl],
                                                        start=True, stop=True), r=[('BT', gi), ('CT', gi)], w=[kS])
                      for gi in range(2):
                          S.op('dve', lambda e: e.tensor_tensor(out=MT[:, 4 * gi:4 * gi + 4, :], in0=tmp[:, 4 * gi:4 * gi + 4, :],
                                                                in1=bc(psS[:, gi * 128:(gi + 1) * 128], 1, [128, 4, 128]),
                                                                op=ALU.mult), r=['tmp', kS], w=['MT'])
                      xt3 = x_tm[:, t, :].rearrange("p (h d) -> p h d", d=64)
                      S.op('dve', lambda e: e.tensor_tensor(out=xdt[:], in0=xt3, in1=bc(dtv_t, 2, [128, 8, 64]), op=ALU.mult),
                           r=['x_tm', 'dtv'], w=['xdt'])
                      S.op('dve', lambda e: e.tensor_tensor(out=xw[:], in0=xt3, in1=bc(sm[:, 24:32], 2, [128, 8, 64]), op=ALU.mult),
                           r=['x_tm', 'sm'], w=['xw'])
                      psY, kY = S.next_ps()
                      for hh in range(8):
                          S.op('pe', lambda e: e.matmul(psY[:, hh * 64:(hh + 1) * 64], lhsT=MT[:, hh, :], rhs=xdt[:, hh, :],
                                                        start=True, stop=True), r=['MT', 'xdt'], w=[kY])
                      psO, kO = S.next_ps()
                      for gi in range(2):
                          S.op('pe', lambda e: e.matmul(psO[:, gi * 256:(gi + 1) * 256], lhsT=CT[:, gi, tsl],
                                                        rhs=hst[:, gi * 256:(gi + 1) * 256], start=True, stop=True),
                               r=[('CT', gi), 'hst'], w=[kO])
                      S.op('act', lambda e: e.activation(out=ydsb[:].rearrange("p h d -> p (h d)"), in_=psY[:, :], func=AF.Copy),
                           r=[kY], w=['ydsb'])
                      S.op('dve', lambda e: e.tensor_tensor(out=xdt[:], in0=psO[:, :].rearrange("p (h d) -> p h d", d=64),
                                                            in1=bc(sm[:, 16:24], 2, [128, 8, 64]), op=ALU.mult),
                           r=[kO, 'sm'], w=['xdt'])
                      S.op('pool', lambda e: e.tensor_tensor(out=ydsb[:], in0=ydsb[:], in1=xdt[:], op=ALU.add),
                           r=['ydsb', 'xdt'], w=['ydsb'])
                      S.op('pool', lambda e: e.tensor_tensor(out=y_acc[:, t, :], in0=y_acc[:, t, :],
                                                             in1=ydsb[:].rearrange("p h d -> p (h d)"), op=ALU.add),
                           r=['ydsb', 'y_acc'], w=['y_acc'])
                      psH, kH = S.next_ps()
                      for gi in range(2):
                          S.op('pe', lambda e: e.matmul(psH[:, gi * 256:(gi + 1) * 256], lhsT=B_tm[:, t, gi * 128:(gi + 1) * 128],
                                                        rhs=xw[:, 4 * gi:4 * gi + 4, :].rearrange("p h d -> p (h d)"),
                                                        start=True, stop=True), r=['B_tm', 'xw'], w=[kH])
                      S.op('dve', lambda e: e.tensor_tensor(out=hst[:].rearrange("p (h d) -> p h d", d=64),
                                                            in0=hst[:].rearrange("p (h d) -> p h d", d=64),
                                                            in1=bc(sm[:, 32:40], 2, [128, 8, 64]), op=ALU.mult),
                           r=['hst', 'sm'], w=['hst'])
                      S.op('dve', lambda e: e.tensor_tensor(out=hst[:], in0=hst[:], in1=psH[:, :], op=ALU.add),
                           r=['hst', kH], w=['hst'])
                  S.dma('pool', yo[d][:, hf * 512:(hf + 1) * 512].rearrange("(t p) c -> p t c", p=128), y_acc[:], r=['y_acc'],
                        is_out=True)
        S.finish()
    return nc


def flip_seq(a, axis):
    sl_c = [slice(None)] * a.ndim
    sl_x = [slice(None)] * a.ndim
    sl_c[axis] = slice(0, CTX)
    sl_x[axis] = slice(CTX, LT)
    return np.concatenate([np.flip(a[tuple(sl_c)], axis), np.flip(a[tuple(sl_x)], axis)], axis=axis)


def run_C1(px_tm, px_fm, inputs, layer):
    cwl = inputs['ssd_conv_w'][layer]
    cbl = inputs['ssd_conv_b'][layer]
    in_maps = []
    for c in range(NCORES):
        b, d = c // 2, c % 2
        xbc = px_fm[b, 0:2048, :]
        dt = px_tm[b, :, 1024:1040]
        cw_ = cwl
        if d == 1:
            xbc = flip_seq(xbc, 1)
            dt = flip_seq(dt, 0)
            cw_ = cwl[::-1]
        abd = np.stack([inputs['ssd_a_log'][layer, d], inputs['ssd_dt_bias'][layer, d], inputs['ssd_d'][layer, d]], 0)
        in_maps.append({"xbc": np.ascontiguousarray(xbc), "dt": np.ascontiguousarray(dt),
                        "cw": np.ascontiguousarray(cw_.T.reshape(16, 128, 5).transpose(1, 0, 2)).reshape(128, 80),
                        "cb": np.ascontiguousarray(cbl.reshape(16, 128).T),
                        "abd": np.ascontiguousarray(np.broadcast_to(abd.reshape(1, 48), (128, 48)))})
    res = run_bass_kernel_spmd(build_C1(), in_maps, core_ids=list(range(NCORES)))
    out = np.zeros((NB, 2, LT, 1024), np.float32)
    for c in range(NCORES):
        b, d = c // 2, c % 2
        yv = res.results[c]["y"]
        out[b, d] = flip_seq(yv, 0) if d == 1 else yv
    return out


def rstd_cols(S, ss, cols, inv_ns, key):
    c0 = cols[0]
    for (a, b), inv in zip(cols[1], inv_ns):
        S.op('dve', lambda e: e.tensor_scalar(out=ss[:, a:b], in0=ss[:, a:b], scalar1=inv, scalar2=EPS, op0=ALU.mult,
                                              op1=ALU.add), r=[key], w=[key])
    a, b = c0
    S.op('act', lambda e: e.activation(out=ss[:, a:b], in_=ss[:, a:b], func=AF.Sqrt), r=[key], w=[key])
    S.op('dve', lambda e: e.reciprocal(out=ss[:, a:b], in_=ss[:, a:b]), r=[key], w=[key])


def rope_ops(S, src3, dst3, cos, sin, nh, tmp, rkeys, wkey, tkey):
    s4 = src3.rearrange("p h (i two) -> p h i two", two=2)
    d4 = dst3.rearrange("p h (i two) -> p h i two", two=2)
    x1, x2 = s4[:, :, :, 0], s4[:, :, :, 1]
    cb_, sb_ = bc(cos, 1, [128, nh, 16]), bc(sin, 1, [128, nh, 16])
    for i, (xa, tb) in enumerate([(x1, cb_), (x2, sb_), (x1, sb_), (x2, cb_)]):
        S.op('dve', lambda e: e.tensor_tensor(out=tmp[:, i, :, :], in0=xa, in1=tb, op=ALU.mult), r=rkeys, w=[tkey])
    S.op('dve', lambda e: e.tensor_tensor(out=d4[:, :, :, 0], in0=tmp[:, 0, :, :], in1=tmp[:, 1, :, :], op=ALU.subtract),
         r=[tkey], w=[wkey])
    S.op('dve', lambda e: e.tensor_tensor(out=d4[:, :, :, 1], in0=tmp[:, 2, :, :], in1=tmp[:, 3, :, :], op=ALU.add),
         r=[tkey], w=[wkey])


def build_C2():
    nc = new_nc()
    mla = din(nc, "mla", [LT, 800])
    gqa = din(nc, "gqa", [128, 4])
    gkv = din(nc, "gkv", [128, 2])
    wq = din(nc, "wq", [512, 768])
    wkv = din(nc, "wkv", [256, 1024])
    qg = din(nc, "qg", [128, 96])
    kg = din(nc, "kg", [128, 96])
    cs = din(nc, "cs", [LT, 32])
    oT = dout(nc, "oT", [512, LT])
    SCALE = 96.0 ** -0.5
    with ExitStack() as ctx:
        S = get_sched(nc, ctx)
        S.nrot = 4
        ident = make_ident(S)
        ones = S.sb('ones', [128, 64])
        ones_f = S.sb('ones_f', [128, 64])
        S.op('pool', lambda e: e.memset(ones_f[:], 1.0), w=['ones_f'])
        S.op('dve', lambda e: e.tensor_copy(out=R(ones[:]), in_=ones_f[:]), r=['ones_f'], w=['ones'])
        gqat = S.sb('gqat', [128, 4]); gkvt = S.sb('gkvt', [128, 2])
        wqt = S.sb('wqt', [128, 4, 768]); wkvt = S.sb('wkvt', [128, 2, 1024])
        qgt = S.sb('qgt', [128, 96]); kgt = S.sb('kgt', [128, 96])
        cst = S.sb('cst', [128, NTL, 32])
        S.dma('sp', gqat[:], gqa[:, :], w=['gqat'])
        S.dma('sp', gkvt[:], gkv[:, :], w=['gkvt'])
        S.dma('pool', R(wqt[:]), wq.rearrange("(k p) n -> p k n", p=128), w=['wqt'])
        S.dma('pool', R(wkvt[:]), wkv.rearrange("(k p) n -> p k n", p=128), w=['wkvt'])
        S.dma('sp', qgt[:], qg[:, :], w=['qgt'])
        S.dma('sp', kgt[:], kg[:, :], w=['kgt'])
        S.dma('sp', cst[:], cs.rearrange("(t p) c -> p t c", p=128), w=['cst'])
        qaT = S.sb('qaT', [128, 4, LT])
        kvT = S.sb('kvT', [128, 2, LT])
        kpr = S.sb('kpr', [128, NTL, 32])
        mtb = [S.sb(f'mt{i}', [128, 800]) for i in range(2)]
        junk = [S.sb(f'junk{i}', [128, 512]) for i in range(2)]
        ss = [S.sb(f'ss{i}', [128, 16]) for i in range(2)]
        kpn = [S.sb(f'kpn{i}', [128, 1, 32]) for i in range(2)]
        rtmp = [S.sb(f'rtmp{i}', [128, 4, 4, 16]) for i in range(2)]
        for t in range(NTL):
            p = t % 2
            S.rec_lane(p)
            mt = mtb[t % 2]
            mk = ('mt', t % 2)
            S.dma('sp', mt[:], mla[t * 128:(t + 1) * 128, :], w=[mk])
            S.op('dve', lambda e: e.memset(ss[p][:, 0:3], 0.0), w=[('ss', p)])
            for i, (a, b) in enumerate([(0, 512), (512, 768), (768, 800)]):
                S.op('act', lambda e: e.activation(out=junk[p][:, 0:b - a], in_=mt[:, a:b], func=AF.Square,
                                                   accum_out=ss[p][:, i:i + 1]), r=[mk], w=[('junk', p), ('ss', p)])
            rstd_cols(S, ss[p], ((0, 3), [(0, 1), (1, 2), (2, 3)]), [1.0 / 512, 1.0 / 256, 1.0 / 32], ('ss', p))
            S.op('dve', lambda e: e.tensor_scalar(out=mt[:, 0:512], in0=mt[:, 0:512], scalar1=ss[p][:, 0:1], scalar2=None,
                                                  op0=ALU.mult), r=[mk, ('ss', p)], w=[mk])
            S.op('dve', lambda e: e.tensor_scalar(out=mt[:, 512:768], in0=mt[:, 512:768], scalar1=ss[p][:, 1:2], scalar2=None,
                                                  op0=ALU.mult), r=[mk, ('ss', p)], w=[mk])
            S.op('dve', lambda e: e.scalar_tensor_tensor(out=kpn[p][:, 0, :], in0=mt[:, 768:800], scalar=ss[p][:, 2:3],
                                                         in1=kgt[:, 64:96], op0=ALU.mult, op1=ALU.mult),
                 r=[mk, ('ss', p), 'kgt'], w=[('kpn', p)])
            rope_ops(S, kpn[p][:], kpr[:, t:t + 1, :], cst[:, t, 0:16], cst[:, t, 16:32], 1, rtmp[p][:, :, 0:1, :],
                     [('kpn', p), 'cst'], 'kpr', ('rtmp', p))
            ps, pk = S.next_ps()
            for k in range(4):
                S.op('pe', lambda e: e.transpose(out=ps[:, k * 128:(k + 1) * 128], in_=mt[:, k * 128:(k + 1) * 128],
                                                 identity=ident[:]), r=[mk, 'ident'], w=[pk])
            for k in range(4):
                S.op('act', lambda e: e.activation(out=R(qaT[:, k, t * 128:(t + 1) * 128]), in_=ps[:, k * 128:(k + 1) * 128],
                                                   func=AF.Identity, scale=gqat[:, k:k + 1]), r=[pk, 'gqat'], w=['qaT'])
            ps, pk = S.next_ps()
            for k in range(2):
                S.op('pe', lambda e: e.transpose(out=ps[:, k * 128:(k + 1) * 128], in_=mt[:, 512 + k * 128:640 + k * 128],
                                                 identity=ident[:]), r=[mk, 'ident'], w=[pk])
            for k in range(2):
                S.op('act', lambda e: e.activation(out=R(kvT[:, k, t * 128:(t + 1) * 128]), in_=ps[:, k * 128:(k + 1) * 128],
                                                   func=AF.Identity, scale=gkvt[:, k:k + 1]), r=[pk, 'gkvt'], w=['kvT'])
            if p == 1 or t == NTL - 1:
                S.rec_flush()
        kT = S.sb('kT', [128, 4, LT])
        vv = S.sb('vv', [128, NTL, 4, 64])
        kfull = [S.sb(f'kfull{i}', [128, 4, 96]) for i in range(2)]
        sq = [S.sb(f'sq{i}', [128, 4, 96]) for i in range(2)]
        t1 = [S.sb(f't1{i}', [128, 4, 64]) for i in range(2)]
        qn = [S.sb(f'qn{i}', [128, 4, 96]) for i in range(2)]
        qp = [S.sb(f'qp{i}', [128, 4, 32]) for i in range(2)]
        qT = S.sb('qT', [128, 4, 512])
        ptb = [S.sb(f'pt{i}', [128, 512]) for i in range(3)]
        rden = S.sb('rden', [64, 512])
        osb = [S.sb(f'osb{i}', [64, 512]) for i in range(2)]
        groups = [[0, 1]] + [list(range(2 + 4 * g, 6 + 4 * g)) for g in range(4)]
        pti = 0
        oi = 0
        hcount = 0
        for hp in range(2):
            for t in range(NTL):
                p = t % 2
                S.rec_lane(p)
                tsl = slice(t * 128, (t + 1) * 128)
                psK, kK = S.next_ps()
                for k in range(2):
                    S.op('pe', lambda e: e.matmul(psK[:, :], lhsT=R(kvT[:, k, tsl]), rhs=R(wkvt[:, k, hp * 512:(hp + 1) * 512]),
                                                  start=(k == 0), stop=(k == 1)), r=['kvT', 'wkvt'], w=[kK])
                pk3 = psK[:, :].rearrange("p (h c) -> p h c", c=128)
                S.op('act', lambda e: e.activation(out=sq[p][:, :, 0:64], in_=pk3[:, :, 0:64], func=AF.Square), r=[kK], w=[('sq', p)])
                S.op('dve', lambda e: e.tensor_reduce(out=ss[p][:, 4:8], in_=sq[p][:, :, 0:64], axis=AX.X, op=ALU.add),
                     r=[('sq', p)], w=[('ss', p)])
                rstd_cols(S, ss[p], ((4, 8), [(4, 8)]), [1.0 / 64], ('ss', p))
                S.op('dve', lambda e: e.tensor_tensor(out=t1[p][:], in0=pk3[:, :, 0:64], in1=bc(ss[p][:, 4:8], 2, [128, 4, 64]),
                                                      op=ALU.mult), r=[kK, ('ss', p)], w=[('t1', p)])
                S.op('dve', lambda e: e.tensor_tensor(out=kfull[p][:, :, 0:64], in0=t1[p][:], in1=bc(kgt[:, 0:64], 1, [128, 4, 64]),
                                                      op=ALU.mult), r=[('t1', p), 'kgt'], w=[('kfull', p)])
                S.op('dve', lambda e: e.tensor_copy(out=kfull[p][:, :, 64:96], in_=bc(kpr[:, t, :], 1, [128, 4, 32])),
                     r=['kpr'], w=[('kfull', p)])
                S.op('act', lambda e: e.activation(out=R(vv[:, t, :, :]), in_=pk3[:, :, 64:128], func=AF.Copy), r=[kK], w=['vv'])
                psT, kTk = S.next_ps()
                for hd in range(4):
                    S.op('pe', lambda e: e.transpose(out=psT[0:96, hd * 128:(hd + 1) * 128], in_=kfull[p][:, hd, :],
                                                     identity=ident[:]), r=[('kfull', p), 'ident'], w=[kTk])
                S.op('act', lambda e: e.activation(out=R(kT[0:96, :, tsl]), in_=psT[0:96, :].rearrange("p (h c) -> p h c", c=128),
                                                   func=AF.Copy), r=[kTk], w=['kT'])
                if p == 1 or t == NTL - 1:
                    S.rec_flush()
            for gi, tiles in enumerate(groups):
                nq = len(tiles) * 128
                key_tiles = [0, 1] if gi == 0 else list(range(NTL))
                for li, t in enumerate(tiles):
                    p = li % 2
                    S.rec_lane(p)
                    tsl = slice(t * 128, (t + 1) * 128)
                    psQ, kQ = S.next_ps()
                    for k in range(4):
                        S.op('pe', lambda e: e.matmul(psQ[:, 0:384], lhsT=R(qaT[:, k, tsl]), rhs=R(wqt[:, k, hp * 384:(hp + 1) * 384]),
                                                      start=(k == 0), stop=(k == 3)), r=['qaT', 'wqt'], w=[kQ])
                    pq3 = psQ[:, 0:384].rearrange("p (h c) -> p h c", c=96)
                    S.op('act', lambda e: e.activation(out=sq[p][:], in_=pq3, func=AF.Square), r=[kQ], w=[('sq', p)])
                    S.op('dve', lambda e: e.tensor_reduce(out=ss[p][:, 8:12], in_=sq[p][:, :, 0:64], axis=AX.X, op=ALU.add),
                         r=[('sq', p)], w=[('ss', p)])
                    S.op('dve', lambda e: e.tensor_reduce(out=ss[p][:, 12:16], in_=sq[p][:, :, 64:96], axis=AX.X, op=ALU.add),
                         r=[('sq', p)], w=[('ss', p)])
                    rstd_cols(S, ss[p], ((8, 16), [(8, 12), (12, 16)]), [1.0 / 64, 1.0 / 32], ('ss', p))
                    S.op('dve', lambda e: e.tensor_tensor(out=t1[p][:], in0=pq3[:, :, 0:64], in1=bc(ss[p][:, 8:12], 2, [128, 4, 64]),
                                                          op=ALU.mult), r=[kQ, ('ss', p)], w=[('t1', p)])
                    S.op('dve', lambda e: e.tensor_tensor(out=qn[p][:, :, 0:64], in0=t1[p][:], in1=bc(qgt[:, 0:64], 1, [128, 4, 64]),
                                                          op=ALU.mult), r=[('t1', p), 'qgt'], w=[('qn', p)])
                    S.op('dve', lambda e: e.tensor_tensor(out=qp[p][:], in0=pq3[:, :, 64:96], in1=bc(ss[p][:, 12:16], 2, [128, 4, 32]),
                                                          op=ALU.mult), r=[kQ, ('ss', p)], w=[('qp', p)])
                    S.op('dve', lambda e: e.tensor_tensor(out=qp[p][:], in0=qp[p][:], in1=bc(qgt[:, 64:96], 1, [128, 4, 32]),
                                                          op=ALU.mult), r=[('qp', p), 'qgt'], w=[('qp', p)])
                    rope_ops(S, qp[p][:], qn[p][:, :, 64:96], cst[:, t, 0:16], cst[:, t, 16:32], 4, rtmp[p][:], [('qp', p), 'cst'], ('qn', p), ('rtmp', p))
                    psT, kTk = S.next_ps()
                    for hd in range(4):
                        S.op('pe', lambda e: e.transpose(out=psT[0:96, hd * 128:(hd + 1) * 128], in_=qn[p][:, hd, :],
                                                         identity=ident[:]), r=[('qn', p), 'ident'], w=[kTk])
                    S.op('act', lambda e: e.activation(out=R(qT[0:96, :, li * 128:(li + 1) * 128]),
                                                       in_=psT[0:96, :].rearrange("p (h c) -> p h c", c=128), func=AF.Copy),
                         r=[kTk], w=['qT'])
                    if p == 1 or li == len(tiles) - 1:
                        S.rec_flush()
                for hd in range(4):
                    ao, ad = (4, 5) if hcount % 2 == 0 else (6, 7)
                    hcount += 1
                    pso, psd = S.ps[ao], S.ps[ad]
                    ko, kd = ('ps', ao), ('ps', ad)
                    for ki, kc in enumerate(key_tiles):
                        ksl = slice(kc * 128, (kc + 1) * 128)
                        psS, kS = S.next_ps()
                        S.op('pe', lambda e: e.matmul(psS[:, 0:nq], lhsT=R(kT[0:96, hd, ksl]), rhs=R(qT[0:96, hd, 0:nq]),
                                                      start=True, stop=True), r=['kT', 'qT'], w=[kS])
                        pt = ptb[pti % 3]
                        ptk = ('pt', pti % 3)
                        pti += 1
                        S.op('act', lambda e: e.activation(out=R(pt[:, 0:nq]), in_=psS[:, 0:nq], func=AF.Exp, scale=SCALE),
                             r=[kS], w=[ptk])
                        S.op('pe', lambda e: e.matmul(pso[0:64, 0:nq], lhsT=R(vv[:, kc, hd, :]), rhs=R(pt[:, 0:nq]),
                                                      start=(ki == 0), stop=(ki == len(key_tiles) - 1)), r=['vv', ptk], w=[ko])
                        S.op('pe', lambda e: e.matmul(psd[0:64, 0:nq], lhsT=R(ones[:, :]), rhs=R(pt[:, 0:nq]),
                                                      start=(ki == 0), stop=(ki == len(key_tiles) - 1)), r=['ones', ptk], w=[kd])
                    S.op('dve', lambda e: e.reciprocal(out=rden[:, 0:nq], in_=psd[0:64, 0:nq]), r=[kd], w=['rden'])
                    ob = osb[oi % 2]
                    obk = ('osb', oi % 2)
                    oi += 1
                    S.op('dve', lambda e: e.tensor_tensor(out=ob[:, 0:nq], in0=pso[0:64, 0:nq], in1=rden[:, 0:nq], op=ALU.mult),
                         r=[ko, 'rden'], w=[obk])
                    hrow = (hp * 4 + hd) * 64
                    S.dma('sp', oT[hrow:hrow + 64, tiles[0] * 128:tiles[0] * 128 + nq], ob[:, 0:nq], r=[obk], is_out=True)
        S.finish()
    return nc


def rope_table():
    n_freq = 8
    inv = (10000.0 ** (-np.arange(n_freq, dtype=np.float32) / n_freq)).astype(np.float32)
    rows = np.repeat(np.arange(SEQ // 64, dtype=np.float32), 64)
    cols = np.tile(np.arange(64, dtype=np.float32), SEQ // 64)
    ang = np.concatenate([rows[:, None] * inv, cols[:, None] * inv], axis=-1).astype(np.float32)
    cs = np.zeros((LT, 32), np.float32)
    cs[:CTX, 0:16] = 1.0
    cs[CTX:, 0:16] = np.cos(ang)
    cs[CTX:, 16:32] = np.sin(ang)
    return cs


def rep128(v):
    return np.ascontiguousarray(np.broadcast_to(np.asarray(v, np.float32).reshape(1, -1), (128, v.size)))


def run_C2(px_tm, inputs, layer):
    cs = rope_table()
    in_maps = []
    for c in range(NCORES):
        b, hf = c // 2, c % 2
        in_maps.append({
            "mla": np.ascontiguousarray(px_tm[b, :, 1040:1840]),
            "gqa": np.ascontiguousarray(inputs['mla_q_a_gain'][layer].reshape(4, 128).T),
            "gkv": np.ascontiguousarray(inputs['mla_kv_a_gain'][layer].reshape(2, 128).T),
            "wq": np.ascontiguousarray(inputs['mla_w_q_b'][layer][:, hf * 768:(hf + 1) * 768]),
            "wkv": np.ascontiguousarray(inputs['mla_w_kv_b'][layer][:, hf * 1024:(hf + 1) * 1024]),
            "qg": rep128(inputs['mla_q_gain'][layer]), "kg": rep128(inputs['mla_k_gain'][layer]), "cs": cs})
    res = run_bass_kernel_spmd(build_C2(), in_maps, core_ids=list(range(NCORES)))
    out = np.zeros((NB, 1024, LT), np.float32)
    for c in range(NCORES):
        b, hf = c // 2, c % 2
        out[b, hf * 512:(hf + 1) * 512] = res.results[c]["oT"]
    return out


PI = float(np.pi)
S5_CH = [(0, 512), (512, 1024), (1024, 1536), (1536, 2048), (2048, 2304)]


def sin_reduced(S, dst, src, offset, sign, tmps, rkeys, wkey, pref):
    ki, kf, r, g = tmps
    kk = [pref + n for n in ('ki', 'kf', 'r', 'g')]
    S.op('dve', lambda e: e.tensor_scalar(out=r, in0=src, scalar1=offset + 8.0 * PI, scalar2=None, op0=ALU.add),
         r=rkeys, w=[kk[2]])
    S.op('dve', lambda e: e.tensor_scalar(out=ki, in0=r, scalar1=1.0 / (2.0 * PI), scalar2=None, op0=ALU.mult),
         r=[kk[2]], w=[kk[0]])
    S.op('dve', lambda e: e.tensor_copy(out=kf, in_=ki), r=[kk[0]], w=[kk[1]])
    S.op('dve', lambda e: e.scalar_tensor_tensor(out=r, in0=kf, scalar=-2.0 * PI, in1=r, op0=ALU.mult, op1=ALU.add),
         r=[kk[1], kk[2]], w=[kk[2]])
    S.op('dve', lambda e: e.tensor_scalar(out=g, in0=r, scalar1=PI, scalar2=-2.0 * PI, op0=ALU.is_gt, op1=ALU.mult),
         r=[kk[2]], w=[kk[3]])
    S.op('dve', lambda e: e.tensor_tensor(out=r, in0=r, in1=g, op=ALU.add), r=[kk[2], kk[3]], w=[kk[2]])
    S.op('dve', lambda e: e.tensor_scalar(out=r, in0=r, scalar1=-PI, scalar2=PI, op0=ALU.max, op1=ALU.min),
         r=[kk[2]], w=[kk[2]])
    S.op('act', lambda e: e.activation(out=dst, in_=r, func=AF.Sin, scale=float(sign)), r=[kk[2]], w=[wkey])


def cos_from_reduced(S, dst, r_ap, sign, tmps, rkey, wkey, pref):
    r2, g = tmps
    k2, kg = pref + 'r2', pref + 'g2'
    S.op('dve', lambda e: e.tensor_scalar(out=r2, in0=r_ap, scalar1=0.5 * PI, scalar2=None, op0=ALU.add), r=[rkey], w=[k2])
    S.op('dve', lambda e: e.tensor_scalar(out=g, in0=r2, scalar1=PI, scalar2=-2.0 * PI, op0=ALU.is_gt, op1=ALU.mult), r=[k2], w=[kg])
    S.op('dve', lambda e: e.tensor_tensor(out=r2, in0=r2, in1=g, op=ALU.add), r=[k2, kg], w=[k2])
    S.op('dve', lambda e: e.tensor_scalar(out=r2, in0=r2, scalar1=-PI, scalar2=PI, op0=ALU.max, op1=ALU.min), r=[k2], w=[k2])
    S.op('act', lambda e: e.activation(out=dst, in_=r2, func=AF.Sin, scale=float(sign)), r=[k2], w=[wkey])


def build_C3():
    nc = new_nc()
    u = din(nc, "u", [1024, LT])
    lamp_ = din(nc, "lamp", [128, 2 * 96])
    lamr_ = din(nc, "lamr", [32, 2 * 3 * 4096])
    bT_ = din(nc, "bT", [32, 2 * 2 * 4096])
    cbd_ = din(nc, "cbd", [128, 2 * 2 * 1024])
    yTo = [dout(nc, "yT0", [1024, LT]), dout(nc, "yT1", [1024, LT])]
    with ExitStack() as ctx:
        S = get_sched(nc, ctx)
        lp = S.sb('lp', [128, 3, 32])
        cb_ = S.sb('cbd_sb', [128, 2, 32, 32])
        dtp = S.sb('dtp', [128, 32]); magp = S.sb('magp', [128, 32]); thp = S.sb('thp', [128, 32])
        BbT = S.sb('BbT', [32, 2, 32, 128])
        W = 512
        names = ['lr', 'li', 'ld', 'br', 'bi', 'mag', 'th', 'sn', 'cs', 'm', 'abr', 'abi', 'den', 'fr', 'fi', 'ta', 'tb']
        T = {n: S.sb('r_' + n, [32, W]) for n in names}
        T['ki'] = S.sb('r_ki', [32, W], I32)
        T['kf'] = S.sb('r_kf', [32, W])
        T['g'] = S.sb('r_g', [32, W])
        io_i = S.sb('io_i', [128, 512], I32)
        io_f = S.sb('io_f', [128, 512])
        S.op('pool', lambda e: e.iota(io_i[:], pattern=[[1, 512]], base=1, channel_multiplier=0), w=['io_i'])
        S.op('dve', lambda e: e.tensor_copy(out=io_f[:], in_=io_i[:]), r=['io_i'], w=['io_f'])
        P2 = range(2)
        Er = [S.sb(f'Er{p}', [128, 512]) for p in P2]; Ei = [S.sb(f'Ei{p}', [128, 512]) for p in P2]
        amag = [S.sb(f'amag{p}', [128, 512]) for p in P2]
        phi = [S.sb(f'phi{p}', [128, 512]) for p in P2]; mm = [S.sb(f'mm{p}', [128, 512]) for p in P2]
        pki = [S.sb(f'pki{p}', [128, 512], I32) for p in P2]; pkf = [S.sb(f'pkf{p}', [128, 512]) for p in P2]
        pg = [S.sb(f'pg{p}', [128, 512]) for p in P2]
        ub = [S.sb(f'ub{i}', [32, LT]) for i in range(2)]
        wre = [S.sb(f'wre{p}', [128, 512]) for p in P2]; wim = [S.sb(f'wim{p}', [128, 512]) for p in P2]
        ta = [S.sb(f'ta{p}', [128, 512]) for p in P2]; tb = [S.sb(f'tb{p}', [128, 512]) for p in P2]
        zre = [S.sb(f'zre{p}', [128, 512]) for p in P2]; zim = [S.sb(f'zim{p}', [128, 512]) for p in P2]
        xre = [[S.sb(f'xre{p}_{i}', [128, 512]) for i in range(2)] for p in P2]
        xim = [[S.sb(f'xim{p}_{i}', [128, 512]) for i in range(2)] for p in P2]
        tc_ = [S.sb(f'tc{p}', [128, 512]) for p in P2]; td = [S.sb(f'td{p}', [128, 512]) for p in P2]
        yb = [S.sb(f'yb{i}', [32, 512]) for i in range(4)]

        def tt(o, a_, b_, op, eng='dve'):
            S.op(eng, lambda e: e.tensor_tensor(out=T[o][:], in0=T[a_][:], in1=T[b_][:], op=op), r=[a_, b_], w=[o])

        yi = 0
        for d in range(2):
            S.dma('sp', lp[:].rearrange("p a j -> p (a j)"), lamp_[:, d * 96:(d + 1) * 96], w=['lp'])
            S.dma('sp', cb_[:].rearrange("p a j s -> p (a j s)"), cbd_[:, d * 2048:(d + 1) * 2048], w=['cbd'])
            S.op('dve', lambda e: e.tensor_scalar(out=cb_[:, 1], in0=cb_[:, 1], scalar1=-1.0, scalar2=None, op0=ALU.mult),
                 r=['cbd'], w=['cbd'])
            S.op('act', lambda e: e.activation(out=dtp[:], in_=lp[:, 2, :], func=AF.Exp), r=['lp'], w=['dtp'])
            S.op('dve', lambda e: e.tensor_tensor(out=magp[:], in0=lp[:, 0, :], in1=dtp[:], op=ALU.mult), r=['lp', 'dtp'], w=['magp'])
            S.op('act', lambda e: e.activation(out=magp[:], in_=magp[:], func=AF.Exp), r=['magp'], w=['magp'])
            S.op('dve', lambda e: e.tensor_tensor(out=thp[:], in0=lp[:, 1, :], in1=dtp[:], op=ALU.mult), r=['lp', 'dtp'], w=['thp'])
            for pc in range(8):
                for i, n in enumerate(['lr', 'li', 'ld']):
                    o0 = d * 3 * 4096 + i * 4096 + pc * W
                    S.dma('sp', T[n][:], lamr_[:, o0:o0 + W], w=[n])
                for i, n in enumerate(['br', 'bi']):
                    o0 = d * 2 * 4096 + i * 4096 + pc * W
                    S.dma('sp', T[n][:], bT_[:, o0:o0 + W], w=[n])
                S.op('act', lambda e: e.activation(out=T['ld'][:], in_=T['ld'][:], func=AF.Exp), r=['ld'], w=['ld'])
                tt('mag', 'lr', 'ld', ALU.mult)
                S.op('act', lambda e: e.activation(out=T['mag'][:], in_=T['mag'][:], func=AF.Exp), r=['mag'], w=['mag'])
                tt('th', 'li', 'ld', ALU.mult)
                rt = (T['ki'][:], T['kf'][:], T['m'][:], T['g'][:])
                sin_reduced(S, T['sn'][:], T['th'][:], 0.0, 1.0, rt, ['th'], 'sn', 'R')
                sin_reduced(S, T['cs'][:], T['th'][:], 0.5 * PI, 1.0, rt, ['th'], 'cs', 'R')
                tt('abr', 'mag', 'cs', ALU.mult)
                tt('abi', 'mag', 'sn', ALU.mult)
                S.op('dve', lambda e: e.tensor_scalar(out=T['abr'][:], in0=T['abr'][:], scalar1=-1.0, scalar2=None, op0=ALU.add),
                     r=['abr'], w=['abr'])
                tt('den', 'lr', 'lr', ALU.mult)
                tt('ta', 'li', 'li', ALU.mult)
                tt('den', 'den', 'ta', ALU.add)
                S.op('dve', lambda e: e.reciprocal(out=T['den'][:], in_=T['den'][:]), r=['den'], w=['den'])
                tt('fr', 'abr', 'lr', ALU.mult)
                tt('ta', 'abi', 'li', ALU.mult)
                tt('fr', 'fr', 'ta', ALU.add)
                tt('fr', 'fr', 'den', ALU.mult)
                tt('fi', 'abi', 'lr', ALU.mult)
                tt('ta', 'abr', 'li', ALU.mult)
                tt('fi', 'fi', 'ta', ALU.subtract)
                tt('fi', 'fi', 'den', ALU.mult)
                o_re = BbT[:, 0, 4 * pc:4 * pc + 4, :].rearrange("p j c -> p (j c)")
                o_im = BbT[:, 1, 4 * pc:4 * pc + 4, :].rearrange("p j c -> p (j c)")
                tt('ta', 'br', 'fr', ALU.mult)
                tt('tb', 'bi', 'fi', ALU.mult)
                S.op('dve', lambda e: e.tensor_tensor(out=o_re, in0=T['ta'][:], in1=T['tb'][:], op=ALU.subtract),
                     r=['ta', 'tb'], w=['BbT'])
                tt('ta', 'br', 'fi', ALU.mult)
                tt('tb', 'bi', 'fr', ALU.mult)
                S.op('dve', lambda e: e.tensor_tensor(out=o_im, in0=T['ta'][:], in1=T['tb'][:], op=ALU.add),
                     r=['ta', 'tb'], w=['BbT'])
            if d == 0:
                chunks = S5_CH
            else:
                chunks = [(0, 256), (1792, 2304), (1280, 1792), (768, 1280), (256, 768)]

            def V(ap, n):
                v = ap[:, 0:n]
                return v if d == 0 else v[:, ::-1]

            def tables(j, p):
                S.dma('sp', ub[p][:], u[32 * j:32 * j + 32, :], w=[('ub', p)])
                S.op('dve', lambda e: e.tensor_scalar(out=phi[p][:], in0=io_f[:], scalar1=thp[:, j:j + 1], scalar2=None, op0=ALU.mult),
                     r=['io_f', 'thp'], w=[('phi', p)])
                pt_ = (pki[p][:], pkf[p][:], mm[p][:], pg[p][:])
                sin_reduced(S, Ei[p][:], phi[p][:], 0.0, -1.0, pt_, [('phi', p)], ('Ei', p), f'P{p}')
                cos_from_reduced(S, Er[p][:], mm[p][:], 1.0, (pkf[p][:], pg[p][:]), f'P{p}r', ('Er', p), f'P{p}')
                S.op('pool', lambda e: e.tensor_copy(out=amag[p][:], in_=magp[:, j:j + 1].to_broadcast([128, 512])),
                     r=['magp'], w=[('amag', p)])

            def ph_w(j, p, ci, st):
                c0, c1 = chunks[ci]
                n = c1 - c0
                Erv, Eiv = V(Er[p], n), V(Ei[p], n)
                psR, kR = S.next_ps()
                psI, kI = S.next_ps()
                st['ps'] = (psR, kR, psI, kI)
                uk = ('ub', p)
                S.op('pe', lambda e: e.matmul(psR[:, 0:n], lhsT=BbT[:, 0, j, :], rhs=ub[p][:, c0:c1], start=True, stop=True),
                     r=['BbT', uk], w=[kR])
                S.op('pe', lambda e: e.matmul(psI[:, 0:n], lhsT=BbT[:, 1, j, :], rhs=ub[p][:, c0:c1], start=True, stop=True),
                     r=['BbT', uk], w=[kI])
                S.op('dve', lambda e: e.tensor_tensor(out=wre[p][:, 0:n], in0=psR[:, 0:n], in1=Erv, op=ALU.mult), r=[kR, ('Er', p)], w=[('wre', p)])
                S.op('dve', lambda e: e.tensor_tensor(out=ta[p][:, 0:n], in0=psI[:, 0:n], in1=Eiv, op=ALU.mult), r=[kI, ('Ei', p)], w=[('ta', p)])
                S.op('dve', lambda e: e.tensor_tensor(out=wim[p][:, 0:n], in0=psI[:, 0:n], in1=Erv, op=ALU.mult), r=[kI, ('Er', p)], w=[('wim', p)])
                S.op('dve', lambda e: e.tensor_tensor(out=tb[p][:, 0:n], in0=psR[:, 0:n], in1=Eiv, op=ALU.mult), r=[kR, ('Ei', p)], w=[('tb', p)])
                S.op('pool', lambda e: e.tensor_tensor(out=wre[p][:, 0:n], in0=wre[p][:, 0:n], in1=ta[p][:, 0:n], op=ALU.subtract),
                     r=[('wre', p), ('ta', p)], w=[('wre', p)])
                S.op('pool', lambda e: e.tensor_tensor(out=wim[p][:, 0:n], in0=wim[p][:, 0:n], in1=tb[p][:, 0:n], op=ALU.add),
                     r=[('wim', p), ('tb', p)], w=[('wim', p)])

            def ph_scan(j, p, ci, st):
                c0, c1 = chunks[ci]
                n = c1 - c0
                if ci == 0:
                    ini_r, ini_i, rk = 0.0, 0.0, []
                else:
                    pn = chunks[ci - 1][1] - chunks[ci - 1][0]
                    col = pn - 1 if d == 0 else 0
                    ini_r = xre[p][(ci - 1) % 2][:, col:col + 1]
                    ini_i = xim[p][(ci - 1) % 2][:, col:col + 1]
                    rk = [('xre', p, (ci - 1) % 2), ('xim', p, (ci - 1) % 2)]
                S.op('dve', lambda e: e.tensor_tensor_scan(out=V(zre[p], n), data0=amag[p][:, 0:n], data1=V(wre[p], n), initial=ini_r,
                                                           op0=ALU.mult, op1=ALU.add), r=[('amag', p), ('wre', p)] + rk, w=[('zre', p)])
                S.op('dve', lambda e: e.tensor_tensor_scan(out=V(zim[p], n), data0=amag[p][:, 0:n], data1=V(wim[p], n), initial=ini_i,
                                                           op0=ALU.mult, op1=ALU.add), r=[('amag', p), ('wim', p)] + rk, w=[('zim', p)])

            def ph_x(j, p, ci, st):
                nonlocal yi
                c0, c1 = chunks[ci]
                n = c1 - c0
                Erv, Eiv = V(Er[p], n), V(Ei[p], n)
                xr, xi_ = xre[p][ci % 2], xim[p][ci % 2]
                xrk, xik = ('xre', p, ci % 2), ('xim', p, ci % 2)
                S.op('dve', lambda e: e.tensor_tensor(out=xr[:, 0:n], in0=zre[p][:, 0:n], in1=Erv, op=ALU.mult), r=[('zre', p), ('Er', p)], w=[xrk])
                S.op('pool', lambda e: e.tensor_tensor(out=tc_[p][:, 0:n], in0=zim[p][:, 0:n], in1=Eiv, op=ALU.mult), r=[('zim', p), ('Ei', p)], w=[('tc', p)])
                S.op('dve', lambda e: e.tensor_tensor(out=xr[:, 0:n], in0=xr[:, 0:n], in1=tc_[p][:, 0:n], op=ALU.add), r=[xrk, ('tc', p)], w=[xrk])
                S.op('pool', lambda e: e.tensor_tensor(out=xi_[:, 0:n], in0=zim[p][:, 0:n], in1=Erv, op=ALU.mult), r=[('zim', p), ('Er', p)], w=[xik])
                S.op('pool', lambda e: e.tensor_tensor(out=td[p][:, 0:n], in0=zre[p][:, 0:n], in1=Eiv, op=ALU.mult), r=[('zre', p), ('Ei', p)], w=[('td', p)])
                S.op('pool', lambda e: e.tensor_tensor(out=xi_[:, 0:n], in0=xi_[:, 0:n], in1=td[p][:, 0:n], op=ALU.subtract), r=[xik, ('td', p)], w=[xik])
                psY, kY = S.next_ps()
                S.op('pe', lambda e: e.matmul(psY[0:32, 0:n], lhsT=cb_[:, 0, j, :], rhs=xr[:, 0:n], start=True, stop=False),
                     r=['cbd', xrk], w=[kY])
                S.op('pe', lambda e: e.matmul(psY[0:32, 0:n], lhsT=cb_[:, 1, j, :], rhs=xi_[:, 0:n], start=False, stop=True),
                     r=['cbd', xik], w=[kY])
                ybb = yb[yi % 4]
                ybk = ('yb', yi % 4)
                yi += 1
                S.op('act', lambda e: e.activation(out=ybb[:, 0:n], in_=psY[0:32, 0:n], func=AF.Copy), r=[kY], w=[ybk])
                S.dma('sp', yTo[d][32 * j:32 * j + 32, c0:c1], ybb[:, 0:n], r=[ybk], is_out=True)

            for jp in range(16):
                js = [(2 * jp, 0), (2 * jp + 1, 1)]
                for j, p in js:
                    tables(j, p)
                for ci in range(len(chunks)):
                    sts = [dict(), dict()]
                    for ph in (ph_w, ph_scan, ph_x):
                        for j, p in js:
                            ph(j, p, ci, sts[p])
        S.finish()
    return nc


def run_C3(px_fm, inputs, layer):
    in_maps = []
    for c in range(NCORES):
        b, d = c // 2, c % 2
        u = px_fm[b, 2048:3072, :]
        if d == 1:
            u = flip_seq(u, 1)
        lre = inputs['s5_lam_re'][layer, d]
        lim = inputs['s5_lam_im'][layer, d]
        ldt = np.broadcast_to(inputs['s5_log_dt'][layer, d][:, None], (64, 64))
        P = lambda a: a.reshape(32, 128).T
        lamp = np.concatenate([P(lre), P(lim), P(ldt)], axis=1)
        Rl = lambda a: np.broadcast_to(a.reshape(1, 4096), (32, 4096))
        lamr = np.concatenate([Rl(lre), Rl(lim), Rl(ldt)], axis=1)

        def bdT(bm):
            o = np.zeros((2, 16, 32, 2, 64), np.float32)
            bb = bm.reshape(32, 2, 64, 16)
            for gg in range(2):
                o[gg, :, :, gg, :] = bb[:, gg].transpose(2, 0, 1)
            return o.reshape(32, 4096)

        def cbdf(cm):
            o = np.zeros((2, 64, 32, 2, 16), np.float32)
            cc = cm.reshape(32, 2, 16, 64)
            for gg in range(2):
                o[gg, :, :, gg, :] = cc[:, gg].transpose(2, 0, 1)
            return o.reshape(128, 1024)

        in_maps.append({"u": np.ascontiguousarray(u), "lamp": np.ascontiguousarray(lamp, dtype=np.float32),
                        "lamr": np.ascontiguousarray(lamr, dtype=np.float32),
                        "bT": np.concatenate([bdT(inputs['s5_b_re'][layer, d]), bdT(inputs['s5_b_im'][layer, d])], axis=1),
                        "cbd": np.concatenate([cbdf(inputs['s5_c_re'][layer, d]), cbdf(inputs['s5_c_im'][layer, d])], axis=1)})
    res = run_bass_kernel_spmd(build_C3(), in_maps, core_ids=list(range(NCORES)))
    out = np.zeros((NB, 2, 1024, LT), np.float32)
    for c in range(NCORES):
        b, d = c // 2, c % 2
        yv = res.results[c]["yT"]
        out[b, d] = flip_seq(yv, 1) if d == 1 else yv
    return out


TOK_RANGES = [(0, 512), (512, 1024), (1024, 1152)]
GELU_K2 = 2.0 * 0.7978845608028654


def build_D1():
    nc = new_nc()
    NT = NT_B
    NTOK = NT * 128
    y0 = din(nc, "y0", [NTOK, 1024]); y1 = din(nc, "y1", [NTOK, 1024]); z = din(nc, "z", [NTOK, 1024])
    gssd = din(nc, "gssd", [128, 8])
    y5a = din(nc, "y5a", [1024, NTOK]); y5b = din(nc, "y5b", [1024, NTOK]); uT = din(nc, "uT", [1024, NTOK])
    s5d = din(nc, "s5d", [128, 8])
    wglu = din(nc, "wglu", [1024, 1024])
    ysT = dout(nc, "ysT", [1024, NTOK]); y5T = dout(nc, "y5T", [1024, NTOK])
    with ExitStack() as ctx:
        S = get_sched(nc, ctx)
        ident = make_ident(S)
        gs = S.sb('gs', [128, 8]); sd = S.sb('sd', [128, 8]); wg = S.sb('wg', [128, 8, 1024])
        S.dma('sp', gs[:], gssd[:, :], w=['gs'])
        S.dma('sp', sd[:], s5d[:, :], w=['sd'])
        S.dma('sp', wg[:], wglu.rearrange("(k p) n -> p k n", p=128), w=['wg'])
        yb0 = [S.sb(f'yb0_{i}', [128, 1024]) for i in range(2)]
        yb1 = [S.sb(f'yb1_{i}', [128, 1024]) for i in range(2)]
        zb = [S.sb(f'zb_{i}', [128, 1024]) for i in range(2)]
        junk = S.sb('junk', [128, 1024])
        ss = S.sb('ss', [128, NT])
        S.op('dve', lambda e: e.memset(ss[:], 0.0), w=['ss'])
        ob = [S.sb(f'ob{i}', [128, 8, 128]) for i in range(2)]
        ysv = ysT.rearrange("(k p) n -> p k n", p=128)
        for t in range(NT):
            i = t % 2
            rows = slice(t * 128, (t + 1) * 128)
            S.dma('sp', yb0[i][:], y0[rows, :], w=[('yb0', i)])
            S.dma('sp', yb1[i][:], y1[rows, :], w=[('yb1', i)])
            S.dma('sp', zb[i][:], z[rows, :], w=[('zb', i)])
            S.op('dve', lambda e: e.tensor_tensor(out=yb0[i][:], in0=yb0[i][:], in1=yb1[i][:], op=ALU.add),
                 r=[('yb0', i), ('yb1', i)], w=[('yb0', i)])
            S.op('act', lambda e: e.activation(out=zb[i][:], in_=zb[i][:], func=AF.Silu), r=[('zb', i)], w=[('zb', i)])
            S.op('dve', lambda e: e.tensor_tensor(out=yb0[i][:], in0=yb0[i][:], in1=zb[i][:], op=ALU.mult),
                 r=[('yb0', i), ('zb', i)], w=[('yb0', i)])
            S.op('act', lambda e: e.activation(out=junk[:], in_=yb0[i][:], func=AF.Square, accum_out=ss[:, t:t + 1]),
                 r=[('yb0', i)], w=['junk', 'ss'])
            rstd_cols(S, ss, ((t, t + 1), [(t, t + 1)]), [1.0 / 1024], 'ss')
            S.op('dve', lambda e: e.tensor_scalar(out=yb0[i][:], in0=yb0[i][:], scalar1=ss[:, t:t + 1], scalar2=None,
                                                  op0=ALU.mult), r=[('yb0', i), 'ss'], w=[('yb0', i)])
            o = ob[i]
            for kk in range(2):
                ps, pk = S.next_ps()
                for j in range(4):
                    k = kk * 4 + j
                    S.op('pe', lambda e: e.transpose(out=ps[:, j * 128:(j + 1) * 128], in_=yb0[i][:, k * 128:(k + 1) * 128],
                                                     identity=ident[:]), r=[('yb0', i), 'ident'], w=[pk])
                for j in range(4):
                    k = kk * 4 + j
                    S.op('act', lambda e: e.activation(out=o[:, k, :], in_=ps[:, j * 128:(j + 1) * 128], func=AF.Identity,
                                                       scale=gs[:, k:k + 1]), r=[pk, 'gs'], w=[('ob', i)])
            S.dma('pool', ysv[:, :, rows], o[:], r=[('ob', i)], is_out=True)
        va = S.sb('va', [128, 8, 512]); vb = S.sb('vb', [128, 8, 512]); vu = S.sb('vu', [128, 8, 512])
        x2 = S.sb('x2', [128, 8, 512]); sg = S.sb('sg', [128, 512]); o5 = [S.sb(f'o5_{i}', [128, 512]) for i in range(2)]
        y5v = y5T.rearrange("(k p) n -> p k n", p=128)
        oi = 0
        for (n0, n1) in TOK_RANGES:
            n = n1 - n0
            for src, dst, kname in [(y5a, va, 'va'), (y5b, vb, 'vb'), (uT, vu, 'vu')]:
                S.dma('sp', dst[:, :, 0:n], src.rearrange("(k p) n -> p k n", p=128)[:, :, n0:n1], w=[kname])
            S.op('dve', lambda e: e.tensor_tensor(out=vu[:, :, 0:n], in0=vu[:, :, 0:n], in1=bc(sd[:], 2, [128, 8, n]), op=ALU.mult),
                 r=['vu', 'sd'], w=['vu'])
            S.op('dve', lambda e: e.tensor_tensor(out=va[:, :, 0:n], in0=va[:, :, 0:n], in1=vb[:, :, 0:n], op=ALU.add),
                 r=['va', 'vb'], w=['va'])
            S.op('dve', lambda e: e.tensor_tensor(out=va[:, :, 0:n], in0=va[:, :, 0:n], in1=vu[:, :, 0:n], op=ALU.add),
                 r=['va', 'vu'], w=['va'])
            S.op('dve', lambda e: e.tensor_tensor(out=x2[:, :, 0:n], in0=va[:, :, 0:n], in1=va[:, :, 0:n], op=ALU.mult),
                 r=['va'], w=['x2'])
            S.op('dve', lambda e: e.tensor_scalar(out=x2[:, :, 0:n], in0=x2[:, :, 0:n], scalar1=0.044715, scalar2=1.0,
                                                  op0=ALU.mult, op1=ALU.add), r=['x2'], w=['x2'])
            S.op('dve', lambda e: e.tensor_tensor(out=x2[:, :, 0:n], in0=x2[:, :, 0:n], in1=va[:, :, 0:n], op=ALU.mult),
                 r=['x2', 'va'], w=['x2'])
            S.op('act', lambda e: e.activation(out=x2[:, :, 0:n], in_=x2[:, :, 0:n], func=AF.Sigmoid, scale=GELU_K2),
                 r=['x2'], w=['x2'])
            S.op('dve', lambda e: e.tensor_tensor(out=va[:, :, 0:n], in0=va[:, :, 0:n], in1=x2[:, :, 0:n], op=ALU.mult),
                 r=['va', 'x2'], w=['va'])
            for m in range(8):
                ps, pk = S.next_ps()
                for k in range(8):
                    S.op('pe', lambda e: e.matmul(ps[:, 0:n], lhsT=wg[:, k, m * 128:(m + 1) * 128], rhs=va[:, k, 0:n],
                                                  start=(k == 0), stop=(k == 7)), r=['wg', 'va'], w=[pk])
                S.op('act', lambda e: e.activation(out=sg[:, 0:n], in_=ps[:, 0:n], func=AF.Sigmoid), r=[pk], w=['sg'])
                o = o5[oi % 2]
                ok = ('o5', oi % 2)
                oi += 1
                S.op('dve', lambda e: e.tensor_tensor(out=o[:, 0:n], in0=va[:, m, 0:n], in1=sg[:, 0:n], op=ALU.mult),
                     r=['va', 'sg'], w=[ok])
                S.dma('pool', y5v[:, m, n0:n1], o[:, 0:n], r=[ok], is_out=True)
        S.finish()
    return nc


def run_D1(yssd, ys5, px_tm, px_fm, inputs, layer):
    in_maps = []
    for c in range(NCORES):
        b, h = c // 2, c % 2
        r0, r1 = core_rows(b, h)
        ca = np.ascontiguousarray
        in_maps.append({"y0": ca(yssd[b, 0, r0:r1]), "y1": ca(yssd[b, 1, r0:r1]), "z": ca(px_tm[b, r0:r1, 0:1024]),
                        "gssd": ca(inputs['ssd_norm_gain'][layer].reshape(8, 128).T),
                        "y5a": ca(ys5[b, 0, :, r0:r1]), "y5b": ca(ys5[b, 1, :, r0:r1]), "uT": ca(px_fm[b, 2048:3072, r0:r1]),
                        "s5d": ca(inputs['s5_d'][layer].reshape(8, 128).T), "wglu": inputs['s5_w_glu'][layer]})
    res = run_bass_kernel_spmd(build_D1(), in_maps, core_ids=list(range(NCORES)))
    ysT = np.zeros((NB, 1024, LT), np.float32)
    y5T = np.zeros((NB, 1024, LT), np.float32)
    for c in range(NCORES):
        b, h = c // 2, c % 2
        r0, r1 = core_rows(b, h)
        ysT[b, :, r0:r1] = res.results[c]["ysT"]
        y5T[b, :, r0:r1] = res.results[c]["y5T"]
    return ysT, y5T


def tile_gate(S, dst, gx, flags, t, cols, key):
    S.op('dve', lambda e: e.scalar_tensor_tensor(out=dst, in0=gx[:, 1, cols], scalar=flags[:, t:t + 1], in1=gx[:, 0, cols],
                                                 op0=ALU.mult, op1=ALU.add), r=['gx', 'flags'], w=[key])


def load_gx(S, gxr, flg, NT):
    gx = S.sb('gx', [128, 2, D])
    flags = S.sb('flags', [128, NT])
    S.dma('sp', gx[:, 0, :], gxr[0:1, :].to_broadcast([128, D]), w=['gx'])
    S.dma('sp', gx[:, 1, :], gxr[1:2, :].to_broadcast([128, D]), w=['gx'])
    S.dma('sp', flags[:], flg[:, :], w=['flags'])
    S.op('dve', lambda e: e.tensor_tensor(out=gx[:, 1, :], in0=gx[:, 1, :], in1=gx[:, 0, :], op=ALU.subtract),
         r=['gx'], w=['gx'])
    return gx, flags


def build_D2():
    nc = new_nc()
    NT = NT_B
    NTOK = NT * 128
    yin = [din(nc, nm, [1024, NTOK]) for nm in ("ysT", "omT", "y5T")]
    gT = din(nc, "gT", [6144, NTOK])
    wb = [din(nc, nm, [1024, D]) for nm in ("wbs", "wbm", "wb5")]
    wo = din(nc, "wo", [D, D])
    x = din(nc, "x", [NTOK, D])
    gxr = din(nc, "gxr", [2, D])
    flg = din(nc, "flg", [128, NT])
    x1 = dout(nc, "x1", [NTOK, D])
    with ExitStack() as ctx:
        S = get_sched(nc, ctx)
        gx, flags = load_gx(S, gxr, flg, NT)
        mT = S.sb('mT', [128, 16, NTOK])
        big = S.sb('big', [128, 18432])
        yT = [big[:, br * 4096:(br + 1) * 4096].rearrange("p (k n) -> p k n", k=8) for br in range(3)]
        wbt = [big[:, 12288 + br * 2048:12288 + (br + 1) * 2048].rearrange("p (k n) -> p k n", k=8) for br in range(3)]
        wot = [big[:, i * 8192:(i + 1) * 8192].rearrange("p (k n) -> p k n", k=16) for i in range(2)]
        gtl = [S.sb(f'gt{i}', [128, 512]) for i in range(3)]
        tmp = S.sb('tmp', [128, 512])
        gi = 0
        for (n0, n1) in TOK_RANGES:
            n = n1 - n0
            for br in range(3):
                S.dma('pool', R(yT[br][:, :, 0:n]), yin[br].rearrange("(k p) n -> p k n", p=128)[:, :, n0:n1], w=[('yT', br)])
            for db in range(8):
                for br in range(3):
                    S.dma('pool', R(wbt[br][:, :, :]), wb[br].rearrange("(k p) n -> p k n", p=128)[:, :, db * 256:(db + 1) * 256],
                          w=[('wbt', br)])
                for mm in range(2):
                    m = db * 2 + mm
                    for br in range(3):
                        g = gtl[gi % 3]
                        gk = ('gt', gi % 3)
                        gi += 1
                        S.dma('sp', g[:, 0:n], gT[br * 2048 + m * 128:br * 2048 + (m + 1) * 128, n0:n1], w=[gk])
                        S.op('act', lambda e: e.activation(out=g[:, 0:n], in_=g[:, 0:n], func=AF.Sigmoid), r=[gk], w=[gk])
                        ps, pk = S.next_ps()
                        for k in range(8):
                            S.op('pe', lambda e: e.matmul(ps[:, 0:n], lhsT=R(wbt[br][:, k, mm * 128:(mm + 1) * 128]),
                                                          rhs=R(yT[br][:, k, 0:n]), start=(k == 0), stop=(k == 7)),
                                 r=[('wbt', br), ('yT', br)], w=[pk])
                        if br == 0:
                            S.op('dve', lambda e: e.tensor_tensor(out=R(mT[:, m, n0:n1]), in0=ps[:, 0:n], in1=g[:, 0:n], op=ALU.mult),
                                 r=[pk, gk], w=[('mT', m)])
                        else:
                            S.op('dve', lambda e: e.tensor_tensor(out=tmp[:, 0:n], in0=ps[:, 0:n], in1=g[:, 0:n], op=ALU.mult),
                                 r=[pk, gk], w=['tmp'])
                            S.op('pool', lambda e: e.tensor_tensor(out=R(mT[:, m, n0:n1]), in0=mT[:, m, n0:n1], in1=tmp[:, 0:n],
                                                                   op=ALU.add), r=[('mT', m), 'tmp'], w=[('mT', m)])
        allk = [('yT', br) for br in range(3)] + [('wbt', br) for br in range(3)]
        mkeys = [('mT', m) for m in range(16)]
        xt = [S.sb(f'xt{i}', [128, 512]) for i in range(3)]
        gtb = S.sb('gtb', [128, 512])
        xi = 0
        for cb in range(4):
            cols = slice(cb * 512, (cb + 1) * 512)
            w_ = wot[cb % 2]
            wk = ('wot', cb % 2)
            S.dma('pool', R(w_[:, 0:8, :]), wo.rearrange("(k p) n -> p k n", p=128)[:, 0:8, cols], w=[wk] + (allk if cb < 2 else []))
            S.dma('pool', R(w_[:, 8:16, :]), wo.rearrange("(k p) n -> p k n", p=128)[:, 8:16, cols], w=[wk])
            for t in range(NT):
                rows = slice(t * 128, (t + 1) * 128)
                xx = xt[xi % 3]
                xk = ('xt', xi % 3)
                xi += 1
                S.dma('sp', xx[:], x[rows, cols], w=[xk])
                ps, pk = S.next_ps()
                for k in range(16):
                    S.op('pe', lambda e: e.matmul(ps[:, :], lhsT=R(mT[:, k, rows]), rhs=R(w_[:, k, :]), start=(k == 0), stop=(k == 15)),
                         r=mkeys + [wk], w=[pk])
                tile_gate(S, gtb[:], gx, flags, t, cols, 'gtb')
                S.op('dve', lambda e: e.tensor_tensor(out=gtb[:], in0=ps[:, :], in1=gtb[:], op=ALU.mult), r=[pk, 'gtb'], w=['gtb'])
                S.op('pool', lambda e: e.tensor_tensor(out=xx[:], in0=xx[:], in1=gtb[:], op=ALU.add), r=[xk, 'gtb'], w=[xk])
                S.dma('sp', x1[rows, cols], xx[:], r=[xk], is_out=True)
        S.finish()
    return nc


def make_flags(h, NT=9):
    f = np.zeros((128, NT), np.float32)
    for t in range(NT):
        if tile_is_ctx(h, t):
            f[:, t] = 1.0
    return f


def mod_vec(modT, which, r):
    return np.ascontiguousarray(modT[:, which * 16:(which + 1) * 16, r].T).reshape(D)


def run_D2(ysT, omT, y5T, px_fm, xseq, modT, inputs, layer):
    in_maps = []
    ca = np.ascontiguousarray
    for c in range(NCORES):
        b, h = c // 2, c % 2
        r0, r1 = core_rows(b, h)
        in_maps.append({"ysT": ca(ysT[b, :, r0:r1]), "omT": ca(omT[b, :, r0:r1]), "y5T": ca(y5T[b, :, r0:r1]),
                        "gT": ca(px_fm[b, 3072:9216, r0:r1]),
                        "wbs": inputs['w_branch_ssd'][layer], "wbm": inputs['w_branch_mla'][layer],
                        "wb5": inputs['w_branch_s5'][layer], "wo": inputs['w_out'][layer],
                        "x": ca(xseq[b, r0:r1]), "gxr": np.stack([mod_vec(modT, 2, b), mod_vec(modT, 2, 4)], 0),
                        "flg": make_flags(h)})
    res = run_bass_kernel_spmd(build_D2(), in_maps, core_ids=list(range(NCORES)))
    x1 = np.zeros((NB, LT, D), np.float32)
    for c in range(NCORES):
        b, h = c // 2, c % 2
        r0, r1 = core_rows(b, h)
        x1[b, r0:r1] = res.results[c]["x1"]
    return x1


def build_D3():
    nc = new_nc()
    NT = NT_B
    NTOK = NT * 128
    xin = din(nc, "x", [NTOK, D])
    msel = din(nc, "msel", [128, NT * 16 * 2])
    gn = din(nc, "gn", [128, 16])
    wr = din(nc, "wr", [D, 16])
    hx = dout(nc, "hx", [NTOK, D])
    aff = dout(nc, "aff", [NTOK, 16])
    affTo = dout(nc, "affT", [16, NTOK])
    with ExitStack() as ctx:
        S = get_sched(nc, ctx)
        ident = make_ident(S)
        hT = S.sb('hT', [128, 16, NTOK])
        ms, g1 = load_mod(S, msel, gn, NT)
        wrt = S.sb('wrt', [128, 16, 16])
        S.dma('sp', wrt[:], wr.rearrange("(k p) e -> p k e", p=128), w=['wrt'])
        norm_mod_tiles(S, xin, NT, ms, g1, hT, ident, 'n2')
        hb = [S.sb(f'hb{i}', [128, D]) for i in range(2)]
        lg = S.sb('lg', [128, 16]); mx = S.sb('mx', [128, 1]); sm_ = S.sb('sm', [128, 1])
        ab = [S.sb(f'ab{i}', [128, 16]) for i in range(2)]
        atb = [S.sb(f'atb{i}', [16, 128]) for i in range(2)]
        for t in range(NT):
            rows = slice(t * 128, (t + 1) * 128)
            h = hb[t % 2]
            hk = ('hb', t % 2)
            for kk in range(4):
                ps, pk = S.next_ps()
                for j in range(4):
                    k = kk * 4 + j
                    S.op('pe', lambda e: e.transpose(out=ps[:, j * 128:(j + 1) * 128], in_=hT[:, k, rows], identity=ident[:]),
                         r=[('hT', t), 'ident'], w=[pk])
                if kk % 2 == 0:
                    S.op('act', lambda e: e.activation(out=h[:, kk * 512:(kk + 1) * 512], in_=ps[:, :], func=AF.Copy), r=[pk], w=[hk])
                else:
                    S.op('dve', lambda e: e.tensor_copy(out=h[:, kk * 512:(kk + 1) * 512], in_=ps[:, :]), r=[pk], w=[hk])
            S.dma('pool', hx[rows, :], h[:], r=[hk], is_out=True)
            ps, pk = S.next_ps()
            for k in range(16):
                S.op('pe', lambda e: e.matmul(ps[:, 0:16], lhsT=hT[:, k, rows], rhs=wrt[:, k, :], start=(k == 0), stop=(k == 15)),
                     r=[('hT', t), 'wrt'], w=[pk])
            a = ab[t % 2]
            ak = ('ab', t % 2)
            S.op('dve', lambda e: e.tensor_copy(out=lg[:], in_=ps[:, 0:16]), r=[pk], w=['lg'])
            S.op('dve', lambda e: e.tensor_reduce(out=mx[:], in_=lg[:], axis=AX.X, op=ALU.max), r=['lg'], w=['mx'])
            S.op('dve', lambda e: e.tensor_scalar(out=mx[:], in0=mx[:], scalar1=-1.0, scalar2=None, op0=ALU.mult), r=['mx'], w=['mx'])
            S.op('dve', lambda e: e.memset(sm_[:], 0.0), w=['sm'])
            S.op('act', lambda e: e.activation(out=a[:], in_=lg[:], func=AF.Exp, bias=mx[:, 0:1], scale=1.0, accum_out=sm_[:, 0:1]),
                 r=['lg', 'mx', 'sm'], w=[ak, 'sm'])
            S.op('dve', lambda e: e.reciprocal(out=sm_[:], in_=sm_[:]), r=['sm'], w=['sm'])
            S.op('dve', lambda e: e.tensor_scalar(out=a[:], in0=a[:], scalar1=sm_[:, 0:1], scalar2=None, op0=ALU.mult),
                 r=[ak, 'sm'], w=[ak])
            S.dma('pool', aff[rows, :], a[:], r=[ak], is_out=True)
            ps, pk = S.next_ps()
            S.op('pe', lambda e: e.transpose(out=ps[0:16, 0:128], in_=a[:, 0:16], identity=ident[:]), r=[ak, 'ident'], w=[pk])
            at = atb[t % 2]
            atk = ('atb', t % 2)
            S.op('dve', lambda e: e.tensor_copy(out=at[:], in_=ps[0:16, 0:128]), r=[pk], w=[atk])
            S.dma('pool', affTo[:, rows], at[:], r=[atk], is_out=True)
        S.finish()
    return nc


def run_D3(x1, modT, inputs, layer):
    gn = np.ascontiguousarray(inputs['norm2_gain'][layer].reshape(16, 128).T)
    in_maps = []
    for c in range(NCORES):
        b, h = c // 2, c % 2
        r0, r1 = core_rows(b, h)
        in_maps.append({"x": np.ascontiguousarray(x1[b, r0:r1]), "msel": make_msel(modT, b, h, 4, 3), "gn": gn,
                        "wr": inputs['moe_router'][layer]})
    res = run_bass_kernel_spmd(build_D3(), in_maps, core_ids=list(range(NCORES)))
    hx2 = np.zeros((NB, LT, D), np.float32)
    aff = np.zeros((NB, LT, 16), np.float32)
    for c in range(NCORES):
        b, h = c // 2, c % 2
        r0, r1 = core_rows(b, h)
        hx2[b, r0:r1] = res.results[c]["hx"]
        aff[b, r0:r1] = res.results[c]["aff"]
    return hx2, aff


def build_E(with_ctx, do_zero=True):
    nc = new_nc()
    NE = 8
    NSL = 288 if with_ctx else 256
    affT = din(nc, "affT", [NE, LT])
    hx = din(nc, "hx", [LT, D])
    wg = din(nc, "wg", [NE, D, D]); wu = din(nc, "wu", [NE, D, D]); wd = din(nc, "wd", [NE, D, D])
    delta = dout(nc, "delta", [LT, D])
    with ExitStack() as ctx:
        S = get_sched(nc, ctx)
        ident = make_ident(S)
        ysb = [S.sb(f'ys{i}', [128, D]) for i in range(2)]
        if do_zero:
            S.op('pool', lambda e: e.memset(ysb[0][:], 0.0), w=[('ys', 0)])
            for t in range(NTL):
                S.dma('pool', delta[t * 128:(t + 1) * 128, :], ysb[0][:], r=[('ys', 0)], w=['delta'], is_out=True)
        work = S.sb('work', [NE, LT])
        S.dma('sp', work[:], affT[:, :], w=['work'])
        vals = S.sb('vals', [NE, 288]); idxu = S.sb('idxu', [NE, 288], U32); idxf = S.sb('idxf', [NE, 288])
        S.op('dve', lambda e: e.memset(vals[:], 0.0), w=['vals'])
        S.op('dve', lambda e: e.memset(idxf[:], 0.0), w=['idxf'])
        segs = [(CTX, LT, 0, 32, float(CTX))] + ([(0, CTX, 256, 4, 0.0)] if with_ctx else [])
        for (a0, a1, s0, rounds, off) in segs:
            for r in range(rounds):
                sl = slice(s0 + r * 8, s0 + r * 8 + 8)
                S.op('dve', lambda e: e.max(out=vals[:, sl], in_=work[:, a0:a1]), r=['work'], w=['vals'])
                S.op('dve', lambda e: e.max_index(out=idxu[:, sl], in_max=vals[:, sl], in_values=work[:, a0:a1]),
                     r=['work', 'vals'], w=['idxu'])
                S.op('dve', lambda e: e.match_replace(out=work[:, a0:a1], in_to_replace=vals[:, sl], in_values=work[:, a0:a1],
                                                      imm_value=-1.0), r=['work', 'vals'], w=['work'])
            S.op('dve', lambda e: e.tensor_copy(out=idxf[:, s0:s0 + rounds * 8], in_=idxu[:, s0:s0 + rounds * 8]),
                 r=['idxu'], w=['idxf'])
            if off != 0.0:
                S.op('dve', lambda e: e.tensor_scalar(out=idxf[:, s0:s0 + rounds * 8], in0=idxf[:, s0:s0 + rounds * 8],
                                                      scalar1=off, scalar2=None, op0=ALU.add), r=['idxf'], w=['idxf'])
        gTt = S.sb('gTt', [128, 3, NE]); iTf = S.sb('iTf', [128, 3, NE]); iTu = S.sb('iTu', [128, 3, NE], U32)
        S.op('dve', lambda e: e.memset(iTf[:], 0.0), w=['iTf'])
        S.op('dve', lambda e: e.memset(gTt[:], 0.0), w=['gTt'])
        tiles = [(0, 128), (128, 128)] + ([(256, 32)] if with_ctx else [])
        for st, (c0, nr) in enumerate(tiles):
            for src, dst, dk in [(vals, gTt, 'gTt'), (idxf, iTf, 'iTf')]:
                ps, pk = S.next_ps()
                S.op('pe', lambda e: e.transpose(out=ps[0:nr, 0:NE], in_=src[:, c0:c0 + nr], identity=ident[0:NE, 0:NE]),
                     r=['vals', 'idxf', 'ident'], w=[pk])
                S.op('dve', lambda e: e.tensor_copy(out=dst[0:nr, st, :], in_=ps[0:nr, 0:NE]), r=[pk], w=[dk])
        S.op('dve', lambda e: e.tensor_copy(out=iTu[:], in_=iTf[:]), r=['iTf'], w=['iTu'])
        NWB = 4
        wbuf = [S.sb(f'wbuf{i}', [128, 16, 512]) for i in range(NWB)]
        wcnt = [0]

        def load_w(src2d, cols):
            i = wcnt[0] % NWB
            wcnt[0] += 1
            v = src2d.rearrange("(k p) n -> p k n", p=128)
            S.dma('pool', R(wbuf[i][:, 0:8, :]), v[:, 0:8, cols], w=[('wbuf', i)])
            S.dma('pool', R(wbuf[i][:, 8:16, :]), v[:, 8:16, cols], w=[('wbuf', i)])
            return wbuf[i], ('wbuf', i)

        xs = [S.sb(f'xs{i}', [128, D]) for i in range(1)]
        xsT = S.sb('xsT', [128, 16, NSL])
        hidT = S.sb('hidT', [128, 16, NSL])
        sg = S.sb('sg', [128, NSL])
        xi = 0
        yi = 0
        for e_ in range(NE):
            for st, (c0, nr) in enumerate(tiles):
                xx = xs[0]
                xk = ('xs', 0)
                xi += 1
                S._deps('pool', ['iTu'], [xk])
                S._guard_dma('pool')
                ins = nc.gpsimd.indirect_dma_start(out=xx[0:nr, :], out_offset=None, in_=hx[:, :],
                                                   in_offset=bass.IndirectOffsetOnAxis(ap=iTu[0:nr, st, e_:e_ + 1], axis=0))
                tok = S._finish_dma('pool', ins, ['iTu'], [xk])
                for kk in range(4):
                    ps, pk = S.next_ps()
                    for j in range(4):
                        k = kk * 4 + j
                        S.op('pe', lambda e: e.transpose(out=ps[:, j * 128:j * 128 + nr], in_=xx[0:nr, k * 128:(k + 1) * 128],
                                                         identity=ident[0:nr, 0:nr]), r=[xk, 'ident'], w=[pk])
                    S.op('act', lambda e: e.activation(out=R(xsT[:, kk * 4:kk * 4 + 4, c0:c0 + nr]),
                                                       in_=ps[:, :].rearrange("p (j c) -> p j c", c=128)[:, :, 0:nr], func=AF.Copy),
                         r=[pk], w=['xsT'])
            for fb in range(4):
                fcols = slice(fb * 512, (fb + 1) * 512)
                wgt, wgk = load_w(wg[e_], fcols)
                wut, wuk = load_w(wu[e_], fcols)
                for ff in range(4):
                    f = fb * 4 + ff
                    psg, kg_ = S.next_ps()
                    psu, ku_ = S.next_ps()
                    for k in range(16):
                        S.op('pe', lambda e: e.matmul(psg[:, 0:NSL], lhsT=R(wgt[:, k, ff * 128:(ff + 1) * 128]), rhs=R(xsT[:, k, :]),
                                                      start=(k == 0), stop=(k == 15)), r=[wgk, 'xsT'], w=[kg_])
                    for k in range(16):
                        S.op('pe', lambda e: e.matmul(psu[:, 0:NSL], lhsT=R(wut[:, k, ff * 128:(ff + 1) * 128]), rhs=R(xsT[:, k, :]),
                                                      start=(k == 0), stop=(k == 15)), r=[wuk, 'xsT'], w=[ku_])
                    S.op('act', lambda e: e.activation(out=sg[:, :], in_=psg[:, 0:NSL], func=AF.Silu), r=[kg_], w=['sg'])
                    S.op('dve', lambda e: e.tensor_tensor(out=R(hidT[:, f, :]), in0=psu[:, 0:NSL], in1=sg[:, :], op=ALU.mult),
                         r=[ku_, 'sg'], w=['hidT'])
            yts = []
            for st, (c0, nr) in enumerate(tiles):
                yts.append((ysb[yi % 2] if st < 2 else xs[0], ('ys', yi % 2) if st < 2 else ('xs', 0)))
                if st < 2:
                    yi += 1
            for cb in range(4):
                cols = slice(cb * 512, (cb + 1) * 512)
                wdt, wdk = load_w(wd[e_], cols)
                for st, (c0, nr) in enumerate(tiles):
                    yt, yk = yts[st]
                    ps, pk = S.next_ps()
                    for f in range(16):
                        S.op('pe', lambda e: e.matmul(ps[0:nr, :], lhsT=R(hidT[:, f, c0:c0 + nr]), rhs=R(wdt[:, f, :]),
                                                      start=(f == 0), stop=(f == 15)), r=['hidT', wdk], w=[pk])
                    S.op('act', lambda e: e.activation(out=yt[0:nr, cols], in_=ps[0:nr, :], func=AF.Copy,
                                                       scale=gTt[0:nr, st, e_:e_ + 1]), r=[pk, 'gTt'], w=[yk])
            for st, (c0, nr) in enumerate(tiles):
                yt, yk = yts[st]
                S._deps('pool', [yk, 'iTu'], ['delta'])
                S._guard_dma('pool')
                ins = nc.gpsimd.indirect_dma_start(out=delta[:, :], out_offset=bass.IndirectOffsetOnAxis(ap=iTu[0:nr, st, e_:e_ + 1], axis=0),
                                                   in_=yt[0:nr, :], in_offset=None, compute_op=ALU.add)
                tok = S._finish_dma('pool', ins, [yk, 'iTu'], ['delta'])
                S.out_tokens.append(tok)
        S.finish()
    return nc


def run_E(aff, hx2, inputs, layer, with_ctx, batches=(0, 1, 2, 3)):
    in_maps = []
    cores = []
    for b in batches:
        for hf in range(2):
            cores.append((b, hf))
            es = slice(8 * hf, 8 * hf + 8)
            in_maps.append({"affT": np.ascontiguousarray(aff[b].T[es]), "hx": np.ascontiguousarray(hx2[b]),
                            "wg": inputs['moe_w_gate'][layer, es], "wu": inputs['moe_w_up'][layer, es],
                            "wd": inputs['moe_w_down'][layer, es]})
    res = run_bass_kernel_spmd(build_E(with_ctx), in_maps, core_ids=list(range(len(cores))))
    out = np.zeros((NB, 2, LT, D), np.float32)
    for i, (b, hf) in enumerate(cores):
        out[b, hf] = res.results[i]["delta"]
    return out


def build_F(single=False):
    nc = new_nc()
    NT = NT_B
    NTOK = NT * 128
    x1 = din(nc, "x1", [NTOK, D]); da = din(nc, "da", [NTOK, D])
    db = None if single else din(nc, "db", [NTOK, D])
    gxr = din(nc, "gxr", [2, D]); flg = din(nc, "flg", [128, NT])
    x2 = dout(nc, "x2", [NTOK, D])
    with ExitStack() as ctx:
        S = get_sched(nc, ctx)
        gx, flags = load_gx(S, gxr, flg, NT)
        xa = [S.sb(f'xa{i}', [128, D]) for i in range(2)]
        ta = [S.sb(f'ta{i}', [128, D]) for i in range(2)]
        tb = [S.sb(f'tb{i}', [128, D]) for i in range(2)]
        gt = S.sb('gt', [128, D])
        for t in range(NT):
            i = t % 2
            rows = slice(t * 128, (t + 1) * 128)
            S.dma('sp', xa[i][:], x1[rows, :], w=[('xa', i)])
            S.dma('sp', ta[i][:], da[rows, :], w=[('ta', i)])
            if not single:
                S.dma('act', tb[i][:], db[rows, :], w=[('tb', i)])
            tile_gate(S, gt[:], gx, flags, t, slice(0, D), 'gt')
            if not single:
                S.op('pool', lambda e: e.tensor_tensor(out=ta[i][:], in0=ta[i][:], in1=tb[i][:], op=ALU.add),
                     r=[('ta', i), ('tb', i)], w=[('ta', i)])
            S.op('dve', lambda e: e.tensor_tensor(out=ta[i][:], in0=ta[i][:], in1=gt[:], op=ALU.mult), r=[('ta', i), 'gt'], w=[('ta', i)])
            S.op('pool', lambda e: e.tensor_tensor(out=xa[i][:], in0=xa[i][:], in1=ta[i][:], op=ALU.add),
                 r=[('xa', i), ('ta', i)], w=[('xa', i)])
            S.dma('pool', x2[rows, :], xa[i][:], r=[('xa', i)], is_out=True)
        S.finish()
    return nc


def run_F(x1, dl, modT):
    in_maps = []
    ca = np.ascontiguousarray
    for c in range(NCORES):
        b, h = c // 2, c % 2
        r0, r1 = core_rows(b, h)
        in_maps.append({"x1": ca(x1[b, r0:r1]), "da": ca(dl[b, 0, r0:r1]), "db": ca(dl[b, 1, r0:r1]),
                        "gxr": np.stack([mod_vec(modT, 5, b), mod_vec(modT, 5, 4)], 0), "flg": make_flags(h)})
    res = run_bass_kernel_spmd(build_F(), in_maps, core_ids=list(range(NCORES)))
    x2 = np.zeros((NB, LT, D), np.float32)
    for c in range(NCORES):
        b, h = c // 2, c % 2
        r0, r1 = core_rows(b, h)
        x2[b, r0:r1] = res.results[c]["x2"]
    return x2


def build_A2():
    nc = new_nc()
    cT = din(nc, "cT2", [128, 32])
    w = din(nc, "wmod", [D, 12288])
    b = din(nc, "bmod", [128, 96])
    modT_d = dout(nc, "modT", [128, 2 * 96])
    gvec_d = dout(nc, "gvec", [2, 12288])
    with ExitStack() as ctx:
        S = get_sched(nc, ctx)
        ident = make_ident(S)
        ct = S.sb('ct', [128, 16, 2]); ca = S.sb('ca', [128, 16, 2]); bt = S.sb('bt', [128, 96])
        mt = S.sb('mt', [128, 2, 96])
        wt = [S.sb(f'wt{i}', [128, 16, 768]) for i in range(2)]
        S.dma('sp', ct[:].rearrange("p k r -> p (k r)"), cT[:, :], w=['ct'])
        S.dma('sp', bt[:], b[:, :], w=['bt'])
        S.op('act', lambda e: e.activation(out=ca[:], in_=ct[:], func=AF.Silu), r=['ct'], w=['ca'])
        wv = w.rearrange("(k p) n -> p k n", p=128)
        for j in range(16):
            wtj = wt[j % 2]
            for g in range(4):
                S.dma('sp' if g % 2 == 0 else 'act', wtj[:, 4 * g:4 * g + 4, :], wv[:, 4 * g:4 * g + 4, j * 768:(j + 1) * 768],
                      w=[('wt', j % 2, g)])
            ps, pk = S.next_ps()
            for m in range(6):
                for k in range(16):
                    S.op('pe', lambda e: e.matmul(ps[:, m * 2:m * 2 + 2], lhsT=wtj[:, k, m * 128:(m + 1) * 128],
                                                  rhs=ca[:, k, :], start=(k == 0), stop=(k == 15)),
                         r=['ca', ('wt', j % 2, k // 4)], w=[pk])
            for m in range(6):
                c = j * 6 + m
                S.op('act', lambda e: e.activation(out=mt[:, :, c], in_=ps[:, m * 2:m * 2 + 2], func=AF.Identity,
                                                   bias=bt[:, c:c + 1], scale=1.0), r=[pk, 'bt'], w=['mt'])
        S.dma('pool', modT_d[:, :], mt[:].rearrange("p r c -> p (r c)"), r=['mt'], is_out=True)
        gv = S.sb('gv', [96, 2, 128])
        for r in range(2):
            ps, pk = S.next_ps()
            S.op('pe', lambda e: e.transpose(out=ps[0:96, 0:128], in_=mt[:, r, :], identity=ident[:]), r=['mt', 'ident'], w=[pk])
            S.op('dve', lambda e: e.tensor_copy(out=gv[:, r, :], in_=ps[0:96, 0:128]), r=[pk], w=['gv'])
            S.dma('pool', gvec_d[r, :].rearrange("(c p) -> c p", p=128), gv[:, r, :], r=['gv'], is_out=True)
        S.finish()
    return nc


def build_msel(modT_d, out_d, rows, sc_i, sh_i):
    nc = new_nc()
    with ExitStack() as ctx:
        S = get_sched(nc, ctx)
        mv = modT_d.rearrange("p (r c) -> p r c", r=2)
        ov = out_d.rearrange("p (t s k) -> p t s k", s=2, k=16)
        for t, r in enumerate(rows):
            S.dma('sp', ov[:, t, 0, :], mv[:, r, sc_i * 16:(sc_i + 1) * 16])
            S.dma('act', ov[:, t, 1, :], mv[:, r, sh_i * 16:(sh_i + 1) * 16])
        S.finish()


FUSED_INPUT_SPECS = None


_DBG = {'export': (), 'stop': None}


def build_fused():
    nc = bass.Bass("TRN2", target_bir_lowering=False)
    ext = {}

    def EI(name, shape, dt=F32):
        ext[name] = nc.dram_tensor(name, list(shape), dt, kind="ExternalInput").ap()
        return ext[name]

    def SC(name, shape, dt=F32):
        kind = "ExternalOutput" if name in _DBG['export'] else "Internal"
        return nc.dram_tensor(name, list(shape), dt, kind=kind).ap()

    class _Stop(Exception):
        pass

    nst = [0]

    def chk():
        nst[0] += 1
        if _DBG['stop'] is not None and nst[0] >= _DBG['stop']:
            raise _Stop()

    xs0 = EI("xseq", [LT, D])
    out = nc.dram_tensor("out", [SEQ, D], F32, kind="ExternalOutput").ap()
    cs = EI("cs", [LT, 32])
    flg = [EI("flg0", [128, 9]), EI("flg1", [128, 9])]
    WSPEC = dict(cT2=[128, 32], wmod=[D, 12288], bmod=[128, 96], gn1=[128, 16], gn2=[128, 16], win=[D, PROJ_IN],
                 cw=[128, 80], cb=[128, 16], abd=[128, 96], gqa=[128, 4], gkv=[128, 2], wq=[512, 1536], wkv=[256, 2048],
                 qg=[128, 96], kg=[128, 96], lamp=[128, 192], lamr=[32, 6 * 4096], bT=[32, 4 * 4096], cbd=[128, 4096],
                 gssd=[128, 8], s5d=[128, 8], wglu=[1024, 1024], wbs=[1024, D], wbm=[1024, D], wb5=[1024, D], wo=[D, D],
                 wr=[D, 16], wg=[16, D, D], wu=[16, D, D], wd=[16, D, D])

    class LazyW(dict):
        def __init__(self, l):
            super().__init__()
            self.l = l

        def __missing__(self, k):
            self[k] = EI(f"l{self.l}_{k}", WSPEC[k])
            return self[k]

    L = [LazyW(0), LazyW(1)]
    modT = SC("modT", [128, 192]); gvec = SC("gvec", [2, 12288])
    msel1 = [SC(f"msel1_{h}", [128, 9 * 32]) for h in range(2)]
    msel2 = [SC(f"msel2_{h}", [128, 9 * 32]) for h in range(2)]
    px_tm = SC("px_tm", [LT, 1840]); px_fm = SC("px_fm", [9216, LT])
    y0 = SC("y0", [LT, 1024]); y1 = SC("y1", [LT, 1024]); omT = SC("omT", [1024, LT])
    yT0 = SC("yT0", [1024, LT]); yT1 = SC("yT1", [1024, LT])
    ysT = SC("ysT", [1024, LT]); y5T = SC("y5T", [1024, LT])
    x1 = SC("x1", [LT, D]); hx = SC("hx", [LT, D]); aff = SC("aff", [LT, 16]); affT = SC("affT", [16, LT])
    delta = SC("delta", [LT, D])
    xs1 = SC("xs1", [LT, D]); xs2 = SC("xs2", [LT, D])
    with ExitStack() as gctx:
        S = Sched(nc, gctx)
        S.fused = True
        _F['nc'], _F['S'] = nc, S
        try:
            xcur = xs0
            halves = [(0, 1152), (1152, 2304)]
            rows_h = [[1, 1] + [0] * 7, [0] * 9]
            try:
                for l in range(2):
                    if nst[0] < 0:
                        break
                    W = L[l]
                    _F['io'] = dict(cT2=W['cT2'], wmod=W['wmod'], bmod=W['bmod'], modT=modT, gvec=gvec)
                    build_A2()
                    chk()
                    for h in range(2):
                        build_msel(modT, msel1[h], rows_h[h], 1, 0)
                        build_msel(modT, msel2[h], rows_h[h], 4, 3)
                    for h, (r0, r1) in enumerate(halves * _DBG.get('brep', 1)):
                        h = h % 2
                        if 'B' in _DBG.get('skip', ()):
                            continue
                        _F['io'] = dict(x=xcur[r0:r1, :], msel=msel1[h], gn=W['gn1'], w=W['win'], otm=px_tm[r0:r1, :], ofm=px_fm[:, r0:r1])
                        build_B()
                        chk()
                    _F['io'] = dict(xbc=px_fm[0:2048, :], dt=px_tm[:, 1024:1040], cw=W['cw'], cb=W['cb'], abd=W['abd'], y0=y0, y1=y1)
                    c1io = _F['io']
                    if 'C1' not in _DBG.get('skip', ()) and not _DBG.get('c1late'):
                        build_C1()
                    chk()
                    for hf in _DBG.get('c2', (0, 1)):
                        _F['io'] = dict(mla=px_tm[:, 1040:1840], gqa=W['gqa'], gkv=W['gkv'], wq=W['wq'][:, hf * 768:(hf + 1) * 768],
                                        wkv=W['wkv'][:, hf * 1024:(hf + 1) * 1024], qg=W['qg'], kg=W['kg'], cs=cs,
                                        oT=omT[hf * 512:(hf + 1) * 512, :])
                        build_C2()
                        chk()
                    if _DBG.get('c1late'):
                        _F['io'] = c1io
                        build_C1()
                        chk()
                    _F['io'] = dict(u=px_fm[2048:3072, :], lamp=W['lamp'], lamr=W['lamr'], bT=W['bT'], cbd=W['cbd'], yT0=yT0, yT1=yT1)
                    build_C3()
                    chk()
                    for h, (r0, r1) in enumerate(halves):
                        _F['io'] = dict(y0=y0[r0:r1, :], y1=y1[r0:r1, :], z=px_tm[r0:r1, 0:1024], gssd=W['gssd'],
                                        y5a=yT0[:, r0:r1], y5b=yT1[:, r0:r1], uT=px_fm[2048:3072, r0:r1], s5d=W['s5d'], wglu=W['wglu'],
                                        ysT=ysT[:, r0:r1], y5T=y5T[:, r0:r1])
                        build_D1()
                        chk()
                    for h, (r0, r1) in enumerate(halves):
                        _F['io'] = dict(ysT=ysT[:, r0:r1], omT=omT[:, r0:r1], y5T=y5T[:, r0:r1], gT=px_fm[3072:9216, r0:r1],
                                        wbs=W['wbs'], wbm=W['wbm'], wb5=W['wb5'], wo=W['wo'], x=xcur[r0:r1, :],
                                        gxr=gvec[:, 2 * D:3 * D], flg=flg[h], x1=x1[r0:r1, :])
                        build_D2()
                        chk()
                    for h, (r0, r1) in enumerate(halves):
                        _F['io'] = dict(x=x1[r0:r1, :], msel=msel2[h], gn=W['gn2'], wr=W['wr'], hx=hx[r0:r1, :], aff=aff[r0:r1, :],
                                        affT=affT[:, r0:r1])
                        build_D3()
                        chk()
                    for hf in range(2):
                        es = slice(8 * hf, 8 * hf + 8)
                        _F['io'] = dict(affT=affT[es, :], hx=hx, wg=W['wg'][es], wu=W['wu'][es], wd=W['wd'][es], delta=delta)
                        build_E(with_ctx=(l == 0), do_zero=(hf == 0))
                        chk()
                    xnext = xs1 if l == 0 else xs2
                    for h, (r0, r1) in enumerate(halves):
                        if l == 1:
                            pass
                        _F['io'] = dict(x1=x1[r0:r1, :], da=delta[r0:r1, :], gxr=gvec[:, 5 * D:6 * D], flg=flg[h],
                                        x2=xnext[r0:r1, :])
                        build_F(single=True)
                        chk()
                    xcur = xnext
                nc_ = new_nc()
                with ExitStack() as c3:
                    S3 = get_sched(nc_, c3)
                    S3.fused = False
                    for i in range(4):
                        S3.dma('sp' if i % 2 == 0 else 'act', out[i * 512:(i + 1) * 512, :], xcur[CTX + i * 512:CTX + (i + 1) * 512, :],
                               is_out=True)
                    S3.finish()
            except _Stop:
                pass
        finally:
            _F['nc'], _F['S'], _F['io'] = None, None, {}
    global FUSED_INPUT_SPECS
    FUSED_INPUT_SPECS = set(ext.keys())
    return nc


def fused_inputs(inputs, b):
    ca = np.ascontiguousarray
    m = {}
    m["xseq"] = ca(np.concatenate([inputs['ctx'][b], inputs['x'][b]], axis=0))
    m["cs"] = rope_table()
    m["flg0"] = make_flags(0)
    m["flg1"] = make_flags(1)
    for l in range(2):
        p = f"l{l}_"
        c2 = np.stack([inputs['c'][b], inputs['c_ctx']], axis=0)
        m[p + "cT2"] = ca(c2.reshape(2, 16, 128).transpose(2, 1, 0)).reshape(128, 32)
        m[p + "wmod"] = inputs['w_mod'][l]
        m[p + "bmod"] = ca(inputs['b_mod'][l].reshape(96, 128).T)
        m[p + "gn1"] = ca(inputs['norm1_gain'][l].reshape(16, 128).T)
        m[p + "gn2"] = ca(inputs['norm2_gain'][l].reshape(16, 128).T)
        m[p + "win"] = inputs['w_in'][l]
        m[p + "cw"] = ca(inputs['ssd_conv_w'][l].T.reshape(16, 128, 5).transpose(1, 0, 2)).reshape(128, 80)
        m[p + "cb"] = ca(inputs['ssd_conv_b'][l].reshape(16, 128).T)
        abd = np.stack([inputs['ssd_a_log'][l], inputs['ssd_dt_bias'][l], inputs['ssd_d'][l]], 0)
        m[p + "abd"] = rep128(abd.reshape(-1))
        m[p + "gqa"] = ca(inputs['mla_q_a_gain'][l].reshape(4, 128).T)
        m[p + "gkv"] = ca(inputs['mla_kv_a_gain'][l].reshape(2, 128).T)
        m[p + "wq"] = inputs['mla_w_q_b'][l]
        m[p + "wkv"] = inputs['mla_w_kv_b'][l]
        m[p + "qg"] = rep128(inputs['mla_q_gain'][l])
        m[p + "kg"] = rep128(inputs['mla_k_gain'][l])
        lamp, lamr, bTs, cbds = [], [], [], []
        for d in range(2):
            lre = inputs['s5_lam_re'][l, d]
            lim = inputs['s5_lam_im'][l, d]
            ldt = np.broadcast_to(inputs['s5_log_dt'][l, d][:, None], (64, 64))
            P = lambda a: a.reshape(32, 128).T
            Rl = lambda a: np.broadcast_to(a.reshape(1, 4096), (32, 4096))
            lamp += [P(lre), P(lim), P(ldt)]
            lamr += [Rl(lre), Rl(lim), Rl(ldt)]
            for bm in (inputs['s5_b_re'][l, d], inputs['s5_b_im'][l, d]):
                o = np.zeros((2, 16, 32, 2, 64), np.float32)
                bb = bm.reshape(32, 2, 64, 16)
                for gg in range(2):
                    o[gg, :, :, gg, :] = bb[:, gg].transpose(2, 0, 1)
                bTs.append(o.reshape(32, 4096))
            for cm in (inputs['s5_c_re'][l, d], inputs['s5_c_im'][l, d]):
                o = np.zeros((2, 64, 32, 2, 16), np.float32)
                cc = cm.reshape(32, 2, 16, 64)
                for gg in range(2):
                    o[gg, :, :, gg, :] = cc[:, gg].transpose(2, 0, 1)
                cbds.append(o.reshape(128, 1024))
        m[p + "lamp"] = ca(np.concatenate(lamp, axis=1), dtype=np.float32)
        m[p + "lamr"] = ca(np.concatenate(lamr, axis=1), dtype=np.float32)
        m[p + "bT"] = ca(np.concatenate(bTs, axis=1))
        m[p + "cbd"] = ca(np.concatenate(cbds, axis=1))
        m[p + "gssd"] = ca(inputs['ssd_norm_gain'][l].reshape(8, 128).T)
        m[p + "s5d"] = ca(inputs['s5_d'][l].reshape(8, 128).T)
        m[p + "wglu"] = inputs['s5_w_glu'][l]
        m[p + "wbs"] = inputs['w_branch_ssd'][l]
        m[p + "wbm"] = inputs['w_branch_mla'][l]
        m[p + "wb5"] = inputs['w_branch_s5'][l]
        m[p + "wo"] = inputs['w_out'][l]
        m[p + "wr"] = inputs['moe_router'][l]
        m[p + "wg"] = inputs['moe_w_gate'][l]
        m[p + "wu"] = inputs['moe_w_up'][l]
        m[p + "wd"] = inputs['moe_w_down'][l]
    return {k: np.ascontiguousarray(v, dtype=np.float32) for k, v in m.items() if k in FUSED_INPUT_SPECS}


def kernel(**inputs):
    inputs = {k: np.asarray(v, dtype=np.float32) for k, v in inputs.items()}
    nc = build_fused()
    maps = [fused_inputs(inputs, b) for b in range(NB)]
    in_maps = [maps[c % NB] for c in range(NCORES_F)]
    res = run_bass_kernel_spmd(nc, in_maps, core_ids=list(range(NCORES_F)))
    return np.stack([res.results[b]["out"] for b in range(NB)], axis=0).astype(np.float32)
```
